# Optimizing a Trainium2 kernel written in Bass

```python
import math
import jax
import jax.numpy as jnp
from jax import lax
import numpy as np

D_MODEL = 1024
BATCH = 16
SEQ = 2048
DEPTH = 4

GRID_W = 64
CTX_LEN = 256
HEAD_DIM = 64
EPS = 1e-6
NEG_INF = -1e30

S5_CH = 256
S5_GROUP_CH = 16
S5_GROUPS = S5_CH // S5_GROUP_CH
S5_STATE = 64
NA_HEADS = 4
NA_DIM = NA_HEADS * HEAD_DIM
NA_KH = 8
NA_KW = 16
NA_QB = NA_KW
GLA_HEADS = 4
GLA_DK = 64
GLA_DV = 64
GLA_QK = GLA_HEADS * GLA_DK
GLA_V = GLA_HEADS * GLA_DV
GLA_RANK = 16
GLA_TAU = 16.0
GLA_CHUNK = 64
SWA_HEADS = 4
SWA_KV_HEADS = 2
SWA_Q = SWA_HEADS * HEAD_DIM
SWA_KV = SWA_KV_HEADS * HEAD_DIM
SWA_WINDOW = 128
SWA_BLOCK = 128
ROPE_BASE = 10000.0
D_FF = 2816
FFN_CONV = 3

IN_SIZES = (S5_CH, NA_DIM, NA_DIM, NA_DIM, GLA_QK, GLA_QK, GLA_V, GLA_RANK, GLA_RANK, GLA_V, SWA_Q, SWA_KV, SWA_KV)
D_IN = sum(IN_SIZES)
D_MIX = S5_CH + NA_DIM + GLA_V + SWA_Q

kernel_name = "hybrid_parallel_heads_diffusion_trunk"


def rmsnorm(x, g):
    xf = x.astype(jnp.float32)
    y = xf * lax.rsqrt(jnp.mean(xf * xf, axis=-1, keepdims=True) + EPS)
    return (y * g.astype(jnp.float32)).astype(x.dtype)


def split_in(p):
    out, off = [], 0
    for s in IN_SIZES:
        out.append(p[..., off:off + s])
        off += s
    return out


def joint_softmax(*logits):
    sizes = [t.shape[-1] for t in logits]
    p = jax.nn.softmax(jnp.concatenate([t.astype(jnp.float32) for t in logits], axis=-1), axis=-1)
    out, off = [], 0
    for s in sizes:
        out.append(p[..., off:off + s])
        off += s
    return out


def _linear_combine(e1, e2):
    a1, b1 = e1
    a2, b2 = e2
    return a2 * a1, a2 * b1 + b2


def s5_discretize(lam_re, lam_im, b_re, b_im, log_step):
    lam = lax.complex(lam_re.astype(jnp.float32), lam_im.astype(jnp.float32))
    step = jnp.exp(log_step.astype(jnp.float32))[:, None]
    lam_bar = jnp.exp(lam * step)
    b = lax.complex(b_re.astype(jnp.float32), b_im.astype(jnp.float32))
    b_bar = ((lam_bar - 1.0) / lam)[..., None] * b
    return lam_bar, b_bar


def s5_scan(u, lam_bar, b_bar, h0, reverse):
    bu = jnp.einsum("gpc,blgc->blgp", b_bar, u.astype(jnp.complex64))
    if h0 is not None:
        bu = bu.at[:, -1 if reverse else 0].add(lam_bar * h0)
    a = jnp.broadcast_to(lam_bar, bu.shape)
    _, h = lax.associative_scan(_linear_combine, (a, bu), reverse=reverse, axis=1)
    return h


def s5_mixer(u_c, u_l, lam_re, lam_im, b_re, b_im, c_re, c_im, log_step, d_skip, w_glu, ctx_out):
    bsz = u_l.shape[0]
    uc = u_c.astype(jnp.float32).reshape(bsz, -1, S5_GROUPS, S5_GROUP_CH)
    ul = u_l.astype(jnp.float32).reshape(bsz, -1, S5_GROUPS, S5_GROUP_CH)
    dsk = d_skip.astype(jnp.float32).reshape(S5_GROUPS, S5_GROUP_CH)
    y_l = dsk * ul
    y_c = dsk * uc if ctx_out else None
    for direction, reverse in ((0, False), (1, True)):
        lam_bar, b_bar = s5_discretize(lam_re[direction], lam_im[direction], b_re[direction], b_im[direction], log_step[direction])
        c_mat = lax.complex(c_re[direction].astype(jnp.float32), c_im[direction].astype(jnp.float32))
        h_c = s5_scan(uc, lam_bar, b_bar, None, reverse)
        h0 = h_c[:, 0] if reverse else h_c[:, -1]
        h_l = s5_scan(ul, lam_bar, b_bar, h0, reverse)
        y_l = y_l + jnp.einsum("gcp,blgp->blgc", c_mat, h_l).real
        if ctx_out:
            y_c = y_c + jnp.einsum("gcp,blgp->blgc", c_mat, h_c).real

    def glu(y):
        z = jax.nn.gelu(y.reshape(bsz, -1, S5_CH))
        return z * jax.nn.sigmoid(z @ w_glu.astype(jnp.float32))

    return (glu(y_c) if ctx_out else None), glu(y_l)


def dense_ctx_attention(q, k, v):
    s = jnp.einsum("bqhd,bkhd->bhqk", q, k).astype(jnp.float32) * HEAD_DIM ** -0.5
    p = jax.nn.softmax(s, axis=-1)
    return jnp.einsum("bhqk,bkhd->bqhd", p, v)


def na_mixer(q_c, k_c, v_c, q_l, k_l, v_l, rpb, rows, ctx_out):
    bsz, seq, _ = q_l.shape
    kh = min(NA_KH, rows)
    kw = NA_KW
    ncb = GRID_W // NA_QB
    kbw = 2 * kw
    scale = HEAD_DIM ** -0.5
    kc = k_c.reshape(bsz, -1, NA_HEADS, HEAD_DIM)
    vc = v_c.reshape(bsz, -1, NA_HEADS, HEAD_DIM)
    ql = q_l.reshape(bsz, rows, ncb, NA_QB, NA_HEADS, HEAD_DIM)
    kg = k_l.reshape(bsz, rows, GRID_W, NA_HEADS, HEAD_DIM)
    vg = v_l.reshape(bsz, rows, GRID_W, NA_HEADS, HEAD_DIM)
    r = jnp.arange(rows)
    row_idx = jnp.clip(r - kh // 2, 0, rows - kh)[:, None] + jnp.arange(kh)[None, :]
    q_col = jnp.arange(GRID_W).reshape(ncb, NA_QB)
    col_idx = jnp.clip(q_col[:, 0] - kw // 2, 0, GRID_W - kbw)[:, None] + jnp.arange(kbw)[None, :]
    k_blk = kg[:, row_idx[:, None, :, None], col_idx[None, :, None, :]]
    v_blk = vg[:, row_idx[:, None, :, None], col_idx[None, :, None, :]]
    win_start = jnp.clip(q_col - kw // 2, 0, GRID_W - kw)
    kcol = col_idx[:, None, :]
    in_win = (kcol >= win_start[..., None]) & (kcol < win_start[..., None] + kw)
    dr = row_idx - r[:, None] + (NA_KH - 1)
    dc = jnp.clip(kcol - q_col[..., None], -(kw - 1), kw - 1) + (NA_KW - 1)
    bias = rpb[:, dr[:, None, None, :, None], dc[None, :, :, None, :]]
    s_nb = jnp.einsum("brnqhd,brnkwhd->bhrnqkw", ql, k_blk).astype(jnp.float32) * scale + bias.astype(jnp.float32)
    s_nb = jnp.where(in_win[:, :, None, :], s_nb, NEG_INF)
    s_cx = jnp.einsum("brnqhd,bchd->bhrnqc", ql, kc).astype(jnp.float32) * scale
    p_nb, p_cx = joint_softmax(s_nb.reshape(s_nb.shape[:5] + (kh * kbw,)), s_cx)
    p_nb = p_nb.reshape(s_nb.shape)
    o = jnp.einsum("bhrnqkw,brnkwhd->brnqhd", p_nb, v_blk) + jnp.einsum("bhrnqc,bchd->brnqhd", p_cx, vc)
    o_l = o.reshape(bsz, seq, NA_DIM)
    o_c = None
    if ctx_out:
        qc = q_c.reshape(bsz, -1, NA_HEADS, HEAD_DIM)
        o_c = dense_ctx_attention(qc, kc, vc).reshape(bsz, -1, NA_DIM)
    return o_c, o_l


def gla_chunked(q, k, v, log_a, s0):
    bsz, L, H, dk = q.shape
    dv = v.shape[-1]
    n = L // GLA_CHUNK

    def chunks(t):
        return t.reshape(bsz, n, GLA_CHUNK, H, t.shape[-1]).transpose(0, 3, 1, 2, 4)

    qh, kh, vh, g = chunks(q), chunks(k), chunks(v), chunks(log_a)
    b = jnp.cumsum(g, axis=3)
    b_last = b[:, :, :, -1:]
    q_dec = qh * jnp.exp(b)
    k_dec = kh * jnp.exp(-b)
    k_end = kh * jnp.exp(b_last - b)
    causal = jnp.tril(jnp.ones((GLA_CHUNK, GLA_CHUNK), dtype=bool))
    a_intra = jnp.where(causal, jnp.einsum("bhnid,bhnjd->bhnij", q_dec, k_dec), 0.0)
    o_intra = jnp.einsum("bhnij,bhnjv->bhniv", a_intra, vh)
    ds = jnp.einsum("bhnjd,bhnjv->bhndv", k_end, vh)
    decay = jnp.exp(b_last[:, :, :, 0])
    if s0 is None:
        s0 = jnp.zeros((bsz, H, dk, dv), jnp.float32)

    def step(s, inp):
        dec, d = inp
        return dec[..., None] * s + d, s

    s_final, s_prev = lax.scan(step, s0, (jnp.moveaxis(decay, 2, 0), jnp.moveaxis(ds, 2, 0)))
    s_prev = jnp.moveaxis(s_prev, 0, 2)
    o_inter = jnp.einsum("bhnid,bhndv->bhniv", q_dec, s_prev)
    o = (o_intra + o_inter).transpose(0, 2, 3, 1, 4).reshape(bsz, L, H, dv)
    return o, s_final


def gla_prep(q, k, v, gf, gb, w_gate2, b_gate):
    bsz, L, _ = q.shape
    shp = (bsz, L, GLA_HEADS, GLA_DK)
    qh = q.astype(jnp.float32).reshape(shp) * GLA_DK ** -0.5
    kh = k.astype(jnp.float32).reshape(shp)
    vh = v.astype(jnp.float32).reshape(bsz, L, GLA_HEADS, GLA_DV)
    la_f = (jax.nn.log_sigmoid((gf @ w_gate2[0] + b_gate[0]).astype(jnp.float32)) / GLA_TAU).reshape(shp)
    la_b = (jax.nn.log_sigmoid((gb @ w_gate2[1] + b_gate[1]).astype(jnp.float32)) / GLA_TAU).reshape(shp)
    return qh, kh, vh, la_f, la_b


def _flip(t):
    return jnp.flip(t, axis=1)


def gla_mixer(q_c, k_c, v_c, gf_c, gb_c, r_c, q_l, k_l, v_l, gf_l, gb_l, r_l, w_gate2, b_gate, g_norm, ctx_out):
    qc, kc, vc, lfc, lbc = gla_prep(q_c, k_c, v_c, gf_c, gb_c, w_gate2, b_gate)
    ql, kl, vl, lfl, lbl = gla_prep(q_l, k_l, v_l, gf_l, gb_l, w_gate2, b_gate)
    o_cf, s_cf = gla_chunked(qc, kc, vc, lfc, None)
    o_cb, s_cb = gla_chunked(_flip(qc), _flip(kc), _flip(vc), _flip(lbc), None)
    o_lf, _ = gla_chunked(ql, kl, vl, lfl, s_cf)
    o_lb, _ = gla_chunked(_flip(ql), _flip(kl), _flip(vl), _flip(lbl), s_cb)

    def finish(o, r):
        bsz, L = o.shape[:2]
        return rmsnorm(o, g_norm).reshape(bsz, L, GLA_V) * jax.nn.silu(r.astype(jnp.float32))

    o_l = finish(o_lf + _flip(o_lb), r_l)
    o_c = finish(o_cf + _flip(o_cb), r_c) if ctx_out else None
    return o_c, o_l


def rotate_axis(t, pos):
    d = t.shape[-1]
    inv_freq = ROPE_BASE ** (-jnp.arange(0, d, 2, dtype=jnp.float32) / d)
    ang = pos.astype(jnp.float32)[:, None] * inv_freq[None, :]
    cos = jnp.cos(ang)[None, :, None, :]
    sin = jnp.sin(ang)[None, :, None, :]
    t1 = t[..., : d // 2].astype(jnp.float32)
    t2 = t[..., d // 2:].astype(jnp.float32)
    return jnp.concatenate([t1 * cos - t2 * sin, t1 * sin + t2 * cos], axis=-1).astype(t.dtype)


def rope_2d(t, pos_row, pos_col):
    half = t.shape[-1] // 2
    return jnp.concatenate([rotate_axis(t[..., :half], pos_row), rotate_axis(t[..., half:], pos_col)], axis=-1)


def swa_mixer(q_c, k_c, v_c, q_l, k_l, v_l, sink, pos_row, pos_col, ctx_out):
    bsz, seq, _ = q_l.shape
    grp = SWA_HEADS // SWA_KV_HEADS
    nb = seq // SWA_BLOCK
    scale = HEAD_DIM ** -0.5
    ql = rope_2d(q_l.reshape(bsz, seq, SWA_HEADS, HEAD_DIM), pos_row, pos_col)
    ql = ql.reshape(bsz, nb, SWA_BLOCK, SWA_KV_HEADS, grp, HEAD_DIM)
    kl = rope_2d(k_l.reshape(bsz, seq, SWA_KV_HEADS, HEAD_DIM), pos_row, pos_col)
    vl = v_l.reshape(bsz, seq, SWA_KV_HEADS, HEAD_DIM)

    def band(t):
        tp = jnp.pad(t, ((0, 0), (SWA_BLOCK, SWA_BLOCK), (0, 0), (0, 0)))
        tp = tp.reshape(bsz, nb + 2, SWA_BLOCK, SWA_KV_HEADS, HEAD_DIM)
        return jnp.concatenate([tp[:, :-2], tp[:, 1:-1], tp[:, 2:]], axis=2)

    k_band, v_band = band(kl), band(vl)
    qpos = jnp.arange(nb)[:, None] * SWA_BLOCK + jnp.arange(SWA_BLOCK)[None, :]
    kpos = (jnp.arange(nb)[:, None] - 1) * SWA_BLOCK + jnp.arange(3 * SWA_BLOCK)[None, :]
    valid = ((kpos[:, None, :] >= 0) & (kpos[:, None, :] < seq)
             & (jnp.abs(qpos[:, :, None] - kpos[:, None, :]) <= SWA_WINDOW))
    kc = k_c.reshape(bsz, -1, SWA_KV_HEADS, HEAD_DIM)
    vc = v_c.reshape(bsz, -1, SWA_KV_HEADS, HEAD_DIM)
    sink_f = sink.astype(jnp.float32)
    s_band = jnp.einsum("bnqhgd,bnkhd->bhgnqk", ql, k_band).astype(jnp.float32) * scale
    s_band = jnp.where(valid, s_band, NEG_INF)
    s_cx = jnp.einsum("bnqhgd,bchd->bhgnqc", ql, kc).astype(jnp.float32) * scale
    s_sink = jnp.broadcast_to(sink_f.reshape(SWA_KV_HEADS, grp, 1, 1, 1), s_cx.shape[:-1] + (1,))
    p_band, p_cx, _ = joint_softmax(s_band, s_cx, s_sink)
    o = jnp.einsum("bhgnqk,bnkhd->bnqhgd", p_band, v_band) + jnp.einsum("bhgnqc,bchd->bnqhgd", p_cx, vc)
    o_l = o.reshape(bsz, seq, SWA_Q)
    o_c = None
    if ctx_out:
        qc = q_c.reshape(bsz, -1, SWA_KV_HEADS, grp, HEAD_DIM)
        sc = jnp.einsum("bqhgd,bkhd->bhgqk", qc, kc).astype(jnp.float32) * scale
        sc_sink = jnp.broadcast_to(sink_f.reshape(SWA_KV_HEADS, grp, 1, 1), sc.shape[:-1] + (1,))
        pc, _ = joint_softmax(sc, sc_sink)
        o_c = jnp.einsum("bhgqk,bkhd->bqhgd", pc, vc).reshape(bsz, -1, SWA_Q)
    return o_c, o_l


def conv_ffn(h, w_up, conv_w, w_down):
    gate, val = jnp.split(h @ w_up, 2, axis=-1)
    gp = jnp.pad(gate, ((0, 0), (1, 1), (0, 0)))
    gate = gp[:, :-2] * conv_w[0] + gp[:, 1:-1] * conv_w[1] + gp[:, 2:] * conv_w[2]
    return (jax.nn.gelu(gate) * val) @ w_down


def token_mixing(pc, pl, rows, pos_row, pos_col, s5_lam_re, s5_lam_im, s5_b_re, s5_b_im, s5_c_re, s5_c_im,
                 s5_log_step, s5_d, s5_w_glu, na_rpb, gla_w_gate2, gla_b_gate, gla_g_norm, swa_sink, ctx_out):
    (a_c, naq_c, nak_c, nav_c, gq_c, gk_c, gv_c, gf_c, gb_c, gr_c, sq_c, sk_c, sv_c) = split_in(pc)
    (a_l, naq_l, nak_l, nav_l, gq_l, gk_l, gv_l, gf_l, gb_l, gr_l, sq_l, sk_l, sv_l) = split_in(pl)
    oa_c, oa_l = s5_mixer(a_c, a_l, s5_lam_re, s5_lam_im, s5_b_re, s5_b_im, s5_c_re, s5_c_im,
                          s5_log_step, s5_d, s5_w_glu, ctx_out)
    ob_c, ob_l = na_mixer(naq_c, nak_c, nav_c, naq_l, nak_l, nav_l, na_rpb, rows, ctx_out)
    oc_c, oc_l = gla_mixer(gq_c, gk_c, gv_c, gf_c, gb_c, gr_c, gq_l, gk_l, gv_l, gf_l, gb_l, gr_l,
                           gla_w_gate2, gla_b_gate, gla_g_norm, ctx_out)
    od_c, od_l = swa_mixer(sq_c, sk_c, sv_c, sq_l, sk_l, sv_l, swa_sink, pos_row, pos_col, ctx_out)
    dt = pl.dtype
    y_l = jnp.concatenate([oa_l.astype(dt), ob_l.astype(dt), oc_l.astype(dt), od_l.astype(dt)], axis=-1)
    y_c = None
    if ctx_out:
        y_c = jnp.concatenate([oa_c.astype(dt), ob_c.astype(dt), oc_c.astype(dt), od_c.astype(dt)], axis=-1)
    return y_c, y_l


def setup_inputs(seed: int = 0) -> dict:
    key = jax.random.key(seed)
    ks = jax.random.split(key, 29)
    f32 = jnp.float32
    L = DEPTH

    def nrm(k, shape, s=1.0):
        return s * jax.random.normal(k, shape, f32)

    n_idx = jnp.arange(S5_STATE, dtype=f32)
    return {
        "x": nrm(ks[0], (BATCH, SEQ, D_MODEL)),
        "c": nrm(ks[1], (BATCH, D_MODEL)),
        "ctx": nrm(ks[2], (BATCH, CTX_LEN, D_MODEL)),
        "c_ctx": nrm(ks[3], (D_MODEL,)),
        "w_mod": nrm(ks[4], (L, D_MODEL, 6 * D_MODEL), D_MODEL ** -0.5),
        "b_mod": nrm(ks[5], (L, 6 * D_MODEL), 0.02),
        "g_pre_mix": 1.0 + nrm(ks[6], (L, D_MODEL), 0.05),
        "g_post_mix": 1.0 + nrm(ks[7], (L, D_MODEL), 0.05),
        "g_pre_ffn": 1.0 + nrm(ks[8], (L, D_MODEL), 0.05),
        "g_post_ffn": 1.0 + nrm(ks[9], (L, D_MODEL), 0.05),
        "w_in": nrm(ks[10], (L, D_MODEL, D_IN), D_MODEL ** -0.5),
        "w_out": nrm(ks[11], (L, D_MIX, D_MODEL), D_MIX ** -0.5),
        "s5_lam_re": -0.5 + nrm(ks[12], (L, 2, S5_GROUPS, S5_STATE), 0.01),
        "s5_lam_im": math.pi * n_idx + nrm(ks[13], (L, 2, S5_GROUPS, S5_STATE), 0.01),
        "s5_b_re": nrm(ks[14], (L, 2, S5_GROUPS, S5_STATE, S5_GROUP_CH), (2 * S5_GROUP_CH) ** -0.5),
        "s5_b_im": nrm(ks[15], (L, 2, S5_GROUPS, S5_STATE, S5_GROUP_CH), (2 * S5_GROUP_CH) ** -0.5),
        "s5_c_re": nrm(ks[16], (L, 2, S5_GROUPS, S5_GROUP_CH, S5_STATE), S5_STATE ** -0.5),
        "s5_c_im": nrm(ks[17], (L, 2, S5_GROUPS, S5_GROUP_CH, S5_STATE), S5_STATE ** -0.5),
        "s5_log_step": jax.random.uniform(ks[18], (L, 2, S5_GROUPS), f32, math.log(1e-3), math.log(1e-1)),
        "s5_d": nrm(ks[19], (L, S5_CH)),
        "s5_w_glu": nrm(ks[20], (L, S5_CH, S5_CH), S5_CH ** -0.5),
        "na_rpb": nrm(ks[21], (L, NA_HEADS, 2 * NA_KH - 1, 2 * NA_KW - 1), 0.1),
        "gla_w_gate2": nrm(ks[22], (L, 2, GLA_RANK, GLA_QK), GLA_RANK ** -0.5),
        "gla_b_gate": nrm(ks[23], (L, 2, GLA_QK), 0.1),
        "gla_g_norm": 1.0 + nrm(ks[24], (L, GLA_DV), 0.05),
        "swa_sink": nrm(ks[25], (L, SWA_HEADS), 0.5),
        "ffn_w_up": nrm(ks[26], (L, D_MODEL, 2 * D_FF), D_MODEL ** -0.5),
        "ffn_conv": nrm(ks[27], (L, FFN_CONV, D_FF), FFN_CONV ** -0.5),
        "ffn_w_down": nrm(ks[28], (L, D_FF, D_MODEL), D_FF ** -0.5),
    }


def reference(x, c, ctx, c_ctx, w_mod, b_mod, g_pre_mix, g_post_mix, g_pre_ffn, g_post_ffn, w_in, w_out,
              s5_lam_re, s5_lam_im, s5_b_re, s5_b_im, s5_c_re, s5_c_im, s5_log_step, s5_d, s5_w_glu,
              na_rpb, gla_w_gate2, gla_b_gate, gla_g_norm, swa_sink, ffn_w_up, ffn_conv, ffn_w_down):
    seq = x.shape[1]
    rows = seq // GRID_W
    t = jnp.arange(seq)
    pos_row, pos_col = t // GRID_W, t % GRID_W
    silu_c = jax.nn.silu(c)
    silu_cc = jax.nn.silu(c_ctx)
    xl, xc = x, ctx
    for l in range(DEPTH):
        ctx_out = l < DEPTH - 1
        mod_l = (silu_c @ w_mod[l] + b_mod[l])[:, None, :]
        mod_c = (silu_cc @ w_mod[l] + b_mod[l])[None, None, :]
        sh_ml, sc_ml, gt_ml, sh_fl, sc_fl, gt_fl = jnp.split(mod_l, 6, axis=-1)
        sh_mc, sc_mc, gt_mc, sh_fc, sc_fc, gt_fc = jnp.split(mod_c, 6, axis=-1)
        hl = rmsnorm(xl, g_pre_mix[l]) * (1.0 + sc_ml) + sh_ml
        hc = rmsnorm(xc, g_pre_mix[l]) * (1.0 + sc_mc) + sh_mc
        y_c, y_l = token_mixing(hc @ w_in[l], hl @ w_in[l], rows, pos_row, pos_col,
                                s5_lam_re[l], s5_lam_im[l], s5_b_re[l], s5_b_im[l], s5_c_re[l], s5_c_im[l],
                                s5_log_step[l], s5_d[l], s5_w_glu[l], na_rpb[l], gla_w_gate2[l], gla_b_gate[l],
                                gla_g_norm[l], swa_sink[l], ctx_out)
        xl = xl + gt_ml * rmsnorm(y_l @ w_out[l], g_post_mix[l])
        hl = rmsnorm(xl, g_pre_ffn[l]) * (1.0 + sc_fl) + sh_fl
        xl = xl + gt_fl * rmsnorm(conv_ffn(hl, ffn_w_up[l], ffn_conv[l], ffn_w_down[l]), g_post_ffn[l])
        if ctx_out:
            xc = xc + gt_mc * rmsnorm(y_c @ w_out[l], g_post_mix[l])
            hc = rmsnorm(xc, g_pre_ffn[l]) * (1.0 + sc_fc) + sh_fc
            xc = xc + gt_fc * rmsnorm(conv_ffn(hc, ffn_w_up[l], ffn_conv[l], ffn_w_down[l]), g_post_ffn[l])
    return xl
```

```python
import contextlib
import math
import numpy as np
import ml_dtypes
import concourse.bass as bass
import concourse.mybir as mybir
from concourse.bass_utils import run_bass_kernel_spmd

F32 = mybir.dt.float32
BF16 = mybir.dt.bfloat16
ALU = mybir.AluOpType
AF = mybir.ActivationFunctionType

ENG = ("pe", "act", "dve", "pool", "sp")
NDMA = 24


class P:
    def __init__(self, nc, same_eng_sync=True):
        self.nc = nc
        self.ops = {e: [] for e in ENG}
        self.cnt = {e: 0 for e in ENG}
        self.waited = {e: {} for e in ENG}
        self.last_w = {}
        self.readers = {}
        self.dma_next = 0
        self.dma_cnt = [0] * NDMA
        self.dma_last_tok = [None] * NDMA
        self.same = same_eng_sync
        self.out_toks = []
        self.bar = []

    def barrier(self):
        self.bar = [("e", e, self.cnt[e]) for e in ENG if self.cnt[e]] + \
                   [("d", k, self.dma_cnt[k]) for k in range(NDMA) if self.dma_cnt[k]]

    def _deps(self, eng, reads, writes):
        deps = list(self.bar)
        for k in reads:
            t = self.last_w.get(k)
            if t is not None:
                deps.append(t)
        for k in writes:
            t = self.last_w.get(k)
            if t is not None:
                deps.append(t)
            deps.extend(self.readers.get(k, ()))
        return deps

    def _waits(self, eng, deps):
        w = self.waited[eng]
        best = {}
        for t in deps:
            if t[0] == "e":
                _, e2, idx = t
                if e2 == eng and (not self.same or eng == "pe"):
                    continue
                key = e2
            else:
                key = ("d", t[1])
            if w.get(key, 0) >= t[2]:
                continue
            if key not in best or best[key][2] < t[2]:
                best[key] = t
        for key, t in best.items():
            w[key] = t[2]
        return list(best.values())

    def _record(self, tok, reads, writes):
        for k in reads:
            lst = self.readers.setdefault(k, [])
            lst.append(tok)
            if len(lst) > 64:
                best = {}
                for t in lst:
                    kk = t[:2]
                    if kk not in best or best[kk][2] < t[2]:
                        best[kk] = t
                self.readers[k] = list(best.values())
        for k in writes:
            self.last_w[k] = tok
            self.readers[k] = []

    def op(self, eng, fn, reads=(), writes=()):
        deps = self._deps(eng, reads, writes)
        waits = self._waits(eng, deps)
        self.cnt[eng] += 1
        tok = ("e", eng, self.cnt[eng])
        self.ops[eng].append((waits, fn, ("e", eng)))
        self._record(tok, reads, writes)
        return tok

    def dma(self, q, fn, reads=(), writes=(), is_out=False):
        k = self.dma_next
        self.dma_next = (self.dma_next + 1) % NDMA
        deps = self._deps(q, reads, writes)
        if self.dma_last_tok[k] is not None:
            deps.append(self.dma_last_tok[k])
        waits = self._waits(q, deps)
        self.dma_cnt[k] += 16
        tok = ("d", k, self.dma_cnt[k])
        self.dma_last_tok[k] = tok
        self.ops[q].append((waits, fn, ("d", k)))
        self._record(tok, reads, writes)
        if is_out:
            self.out_toks.append(tok)
        return tok

    def emit(self):
        nc = self.nc
        with contextlib.ExitStack() as es:
            esem = {e: es.enter_context(nc.semaphore("s_" + e)) for e in ENG}
            dsem = [es.enter_context(nc.semaphore("d%d" % i)) for i in range(NDMA)]
            fin = list(self.out_toks)
            for e in ENG:
                if self.cnt[e]:
                    fin.append(("e", e, self.cnt[e]))
            for k in range(NDMA):
                if self.dma_cnt[k]:
                    fin.append(("d", k, self.dma_cnt[k]))
            block = es.enter_context(nc.Block())

            def run(eng_name, eng):
                for waits, fn, kind in self.ops[eng_name]:
                    for t in waits:
                        if t[0] == "e":
                            eng.wait_ge(esem[t[1]], t[2])
                        else:
                            eng.wait_ge(dsem[t[1]], t[2])
                    ins = fn(eng)
                    if kind[0] == "e":
                        ins.then_inc(esem[kind[1]], 1)
                    else:
                        ins.then_inc(dsem[kind[1]], 16)

            @block.tensor
            def _(e):
                run("pe", e)

            @block.scalar
            def _(e):
                run("act", e)

            @block.vector
            def _(e):
                run("dve", e)

            @block.gpsimd
            def _(e):
                run("pool", e)

            @block.sync
            def _(e):
                run("sp", e)
                for t in fin:
                    if t[0] == "e":
                        if t[1] != "sp":
                            e.wait_ge(esem[t[1]], t[2])
                    else:
                        e.wait_ge(dsem[t[1]], t[2])


NT = 2304
NCX = 256
NLAT = 2048
D = 1024
DEPTH = 4
LD = [4]
DFF = 2816
EPS = 1e-6
TB = [(0, 256)] + [(256 + 512 * i, 256 + 512 * (i + 1)) for i in range(4)]
C_A, C_NAQ, C_NAK, C_NAV = 0, 256, 512, 768
C_GQ, C_GK, C_GV, C_GF, C_GB, C_GR = 1024, 1280, 1536, 1792, 1808, 1824
C_SQ, C_SK, C_SV = 2080, 2336, 2464
C_SQP, C_SKD, C_SKDP, NEXT = 2592, 2848, 3104, 3360


def _rope_perm(nh):
    idx = np.arange(nh * 64).reshape(nh, 4, 16)
    return idx[:, [1, 0, 3, 2], :].reshape(-1)


def _rope_tables():
    cos = np.ones((64, NT), np.float32)
    sin = np.zeros((64, NT), np.float32)
    t = np.arange(NLAT)
    pos = (t // 64, t % 64)
    inv = 10000.0 ** (-np.arange(0, 32, 2, dtype=np.float32) / 32)
    for half in range(2):
        ang = pos[half].astype(np.float32)[None, :] * inv[:, None]
        c, s = np.cos(ang), np.sin(ang)
        b = 32 * half
        cos[b:b + 16, NCX:] = c
        cos[b + 16:b + 32, NCX:] = c
        sin[b:b + 16, NCX:] = -s
        sin[b + 16:b + 32, NCX:] = s
    return np.concatenate([cos, cos], 0), np.concatenate([sin, sin], 0)


class Ctx:
    pass


_UN = [0]


def UN():
    _UN[0] += 1
    return "t%d_" % _UN[0]


def build(nc, dbg=None, nlayers=DEPTH, nbatch=2, mixers=("s5", "na", "gla", "swa"), ldim=DEPTH, full_out=False):
    LD[0] = ldim
    _UN[0] = 0
    p = P(nc)
    g = Ctx()
    g.p, g.nc = p, nc
    dram = {}

    def din(name, shape, dt=F32):
        dram[name] = nc.dram_tensor(name, list(shape), dt, kind="ExternalInput").ap()
        return dram[name]

    xcat = din("xcat", [2, NT, D])
    cT = din("cT", [128, 8, 3])
    w_mod = din("w_mod", [LD[0], D, 6 * D])
    b_modT = din("b_modT", [128, LD[0], 48])
    gvec = din("gvec", [128, LD[0], 4, 8])
    w_in = din("w_in_ext", [LD[0], D, NEXT])
    w_out = din("w_out", [LD[0], D, D])
    w_up = din("ffn_w_up", [LD[0], D, 2 * DFF])
    w_down = din("ffn_w_down", [LD[0], DFF, D])
    convT = din("convT", [128, LD[0], 3, 22])
    ropeC = din("ropeC", [128, NT], BF16)
    ropeS = din("ropeS", [128, NT], BF16)
    maskAB = din("maskAB", [128, 2, 128], BF16)
    identf = din("identf", [128, 128])
    sinkT = din("sinkT", [128, LD[0], 2])
    na_exp = din("na_exp", [LD[0], 4, 64, 15 * 64])
    mcol = din("mcol", [128, 15 * 64])
    din("tri", [128, 2, 128])
    din("bd64", [128, 128], BF16)
    din("gnT", [128, LD[0]])
    din("gla_w_gate2", [LD[0], 2, 16, 256])
    din("gla_b_gate", [LD[0], 2, 256])
    din("lamT", [128, LD[0], 2, 32])
    din("stepT", [128, LD[0], 32])
    din("dskT", [128, LD[0], 2])
    din("sgn", [128, 2])
    din("jmat", [128, 128])
    din("s5_w_glu", [LD[0], 256, 256])
    for nm in ("s5B1", "s5B2", "s5C1", "s5C2"):
        din(nm, [LD[0], 2, 16, 128, 128])
        dram[nm + "_bf"] = nc.dram_tensor(nm + "_bf", [LD[0], 2, 16, 128, 128], BF16).ap()
    g.dram = dram
    out = nc.dram_tensor("out", [2, NT if full_out else NLAT, D], F32, kind="ExternalOutput").ap()
    dbg_aps = {}
    if dbg:
        for name, shape in dbg.items():
            dbg_aps[name] = nc.dram_tensor("dbg_" + name, list(shape), F32, kind="ExternalOutput").ap()

    w_in_bf = nc.dram_tensor("w_in_bf", [LD[0], D, NEXT], BF16).ap()
    w_out_bf = nc.dram_tensor("w_out_bf", [LD[0], D, D], BF16).ap()
    w_up_bf = nc.dram_tensor("w_up_bf", [LD[0], D, 2 * DFF], BF16).ap()
    w_down_bf = nc.dram_tensor("w_down_bf", [LD[0], DFF, D], BF16).ap()

    def wkeys(key, l, n):
        return ["%s%d_%d" % (key, l, i) for i in range(n)]
    g.wkeys = wkeys
    g._uid = [0]

    def OP(eng, method, r=(), w=(), **kw):
        return p.op(eng, lambda e, kw=kw, method=method: getattr(e, method)(**kw), reads=r, writes=w)

    def MM(out_, lhsT, rhs, start, stop, r, w):
        return p.op("pe", lambda e: e.matmul(out_, lhsT=lhsT, rhs=rhs, start=start, stop=stop), reads=r, writes=w)

    def DMA(q, out_, in_, r=(), w=(), is_out=False, **kw):
        return p.dma(q, lambda e, kw=kw: e.dma_start(out=out_, in_=in_, **kw), reads=r, writes=w, is_out=is_out)

    g.OP, g.MM, g.DMA = OP, MM, DMA

    for l in range(nlayers):
        for (src, dst, rows, key) in ((w_in, w_in_bf, D, "w_in_bf"), (w_out, w_out_bf, D, "w_out_bf"),
                                      (w_up, w_up_bf, D, "w_up_bf"), (w_down, w_down_bf, DFF, "w_down_bf")):
            for r0 in range(0, rows, 128):
                DMA("pool", dst[l, r0:r0 + 128, :], src[l, r0:r0 + 128, :], w=["%s%d_%d" % (key, l, r0 // 128)],
                    max_dma_last_dim=4096)

    for l in range(nlayers):
        for nm in ("s5B1", "s5B2", "s5C1", "s5C2"):
            for d in range(2):
                DMA("pool", dram[nm + "_bf"][l, d].rearrange("g p s -> (g p) s"), dram[nm][l, d].rearrange("g p s -> (g p) s"),
                    w=["%s_bf%d" % (nm, l)] if d == 1 else ["%s_bf%d_d0" % (nm, l)], max_dma_last_dim=4096)

    es = contextlib.ExitStack()

    def sb(name, shape, dt):
        return es.enter_context(nc.sbuf_tensor(UN() + name, list(shape), dt))

    MODT = sb("MODT", [128, LD[0], 48, 3], F32)
    DER = sb("DER", [128, LD[0], 3, 6, 8], F32)
    GV = sb("GV", [128, LD[0], 4, 8], F32)
    CONV = sb("CONV", [128, LD[0], 3, 22], F32)
    ONESB = sb("ONESB", [128, 128], BF16)
    IDF = sb("IDF", [128, 128], F32)
    MAB = sb("MAB", [128, 2, 128], BF16)
    ESINK = sb("ESINK", [128, LD[0], 2], F32)
    PS = [es.enter_context(nc.psum_tensor("ps%d" % i, [128, 512], F32)) for i in range(8)]
    g.PS = PS

    OP("dve", "memset", ap=ONESB[:], constant=1.0, w=["ONESB"])
    DMA("sp", IDF[:], identf, w=["IDF"])
    DMA("sp", MAB[:], maskAB, w=["MAB"])
    DMA("sp", GV[:], gvec, w=["GV"])
    DMA("sp", CONV[:], convT, w=["CONV"])
    DMA("sp", ESINK[:], sinkT, w=["ESINK"])
    OP("act", "activation", out=ESINK[:], in_=ESINK[:], func=AF.Exp, r=["ESINK"], w=["ESINK"])

    with contextlib.ExitStack() as s1:
        SCT = s1.enter_context(nc.sbuf_tensor(UN() + "SCT", [128, 8, 3], F32))
        BM = s1.enter_context(nc.sbuf_tensor(UN() + "BM", [128, LD[0], 48], F32))
        WM = [s1.enter_context(nc.sbuf_tensor(UN() + "WM%d" % i, [128, 8, 512], F32)) for i in range(2)]
        DMA("sp", SCT[:], cT, w=["SCT"])
        DMA("sp", BM[:], b_modT, w=["BM"])
        OP("act", "activation", out=SCT[:], in_=SCT[:], func=AF.Silu, r=["SCT"], w=["SCT"])
        it = 0
        for l in range(nlayers):
            for cg in range(12):
                wb = WM[it % 2]
                wk = "WM%d" % (it % 2)
                it += 1
                DMA("sp", wb[:], w_mod[l, :, cg * 512:(cg + 1) * 512].rearrange("(k p) c -> p k c", p=128), w=[wk])
                for fc in range(4):
                    f = cg * 4 + fc
                    for k in range(8):
                        MM(PS[0][:, f * 3:f * 3 + 3], wb[:, k, fc * 128:(fc + 1) * 128], SCT[:, k, :], k == 0, k == 7,
                           [wk, "SCT"], ["ps0"])
            for j in range(3):
                OP("dve", "tensor_tensor", out=MODT[:, l, :, j], in0=PS[0][:, 0:144].rearrange("p (f j) -> p f j", j=3)[:, :, j],
                   in1=BM[:, l, :], op=ALU.add, r=["ps0", "BM"], w=["MODT"])
            for j in range(3):
                OP("dve", "scalar_tensor_tensor", out=DER[:, l, j, 0, :], in0=MODT[:, l, 8:16, j], scalar=1.0, in1=GV[:, l, 0, :],
                   op0=ALU.add, op1=ALU.mult, r=["MODT", "GV"], w=["DER"])
                OP("dve", "tensor_copy", out=DER[:, l, j, 1, :], in_=MODT[:, l, 0:8, j], r=["MODT"], w=["DER"])
                OP("dve", "tensor_tensor", out=DER[:, l, j, 2, :], in0=MODT[:, l, 16:24, j], in1=GV[:, l, 1, :], op=ALU.mult,
                   r=["MODT", "GV"], w=["DER"])
                OP("dve", "scalar_tensor_tensor", out=DER[:, l, j, 3, :], in0=MODT[:, l, 32:40, j], scalar=1.0, in1=GV[:, l, 2, :],
                   op0=ALU.add, op1=ALU.mult, r=["MODT", "GV"], w=["DER"])
                OP("dve", "tensor_copy", out=DER[:, l, j, 4, :], in_=MODT[:, l, 24:32, j], r=["MODT"], w=["DER"])
                OP("dve", "tensor_tensor", out=DER[:, l, j, 5, :], in0=MODT[:, l, 40:48, j], in1=GV[:, l, 3, :], op=ALU.mult,
                   r=["MODT", "GV"], w=["DER"])
    p.barrier()

    X = sb("X", [128, 8, NT], F32)
    g.X, g.DER, g.ONESB, g.MAB, g.ESINK, g.CONV = X, DER, ONESB, MAB, ESINK, CONV
    g.w_in_bf, g.w_out_bf, g.w_up_bf, g.w_down_bf = w_in_bf, w_out_bf, w_up_bf, w_down_bf
    g.ropeC, g.ropeS, g.na_exp, g.mcol = ropeC, ropeS, na_exp, mcol
    g.dbg_aps = dbg_aps

    def dump(name, ap, keys):
        if name in dbg_aps:
            DMA("pool", dbg_aps[name], ap, r=keys, is_out=True, max_dma_last_dim=2048)
    g.dump = dump

    for bi in range(nbatch):
        with contextlib.ExitStack() as s2:
            XS = [s2.enter_context(nc.sbuf_tensor(UN() + "XS%d" % i, [128, D], F32)) for i in range(2)]
            for tt in range(18):
                xs, xk = XS[tt % 2], "XS%d" % (tt % 2)
                DMA("sp", xs[:], xcat[bi, tt * 128:(tt + 1) * 128, :], w=[xk])
                for hh in range(2):
                    bank = PS[hh]
                    for kk in range(4):
                        k = hh * 4 + kk
                        p.op("pe", lambda e, o=bank[:, kk * 128:(kk + 1) * 128], i=xs[:, k * 128:(k + 1) * 128]:
                             e.transpose(out=o, in_=i, identity=IDF[:]), reads=[xk, "IDF"], writes=["ps%d" % hh])
                    if hh == 0:
                        OP("act", "activation", out=X[:, 0:4, tt * 128:(tt + 1) * 128],
                           in_=bank[:, :].rearrange("p (k t) -> p k t", t=128), func=AF.Copy, r=["ps0"], w=["X"])
                    else:
                        OP("dve", "tensor_copy", out=X[:, 4:8, tt * 128:(tt + 1) * 128],
                           in_=bank[:, :].rearrange("p (k t) -> p k t", t=128), r=["ps1"], w=["X"])
        p.barrier()
        for l in range(nlayers):
            layer(g, bi, l, (l == DEPTH - 1) and not full_out, mixers)
        with contextlib.ExitStack() as s3:
            OS_ = [s3.enter_context(nc.sbuf_tensor(UN() + "OST%d" % i, [128, D], F32)) for i in range(2)]
            for tt in range(18 if full_out else 16):
                ot, ok = OS_[tt % 2], "OST%d" % (tt % 2)
                t0 = (0 if full_out else NCX) + tt * 128
                for hh in range(2):
                    bank = PS[hh]
                    for kk in range(4):
                        k = hh * 4 + kk
                        p.op("pe", lambda e, o=bank[:, kk * 128:(kk + 1) * 128], i=X[:, k, t0:t0 + 128]:
                             e.transpose(out=o, in_=i, identity=IDF[:]), reads=["X", "IDF"], writes=["ps%d" % hh])
                    if hh == 0:
                        OP("act", "activation", out=ot[:, 0:512], in_=bank[:, :], func=AF.Copy, r=["ps0"], w=[ok])
                    else:
                        OP("dve", "tensor_copy", out=ot[:, 512:1024], in_=bank[:, :], r=["ps1"], w=[ok])
                DMA("sp", out[bi, tt * 128:(tt + 1) * 128, :], ot[:], r=[ok], is_out=True)
        p.barrier()

    p.emit()
    es.close()
    return nc


def rms_stats(g, src_fn, nk, w, SQ, RS, psb, src_keys):
    OP, MM = g.OP, g.MM
    for k in range(nk):
        OP("act", "activation", out=SQ[:, k, :w], in_=src_fn(k), func=AF.Square, r=src_keys, w=["SQ"])
    for k in range(nk):
        MM(g.PS[psb][:, :w], g.ONESB[:], SQ[:, k, :w], k == 0, k == nk - 1, ["SQ", "ONESB"], ["ps%d" % psb])
    OP("act", "activation", out=RS[:, :w], in_=g.PS[psb][:, :w], func=AF.Sqrt, scale=1.0 / D, bias=g.EPSC[:, 0:1],
       r=["ps%d" % psb, "EPSC"], w=["RS"])
    OP("dve", "reciprocal", out=RS[:, :w], in_=RS[:, :w], r=["RS"], w=["RS"])


def layer(g, bi, l, last, mixers):
    nc, p, OP, MM, DMA = g.nc, g.p, g.OP, g.MM, g.DMA
    X, DER, PS = g.X, g.DER, g.PS
    with contextlib.ExitStack() as sl:
        def sb(name, shape, dt):
            return sl.enter_context(nc.sbuf_tensor(UN() + name, list(shape), dt))
        HT = sb("HT", [128, 8, NT], BF16)
        YT = sb("YT", [128, 8, NT], BF16)
        g.HT, g.YT = HT, YT
        EPSC = sb("EPSC", [128, 1], F32)
        g.EPSC = EPSC
        OP("dve", "memset", ap=EPSC[:], constant=EPS, w=["EPSC"])
        with contextlib.ExitStack() as sa:
            SQ = sa.enter_context(nc.sbuf_tensor(UN() + "SQ", [128, 8, 512], BF16))
            RS = sa.enter_context(nc.sbuf_tensor(UN() + "RS", [128, 512], F32))
            TMP = [sa.enter_context(nc.sbuf_tensor(UN() + "TMPa%d" % i, [128, 512], F32)) for i in range(2)]
            for (a, b) in TB:
                w = b - a
                j = 2 if a < NCX else bi
                rms_stats(g, lambda k: X[:, k, a:b], 8, w, SQ, RS, 0, ["X"])
                for k in range(8):
                    tm, tk = TMP[k % 2], "TMPa%d" % (k % 2)
                    OP("dve", "tensor_tensor", out=tm[:, :w], in0=X[:, k, a:b], in1=RS[:, :w], op=ALU.mult,
                       r=["X", "RS"], w=[tk])
                    OP("act", "activation", out=HT[:, k, a:b], in_=tm[:, :w], func=AF.Identity,
                       scale=DER[:, l, j, 0, k:k + 1], bias=DER[:, l, j, 1, k:k + 1], r=[tk, "DER"], w=["HT"])
        p.barrier()
        g.dump("HT%d_%d" % (bi, l), HT[:], ["HT"])
        OP("pool", "memset", ap=YT[:], constant=0.0, w=["YT"])
        if "s5" in mixers:
            s5_mixer(g, bi, l)
            p.barrier()
        if "swa" in mixers:
            attn_mixer(g, bi, l, "swa")
            p.barrier()
        if "na" in mixers:
            attn_mixer(g, bi, l, "na")
            p.barrier()
        if "gla" in mixers:
            gla_mixer(g, bi, l)
            p.barrier()
        g.dump("YT%d_%d" % (bi, l), YT[:], ["YT"])
        with contextlib.ExitStack() as sc:
            WO = sc.enter_context(nc.sbuf_tensor(UN() + "WO", [128, 8, D], BF16))
            OS_ = sc.enter_context(nc.sbuf_tensor(UN() + "OS", [128, 8, 512], F32))
            SQ = sc.enter_context(nc.sbuf_tensor(UN() + "SQ", [128, 8, 512], BF16))
            RS = sc.enter_context(nc.sbuf_tensor(UN() + "RS", [128, 512], F32))
            TMP = [sc.enter_context(nc.sbuf_tensor(UN() + "TMPc%d" % i, [128, 512], F32)) for i in range(2)]
            DMA("sp", WO[:], g.w_out_bf[l].rearrange("(k p) c -> p k c", p=128), r=g.wkeys("w_out_bf", l, 8), w=["WO"])
            for (a, b) in TB:
                if last and a < NCX:
                    continue
                w = b - a
                j = 2 if a < NCX else bi
                for dc in range(8):
                    bank = 1 + dc % 2
                    for k in range(8):
                        MM(PS[bank][:, :w], WO[:, k, dc * 128:(dc + 1) * 128], YT[:, k, a:b], k == 0, k == 7,
                           ["WO", "YT"], ["ps%d" % bank])
                    OP("dve", "tensor_copy", out=OS_[:, dc, :w], in_=PS[bank][:, :w], r=["ps%d" % bank], w=["OS%d" % dc])
                rms_stats(g, lambda k: OS_[:, k, :w], 8, w, SQ, RS, 0, ["OS%d" % k for k in range(8)])
                for k in range(8):
                    tm, tk = TMP[k % 2], "TMPc%d" % (k % 2)
                    OP("pool", "tensor_tensor", out=tm[:, :w], in0=OS_[:, k, :w], in1=RS[:, :w], op=ALU.mult,
                       r=["OS%d" % k, "RS"], w=[tk])
                    OP("dve", "scalar_tensor_tensor", out=X[:, k, a:b], in0=tm[:, :w], scalar=DER[:, l, j, 2, k:k + 1],
                       in1=X[:, k, a:b], op0=ALU.mult, op1=ALU.add, r=[tk, "DER", "X"], w=["X"])
    p.barrier()
    g.dump("X1_%d_%d" % (bi, l), X[:], ["X"])
    ffn(g, bi, l, last)
    p.barrier()
    g.dump("X2_%d_%d" % (bi, l), X[:], ["X"])


def ffn_blocks():
    blks = [(0, NCX, 0, NCX)]
    for i in range(5):
        oa = NCX + 410 * i
        ob = min(NCX + 410 * (i + 1), NT)
        blks.append((max(oa - 1, NCX), min(ob + 1, NT), oa, ob))
    return blks


def ffn(g, bi, l, last):
    nc, p, OP, MM, DMA = g.nc, g.p, g.OP, g.MM, g.DMA
    X, DER, PS = g.X, g.DER, g.PS
    with contextlib.ExitStack() as sf:
        def sb(name, shape, dt):
            return sf.enter_context(nc.sbuf_tensor(UN() + name, list(shape), dt))
        EPSC = sb("EPSC", [128, 1], F32)
        g.EPSC = EPSC
        OP("dve", "memset", ap=EPSC[:], constant=EPS, w=["EPSC"])
        HBs = [sb("HB%d" % i, [128, 8, 512], BF16) for i in range(2)]
        GB = sb("GB", [128, 22, 512], BF16)
        SQ = sb("SQ", [128, 8, 512], BF16)
        RS = sb("RS", [128, 512], F32)
        OS_ = sb("OS", [128, 8, 512], F32)
        TMP = [sb("TMPf%d" % i, [128, 512], F32) for i in range(2)]
        GS = [sb("GS%d" % i, [128, 514], F32) for i in range(2)]
        CV = [sb("CV%d" % i, [128, 512], F32) for i in range(2)]
        U1 = [sb("U1%d" % i, [128, 512], F32) for i in range(2)]
        WU = [sb("WU%d" % i, [128, 8, 256], BF16) for i in range(3)]
        WD = [sb("WD%d" % i, [128, 22, 128], BF16) for i in range(2)]
        wu_it = 0
        wd_it = 0
        blks = [bk for bk in ffn_blocks() if not (last and bk[0] < NCX)]

        def make_hb(bidx):
            (ca, cb, oa, ob) = blks[bidx]
            w = cb - ca
            j = 2 if ca < NCX else bi
            HB, hbk = HBs[bidx % 2], "HB%d" % (bidx % 2)
            rms_stats(g, lambda k: X[:, k, ca:cb], 8, w, SQ, RS, 0, ["X"])
            for k in range(8):
                tm, tk = TMP[k % 2], "TMPf%d" % (k % 2)
                OP("dve", "tensor_tensor", out=tm[:, :w], in0=X[:, k, ca:cb], in1=RS[:, :w], op=ALU.mult,
                   r=["X", "RS"], w=[tk])
                OP("act", "activation", out=HB[:, k, :w], in_=tm[:, :w], func=AF.Identity,
                   scale=DER[:, l, j, 3, k:k + 1], bias=DER[:, l, j, 4, k:k + 1], r=[tk, "DER"], w=[hbk])

        make_hb(0)
        for bidx, (ca, cb, oa, ob) in enumerate(blks):
            w = cb - ca
            wo = ob - oa
            off = oa - ca
            j = 2 if ca < NCX else bi
            HB, hbk = HBs[bidx % 2], "HB%d" % (bidx % 2)
            if bidx + 1 < len(blks):
                make_hb(bidx + 1)
            for jc in range(22):
                wu, wuk = WU[wu_it % 3], "WU%d" % (wu_it % 3)
                wu_it += 1
                DMA("sp", wu[:, :, 0:128], g.w_up_bf[l, :, jc * 128:(jc + 1) * 128].rearrange("(k p) c -> p k c", p=128),
                    r=g.wkeys("w_up_bf", l, 8), w=[wuk])
                DMA("sp", wu[:, :, 128:256],
                    g.w_up_bf[l, :, DFF + jc * 128:DFF + (jc + 1) * 128].rearrange("(k p) c -> p k c", p=128),
                    r=g.wkeys("w_up_bf", l, 8), w=[wuk])
                bg, bv = 1 + 2 * (jc % 2), 2 + 2 * (jc % 2)
                for k in range(8):
                    MM(PS[bg][:, :w], wu[:, k, 0:128], HB[:, k, :w], k == 0, k == 7, [wuk, hbk], ["ps%d" % bg])
                for k in range(8):
                    MM(PS[bv][:, :w], wu[:, k, 128:256], HB[:, k, :w], k == 0, k == 7, [wuk, hbk], ["ps%d" % bv])
                gs, gk = GS[jc % 2], "GS%d" % (jc % 2)
                cv, ck = CV[jc % 2], "CV%d" % (jc % 2)
                u1, uk = U1[jc % 2], "U1%d" % (jc % 2)
                OP("pool", "memset", ap=gs[:, 0:1], constant=0.0, w=[gk])
                OP("pool", "memset", ap=gs[:, w + 1:w + 2], constant=0.0, w=[gk])
                OP("act", "activation", out=gs[:, 1:w + 1], in_=PS[bg][:, :w], func=AF.Copy, r=["ps%d" % bg], w=[gk])
                s = 1 + off
                OP("pool", "tensor_scalar", out=cv[:, :wo], in0=gs[:, s - 1:s - 1 + wo], scalar1=g.CONV[:, l, 0, jc:jc + 1],
                   scalar2=None, op0=ALU.mult, r=[gk, "CONV"], w=[ck])
                OP("dve", "scalar_tensor_tensor", out=cv[:, :wo], in0=gs[:, s:s + wo], scalar=g.CONV[:, l, 1, jc:jc + 1],
                   in1=cv[:, :wo], op0=ALU.mult, op1=ALU.add, r=[gk, "CONV", ck], w=[ck])
                OP("dve", "scalar_tensor_tensor", out=cv[:, :wo], in0=gs[:, s + 1:s + 1 + wo], scalar=g.CONV[:, l, 2, jc:jc + 1],
                   in1=cv[:, :wo], op0=ALU.mult, op1=ALU.add, r=[gk, "CONV", ck], w=[ck])
                OP("act", "activation", out=u1[:, :wo], in_=cv[:, :wo], func=AF.Gelu_apprx_tanh, r=[ck], w=[uk])
                OP("dve", "tensor_tensor", out=GB[:, jc, :wo], in0=PS[bv][:, off:off + wo], in1=u1[:, :wo], op=ALU.mult,
                   r=["ps%d" % bv, uk], w=["GB"])
            for dc in range(8):
                wd, wdk = WD[wd_it % 2], "WD%d" % (wd_it % 2)
                wd_it += 1
                DMA("sp", wd[:], g.w_down_bf[l, :, dc * 128:(dc + 1) * 128].rearrange("(k p) c -> p k c", p=128),
                    r=g.wkeys("w_down_bf", l, 22), w=[wdk])
                bank = 5 + dc % 2
                for jc in range(22):
                    MM(PS[bank][:, :wo], wd[:, jc, :], GB[:, jc, :wo], jc == 0, jc == 21, [wdk, "GB"], ["ps%d" % bank])
                OP("dve", "tensor_copy", out=OS_[:, dc, :wo], in_=PS[bank][:, :wo], r=["ps%d" % bank], w=["OS%d" % dc])
            rms_stats(g, lambda k: OS_[:, k, :wo], 8, wo, SQ, RS, 0, ["OS%d" % k for k in range(8)])
            for k in range(8):
                tm, tk = TMP[k % 2], "TMPf%d" % (k % 2)
                OP("pool", "tensor_tensor", out=tm[:, :wo], in0=OS_[:, k, :wo], in1=RS[:, :wo], op=ALU.mult,
                   r=["OS%d" % k, "RS"], w=[tk])
                OP("dve", "scalar_tensor_tensor", out=X[:, k, oa:ob], in0=tm[:, :wo], scalar=DER[:, l, j, 5, k:k + 1],
                   in1=X[:, k, oa:ob], op0=ALU.mult, op1=ALU.add, r=[tk, "DER", "X"], w=["X"])


def na_rows(kr):
    rs = [r for r in range(32) if min(max(r - 4, 0), 24) <= kr <= min(max(r - 4, 0), 24) + 7]
    assert rs == list(range(rs[0], rs[-1] + 1))
    return rs[0], rs[-1] + 1


def attn_mixer(g, bi, l, kind):
    nc, p, OP, MM, DMA = g.nc, g.p, g.OP, g.MM, g.DMA
    PS, HT, YT = g.PS, g.HT, g.YT
    wkey = g.wkeys("w_in_bf", l, 8)
    swa = kind == "swa"
    with contextlib.ExitStack() as sm:
        def sb(name, shape, dt):
            return sm.enter_context(nc.sbuf_tensor(UN() + name, list(shape), dt))
        QT = sb("QT", [128, NT], BF16)
        KT = sb("KT", [128, NT], BF16)
        VT = sb("VT", [128, 18, 128], BF16)
        WB = [sb("WB%d" % i, [128, 8, 128], BF16) for i in range(3)]
        T1 = sb("T1", [128, 512], F32)
        T2 = sb("T2", [128, 512], F32)
        PT = [sb("PT%d" % i, [128, 512], BF16) for i in range(2)]
        REC = sb("REC", [128, 512], F32)
        if swa:
            RC = sb("RC", [128, NT], BF16)
            RSN = sb("RSN", [128, NT], BF16)
            DMA("sp", RC[:], g.ropeC, w=["RC"])
            DMA("sp", RSN[:], g.ropeS, w=["RSN"])
        else:
            UT = sb("UT", [128, 2, 960], BF16)
            UF = sb("UF", [128, 960], F32)
            MC = sb("MC", [128, 960], F32)
            DMA("sp", MC[:], g.mcol, w=["MC"])
        wb_it = [0]

        def load_w(c0):
            i = wb_it[0] % 3
            wb_it[0] += 1
            DMA("sp", WB[i][:], g.w_in_bf[l, :, c0:c0 + 128].rearrange("(k p) c -> p k c", p=128), r=wkey, w=["WB%d" % i])
            return WB[i], "WB%d" % i

        for c in range(2):
            if swa:
                cq, cqp, ck, ckp, cv_ = C_SQ + 128 * c, C_SQP + 128 * c, C_SKD + 128 * c, C_SKDP + 128 * c, C_SV
            else:
                cq, ck, cv_ = C_NAQ + 128 * c, C_NAK + 128 * c, C_NAV + 128 * c
            for (dst, dk, c1, c2) in ((QT, "QT", cq, cqp if swa else None), (KT, "KT", ck, ckp if swa else None)):
                w1, w1k = load_w(c1)
                if swa:
                    w2, w2k = load_w(c2)
                for (a, b) in TB:
                    w = b - a
                    for k in range(8):
                        MM(PS[0][:, :w], w1[:, k, :], HT[:, k, a:b], k == 0, k == 7, [w1k, "HT"], ["ps0"])
                    if swa:
                        for k in range(8):
                            MM(PS[1][:, :w], w2[:, k, :], HT[:, k, a:b], k == 0, k == 7, [w2k, "HT"], ["ps1"])
                        OP("dve", "tensor_tensor", out=T1[:, :w], in0=PS[0][:, :w], in1=RC[:, a:b], op=ALU.mult,
                           r=["ps0", "RC"], w=["T1"])
                        OP("dve", "tensor_tensor", out=T2[:, :w], in0=PS[1][:, :w], in1=RSN[:, a:b], op=ALU.mult,
                           r=["ps1", "RSN"], w=["T2"])
                        OP("pool", "tensor_tensor", out=dst[:, a:b], in0=T1[:, :w], in1=T2[:, :w], op=ALU.add,
                           r=["T1", "T2"], w=[dk])
                    else:
                        OP("act", "activation", out=dst[:, a:b], in_=PS[0][:, :w], func=AF.Copy, r=["ps0"], w=[dk])
            if (not swa) or c == 0:
                wv, wvk = load_w(cv_)
                for t4 in range(0, 18, 4):
                    nt = min(4, 18 - t4)
                    for ti in range(nt):
                        tt = t4 + ti
                        for k in range(8):
                            MM(PS[2][:, ti * 128:(ti + 1) * 128], HT[:, k, tt * 128:(tt + 1) * 128], wv[:, k, :], k == 0, k == 7,
                               [wvk, "HT"], ["ps2"])
                    OP("act", "activation", out=VT[:, t4:t4 + nt, :],
                       in_=PS[2][:, 0:nt * 128].rearrange("p (t c) -> p t c", c=128), func=AF.Copy, r=["ps2"], w=["VT"])
            if not swa:
                for hh in range(2):
                    for half in range(2):
                        DMA("sp", UF[half * 64:(half + 1) * 64, :], g.na_exp[l, 2 * c + hh], w=["UF"])
                    OP("act", "activation", out=UF[:], in_=UF[:], func=AF.Exp, r=["UF"], w=["UF"])
                    OP("dve", "tensor_tensor", out=UT[:, hh, :], in0=UF[:], in1=MC[:], op=ALU.mult, r=["UF", "MC"], w=["UT"])
            for (qa, qb) in TB:
                qw = qb - qa
                for hh in range(2):
                    h = 2 * c + hh
                    hb = 64 * hh
                    items = []
                    for kc in range(2):
                        items.append((kc * 128, 128, 0, kc, qa, qb, None))
                    if qa >= NCX:
                        if swa:
                            for kb in range(16):
                                ka = NCX + 128 * kb
                                a_ = max(qa, ka - 128)
                                b_ = min(qb, ka + 256)
                                if a_ < b_:
                                    items.append((ka, 128, 0, 2 + kb, a_, b_, ("swa", ka)))
                        else:
                            for kr in range(32):
                                r0, r1 = na_rows(kr)
                                a_ = max(qa, NCX + 64 * r0)
                                b_ = min(qb, NCX + 64 * r1)
                                if a_ < b_:
                                    items.append((NCX + 64 * kr, 64, 64 * (kr % 2), 2 + kr // 2, a_, b_, ("na", kr)))
                    vc0 = 64 * (h // 2) if swa else 64 * hh
                    for ii, (ka, nk, pb, vt, a_, b_, post) in enumerate(items):
                        n = b_ - a_
                        sbank = 3 + ii % 2
                        pt, ptk = PT[ii % 2], "PT%d" % (ii % 2)
                        MM(PS[sbank][pb:pb + nk, :n], KT[hb:hb + 64, ka:ka + nk], QT[hb:hb + 64, a_:b_], True, True,
                           ["KT", "QT"], ["ps%d" % sbank])
                        OP("act", "activation", out=pt[pb:pb + nk, :n], in_=PS[sbank][pb:pb + nk, :n], func=AF.Exp, scale=0.125,
                           r=["ps%d" % sbank], w=[ptk])
                        if post is not None and post[0] == "swa":
                            kst = post[1]
                            if a_ < kst:
                                OP("pool", "tensor_tensor", out=pt[:, 0:128], in0=pt[:, 0:128], in1=g.MAB[:, 0, :], op=ALU.mult,
                                   r=[ptk, "MAB"], w=[ptk])
                            if b_ > kst + 128:
                                o_ = kst + 128 - a_
                                OP("pool", "tensor_tensor", out=pt[:, o_:o_ + 128], in0=pt[:, o_:o_ + 128], in1=g.MAB[:, 1, :],
                                   op=ALU.mult, r=[ptk, "MAB"], w=[ptk])
                        elif post is not None:
                            kr = post[1]
                            ra = (a_ - NCX) // 64
                            i0 = ra - kr + 7
                            nr = n // 64
                            OP("pool", "tensor_tensor", out=pt[pb:pb + nk, :n], in0=pt[pb:pb + nk, :n],
                               in1=UT[pb:pb + nk, hh, i0 * 64:(i0 + nr) * 64], op=ALU.mult, r=[ptk, "UT"], w=[ptk])
                        MM(PS[5][hb:hb + 64, a_ - qa:b_ - qa], VT[pb:pb + nk, vt, vc0:vc0 + 64], pt[pb:pb + nk, :n], ii == 0,
                           ii == len(items) - 1, ["VT", ptk], ["ps5"])
                        MM(PS[6][hb:hb + 64, a_ - qa:b_ - qa], g.ONESB[pb:pb + nk, 0:64], pt[pb:pb + nk, :n], ii == 0,
                           ii == len(items) - 1, ["ONESB", ptk], ["ps6"])
                if swa:
                    OP("dve", "tensor_scalar", out=REC[:, :qw], in0=PS[6][:, :qw], scalar1=g.ESINK[:, l, c:c + 1], scalar2=None,
                       op0=ALU.add, r=["ps6", "ESINK"], w=["REC"])
                    OP("dve", "reciprocal", out=REC[:, :qw], in_=REC[:, :qw], r=["REC"], w=["REC"])
                else:
                    OP("dve", "reciprocal", out=REC[:, :qw], in_=PS[6][:, :qw], r=["ps6"], w=["REC"])
                yc = (6 if swa else 2) + c
                OP("dve", "tensor_tensor", out=YT[:, yc, qa:qb], in0=PS[5][:, :qw], in1=REC[:, :qw], op=ALU.mult,
                   r=["ps5", "REC"], w=["YT"])


TC = 64


def s5_mixer(g, bi, l):
    nc, p, OP, MM, DMA = g.nc, g.p, g.OP, g.MM, g.DMA
    PS, HT, YT = g.PS, g.HT, g.YT
    wkey = g.wkeys("w_in_bf", l, 8)
    PI = math.pi
    with contextlib.ExitStack() as sm:
        def sb(name, shape, dt):
            return sm.enter_context(nc.sbuf_tensor(UN() + name, list(shape), dt))
        UT = sb("sUT", [128, 2, NT], BF16)
        BP = [sb("sBP%d" % i, [128, 16, 128], BF16) for i in range(2)]
        CP = [sb("sCP%d" % i, [128, 16, 128], BF16) for i in range(2)]
        TAB = [sb("sTAB%d" % i, [128, 16, TC], BF16) for i in range(4)]
        Z = sb("sZ", [128, 16, TC], F32)
        W = sb("sW", [128, 16, TC], F32)
        ZA = sb("sZA", [128, 8, TC], F32)
        ZB = sb("sZB", [128, 8, TC], F32)
        HC = sb("sHC", [128, 8, TC], BF16)
        HS = sb("sHS", [128, 8, TC], BF16)
        WA = [sb("sWA%d" % i, [128, 8, 128], BF16) for i in range(2)]
        WGL = sb("sWGL", [128, 2, 256], BF16)
        LAM = sb("sLAM", [128, 2, 32], F32)
        STP = sb("sSTP", [128, 32], F32)
        DSK = sb("sDSK", [128, LD[0], 2], F32)
        SGN = sb("sSGN", [128, 2], F32)
        JM = sb("sJM", [128, 128], F32)
        HPI = sb("sHPI", [128, 1], F32)
        sm_names = ["RHO", "TH", "M", "SH", "CH", "SN", "CS", "LBR", "LBI", "DEN", "KR", "KI", "T0", "T1", "EC", "ES", "EC2", "ES2"]
        SM = {n: sb("s" + n, [128, 32], F32) for n in sm_names}
        INIT = sb("sINIT", [128, 16], F32)
        ENDS = sb("sENDS", [128, 16], F32)
        RT1 = sb("sRT1", [128, 16], F32)
        TMPY = sb("sTMPY", [128, TC], F32)
        DMA("sp", LAM[:], g.dram["lamT"][:, l], w=["sLAM"])
        DMA("sp", STP[:], g.dram["stepT"][:, l], w=["sSTP"])
        DMA("sp", DSK[:], g.dram["dskT"], w=["sDSK"])
        DMA("sp", SGN[:], g.dram["sgn"], w=["sSGN"])
        DMA("sp", JM[:], g.dram["jmat"], w=["sJM"])
        DMA("pool", WGL[:], g.dram["s5_w_glu"][l].rearrange("(k p) c -> p k c", p=128), w=["sWGL"])
        OP("dve", "memset", ap=HPI[:], constant=PI / 2, w=["sHPI"])

        def V(eng, method, outn, r, **kw):
            OP(eng, method, r=["s" + x for x in r], w=["s" + outn], **kw)

        def TT(outn, an, bn, op):
            V("dve", "tensor_tensor", outn, [an, bn], out=SM[outn][:], in0=SM[an][:], in1=SM[bn][:], op=op)

        OP("act", "activation", out=STP[:], in_=STP[:], func=AF.Exp, r=["sSTP"], w=["sSTP"])
        OP("dve", "tensor_tensor", out=SM["T0"][:], in0=LAM[:, 0, :], in1=STP[:], op=ALU.mult, r=["sLAM", "sSTP"], w=["sT0"])
        OP("act", "activation", out=SM["RHO"][:], in_=SM["T0"][:], func=AF.Exp, r=["sT0"], w=["sRHO"])
        OP("dve", "tensor_tensor", out=SM["TH"][:], in0=LAM[:, 1, :], in1=STP[:], op=ALU.mult, r=["sLAM", "sSTP"], w=["sTH"])
        for _ in range(5):
            V("dve", "tensor_scalar", "M", ["TH"], out=SM["M"][:], in0=SM["TH"][:], scalar1=PI, scalar2=-2 * PI, op0=ALU.is_gt,
              op1=ALU.mult)
            TT("TH", "TH", "M", ALU.add)
        for _ in range(2):
            V("dve", "tensor_scalar", "M", ["TH"], out=SM["M"][:], in0=SM["TH"][:], scalar1=-PI, scalar2=2 * PI, op0=ALU.is_lt,
              op1=ALU.mult)
            TT("TH", "TH", "M", ALU.add)
        OP("act", "activation", out=SM["SH"][:], in_=SM["TH"][:], func=AF.Sin, scale=0.5, r=["sTH"], w=["sSH"])
        OP("act", "activation", out=SM["CH"][:], in_=SM["TH"][:], func=AF.Sin, scale=0.5, bias=HPI[:, 0:1], r=["sTH", "sHPI"],
           w=["sCH"])
        TT("SN", "SH", "CH", ALU.mult)
        V("dve", "tensor_scalar", "SN", ["SN"], out=SM["SN"][:], in0=SM["SN"][:], scalar1=2.0, scalar2=None, op0=ALU.mult)
        TT("T0", "CH", "CH", ALU.mult)
        TT("T1", "SH", "SH", ALU.mult)
        TT("CS", "T0", "T1", ALU.subtract)
        TT("LBR", "RHO", "CS", ALU.mult)
        TT("LBI", "RHO", "SN", ALU.mult)
        V("dve", "tensor_scalar", "LBR", ["LBR"], out=SM["LBR"][:], in0=SM["LBR"][:], scalar1=-1.0, scalar2=None, op0=ALU.add)
        OP("dve", "tensor_tensor", out=SM["T0"][:], in0=LAM[:, 0, :], in1=LAM[:, 0, :], op=ALU.mult, r=["sLAM"], w=["sT0"])
        OP("dve", "tensor_tensor", out=SM["T1"][:], in0=LAM[:, 1, :], in1=LAM[:, 1, :], op=ALU.mult, r=["sLAM"], w=["sT1"])
        TT("DEN", "T0", "T1", ALU.add)
        V("dve", "reciprocal", "DEN", ["DEN"], out=SM["DEN"][:], in_=SM["DEN"][:])
        OP("dve", "tensor_tensor", out=SM["T0"][:], in0=SM["LBR"][:], in1=LAM[:, 0, :], op=ALU.mult, r=["sLBR", "sLAM"], w=["sT0"])
        OP("dve", "tensor_tensor", out=SM["T1"][:], in0=SM["LBI"][:], in1=LAM[:, 1, :], op=ALU.mult, r=["sLBI", "sLAM"], w=["sT1"])
        TT("KR", "T0", "T1", ALU.add)
        TT("KR", "KR", "DEN", ALU.mult)
        OP("dve", "tensor_tensor", out=SM["T0"][:], in0=SM["LBI"][:], in1=LAM[:, 0, :], op=ALU.mult, r=["sLBI", "sLAM"], w=["sT0"])
        OP("dve", "tensor_tensor", out=SM["T1"][:], in0=SM["LBR"][:], in1=LAM[:, 1, :], op=ALU.mult, r=["sLBR", "sLAM"], w=["sT1"])
        TT("KI", "T0", "T1", ALU.subtract)
        TT("KI", "KI", "DEN", ALU.mult)

        for cc in range(2):
            wa, wak = WA[cc], "sWA%d" % cc
            DMA("sp", wa[:], g.w_in_bf[l, :, C_A + 128 * cc:C_A + 128 * cc + 128].rearrange("(k p) c -> p k c", p=128), r=wkey,
                w=[wak])
            for (a, b) in TB:
                w = b - a
                for k in range(8):
                    MM(PS[7][:, :w], wa[:, k, :], HT[:, k, a:b], k == 0, k == 7, [wak, "HT"], ["ps7"])
                OP("act", "activation", out=UT[:, cc, a:b], in_=PS[7][:, :w], func=AF.Copy, r=["ps7"], w=["sUT"])

        nchunk = NT // TC
        for d in range(2):
            qs = slice(16 * d, 16 * d + 16)
            for i, nm in enumerate(("s5B1", "s5B2")):
                DMA("sp", BP[i][:], g.dram[nm + "_bf"][l, d].rearrange("g p s -> p g s"), r=["%s_bf%d" % (nm, l), "%s_bf%d_d0" % (nm, l)], w=["sBP%d" % i])
            for i, nm in enumerate(("s5C1", "s5C2")):
                DMA("sp", CP[i][:], g.dram[nm + "_bf"][l, d].rearrange("g p s -> p g s"), r=["%s_bf%d" % (nm, l), "%s_bf%d_d0" % (nm, l)], w=["sCP%d" % i])
            for which in range(2):
                i0 = 0 if d == 0 else TC - 1
                if which == 0:
                    OP("dve", "tensor_copy", out=Z[:, :, i0], in_=SM["KR"][:, qs], r=["sKR"], w=["sZ"])
                    OP("dve", "tensor_copy", out=W[:, :, i0], in_=SM["KI"][:, qs], r=["sKI"], w=["sW"])
                else:
                    OP("dve", "memset", ap=Z[:, :, i0:i0 + 1], constant=1.0, w=["sZ"])
                    OP("dve", "memset", ap=W[:, :, i0:i0 + 1], constant=0.0, w=["sW"])
                OP("dve", "tensor_copy", out=SM["EC"][:, 0:16], in_=SM["CS"][:, qs], r=["sCS"], w=["sEC"])
                if which == 0:
                    OP("dve", "tensor_scalar", out=SM["ES"][:, 0:16], in0=SM["SN"][:, qs], scalar1=-1.0, scalar2=None, op0=ALU.mult,
                       r=["sSN"], w=["sES"])
                else:
                    OP("dve", "tensor_copy", out=SM["ES"][:, 0:16], in_=SM["SN"][:, qs], r=["sSN"], w=["sES"])
                n = 1
                while n < TC:
                    if d == 0:
                        src, dst = slice(0, n), slice(n, 2 * n)
                    else:
                        src, dst = slice(TC - n, TC), slice(TC - 2 * n, TC - n)
                    ecb = SM["EC"][:, 0:16].unsqueeze(2).to_broadcast([128, 16, n])
                    esb = SM["ES"][:, 0:16].unsqueeze(2).to_broadcast([128, 16, n])
                    OP("dve", "tensor_tensor", out=ZA[:, :, :].rearrange("p a b -> p (a b)")[:, 0:16 * n].rearrange("p (g n) -> p g n", n=n),
                       in0=Z[:, :, src], in1=ecb, op=ALU.mult, r=["sZ", "sEC"], w=["sZA"])
                    OP("dve", "tensor_tensor", out=ZB[:, :, :].rearrange("p a b -> p (a b)")[:, 0:16 * n].rearrange("p (g n) -> p g n", n=n),
                       in0=W[:, :, src], in1=esb, op=ALU.mult, r=["sW", "sES"], w=["sZB"])
                    OP("dve", "tensor_tensor", out=Z[:, :, dst],
                       in0=ZA[:, :, :].rearrange("p a b -> p (a b)")[:, 0:16 * n].rearrange("p (g n) -> p g n", n=n),
                       in1=ZB[:, :, :].rearrange("p a b -> p (a b)")[:, 0:16 * n].rearrange("p (g n) -> p g n", n=n),
                       op=ALU.subtract, r=["sZA", "sZB"], w=["sZ"])
                    OP("dve", "tensor_tensor", out=ZA[:, :, :].rearrange("p a b -> p (a b)")[:, 0:16 * n].rearrange("p (g n) -> p g n", n=n),
                       in0=W[:, :, src], in1=ecb, op=ALU.mult, r=["sW", "sEC"], w=["sZA"])
                    OP("dve", "tensor_tensor", out=ZB[:, :, :].rearrange("p a b -> p (a b)")[:, 0:16 * n].rearrange("p (g n) -> p g n", n=n),
                       in0=Z[:, :, src], in1=esb, op=ALU.mult, r=["sZ", "sES"], w=["sZB"])
                    OP("dve", "tensor_tensor", out=W[:, :, dst],
                       in0=ZA[:, :, :].rearrange("p a b -> p (a b)")[:, 0:16 * n].rearrange("p (g n) -> p g n", n=n),
                       in1=ZB[:, :, :].rearrange("p a b -> p (a b)")[:, 0:16 * n].rearrange("p (g n) -> p g n", n=n),
                       op=ALU.add, r=["sZA", "sZB"], w=["sW"])
                    OP("dve", "tensor_tensor", out=SM["EC2"][:, 0:16], in0=SM["EC"][:, 0:16], in1=SM["EC"][:, 0:16], op=ALU.mult,
                       r=["sEC"], w=["sEC2"])
                    OP("dve", "tensor_tensor", out=SM["ES2"][:, 0:16], in0=SM["ES"][:, 0:16], in1=SM["ES"][:, 0:16], op=ALU.mult,
                       r=["sES"], w=["sES2"])
                    OP("dve", "tensor_tensor", out=SM["ES"][:, 0:16], in0=SM["ES"][:, 0:16], in1=SM["EC"][:, 0:16], op=ALU.mult,
                       r=["sES", "sEC"], w=["sES"])
                    OP("dve", "tensor_scalar", out=SM["ES"][:, 0:16], in0=SM["ES"][:, 0:16], scalar1=2.0, scalar2=None, op0=ALU.mult,
                       r=["sES"], w=["sES"])
                    OP("dve", "tensor_tensor", out=SM["EC"][:, 0:16], in0=SM["EC2"][:, 0:16], in1=SM["ES2"][:, 0:16], op=ALU.subtract,
                       r=["sEC2", "sES2"], w=["sEC"])
                    n *= 2
                if which == 0:
                    OP("dve", "tensor_copy", out=TAB[0][:], in_=Z[:], r=["sZ"], w=["sTAB0"])
                    OP("dve", "tensor_scalar", out=TAB[1][:], in0=W[:], scalar1=SGN[:, 0:1], scalar2=None, op0=ALU.mult,
                       r=["sW", "sSGN"], w=["sTAB1"])
                else:
                    OP("dve", "tensor_scalar", out=TAB[2][:], in0=Z[:], scalar1=SGN[:, 1:2], scalar2=None, op0=ALU.mult,
                       r=["sZ", "sSGN"], w=["sTAB2"])
                    OP("dve", "tensor_scalar", out=TAB[3][:], in0=W[:], scalar1=-1.0, scalar2=None, op0=ALU.mult,
                       r=["sW"], w=["sTAB3"])
                    OP("dve", "tensor_copy", out=SM["EC2"][:, 0:16], in_=SM["EC"][:, 0:16], r=["sEC"], w=["sEC2"])
                    OP("dve", "tensor_copy", out=SM["ES2"][:, 0:16], in_=SM["ES"][:, 0:16], r=["sES"], w=["sES2"])
            OP("dve", "memset", ap=INIT[:], constant=0.0, w=["sINIT"])
            order = list(range(nchunk)) if d == 0 else list(range(NCX // TC - 1, -1, -1)) + list(range(nchunk - 1, NCX // TC - 1, -1))
            for ci, m in enumerate(order):
                t0 = m * TC
                for cc in range(2):
                    par = (ci * 2 + cc) % 2
                    b1, b2, by = PS[0 + par], PS[2 + par], PS[4 + par]
                    k1, k2, ky = "ps%d" % (0 + par), "ps%d" % (2 + par), "ps%d" % (4 + par)
                    for gg in range(8):
                        gi = 8 * cc + gg
                        MM(b1[:, gg * TC:(gg + 1) * TC], BP[0][:, gi, :], UT[:, cc, t0:t0 + TC], True, True, ["sBP0", "sUT"], [k1])
                        MM(b2[:, gg * TC:(gg + 1) * TC], BP[1][:, gi, :], UT[:, cc, t0:t0 + TC], True, True, ["sBP1", "sUT"], [k2])
                    gsl = slice(8 * cc, 8 * cc + 8)
                    OP("dve", "tensor_tensor", out=ZA[:], in0=b1[:, :].rearrange("p (g n) -> p g n", n=TC), in1=TAB[0][:, gsl, :],
                       op=ALU.mult, r=[k1, "sTAB0"], w=["sZA"])
                    OP("dve", "tensor_tensor", out=ZB[:], in0=b2[:, :].rearrange("p (g n) -> p g n", n=TC), in1=TAB[1][:, gsl, :],
                       op=ALU.mult, r=[k2, "sTAB1"], w=["sZB"])
                    OP("pool", "tensor_tensor", out=Z[:, gsl, :], in0=ZA[:], in1=ZB[:], op=ALU.add, r=["sZA", "sZB"], w=["sZ"])
                    for gg in range(8):
                        gi = 8 * cc + gg
                        q = 16 * d + gi
                        rho_b = SM["RHO"][:, q:q + 1].to_broadcast([128, TC])
                        if d == 0:
                            zin, wout = Z[:, gi, :], W[:, gi, :]
                        else:
                            zin, wout = Z[:, gi, ::-1], W[:, gi, ::-1]
                        OP("dve", "tensor_tensor_scan", out=wout, data0=rho_b, data1=zin, initial=INIT[:, gi:gi + 1], op0=ALU.mult,
                           op1=ALU.add, r=["sZ", "sRHO", "sINIT"], w=["sW"])
                    OP("pool", "tensor_tensor", out=HC[:], in0=W[:, gsl, :], in1=TAB[2][:, gsl, :], op=ALU.mult, r=["sW", "sTAB2"],
                       w=["sHC"])
                    OP("pool", "tensor_tensor", out=HS[:], in0=W[:, gsl, :], in1=TAB[3][:, gsl, :], op=ALU.mult, r=["sW", "sTAB3"],
                       w=["sHS"])
                    for gg in range(8):
                        gi = 8 * cc + gg
                        MM(by[:, 0:TC], CP[0][:, gi, :], HC[:, gg, :], gg == 0, False, ["sCP0", "sHC"], [ky])
                        MM(by[:, 0:TC], CP[1][:, gi, :], HS[:, gg, :], False, gg == 7, ["sCP1", "sHS"], [ky])
                    if d == 0:
                        OP("act", "activation", out=YT[:, cc, t0:t0 + TC], in_=by[:, 0:TC], func=AF.Copy, r=[ky], w=["YT"])
                    else:
                        OP("dve", "scalar_tensor_tensor", out=TMPY[:], in0=UT[:, cc, t0:t0 + TC], scalar=DSK[:, l, cc:cc + 1],
                           in1=YT[:, cc, t0:t0 + TC], op0=ALU.mult, op1=ALU.add, r=["sUT", "sDSK", "YT"], w=["sTMPY"])
                        OP("dve", "tensor_tensor", out=YT[:, cc, t0:t0 + TC], in0=TMPY[:], in1=by[:, 0:TC], op=ALU.add,
                           r=["sTMPY", ky], w=["YT"])
                ecol = TC - 1 if d == 0 else 0
                OP("dve", "tensor_copy", out=ENDS[:], in_=W[:, :, ecol], r=["sW"], w=["sENDS"])
                MM(PS[6][:, 0:16], JM[:], ENDS[:], True, True, ["sJM", "sENDS"], ["ps6"])
                OP("dve", "tensor_tensor", out=RT1[:], in0=ENDS[:], in1=SM["EC2"][:, 0:16], op=ALU.mult, r=["sENDS", "sEC2"], w=["sRT1"])
                OP("dve", "tensor_tensor", out=ENDS[:], in0=PS[6][:, 0:16], in1=SM["ES2"][:, 0:16], op=ALU.mult, r=["ps6", "sES2"],
                   w=["sENDS"])
                OP("dve", "tensor_tensor", out=INIT[:], in0=RT1[:], in1=ENDS[:], op=ALU.add, r=["sRT1", "sENDS"], w=["sINIT"])
        for (a, b) in TB:
            w = b - a
            ZT = ZA[:, :, :].rearrange("p a b -> p (a b)")
            for kc in range(2):
                OP("act", "activation", out=HC[:, :, :].rearrange("p a b -> p (a b)")[:, :w] if kc == 0 else
                   HS[:, :, :].rearrange("p a b -> p (a b)")[:, :w], in_=YT[:, kc, a:b], func=AF.Gelu_apprx_tanh, r=["YT"],
                   w=["sHC" if kc == 0 else "sHS"])
            zt = [HC[:, :, :].rearrange("p a b -> p (a b)"), HS[:, :, :].rearrange("p a b -> p (a b)")]
            for oc in range(2):
                for kc in range(2):
                    MM(PS[7][:, :w], WGL[:, kc, oc * 128:(oc + 1) * 128], zt[kc][:, :w], kc == 0, kc == 1,
                       ["sWGL", "sHC", "sHS"], ["ps7"])
                OP("act", "activation", out=ZT[:, :w], in_=PS[7][:, :w], func=AF.Sigmoid, r=["ps7"], w=["sZA"])
                OP("dve", "tensor_tensor", out=YT[:, oc, a:b], in0=zt[oc][:, :w], in1=ZT[:, :w], op=ALU.mult,
                   r=["sHC", "sHS", "sZA"], w=["YT"])


def gla_mixer(g, bi, l):
    nc, p, OP, MM, DMA = g.nc, g.p, g.OP, g.MM, g.DMA
    PS, HT, YT = g.PS, g.HT, g.YT
    wkey = g.wkeys("w_in_bf", l, 8)
    with contextlib.ExitStack() as sm:
        def sb(name, shape, dt):
            return sm.enter_context(nc.sbuf_tensor(UN() + name, list(shape), dt))
        QT = sb("gQT", [128, NT], BF16)
        KT = sb("gKT", [128, NT], BF16)
        SR = sb("gSR", [128, NT], BF16)
        KTK = sb("gKTK", [128, 18, 128], BF16)
        VTK = sb("gVTK", [128, 18, 128], BF16)
        OF = sb("gOF", [128, NT], F32)
        GA = [sb("gGA%d" % d, [32, NT], BF16) for d in range(2)]
        WG = sb("gWG", [32, 2, 256], BF16)
        TRI = sb("gTRI", [128, 2, 128], F32)
        BD = sb("gBD", [128, 128], BF16)
        GN = sb("gGN", [128, LD[0]], F32)
        ONE = sb("gONE", [128, 1], F32)
        EPS_ = sb("gEPS", [128, 1], F32)
        WB = [sb("gWB%d" % i, [128, 8, 128], BF16) for i in range(2)]
        WGB = sb("gWGB", [128, 8, 32], BF16)
        NEG = sb("gNEG", [128, 128], F32)
        EX = sb("gEX", [128, 128], F32)
        E1 = sb("gE1", [128, 128], F32)
        E2 = sb("gE2", [128, 128], F32)
        E2T = sb("gE2T", [128, 128], F32)
        QD = sb("gQD", [128, 128], BF16)
        KD = sb("gKD", [128, 128], BF16)
        KDT = sb("gKDT", [128, 128], BF16)
        AM = [sb("gAM%d" % i, [128, 128], BF16) for i in range(2)]
        S = sb("gS", [128, 64], F32)
        SB_ = sb("gSB", [128, 64], BF16)
        SQ = sb("gSQ", [128, 512], BF16)
        RS = sb("gRS", [128, 512], F32)
        OP("dve", "memset", ap=ONE[:], constant=1.0, w=["gONE"])
        OP("dve", "memset", ap=EPS_[:], constant=EPS, w=["gEPS"])
        DMA("sp", TRI[:], g.dram["tri"], w=["gTRI"])
        DMA("sp", BD[:], g.dram["bd64"], w=["gBD"])
        DMA("sp", GN[:], g.dram["gnT"], w=["gGN"])
        for d in range(2):
            DMA("pool", WG[0:16, d, :], g.dram["gla_w_gate2"][l, d], w=["gWG"])
            DMA("pool", WG[16:17, d, :], g.dram["gla_b_gate"][l, d:d + 1, :], w=["gWG"])
        wb_it = [0]

        def load_w(c0):
            i = wb_it[0] % 2
            wb_it[0] += 1
            DMA("sp", WB[i][:], g.w_in_bf[l, :, c0:c0 + 128].rearrange("(k p) c -> p k c", p=128), r=wkey, w=["gWB%d" % i])
            return WB[i], "gWB%d" % i

        DMA("sp", WGB[:], g.w_in_bf[l, :, C_GF:C_GF + 32].rearrange("(k p) c -> p k c", p=128), r=wkey, w=["gWGB"])
        for d in range(2):
            OP("pool", "memset", ap=GA[d][:], constant=1.0, w=["gGA%d" % d])
            for (a, b) in TB:
                w = b - a
                for k in range(8):
                    MM(PS[0][0:16, :w], WGB[:, k, 16 * d:16 * d + 16], HT[:, k, a:b], k == 0, k == 7, ["gWGB", "HT"], ["ps0"])
                OP("act", "activation", out=GA[d][0:16, a:b], in_=PS[0][0:16, :w], func=AF.Copy, r=["ps0"], w=["gGA%d" % d])
        for c in range(2):
            for (dst, dk, c0, fn, sc) in ((QT, "gQT", C_GQ + 128 * c, AF.Copy, 0.125), (KT, "gKT", C_GK + 128 * c, AF.Copy, 1.0),
                                          (SR, "gSR", C_GR + 128 * c, AF.Silu, 1.0)):
                w1, w1k = load_w(c0)
                for (a, b) in TB:
                    w = b - a
                    for k in range(8):
                        MM(PS[0][:, :w], w1[:, k, :], HT[:, k, a:b], k == 0, k == 7, [w1k, "HT"], ["ps0"])
                    OP("act", "activation", out=dst[:, a:b], in_=PS[0][:, :w], func=fn, scale=sc, r=["ps0"], w=[dk])
            for (dst, dk, c0) in ((KTK, "gKTK", C_GK + 128 * c), (VTK, "gVTK", C_GV + 128 * c)):
                wv, wvk = load_w(c0)
                for t4 in range(0, 18, 4):
                    nt = min(4, 18 - t4)
                    for ti in range(nt):
                        tt = t4 + ti
                        for k in range(8):
                            MM(PS[1][:, ti * 128:(ti + 1) * 128], HT[:, k, tt * 128:(tt + 1) * 128], wv[:, k, :], k == 0, k == 7,
                               [wvk, "HT"], ["ps1"])
                    OP("act", "activation", out=dst[:, t4:t4 + nt, :],
                       in_=PS[1][:, 0:nt * 128].rearrange("p (t c) -> p t c", c=128), func=AF.Copy, r=["ps1"], w=[dk])
            for d in range(2):
                OP("dve", "memset", ap=S[:], constant=0.0, w=["gS"])
                OP("dve", "memset", ap=SB_[:], constant=0.0, w=["gSB"])
                tiles = list(range(18)) if d == 0 else [1, 0] + list(range(17, 1, -1))
                chunks = (0, 1) if d == 0 else (1, 0)
                for tt in tiles:
                    t0 = tt * 128
                    MM(PS[2][:, 0:128], GA[d][0:17, t0:t0 + 128], WG[0:17, d, 128 * c:128 * c + 128], True, True,
                       ["gGA%d" % d, "gWG"], ["ps2"])
                    OP("act", "activation", out=EX[:], in_=PS[2][:, 0:128], func=AF.Exp, scale=-1.0, r=["ps2"], w=["gEX"])
                    OP("act", "activation", out=NEG[:], in_=EX[:], func=AF.Ln, bias=ONE[:, 0:1], scale=1.0, r=["gEX", "gONE"],
                       w=["gNEG"])
                    MM(PS[3][:, 0:128], NEG[:], TRI[:, d, :], True, True, ["gNEG", "gTRI"], ["ps3"])
                    MM(PS[3][:, 128:256], TRI[:, d, :], NEG[:], True, True, ["gNEG", "gTRI"], ["ps3"])
                    OP("act", "activation", out=E1[:], in_=PS[3][:, 0:128], func=AF.Exp, scale=-1.0 / 16, r=["ps3"], w=["gE1"])
                    OP("act", "activation", out=E2[:], in_=PS[3][:, 0:128], func=AF.Exp, scale=1.0 / 16, r=["ps3"], w=["gE2"])
                    OP("act", "activation", out=E2T[:], in_=PS[3][:, 128:256], func=AF.Exp, scale=1.0 / 16, r=["ps3"], w=["gE2T"])
                    OP("dve", "tensor_tensor", out=QD[:], in0=QT[:, t0:t0 + 128], in1=E1[:], op=ALU.mult, r=["gQT", "gE1"], w=["gQD"])
                    OP("pool", "tensor_tensor", out=KD[:], in0=KT[:, t0:t0 + 128], in1=E2[:], op=ALU.mult, r=["gKT", "gE2"], w=["gKD"])
                    OP("pool", "tensor_tensor", out=KDT[:], in0=KTK[:, tt, :], in1=E2T[:], op=ALU.mult, r=["gKTK", "gE2T"],
                       w=["gKDT"])
                    for hh in range(2):
                        hb = 64 * hh
                        MM(PS[4 + hh][:, 0:128], KD[hb:hb + 64, :], QD[hb:hb + 64, :], True, True, ["gKD", "gQD"], ["ps%d" % (4 + hh)])
                        OP("dve", "tensor_tensor", out=AM[hh][:], in0=PS[4 + hh][:, 0:128], in1=TRI[:, d, :], op=ALU.mult,
                           r=["ps%d" % (4 + hh), "gTRI"], w=["gAM%d" % hh])
                        MM(PS[6][hb:hb + 64, 0:128], VTK[:, tt, hb:hb + 64], AM[hh][:], True, False, ["gVTK", "gAM%d" % hh], ["ps6"])
                    for ch in chunks:
                        cs = 64 * ch
                        for hh in range(2):
                            hb = 64 * hh
                            MM(PS[6][hb:hb + 64, cs:cs + 64], SB_[hb:hb + 64, :], QD[hb:hb + 64, cs:cs + 64], False, True,
                               ["gSB", "gQD"], ["ps6"])
                            MM(PS[7][hb:hb + 64, 0:64], KDT[cs:cs + 64, hb:hb + 64], VTK[cs:cs + 64, tt, hb:hb + 64], True, True,
                               ["gKDT", "gVTK"], ["ps7"])
                        dcol = cs + 63 if d == 0 else cs
                        OP("dve", "tensor_tensor", out=S[:], in0=S[:], in1=PS[7][:, 0:64], op=ALU.add, r=["gS", "ps7"], w=["gS"])
                        OP("dve", "tensor_scalar", out=S[:], in0=S[:], scalar1=E1[:, dcol:dcol + 1], scalar2=None, op0=ALU.mult,
                           r=["gS", "gE1"], w=["gS"])
                        OP("act", "activation", out=SB_[:], in_=S[:], func=AF.Copy, r=["gS"], w=["gSB"])
                    if d == 0:
                        OP("act", "activation", out=OF[:, t0:t0 + 128], in_=PS[6][:, 0:128], func=AF.Copy, r=["ps6"], w=["gOF"])
                    else:
                        OP("dve", "tensor_tensor", out=OF[:, t0:t0 + 128], in0=OF[:, t0:t0 + 128], in1=PS[6][:, 0:128], op=ALU.add,
                           r=["gOF", "ps6"], w=["gOF"])
            for (a, b) in TB:
                w = b - a
                OP("act", "activation", out=SQ[:, :w], in_=OF[:, a:b], func=AF.Square, r=["gOF"], w=["gSQ"])
                MM(PS[0][:, :w], BD[:], SQ[:, :w], True, True, ["gBD", "gSQ"], ["ps0"])
                OP("act", "activation", out=RS[:, :w], in_=PS[0][:, :w], func=AF.Sqrt, scale=1.0 / 64, bias=EPS_[:, 0:1],
                   r=["ps0", "gEPS"], w=["gRS"])
                OP("dve", "reciprocal", out=RS[:, :w], in_=RS[:, :w], r=["gRS"], w=["gRS"])
                OP("dve", "scalar_tensor_tensor", out=RS[:, :w], in0=OF[:, a:b], scalar=GN[:, l:l + 1], in1=RS[:, :w], op0=ALU.mult,
                   op1=ALU.mult, r=["gOF", "gGN", "gRS"], w=["gRS"])
                OP("pool", "tensor_tensor", out=YT[:, 4 + c, a:b], in0=RS[:, :w], in1=SR[:, a:b], op=ALU.mult, r=["gRS", "gSR"],
                   w=["YT"])


def host_prep(inputs):
    f = np.float32
    w_in = np.asarray(inputs["w_in"], f)
    sq = w_in[:, :, C_SQ:C_SQ + 256]
    sk = w_in[:, :, C_SK:C_SK + 128]
    dup = np.concatenate([np.arange(64), np.arange(64), 64 + np.arange(64), 64 + np.arange(64)])
    w_in_ext = np.concatenate([w_in, sq[:, :, _rope_perm(4)], sk[:, :, dup], sk[:, :, _rope_perm(2)][:, :, dup]], axis=2)
    assert w_in_ext.shape[2] == NEXT
    rc, rs = _rope_tables()
    kl = np.arange(128)[:, None]
    ql = np.arange(128)[None, :]
    maskAB = np.stack([(kl <= ql), (ql <= kl)], 1).astype(f)
    gv = np.stack([inputs["g_pre_mix"], inputs["g_post_mix"], inputs["g_pre_ffn"], inputs["g_post_ffn"]], 1)
    gvec = np.ascontiguousarray(np.asarray(gv, f).reshape(DEPTH, 4, 8, 128).transpose(3, 0, 1, 2))
    b_modT = np.ascontiguousarray(np.asarray(inputs["b_mod"], f).reshape(DEPTH, 48, 128).transpose(2, 0, 1))
    convT = np.ascontiguousarray(np.asarray(inputs["ffn_conv"], f).reshape(DEPTH, 3, 22, 128).transpose(3, 0, 1, 2))
    sink = np.asarray(inputs["swa_sink"], f)
    sinkT = np.zeros((128, DEPTH, 2), f)
    for c in range(2):
        sinkT[0:64, :, c] = sink[None, :, 2 * c]
        sinkT[64:128, :, c] = sink[None, :, 2 * c + 1]
    rpb = np.asarray(inputs["na_rpb"], f)
    kc = np.arange(64)[:, None]
    qc = np.arange(64)[None, :]
    dcx = np.clip(kc - qc, -15, 15) + 15
    na_exp = rpb[:, :, ::-1, :][:, :, :, dcx]
    na_exp = np.ascontiguousarray(na_exp.transpose(0, 1, 3, 2, 4)).reshape(DEPTH, 4, 64, 15 * 64)
    ws = np.clip(np.arange(64) - 8, 0, 48)
    mc = ((kc >= ws[None, :]) & (kc < ws[None, :] + 16)).astype(f)
    mcol = np.tile(np.tile(mc[:, None, :], (1, 15, 1)).reshape(64, 960), (2, 1))
    jj = np.arange(128)[:, None]
    ii = np.arange(128)[None, :]
    same = (jj // 64) == (ii // 64)
    tri = np.stack([(same & (jj <= ii)), (same & (jj >= ii))], 1).astype(f)
    bd64 = same.astype(f)
    gnT = np.ascontiguousarray(np.tile(np.asarray(inputs["gla_g_norm"], f).T, (2, 1)))
    L = DEPTH
    lre = np.asarray(inputs["s5_lam_re"], f).reshape(L, 32, 64)
    lim = np.asarray(inputs["s5_lam_im"], f).reshape(L, 32, 64)
    lam = np.stack([lre, lim], 1)
    lamT = np.ascontiguousarray(np.tile(lam.transpose(3, 0, 1, 2), (2, 1, 1, 1)))
    stepT = np.ascontiguousarray(np.broadcast_to(np.asarray(inputs["s5_log_step"], f).reshape(L, 32)[None], (128, L, 32)))
    dskT = np.ascontiguousarray(np.asarray(inputs["s5_d"], f).reshape(L, 2, 128).transpose(2, 0, 1))
    sgn = np.ones((128, 2), f)
    sgn[0:64, 0] = -1.0
    sgn[64:128, 1] = -1.0
    jmat = np.zeros((128, 128), f)
    for sp in range(64):
        jmat[sp + 64, sp] = -1.0
        jmat[sp, sp + 64] = 1.0
    bre = np.asarray(inputs["s5_b_re"], f)
    bim = np.asarray(inputs["s5_b_im"], f)
    cre = np.asarray(inputs["s5_c_re"], f)
    cim = np.asarray(inputs["s5_c_im"], f)
    B1 = np.zeros((L, 2, 16, 128, 128), f)
    B2 = np.zeros((L, 2, 16, 128, 128), f)
    C1 = np.zeros((L, 2, 16, 128, 128), f)
    C2 = np.zeros((L, 2, 16, 128, 128), f)
    for gi in range(16):
        r0 = 16 * (gi % 8)
        B1[:, :, gi, r0:r0 + 16, 0:64] = bre[:, :, gi].transpose(0, 1, 3, 2)
        B1[:, :, gi, r0:r0 + 16, 64:128] = bim[:, :, gi].transpose(0, 1, 3, 2)
        B2[:, :, gi, r0:r0 + 16, 0:64] = bim[:, :, gi].transpose(0, 1, 3, 2)
        B2[:, :, gi, r0:r0 + 16, 64:128] = bre[:, :, gi].transpose(0, 1, 3, 2)
        C1[:, :, gi, 0:64, r0:r0 + 16] = cre[:, :, gi].transpose(0, 1, 3, 2)
        C1[:, :, gi, 64:128, r0:r0 + 16] = cim[:, :, gi].transpose(0, 1, 3, 2)
        C2[:, :, gi, 0:64, r0:r0 + 16] = cim[:, :, gi].transpose(0, 1, 3, 2)
        C2[:, :, gi, 64:128, r0:r0 + 16] = cre[:, :, gi].transpose(0, 1, 3, 2)
    shared = {
        "lamT": lamT, "stepT": stepT, "dskT": dskT, "sgn": sgn, "jmat": jmat, "s5_w_glu": np.asarray(inputs["s5_w_glu"], f),
        "s5B1": B1, "s5B2": B2, "s5C1": C1, "s5C2": C2,
        "tri": tri, "bd64": bd64.astype(ml_dtypes.bfloat16), "gnT": gnT,
        "gla_w_gate2": np.asarray(inputs["gla_w_gate2"], f), "gla_b_gate": np.asarray(inputs["gla_b_gate"], f),
        "w_mod": np.asarray(inputs["w_mod"], f), "b_modT": b_modT, "gvec": gvec, "w_in_ext": np.ascontiguousarray(w_in_ext),
        "w_out": np.asarray(inputs["w_out"], f), "ffn_w_up": np.asarray(inputs["ffn_w_up"], f),
        "ffn_w_down": np.asarray(inputs["ffn_w_down"], f), "convT": convT,
        "ropeC": rc.astype(ml_dtypes.bfloat16), "ropeS": rs.astype(ml_dtypes.bfloat16),
        "maskAB": maskAB.astype(ml_dtypes.bfloat16), "identf": np.eye(128, dtype=f), "sinkT": sinkT,
        "na_exp": na_exp, "mcol": np.ascontiguousarray(mcol),
    }
    x = np.asarray(inputs["x"], f)
    ctx = np.asarray(inputs["ctx"], f)
    c = np.asarray(inputs["c"], f)
    cc = np.asarray(inputs["c_ctx"], f)
    in_maps = []
    for core in range(8):
        b0 = 2 * core
        xcat = np.concatenate([ctx[b0:b0 + 2], x[b0:b0 + 2]], axis=1)
        cs = np.stack([c[b0], c[b0 + 1], cc], 0)
        cTm = np.ascontiguousarray(cs.reshape(3, 8, 128).transpose(2, 1, 0))
        m = dict(shared)
        m["xcat"] = np.ascontiguousarray(xcat)
        m["cT"] = cTm
        in_maps.append(m)
    return in_maps


L_FIRST = ("w_mod", "w_in_ext", "w_out", "ffn_w_up", "ffn_w_down", "na_exp", "s5B1", "s5B2", "s5C1", "s5C2", "s5_w_glu",
           "gla_w_gate2", "gla_b_gate")
L_SECOND = ("b_modT", "gvec", "convT", "sinkT", "lamT", "stepT", "dskT")

FUSED = True


def kernel(**inputs):
    in_maps = host_prep(inputs)
    if FUSED:
        nc = bass.Bass("TRN2", target_bir_lowering=False)
        build(nc)
        res = run_bass_kernel_spmd(nc, in_maps, core_ids=list(range(8)))
        outs = [r["out"] for r in res.results]
        return np.concatenate(outs, axis=0).astype(np.float32)
    cur = [m["xcat"] for m in in_maps]
    for l in range(DEPTH):
        base = {}
        m0 = in_maps[0]
        for k, v in m0.items():
            if k in ("xcat", "cT"):
                continue
            if k in L_FIRST:
                base[k] = np.ascontiguousarray(v[l:l + 1])
            elif k in L_SECOND:
                base[k] = np.ascontiguousarray(v[:, l:l + 1])
            elif k == "gnT":
                base[k] = np.ascontiguousarray(v[:, l:l + 1])
            else:
                base[k] = v
        for bi in range(2):
            maps = []
            for core in range(8):
                m = dict(base)
                m["xcat"] = np.ascontiguousarray(cur[core][[bi, 1 - bi]])
                cTm = in_maps[core]["cT"]
                m["cT"] = np.ascontiguousarray(cTm[:, :, [bi, 1 - bi, 2]])
                maps.append(m)
            nc = bass.Bass("TRN2", target_bir_lowering=False)
            build(nc, nlayers=1, nbatch=1, ldim=1, full_out=True)
            res = run_bass_kernel_spmd(nc, maps, core_ids=list(range(8)))
            for core in range(8):
                new = np.array(cur[core])
                new[bi] = res.results[core]["out"][0]
                cur[core] = new
    outs = [c[:, NCX:, :] for c in cur]
    return np.concatenate(outs, axis=0).astype(np.float32)
```

```python
import contextlib
import math
import numpy as np
import ml_dtypes
import concourse.bass as bass
import concourse.mybir as mybir
from concourse.bass_utils import run_bass_kernel_spmd

F32 = mybir.dt.float32
BF16 = mybir.dt.bfloat16
ALU = mybir.AluOpType
AF = mybir.ActivationFunctionType

ENG = ("pe", "act", "dve", "pool", "sp")
NDMA = 24


class P:
    def __init__(self, nc, same_eng_sync=True):
        self.nc = nc
        self.ops = {e: [] for e in ENG}
        self.cnt = {e: 0 for e in ENG}
        self.waited = {e: {} for e in ENG}
        self.last_w = {}
        self.readers = {}
        self.dma_nextq = {}
        self.dma_cnt = [0] * NDMA
        self.dma_last_tok = [None] * NDMA
        self.same = same_eng_sync
        self.out_toks = []
        self.bar = []

    def barrier(self):
        self.bar = [("e", e, self.cnt[e]) for e in ENG if self.cnt[e]] + \
                   [("d", k, self.dma_cnt[k]) for k in range(NDMA) if self.dma_cnt[k]]

    def _deps(self, eng, reads, writes):
        deps = list(self.bar)
        for k in reads:
            t = self.last_w.get(k)
            if t is not None:
                deps.append(t)
        for k in writes:
            t = self.last_w.get(k)
            if t is not None:
                deps.append(t)
            deps.extend(self.readers.get(k, ()))
        return deps

    def _waits(self, eng, deps):
        w = self.waited[eng]
        best = {}
        for t in deps:
            if t[0] == "e":
                _, e2, idx = t
                if e2 == eng and (not self.same or eng == "pe"):
                    continue
                key = e2
            else:
                key = ("d", t[1])
            if w.get(key, 0) >= t[2]:
                continue
            if key not in best or best[key][2] < t[2]:
                best[key] = t
        for key, t in best.items():
            w[key] = t[2]
        return list(best.values())

    def _record(self, tok, reads, writes):
        for k in reads:
            lst = self.readers.setdefault(k, [])
            lst.append(tok)
            if len(lst) > 64:
                best = {}
                for t in lst:
                    kk = t[:2]
                    if kk not in best or best[kk][2] < t[2]:
                        best[kk] = t
                self.readers[k] = list(best.values())
        for k in writes:
            self.last_w[k] = tok
            self.readers[k] = []

    def op(self, eng, fn, reads=(), writes=()):
        deps = self._deps(eng, reads, writes)
        waits = self._waits(eng, deps)
        self.cnt[eng] += 1
        tok = ("e", eng, self.cnt[eng])
        self.ops[eng].append((waits, fn, ("e", eng)))
        self._record(tok, reads, writes)
        return tok

    def dma(self, q, fn, reads=(), writes=(), is_out=False):
        lo, n = (0, 16) if q == "sp" else (16, NDMA - 16)
        cur = self.dma_nextq.get(q, 0)
        k = lo + cur
        self.dma_nextq[q] = (cur + 1) % n
        deps = self._deps(q, reads, writes)
        if self.dma_last_tok[k] is not None:
            deps.append(self.dma_last_tok[k])
        waits = self._waits(q, deps)
        self.dma_cnt[k] += 16
        tok = ("d", k, self.dma_cnt[k])
        self.dma_last_tok[k] = tok
        self.ops[q].append((waits, fn, ("d", k)))
        self._record(tok, reads, writes)
        if is_out:
            self.out_toks.append(tok)
        return tok

    def emit(self):
        nc = self.nc
        with contextlib.ExitStack() as es:
            esem = {e: es.enter_context(nc.semaphore("s_" + e)) for e in ENG}
            dsem = [es.enter_context(nc.semaphore("d%d" % i)) for i in range(NDMA)]
            fin = list(self.out_toks)
            for e in ENG:
                if self.cnt[e]:
                    fin.append(("e", e, self.cnt[e]))
            for k in range(NDMA):
                if self.dma_cnt[k]:
                    fin.append(("d", k, self.dma_cnt[k]))
            block = es.enter_context(nc.Block())

            def run(eng_name, eng):
                for waits, fn, kind in self.ops[eng_name]:
                    for t in waits:
                        if t[0] == "e":
                            eng.wait_ge(esem[t[1]], t[2])
                        else:
                            eng.wait_ge(dsem[t[1]], t[2])
                    ins = fn(eng)
                    if kind[0] == "e":
                        ins.then_inc(esem[kind[1]], 1)
                    else:
                        ins.then_inc(dsem[kind[1]], 16)

            @block.tensor
            def _(e):
                run("pe", e)

            @block.scalar
            def _(e):
                run("act", e)

            @block.vector
            def _(e):
                run("dve", e)

            @block.gpsimd
            def _(e):
                run("pool", e)

            @block.sync
            def _(e):
                run("sp", e)
                for t in fin:
                    if t[0] == "e":
                        if t[1] != "sp":
                            e.wait_ge(esem[t[1]], t[2])
                    else:
                        e.wait_ge(dsem[t[1]], t[2])


NT = 2304
NCX = 256
NLAT = 2048
D = 1024
DEPTH = 4
LD = [4]
DFF = 2816
EPS = 1e-6
TB = [(0, 256)] + [(256 + 512 * i, 256 + 512 * (i + 1)) for i in range(4)]
C_A, C_NAQ, C_NAK, C_NAV = 0, 256, 512, 768
C_GQ, C_GK, C_GV, C_GF, C_GB, C_GR = 1024, 1280, 1536, 1792, 1808, 1824
C_SQ, C_SK, C_SV = 2080, 2336, 2464
C_SQP, C_SKD, C_SKDP, NEXT = 2592, 2848, 3104, 3360


def _rope_perm(nh):
    idx = np.arange(nh * 64).reshape(nh, 4, 16)
    return idx[:, [1, 0, 3, 2], :].reshape(-1)


def _rope_tables():
    cos = np.ones((64, NT), np.float32)
    sin = np.zeros((64, NT), np.float32)
    t = np.arange(NLAT)
    pos = (t // 64, t % 64)
    inv = 10000.0 ** (-np.arange(0, 32, 2, dtype=np.float32) / 32)
    for half in range(2):
        ang = pos[half].astype(np.float32)[None, :] * inv[:, None]
        c, s = np.cos(ang), np.sin(ang)
        b = 32 * half
        cos[b:b + 16, NCX:] = c
        cos[b + 16:b + 32, NCX:] = c
        sin[b:b + 16, NCX:] = -s
        sin[b + 16:b + 32, NCX:] = s
    return np.concatenate([cos, cos], 0), np.concatenate([sin, sin], 0)


class Ctx:
    pass


_UN = [0]


def UN():
    _UN[0] += 1
    return "t%d_" % _UN[0]


def build(nc, dbg=None, nlayers=DEPTH, nbatch=2, mixers=("s5", "na", "gla", "swa"), ldim=DEPTH, full_out=False):
    LD[0] = ldim
    _UN[0] = 0
    p = P(nc)
    g = Ctx()
    g.p, g.nc = p, nc
    dram = {}

    def din(name, shape, dt=F32):
        dram[name] = nc.dram_tensor(name, list(shape), dt, kind="ExternalInput").ap()
        return dram[name]

    xcat = din("xcat", [2, NT, D])
    cT = din("cT", [128, 8, 3])
    w_mod = din("w_mod", [LD[0], D, 6 * D])
    b_modT = din("b_modT", [128, LD[0], 48])
    gvec = din("gvec", [128, LD[0], 4, 8])
    w_in = din("w_in_ext", [LD[0], D, NEXT])
    w_out = din("w_out", [LD[0], D, D])
    w_up = din("ffn_w_up", [LD[0], D, 2 * DFF])
    w_down = din("ffn_w_down", [LD[0], DFF, D])
    convT = din("convT", [128, LD[0], 3, 22])
    ropeC = din("ropeC", [128, NT], BF16)
    ropeS = din("ropeS", [128, NT], BF16)
    maskAB = din("maskAB", [128, 2, 128], BF16)
    identf = din("identf", [128, 128])
    sinkT = din("sinkT", [128, LD[0], 2])
    na_exp = din("na_exp", [LD[0], 4, 64, 15 * 64])
    mcol = din("mcol", [128, 15 * 64])
    din("tri", [128, 2, 128])
    din("bd64", [128, 128], BF16)
    din("gnT", [128, LD[0]])
    din("gla_w_gate2", [LD[0], 2, 16, 256])
    din("gla_b_gate", [LD[0], 2, 256])
    din("lamT", [128, LD[0], 2, 32])
    din("stepT", [128, LD[0], 32])
    din("dskT", [128, LD[0], 2])
    din("sgn", [128, 2])
    din("jmat", [128, 128])
    din("s5_w_glu", [LD[0], 256, 256])
    for nm in ("s5B1", "s5B2", "s5C1", "s5C2"):
        din(nm, [LD[0], 2, 16, 128, 128])
        dram[nm + "_bf"] = nc.dram_tensor(nm + "_bf", [LD[0], 2, 16, 128, 128], BF16).ap()
    g.dram = dram
    out = nc.dram_tensor("out", [2, NT if full_out else NLAT, D], F32, kind="ExternalOutput").ap()
    dbg_aps = {}
    if dbg:
        for name, shape in dbg.items():
            dbg_aps[name] = nc.dram_tensor("dbg_" + name, list(shape), F32, kind="ExternalOutput").ap()

    w_in_bf = nc.dram_tensor("w_in_bf", [LD[0], D, NEXT], BF16).ap()
    w_out_bf = nc.dram_tensor("w_out_bf", [LD[0], D, D], BF16).ap()
    w_up_bf = nc.dram_tensor("w_up_bf", [LD[0], D, 2 * DFF], BF16).ap()
    w_down_bf = nc.dram_tensor("w_down_bf", [LD[0], DFF, D], BF16).ap()

    def wkeys(key, l, n):
        return ["%s%d_%d" % (key, l, i) for i in range(n)]
    g.wkeys = wkeys
    g._uid = [0]

    def OP(eng, method, r=(), w=(), **kw):
        return p.op(eng, lambda e, kw=kw, method=method: getattr(e, method)(**kw), reads=r, writes=w)

    def MM(out_, lhsT, rhs, start, stop, r, w):
        return p.op("pe", lambda e: e.matmul(out_, lhsT=lhsT, rhs=rhs, start=start, stop=stop), reads=r, writes=w)

    def DMA(q, out_, in_, r=(), w=(), is_out=False, **kw):
        return p.dma(q, lambda e, kw=kw: e.dma_start(out=out_, in_=in_, **kw), reads=r, writes=w, is_out=is_out)

    g.OP, g.MM, g.DMA = OP, MM, DMA

    for l in range(nlayers):
        for (src, dst, rows, key) in ((w_in, w_in_bf, D, "w_in_bf"), (w_out, w_out_bf, D, "w_out_bf"),
                                      (w_up, w_up_bf, D, "w_up_bf"), (w_down, w_down_bf, DFF, "w_down_bf")):
            for r0 in range(0, rows, 128):
                DMA("pool", dst[l, r0:r0 + 128, :], src[l, r0:r0 + 128, :], w=["%s%d_%d" % (key, l, r0 // 128)],
                    max_dma_last_dim=4096)

    for l in range(nlayers):
        for nm in ("s5B1", "s5B2", "s5C1", "s5C2"):
            for d in range(2):
                DMA("pool", dram[nm + "_bf"][l, d].rearrange("g p s -> (g p) s"), dram[nm][l, d].rearrange("g p s -> (g p) s"),
                    w=["%s_bf%d" % (nm, l)] if d == 1 else ["%s_bf%d_d0" % (nm, l)], max_dma_last_dim=4096)

    es = contextlib.ExitStack()

    def sb(name, shape, dt):
        return es.enter_context(nc.sbuf_tensor(UN() + name, list(shape), dt))

    MODT = sb("MODT", [128, LD[0], 48, 3], F32)
    DER = sb("DER", [128, LD[0], 3, 6, 8], F32)
    GV = sb("GV", [128, LD[0], 4, 8], F32)
    CONV = sb("CONV", [128, LD[0], 3, 22], F32)
    ONESB = sb("ONESB", [128, 128], BF16)
    IDF = sb("IDF", [128, 128], F32)
    MAB = sb("MAB", [128, 2, 128], BF16)
    ESINK = sb("ESINK", [128, LD[0], 2], F32)
    PS = [es.enter_context(nc.psum_tensor("ps%d" % i, [128, 512], F32)) for i in range(8)]
    g.PS = PS

    OP("dve", "memset", ap=ONESB[:], constant=1.0, w=["ONESB"])
    DMA("sp", IDF[:], identf, w=["IDF"])
    DMA("sp", MAB[:], maskAB, w=["MAB"])
    DMA("sp", GV[:], gvec, w=["GV"])
    DMA("sp", CONV[:], convT, w=["CONV"])
    DMA("sp", ESINK[:], sinkT, w=["ESINK"])
    OP("act", "activation", out=ESINK[:], in_=ESINK[:], func=AF.Exp, r=["ESINK"], w=["ESINK"])

    with contextlib.ExitStack() as s1:
        SCT = s1.enter_context(nc.sbuf_tensor(UN() + "SCT", [128, 8, 3], F32))
        BM = s1.enter_context(nc.sbuf_tensor(UN() + "BM", [128, LD[0], 48], F32))
        WM = [s1.enter_context(nc.sbuf_tensor(UN() + "WM%d" % i, [128, 8, 512], F32)) for i in range(2)]
        DMA("sp", SCT[:], cT, w=["SCT"])
        DMA("sp", BM[:], b_modT, w=["BM"])
        OP("act", "activation", out=SCT[:], in_=SCT[:], func=AF.Silu, r=["SCT"], w=["SCT"])
        it = 0
        for l in range(nlayers):
            for cg in range(12):
                wb = WM[it % 2]
                wk = "WM%d" % (it % 2)
                it += 1
                DMA("sp", wb[:], w_mod[l, :, cg * 512:(cg + 1) * 512].rearrange("(k p) c -> p k c", p=128), w=[wk])
                for fc in range(4):
                    f = cg * 4 + fc
                    for k in range(8):
                        MM(PS[0][:, f * 3:f * 3 + 3], wb[:, k, fc * 128:(fc + 1) * 128], SCT[:, k, :], k == 0, k == 7,
                           [wk, "SCT"], ["ps0"])
            for j in range(3):
                OP("dve", "tensor_tensor", out=MODT[:, l, :, j], in0=PS[0][:, 0:144].rearrange("p (f j) -> p f j", j=3)[:, :, j],
                   in1=BM[:, l, :], op=ALU.add, r=["ps0", "BM"], w=["MODT"])
            for j in range(3):
                OP("dve", "scalar_tensor_tensor", out=DER[:, l, j, 0, :], in0=MODT[:, l, 8:16, j], scalar=1.0, in1=GV[:, l, 0, :],
                   op0=ALU.add, op1=ALU.mult, r=["MODT", "GV"], w=["DER"])
                OP("dve", "tensor_copy", out=DER[:, l, j, 1, :], in_=MODT[:, l, 0:8, j], r=["MODT"], w=["DER"])
                OP("dve", "tensor_tensor", out=DER[:, l, j, 2, :], in0=MODT[:, l, 16:24, j], in1=GV[:, l, 1, :], op=ALU.mult,
                   r=["MODT", "GV"], w=["DER"])
                OP("dve", "scalar_tensor_tensor", out=DER[:, l, j, 3, :], in0=MODT[:, l, 32:40, j], scalar=1.0, in1=GV[:, l, 2, :],
                   op0=ALU.add, op1=ALU.mult, r=["MODT", "GV"], w=["DER"])
                OP("dve", "tensor_copy", out=DER[:, l, j, 4, :], in_=MODT[:, l, 24:32, j], r=["MODT"], w=["DER"])
                OP("dve", "tensor_tensor", out=DER[:, l, j, 5, :], in0=MODT[:, l, 40:48, j], in1=GV[:, l, 3, :], op=ALU.mult,
                   r=["MODT", "GV"], w=["DER"])
    p.barrier()

    X = sb("X", [128, 8, NT], F32)
    g.X, g.DER, g.ONESB, g.MAB, g.ESINK, g.CONV = X, DER, ONESB, MAB, ESINK, CONV
    g.w_in_bf, g.w_out_bf, g.w_up_bf, g.w_down_bf = w_in_bf, w_out_bf, w_up_bf, w_down_bf
    g.ropeC, g.ropeS, g.na_exp, g.mcol = ropeC, ropeS, na_exp, mcol
    g.dbg_aps = dbg_aps

    def dump(name, ap, keys):
        if name in dbg_aps:
            DMA("pool", dbg_aps[name], ap, r=keys, is_out=True, max_dma_last_dim=2048)
    g.dump = dump

    for bi in range(nbatch):
        with contextlib.ExitStack() as s2:
            XS = [s2.enter_context(nc.sbuf_tensor(UN() + "XS%d" % i, [128, D], F32)) for i in range(2)]
            for tt in range(18):
                xs, xk = XS[tt % 2], "XS%d" % (tt % 2)
                DMA("sp", xs[:], xcat[bi, tt * 128:(tt + 1) * 128, :], w=[xk])
                for hh in range(2):
                    bank = PS[hh]
                    for kk in range(4):
                        k = hh * 4 + kk
                        p.op("pe", lambda e, o=bank[:, kk * 128:(kk + 1) * 128], i=xs[:, k * 128:(k + 1) * 128]:
                             e.transpose(out=o, in_=i, identity=IDF[:]), reads=[xk, "IDF"], writes=["ps%d" % hh])
                    if hh == 0:
                        OP("act", "activation", out=X[:, 0:4, tt * 128:(tt + 1) * 128],
                           in_=bank[:, :].rearrange("p (k t) -> p k t", t=128), func=AF.Copy, r=["ps0"], w=["X"])
                    else:
                        OP("dve", "tensor_copy", out=X[:, 4:8, tt * 128:(tt + 1) * 128],
                           in_=bank[:, :].rearrange("p (k t) -> p k t", t=128), r=["ps1"], w=["X"])
        p.barrier()
        for l in range(nlayers):
            layer(g, bi, l, (l == DEPTH - 1) and not full_out, mixers)
        with contextlib.ExitStack() as s3:
            OS_ = [s3.enter_context(nc.sbuf_tensor(UN() + "OST%d" % i, [128, D], F32)) for i in range(2)]
            for tt in range(18 if full_out else 16):
                ot, ok = OS_[tt % 2], "OST%d" % (tt % 2)
                t0 = (0 if full_out else NCX) + tt * 128
                for hh in range(2):
                    bank = PS[hh]
                    for kk in range(4):
                        k = hh * 4 + kk
                        p.op("pe", lambda e, o=bank[:, kk * 128:(kk + 1) * 128], i=X[:, k, t0:t0 + 128]:
                             e.transpose(out=o, in_=i, identity=IDF[:]), reads=["X", "IDF"], writes=["ps%d" % hh])
                    if hh == 0:
                        OP("act", "activation", out=ot[:, 0:512], in_=bank[:, :], func=AF.Copy, r=["ps0"], w=[ok])
                    else:
                        OP("dve", "tensor_copy", out=ot[:, 512:1024], in_=bank[:, :], r=["ps1"], w=[ok])
                DMA("sp", out[bi, tt * 128:(tt + 1) * 128, :], ot[:], r=[ok], is_out=True)
        p.barrier()

    p.emit()
    es.close()
    return nc


def rms_stats(g, src_fn, nk, w, SQ, RS, psb, src_keys):
    OP, MM = g.OP, g.MM
    for k in range(nk):
        OP("act", "activation", out=SQ[:, k, :w], in_=src_fn(k), func=AF.Square, r=src_keys, w=["SQ"])
    for k in range(nk):
        MM(g.PS[psb][:, :w], g.ONESB[:], SQ[:, k, :w], k == 0, k == nk - 1, ["SQ", "ONESB"], ["ps%d" % psb])
    OP("act", "activation", out=RS[:, :w], in_=g.PS[psb][:, :w], func=AF.Sqrt, scale=1.0 / D, bias=g.EPSC[:, 0:1],
       r=["ps%d" % psb, "EPSC"], w=["RS"])
    OP("dve", "reciprocal", out=RS[:, :w], in_=RS[:, :w], r=["RS"], w=["RS"])


def layer(g, bi, l, last, mixers):
    nc, p, OP, MM, DMA = g.nc, g.p, g.OP, g.MM, g.DMA
    X, DER, PS = g.X, g.DER, g.PS
    with contextlib.ExitStack() as sl:
        def sb(name, shape, dt):
            return sl.enter_context(nc.sbuf_tensor(UN() + name, list(shape), dt))
        YT = sb("YT", [128, 8, NT], BF16)
        g.YT = YT
        EPSC = sb("EPSC", [128, 1], F32)
        g.EPSC = EPSC
        OP("dve", "memset", ap=EPSC[:], constant=EPS, w=["EPSC"])
        with contextlib.ExitStack() as sh:
            HT = sh.enter_context(nc.sbuf_tensor(UN() + "HT", [128, 8, NT], BF16))
            g.HT = HT
            with contextlib.ExitStack() as sa:
                SQ = sa.enter_context(nc.sbuf_tensor(UN() + "SQ", [128, 8, 512], BF16))
                RS = sa.enter_context(nc.sbuf_tensor(UN() + "RS", [128, 512], F32))
                TMP = [sa.enter_context(nc.sbuf_tensor(UN() + "TMPa%d" % i, [128, 512], F32)) for i in range(2)]
                for (a, b) in TB:
                    w = b - a
                    j = 2 if a < NCX else bi
                    rms_stats(g, lambda k: X[:, k, a:b], 8, w, SQ, RS, 0, ["X"])
                    for k in range(8):
                        tm, tk = TMP[k % 2], "TMPa%d" % (k % 2)
                        OP("dve", "tensor_tensor", out=tm[:, :w], in0=X[:, k, a:b], in1=RS[:, :w], op=ALU.mult,
                           r=["X", "RS"], w=[tk])
                        OP("act", "activation", out=HT[:, k, a:b], in_=tm[:, :w], func=AF.Identity,
                           scale=DER[:, l, j, 0, k:k + 1], bias=DER[:, l, j, 1, k:k + 1], r=[tk, "DER"], w=["HT"])
            p.barrier()
            g.dump("HT%d_%d" % (bi, l), HT[:], ["HT"])
            for nm, chs in (("s5", (0, 1)), ("na", (2, 3)), ("gla", (4, 5)), ("swa", (6, 7))):
                if nm not in mixers:
                    OP("pool", "memset", ap=YT[:, chs[0]:chs[1] + 1, :], constant=0.0, w=["YT"])
            if "s5" in mixers:
                s5_project(g, bi, l)
                p.barrier()
            if "swa" in mixers:
                attn_mixer(g, bi, l, "swa")
                p.barrier()
            if "na" in mixers:
                attn_mixer(g, bi, l, "na")
                p.barrier()
            if "gla" in mixers:
                gla_mixer(g, bi, l)
                p.barrier()
        p.barrier()
        if "s5" in mixers:
            s5_main(g, bi, l)
            p.barrier()
        g.dump("YT%d_%d" % (bi, l), YT[:], ["YT"])
        with contextlib.ExitStack() as sc:
            WO = sc.enter_context(nc.sbuf_tensor(UN() + "WO", [128, 8, D], BF16))
            OS_ = sc.enter_context(nc.sbuf_tensor(UN() + "OS", [128, 8, 512], F32))
            SQ = sc.enter_context(nc.sbuf_tensor(UN() + "SQ", [128, 8, 512], BF16))
            RS = sc.enter_context(nc.sbuf_tensor(UN() + "RS", [128, 512], F32))
            TMP = [sc.enter_context(nc.sbuf_tensor(UN() + "TMPc%d" % i, [128, 512], F32)) for i in range(2)]
            DMA("sp", WO[:], g.w_out_bf[l].rearrange("(k p) c -> p k c", p=128), r=g.wkeys("w_out_bf", l, 8), w=["WO"])
            for (a, b) in TB:
                if last and a < NCX:
                    continue
                w = b - a
                j = 2 if a < NCX else bi
                for dc in range(8):
                    bank = 1 + dc % 2
                    for k in range(8):
                        MM(PS[bank][:, :w], WO[:, k, dc * 128:(dc + 1) * 128], YT[:, k, a:b], k == 0, k == 7,
                           ["WO", "YT"], ["ps%d" % bank])
                    OP("dve", "tensor_copy", out=OS_[:, dc, :w], in_=PS[bank][:, :w], r=["ps%d" % bank], w=["OS%d" % dc])
                rms_stats(g, lambda k: OS_[:, k, :w], 8, w, SQ, RS, 0, ["OS%d" % k for k in range(8)])
                for k in range(8):
                    tm, tk = TMP[k % 2], "TMPc%d" % (k % 2)
                    OP("pool", "tensor_tensor", out=tm[:, :w], in0=OS_[:, k, :w], in1=RS[:, :w], op=ALU.mult,
                       r=["OS%d" % k, "RS"], w=[tk])
                    OP("dve", "scalar_tensor_tensor", out=X[:, k, a:b], in0=tm[:, :w], scalar=DER[:, l, j, 2, k:k + 1],
                       in1=X[:, k, a:b], op0=ALU.mult, op1=ALU.add, r=[tk, "DER", "X"], w=["X"])
    p.barrier()
    g.dump("X1_%d_%d" % (bi, l), X[:], ["X"])
    ffn(g, bi, l, last)
    p.barrier()
    g.dump("X2_%d_%d" % (bi, l), X[:], ["X"])


def ffn_blocks():
    blks = [(0, NCX, 0, NCX)]
    for i in range(5):
        oa = NCX + 410 * i
        ob = min(NCX + 410 * (i + 1), NT)
        blks.append((max(oa - 1, NCX), min(ob + 1, NT), oa, ob))
    return blks


def ffn(g, bi, l, last):
    nc, p, OP, MM, DMA = g.nc, g.p, g.OP, g.MM, g.DMA
    X, DER, PS = g.X, g.DER, g.PS
    with contextlib.ExitStack() as sf:
        def sb(name, shape, dt):
            return sf.enter_context(nc.sbuf_tensor(UN() + name, list(shape), dt))
        EPSC = sb("EPSC", [128, 1], F32)
        g.EPSC = EPSC
        OP("dve", "memset", ap=EPSC[:], constant=EPS, w=["EPSC"])
        HBs = [sb("HB%d" % i, [128, 8, 512], BF16) for i in range(2)]
        GB = sb("GB", [128, 22, 512], BF16)
        SQ = sb("SQ", [128, 8, 512], BF16)
        RS = sb("RS", [128, 512], F32)
        OS_ = sb("OS", [128, 8, 512], F32)
        TMP = [sb("TMPf%d" % i, [128, 512], F32) for i in range(2)]
        GS = [sb("GS%d" % i, [128, 514], F32) for i in range(2)]
        CV = [sb("CV%d" % i, [128, 512], F32) for i in range(2)]
        U1 = [sb("U1%d" % i, [128, 512], F32) for i in range(2)]
        WU = [sb("WU%d" % i, [128, 8, 256], BF16) for i in range(3)]
        WD = [sb("WD%d" % i, [128, 22, 128], BF16) for i in range(2)]
        wu_it = 0
        wd_it = 0
        blks = [bk for bk in ffn_blocks() if not (last and bk[0] < NCX)]

        def make_hb(bidx):
            (ca, cb, oa, ob) = blks[bidx]
            w = cb - ca
            j = 2 if ca < NCX else bi
            HB, hbk = HBs[bidx % 2], "HB%d" % (bidx % 2)
            rms_stats(g, lambda k: X[:, k, ca:cb], 8, w, SQ, RS, 0, ["X"])
            for k in range(8):
                tm, tk = TMP[k % 2], "TMPf%d" % (k % 2)
                OP("dve", "tensor_tensor", out=tm[:, :w], in0=X[:, k, ca:cb], in1=RS[:, :w], op=ALU.mult,
                   r=["X", "RS"], w=[tk])
                OP("act", "activation", out=HB[:, k, :w], in_=tm[:, :w], func=AF.Identity,
                   scale=DER[:, l, j, 3, k:k + 1], bias=DER[:, l, j, 4, k:k + 1], r=[tk, "DER"], w=[hbk])

        make_hb(0)
        for bidx, (ca, cb, oa, ob) in enumerate(blks):
            w = cb - ca
            wo = ob - oa
            off = oa - ca
            j = 2 if ca < NCX else bi
            HB, hbk = HBs[bidx % 2], "HB%d" % (bidx % 2)
            if bidx + 1 < len(blks):
                make_hb(bidx + 1)
            for jc in range(22):
                wu, wuk = WU[wu_it % 3], "WU%d" % (wu_it % 3)
                wu_it += 1
                DMA("sp", wu[:, :, 0:128], g.w_up_bf[l, :, jc * 128:(jc + 1) * 128].rearrange("(k p) c -> p k c", p=128),
                    r=g.wkeys("w_up_bf", l, 8), w=[wuk])
                DMA("sp", wu[:, :, 128:256],
                    g.w_up_bf[l, :, DFF + jc * 128:DFF + (jc + 1) * 128].rearrange("(k p) c -> p k c", p=128),
                    r=g.wkeys("w_up_bf", l, 8), w=[wuk])
                bg, bv = 1 + 2 * (jc % 2), 2 + 2 * (jc % 2)
                for k in range(8):
                    MM(PS[bg][:, :w], wu[:, k, 0:128], HB[:, k, :w], k == 0, k == 7, [wuk, hbk], ["ps%d" % bg])
                for k in range(8):
                    MM(PS[bv][:, :w], wu[:, k, 128:256], HB[:, k, :w], k == 0, k == 7, [wuk, hbk], ["ps%d" % bv])
                gs, gk = GS[jc % 2], "GS%d" % (jc % 2)
                cv, ck = CV[jc % 2], "CV%d" % (jc % 2)
                u1, uk = U1[jc % 2], "U1%d" % (jc % 2)
                OP("pool", "memset", ap=gs[:, 0:1], constant=0.0, w=[gk])
                OP("pool", "memset", ap=gs[:, w + 1:w + 2], constant=0.0, w=[gk])
                OP("act", "activation", out=gs[:, 1:w + 1], in_=PS[bg][:, :w], func=AF.Copy, r=["ps%d" % bg], w=[gk])
                s = 1 + off
                OP("pool", "tensor_scalar", out=cv[:, :wo], in0=gs[:, s - 1:s - 1 + wo], scalar1=g.CONV[:, l, 0, jc:jc + 1],
                   scalar2=0.0, op0=ALU.mult, op1=ALU.add, r=[gk, "CONV"], w=[ck])
                OP("dve", "scalar_tensor_tensor", out=cv[:, :wo], in0=gs[:, s:s + wo], scalar=g.CONV[:, l, 1, jc:jc + 1],
                   in1=cv[:, :wo], op0=ALU.mult, op1=ALU.add, r=[gk, "CONV", ck], w=[ck])
                OP("dve", "scalar_tensor_tensor", out=cv[:, :wo], in0=gs[:, s + 1:s + 1 + wo], scalar=g.CONV[:, l, 2, jc:jc + 1],
                   in1=cv[:, :wo], op0=ALU.mult, op1=ALU.add, r=[gk, "CONV", ck], w=[ck])
                OP("act", "activation", out=u1[:, :wo], in_=cv[:, :wo], func=AF.Gelu_apprx_tanh, r=[ck], w=[uk])
                OP("dve", "tensor_tensor", out=GB[:, jc, :wo], in0=PS[bv][:, off:off + wo], in1=u1[:, :wo], op=ALU.mult,
                   r=["ps%d" % bv, uk], w=["GB"])
            for dc in range(8):
                wd, wdk = WD[wd_it % 2], "WD%d" % (wd_it % 2)
                wd_it += 1
                DMA("sp", wd[:], g.w_down_bf[l, :, dc * 128:(dc + 1) * 128].rearrange("(k p) c -> p k c", p=128),
                    r=g.wkeys("w_down_bf", l, 22), w=[wdk])
                bank = 5 + dc % 2
                for jc in range(22):
                    MM(PS[bank][:, :wo], wd[:, jc, :], GB[:, jc, :wo], jc == 0, jc == 21, [wdk, "GB"], ["ps%d" % bank])
                OP("dve", "tensor_copy", out=OS_[:, dc, :wo], in_=PS[bank][:, :wo], r=["ps%d" % bank], w=["OS%d" % dc])
            rms_stats(g, lambda k: OS_[:, k, :wo], 8, wo, SQ, RS, 0, ["OS%d" % k for k in range(8)])
            for k in range(8):
                tm, tk = TMP[k % 2], "TMPf%d" % (k % 2)
                OP("pool", "tensor_tensor", out=tm[:, :wo], in0=OS_[:, k, :wo], in1=RS[:, :wo], op=ALU.mult,
                   r=["OS%d" % k, "RS"], w=[tk])
                OP("dve", "scalar_tensor_tensor", out=X[:, k, oa:ob], in0=tm[:, :wo], scalar=DER[:, l, j, 5, k:k + 1],
                   in1=X[:, k, oa:ob], op0=ALU.mult, op1=ALU.add, r=[tk, "DER", "X"], w=["X"])


def na_rows(kr):
    rs = [r for r in range(32) if min(max(r - 4, 0), 24) <= kr <= min(max(r - 4, 0), 24) + 7]
    assert rs == list(range(rs[0], rs[-1] + 1))
    return rs[0], rs[-1] + 1


def attn_mixer(g, bi, l, kind):
    nc, p, OP, MM, DMA = g.nc, g.p, g.OP, g.MM, g.DMA
    PS, HT, YT = g.PS, g.HT, g.YT
    wkey = g.wkeys("w_in_bf", l, 8)
    swa = kind == "swa"
    with contextlib.ExitStack() as sm:
        def sb(name, shape, dt):
            return sm.enter_context(nc.sbuf_tensor(UN() + name, list(shape), dt))
        QT = sb("QT", [128, NT], BF16)
        KT = sb("KT", [128, NT], BF16)
        VT = sb("VT", [128, 18, 128], BF16)
        WB = [sb("WB%d" % i, [128, 8, 128], BF16) for i in range(3)]
        T1 = sb("T1", [128, 512], F32)
        T2 = sb("T2", [128, 512], F32)
        PT = [sb("PT%d" % i, [128, 512], BF16) for i in range(2)]
        REC = sb("REC", [128, 512], F32)
        if swa:
            RC = sb("RC", [128, NT], BF16)
            RSN = sb("RSN", [128, NT], BF16)
            DMA("sp", RC[:], g.ropeC, w=["RC"])
            DMA("sp", RSN[:], g.ropeS, w=["RSN"])
        else:
            UT = sb("UT", [128, 2, 960], BF16)
            UF = sb("UF", [128, 960], F32)
            MC = sb("MC", [128, 960], F32)
            DMA("sp", MC[:], g.mcol, w=["MC"])
        wb_it = [0]

        def load_w(c0):
            i = wb_it[0] % 3
            wb_it[0] += 1
            DMA("sp", WB[i][:], g.w_in_bf[l, :, c0:c0 + 128].rearrange("(k p) c -> p k c", p=128), r=wkey, w=["WB%d" % i])
            return WB[i], "WB%d" % i

        for c in range(2):
            if swa:
                cq, cqp, ck, ckp, cv_ = C_SQ + 128 * c, C_SQP + 128 * c, C_SKD + 128 * c, C_SKDP + 128 * c, C_SV
            else:
                cq, ck, cv_ = C_NAQ + 128 * c, C_NAK + 128 * c, C_NAV + 128 * c
            for (dst, dk, c1, c2) in ((QT, "QT", cq, cqp if swa else None), (KT, "KT", ck, ckp if swa else None)):
                w1, w1k = load_w(c1)
                if swa:
                    w2, w2k = load_w(c2)
                for (a, b) in TB:
                    w = b - a
                    for k in range(8):
                        MM(PS[0][:, :w], w1[:, k, :], HT[:, k, a:b], k == 0, k == 7, [w1k, "HT"], ["ps0"])
                    if swa:
                        for k in range(8):
                            MM(PS[1][:, :w], w2[:, k, :], HT[:, k, a:b], k == 0, k == 7, [w2k, "HT"], ["ps1"])
                        OP("dve", "tensor_tensor", out=T1[:, :w], in0=PS[0][:, :w], in1=RC[:, a:b], op=ALU.mult,
                           r=["ps0", "RC"], w=["T1"])
                        OP("dve", "tensor_tensor", out=T2[:, :w], in0=PS[1][:, :w], in1=RSN[:, a:b], op=ALU.mult,
                           r=["ps1", "RSN"], w=["T2"])
                        OP("pool", "tensor_tensor", out=dst[:, a:b], in0=T1[:, :w], in1=T2[:, :w], op=ALU.add,
                           r=["T1", "T2"], w=[dk])
                    else:
                        OP("act", "activation", out=dst[:, a:b], in_=PS[0][:, :w], func=AF.Copy, r=["ps0"], w=[dk])
            if (not swa) or c == 0:
                wv, wvk = load_w(cv_)
                for t4 in range(0, 18, 4):
                    nt = min(4, 18 - t4)
                    for ti in range(nt):
                        tt = t4 + ti
                        for k in range(8):
                            MM(PS[2][:, ti * 128:(ti + 1) * 128], HT[:, k, tt * 128:(tt + 1) * 128], wv[:, k, :], k == 0, k == 7,
                               [wvk, "HT"], ["ps2"])
                    OP("act", "activation", out=VT[:, t4:t4 + nt, :],
                       in_=PS[2][:, 0:nt * 128].rearrange("p (t c) -> p t c", c=128), func=AF.Copy, r=["ps2"], w=["VT"])
            if not swa:
                for hh in range(2):
                    for half in range(2):
                        DMA("sp", UF[half * 64:(half + 1) * 64, :], g.na_exp[l, 2 * c + hh], w=["UF"])
                    OP("act", "activation", out=UF[:], in_=UF[:], func=AF.Exp, r=["UF"], w=["UF"])
                    OP("dve", "tensor_tensor", out=UT[:, hh, :], in0=UF[:], in1=MC[:], op=ALU.mult, r=["UF", "MC"], w=["UT"])
            for (qa, qb) in TB:
                qw = qb - qa
                for hh in range(2):
                    h = 2 * c + hh
                    hb = 64 * hh
                    items = []
                    for kc in range(2):
                        items.append((kc * 128, 128, 0, kc, qa, qb, None))
                    if qa >= NCX:
                        if swa:
                            for kb in range(16):
                                ka = NCX + 128 * kb
                                a_ = max(qa, ka - 128)
                                b_ = min(qb, ka + 256)
                                if a_ < b_:
                                    items.append((ka, 128, 0, 2 + kb, a_, b_, ("swa", ka)))
                        else:
                            for kr in range(32):
                                r0, r1 = na_rows(kr)
                                a_ = max(qa, NCX + 64 * r0)
                                b_ = min(qb, NCX + 64 * r1)
                                if a_ < b_:
                                    items.append((NCX + 64 * kr, 64, 64 * (kr % 2), 2 + kr // 2, a_, b_, ("na", kr)))
                    vc0 = 64 * (h // 2) if swa else 64 * hh
                    for ii, (ka, nk, pb, vt, a_, b_, post) in enumerate(items):
                        n = b_ - a_
                        sbank = 3 + ii % 2
                        pt, ptk = PT[ii % 2], "PT%d" % (ii % 2)
                        MM(PS[sbank][pb:pb + nk, :n], KT[hb:hb + 64, ka:ka + nk], QT[hb:hb + 64, a_:b_], True, True,
                           ["KT", "QT"], ["ps%d" % sbank])
                        OP("act", "activation", out=pt[pb:pb + nk, :n], in_=PS[sbank][pb:pb + nk, :n], func=AF.Exp, scale=0.125,
                           r=["ps%d" % sbank], w=[ptk])
                        if post is not None and post[0] == "swa":
                            kst = post[1]
                            if a_ < kst:
                                OP("pool", "tensor_tensor", out=pt[:, 0:128], in0=pt[:, 0:128], in1=g.MAB[:, 0, :], op=ALU.mult,
                                   r=[ptk, "MAB"], w=[ptk])
                            if b_ > kst + 128:
                                o_ = kst + 128 - a_
                                OP("pool", "tensor_tensor", out=pt[:, o_:o_ + 128], in0=pt[:, o_:o_ + 128], in1=g.MAB[:, 1, :],
                                   op=ALU.mult, r=[ptk, "MAB"], w=[ptk])
                        elif post is not None:
                            kr = post[1]
                            ra = (a_ - NCX) // 64
                            i0 = ra - kr + 7
                            nr = n // 64
                            OP("pool", "tensor_tensor", out=pt[pb:pb + nk, :n], in0=pt[pb:pb + nk, :n],
                               in1=UT[pb:pb + nk, hh, i0 * 64:(i0 + nr) * 64], op=ALU.mult, r=[ptk, "UT"], w=[ptk])
                        MM(PS[5][hb:hb + 64, a_ - qa:b_ - qa], VT[pb:pb + nk, vt, vc0:vc0 + 64], pt[pb:pb + nk, :n], ii == 0,
                           ii == len(items) - 1, ["VT", ptk], ["ps5"])
                        MM(PS[6][hb:hb + 64, a_ - qa:b_ - qa], g.ONESB[pb:pb + nk, 0:64], pt[pb:pb + nk, :n], ii == 0,
                           ii == len(items) - 1, ["ONESB", ptk], ["ps6"])
                if swa:
                    OP("dve", "tensor_scalar", out=REC[:, :qw], in0=PS[6][:, :qw], scalar1=g.ESINK[:, l, c:c + 1], scalar2=None,
                       op0=ALU.add, r=["ps6", "ESINK"], w=["REC"])
                    OP("dve", "reciprocal", out=REC[:, :qw], in_=REC[:, :qw], r=["REC"], w=["REC"])
                else:
                    OP("dve", "reciprocal", out=REC[:, :qw], in_=PS[6][:, :qw], r=["ps6"], w=["REC"])
                yc = (6 if swa else 2) + c
                OP("dve", "tensor_tensor", out=YT[:, yc, qa:qb], in0=PS[5][:, :qw], in1=REC[:, :qw], op=ALU.mult,
                   r=["ps5", "REC"], w=["YT"])


TC = 64


def s5_project(g, bi, l):
    nc, p, OP, MM, DMA = g.nc, g.p, g.OP, g.MM, g.DMA
    PS, HT, YT = g.PS, g.HT, g.YT
    wkey = g.wkeys("w_in_bf", l, 8)
    with contextlib.ExitStack() as sm:
        WA = [sm.enter_context(nc.sbuf_tensor(UN() + "sWA%d" % i, [128, 8, 128], BF16)) for i in range(2)]
        for cc in range(2):
            wa, wak = WA[cc], "sWA%d" % cc
            DMA("sp", wa[:], g.w_in_bf[l, :, C_A + 128 * cc:C_A + 128 * cc + 128].rearrange("(k p) c -> p k c", p=128), r=wkey,
                w=[wak])
            for (a, b) in TB:
                w = b - a
                for k in range(8):
                    MM(PS[7][:, :w], wa[:, k, :], HT[:, k, a:b], k == 0, k == 7, [wak, "HT"], ["ps7"])
                OP("act", "activation", out=YT[:, cc, a:b], in_=PS[7][:, :w], func=AF.Copy, r=["ps7"], w=["sUT"])


def s5_main(g, bi, l):
    nc, p, OP, MM, DMA = g.nc, g.p, g.OP, g.MM, g.DMA
    PS, YT = g.PS, g.YT
    wkey = g.wkeys("w_in_bf", l, 8)
    PI = math.pi
    with contextlib.ExitStack() as sm:
        def sb(name, shape, dt):
            return sm.enter_context(nc.sbuf_tensor(UN() + name, list(shape), dt))
        UT = YT[:, 0:2, :]
        YF = sb("sYF", [128, 2, NT], BF16)
        BP = [sb("sBP%d" % i, [128, 16, 128], BF16) for i in range(2)]
        CP = [sb("sCP%d" % i, [128, 16, 128], BF16) for i in range(2)]
        TAB = [sb("sTAB%d" % i, [128, 16, TC], BF16) for i in range(4)]
        Zs = [sb("sZ%d" % i, [128, 16, TC], F32) for i in range(2)]
        Ws = [sb("sW%d" % i, [128, 16, TC], F32) for i in range(2)]
        ZAs = [sb("sZA%d" % i, [128, 8, TC], F32) for i in range(2)]
        ZBs = [sb("sZB%d" % i, [128, 8, TC], F32) for i in range(2)]
        HCs = [sb("sHC%d" % i, [128, 8, TC], BF16) for i in range(2)]
        HSs = [sb("sHS%d" % i, [128, 8, TC], BF16) for i in range(2)]
        Z, W, ZA, ZB, HC, HS = Zs[0], Ws[0], ZAs[0], ZBs[0], HCs[0], HSs[0]
        WGL = sb("sWGL", [128, 2, 256], BF16)
        LAM = sb("sLAM", [128, 2, 32], F32)
        STP = sb("sSTP", [128, 32], F32)
        DSK = sb("sDSK", [128, LD[0], 2], F32)
        SGN = sb("sSGN", [128, 2], F32)
        JM = sb("sJM", [128, 128], F32)
        HPI = sb("sHPI", [128, 1], F32)
        sm_names = ["RHO", "TH", "M", "SH", "CH", "SN", "CS", "LBR", "LBI", "DEN", "KR", "KI", "T0", "T1", "EC", "ES", "EC2", "ES2"]
        SM = {n: sb("s" + n, [128, 32], F32) for n in sm_names}
        INIT = sb("sINIT", [128, 16], F32)
        ENDS = sb("sENDS", [128, 16], F32)
        RT1 = sb("sRT1", [128, 16], F32)
        TMPY = sb("sTMPY", [128, TC], F32)
        DMA("sp", LAM[:], g.dram["lamT"][:, l], w=["sLAM"])
        DMA("sp", STP[:], g.dram["stepT"][:, l], w=["sSTP"])
        DMA("sp", DSK[:], g.dram["dskT"], w=["sDSK"])
        DMA("sp", SGN[:], g.dram["sgn"], w=["sSGN"])
        DMA("sp", JM[:], g.dram["jmat"], w=["sJM"])
        DMA("pool", WGL[:], g.dram["s5_w_glu"][l].rearrange("(k p) c -> p k c", p=128), w=["sWGL"])
        OP("dve", "memset", ap=HPI[:], constant=PI / 2, w=["sHPI"])

        def V(eng, method, outn, r, **kw):
            OP(eng, method, r=["s" + x for x in r], w=["s" + outn], **kw)

        def TT(outn, an, bn, op):
            V("dve", "tensor_tensor", outn, [an, bn], out=SM[outn][:], in0=SM[an][:], in1=SM[bn][:], op=op)

        OP("act", "activation", out=STP[:], in_=STP[:], func=AF.Exp, r=["sSTP"], w=["sSTP"])
        OP("dve", "tensor_tensor", out=SM["T0"][:], in0=LAM[:, 0, :], in1=STP[:], op=ALU.mult, r=["sLAM", "sSTP"], w=["sT0"])
        OP("act", "activation", out=SM["RHO"][:], in_=SM["T0"][:], func=AF.Exp, r=["sT0"], w=["sRHO"])
        OP("dve", "tensor_tensor", out=SM["TH"][:], in0=LAM[:, 1, :], in1=STP[:], op=ALU.mult, r=["sLAM", "sSTP"], w=["sTH"])
        for _ in range(5):
            V("dve", "tensor_scalar", "M", ["TH"], out=SM["M"][:], in0=SM["TH"][:], scalar1=PI, scalar2=-2 * PI, op0=ALU.is_gt,
              op1=ALU.mult)
            TT("TH", "TH", "M", ALU.add)
        for _ in range(2):
            V("dve", "tensor_scalar", "M", ["TH"], out=SM["M"][:], in0=SM["TH"][:], scalar1=-PI, scalar2=2 * PI, op0=ALU.is_lt,
              op1=ALU.mult)
            TT("TH", "TH", "M", ALU.add)
        OP("act", "activation", out=SM["SH"][:], in_=SM["TH"][:], func=AF.Sin, scale=0.5, r=["sTH"], w=["sSH"])
        OP("act", "activation", out=SM["CH"][:], in_=SM["TH"][:], func=AF.Sin, scale=0.5, bias=HPI[:, 0:1], r=["sTH", "sHPI"],
           w=["sCH"])
        TT("SN", "SH", "CH", ALU.mult)
        V("dve", "tensor_scalar", "SN", ["SN"], out=SM["SN"][:], in0=SM["SN"][:], scalar1=2.0, scalar2=None, op0=ALU.mult)
        TT("T0", "CH", "CH", ALU.mult)
        TT("T1", "SH", "SH", ALU.mult)
        TT("CS", "T0", "T1", ALU.subtract)
        TT("LBR", "RHO", "CS", ALU.mult)
        TT("LBI", "RHO", "SN", ALU.mult)
        V("dve", "tensor_scalar", "LBR", ["LBR"], out=SM["LBR"][:], in0=SM["LBR"][:], scalar1=-1.0, scalar2=None, op0=ALU.add)
        OP("dve", "tensor_tensor", out=SM["T0"][:], in0=LAM[:, 0, :], in1=LAM[:, 0, :], op=ALU.mult, r=["sLAM"], w=["sT0"])
        OP("dve", "tensor_tensor", out=SM["T1"][:], in0=LAM[:, 1, :], in1=LAM[:, 1, :], op=ALU.mult, r=["sLAM"], w=["sT1"])
        TT("DEN", "T0", "T1", ALU.add)
        V("dve", "reciprocal", "DEN", ["DEN"], out=SM["DEN"][:], in_=SM["DEN"][:])
        OP("dve", "tensor_tensor", out=SM["T0"][:], in0=SM["LBR"][:], in1=LAM[:, 0, :], op=ALU.mult, r=["sLBR", "sLAM"], w=["sT0"])
        OP("dve", "tensor_tensor", out=SM["T1"][:], in0=SM["LBI"][:], in1=LAM[:, 1, :], op=ALU.mult, r=["sLBI", "sLAM"], w=["sT1"])
        TT("KR", "T0", "T1", ALU.add)
        TT("KR", "KR", "DEN", ALU.mult)
        OP("dve", "tensor_tensor", out=SM["T0"][:], in0=SM["LBI"][:], in1=LAM[:, 0, :], op=ALU.mult, r=["sLBI", "sLAM"], w=["sT0"])
        OP("dve", "tensor_tensor", out=SM["T1"][:], in0=SM["LBR"][:], in1=LAM[:, 1, :], op=ALU.mult, r=["sLBR", "sLAM"], w=["sT1"])
        TT("KI", "T0", "T1", ALU.subtract)
        TT("KI", "KI", "DEN", ALU.mult)

        nchunk = NT // TC
        for d in range(2):
            qs = slice(16 * d, 16 * d + 16)
            for i, nm in enumerate(("s5B1", "s5B2")):
                DMA("sp", BP[i][:], g.dram[nm + "_bf"][l, d].rearrange("g p s -> p g s"), r=["%s_bf%d" % (nm, l), "%s_bf%d_d0" % (nm, l)], w=["sBP%d" % i])
            for i, nm in enumerate(("s5C1", "s5C2")):
                DMA("sp", CP[i][:], g.dram[nm + "_bf"][l, d].rearrange("g p s -> p g s"), r=["%s_bf%d" % (nm, l), "%s_bf%d_d0" % (nm, l)], w=["sCP%d" % i])
            for which in range(2):
                i0 = 0 if d == 0 else TC - 1
                if which == 0:
                    OP("dve", "tensor_copy", out=Z[:, :, i0], in_=SM["KR"][:, qs], r=["sKR"], w=["sZ0"])
                    OP("dve", "tensor_copy", out=W[:, :, i0], in_=SM["KI"][:, qs], r=["sKI"], w=["sW0"])
                else:
                    OP("dve", "memset", ap=Z[:, :, i0:i0 + 1], constant=1.0, w=["sZ0"])
                    OP("dve", "memset", ap=W[:, :, i0:i0 + 1], constant=0.0, w=["sW0"])
                OP("dve", "tensor_copy", out=SM["EC"][:, 0:16], in_=SM["CS"][:, qs], r=["sCS"], w=["sEC"])
                if which == 0:
                    OP("dve", "tensor_scalar", out=SM["ES"][:, 0:16], in0=SM["SN"][:, qs], scalar1=-1.0, scalar2=None, op0=ALU.mult,
                       r=["sSN"], w=["sES"])
                else:
                    OP("dve", "tensor_copy", out=SM["ES"][:, 0:16], in_=SM["SN"][:, qs], r=["sSN"], w=["sES"])
                n = 1
                while n < TC:
                    if d == 0:
                        src, dst = slice(0, n), slice(n, 2 * n)
                    else:
                        src, dst = slice(TC - n, TC), slice(TC - 2 * n, TC - n)
                    ecb = SM["EC"][:, 0:16].unsqueeze(2).to_broadcast([128, 16, n])
                    esb = SM["ES"][:, 0:16].unsqueeze(2).to_broadcast([128, 16, n])
                    OP("dve", "tensor_tensor", out=ZA[:, :, :].rearrange("p a b -> p (a b)")[:, 0:16 * n].rearrange("p (g n) -> p g n", n=n),
                       in0=Z[:, :, src], in1=ecb, op=ALU.mult, r=["sZ0", "sEC"], w=["sZA0"])
                    OP("dve", "tensor_tensor", out=ZB[:, :, :].rearrange("p a b -> p (a b)")[:, 0:16 * n].rearrange("p (g n) -> p g n", n=n),
                       in0=W[:, :, src], in1=esb, op=ALU.mult, r=["sW0", "sES"], w=["sZB0"])
                    OP("dve", "tensor_tensor", out=Z[:, :, dst],
                       in0=ZA[:, :, :].rearrange("p a b -> p (a b)")[:, 0:16 * n].rearrange("p (g n) -> p g n", n=n),
                       in1=ZB[:, :, :].rearrange("p a b -> p (a b)")[:, 0:16 * n].rearrange("p (g n) -> p g n", n=n),
                       op=ALU.subtract, r=["sZA0", "sZB0"], w=["sZ0"])
                    OP("dve", "tensor_tensor", out=ZA[:, :, :].rearrange("p a b -> p (a b)")[:, 0:16 * n].rearrange("p (g n) -> p g n", n=n),
                       in0=W[:, :, src], in1=ecb, op=ALU.mult, r=["sW0", "sEC"], w=["sZA0"])
                    OP("dve", "tensor_tensor", out=ZB[:, :, :].rearrange("p a b -> p (a b)")[:, 0:16 * n].rearrange("p (g n) -> p g n", n=n),
                       in0=Z[:, :, src], in1=esb, op=ALU.mult, r=["sZ0", "sES"], w=["sZB0"])
                    OP("dve", "tensor_tensor", out=W[:, :, dst],
                       in0=ZA[:, :, :].rearrange("p a b -> p (a b)")[:, 0:16 * n].rearrange("p (g n) -> p g n", n=n),
                       in1=ZB[:, :, :].rearrange("p a b -> p (a b)")[:, 0:16 * n].rearrange("p (g n) -> p g n", n=n),
                       op=ALU.add, r=["sZA0", "sZB0"], w=["sW0"])
                    OP("dve", "tensor_tensor", out=SM["EC2"][:, 0:16], in0=SM["EC"][:, 0:16], in1=SM["EC"][:, 0:16], op=ALU.mult,
                       r=["sEC"], w=["sEC2"])
                    OP("dve", "tensor_tensor", out=SM["ES2"][:, 0:16], in0=SM["ES"][:, 0:16], in1=SM["ES"][:, 0:16], op=ALU.mult,
                       r=["sES"], w=["sES2"])
                    OP("dve", "tensor_tensor", out=SM["ES"][:, 0:16], in0=SM["ES"][:, 0:16], in1=SM["EC"][:, 0:16], op=ALU.mult,
                       r=["sES", "sEC"], w=["sES"])
                    OP("dve", "tensor_scalar", out=SM["ES"][:, 0:16], in0=SM["ES"][:, 0:16], scalar1=2.0, scalar2=None, op0=ALU.mult,
                       r=["sES"], w=["sES"])
                    OP("dve", "tensor_tensor", out=SM["EC"][:, 0:16], in0=SM["EC2"][:, 0:16], in1=SM["ES2"][:, 0:16], op=ALU.subtract,
                       r=["sEC2", "sES2"], w=["sEC"])
                    n *= 2
                if which == 0:
                    OP("dve", "tensor_copy", out=TAB[0][:], in_=Z[:], r=["sZ0"], w=["sTAB0"])
                    OP("dve", "tensor_scalar", out=TAB[1][:], in0=W[:], scalar1=SGN[:, 0:1], scalar2=None, op0=ALU.mult,
                       r=["sW0", "sSGN"], w=["sTAB1"])
                else:
                    OP("dve", "tensor_scalar", out=TAB[2][:], in0=Z[:], scalar1=SGN[:, 1:2], scalar2=None, op0=ALU.mult,
                       r=["sZ0", "sSGN"], w=["sTAB2"])
                    OP("dve", "tensor_scalar", out=TAB[3][:], in0=W[:], scalar1=-1.0, scalar2=None, op0=ALU.mult,
                       r=["sW0"], w=["sTAB3"])
                    OP("dve", "tensor_copy", out=SM["EC2"][:, 0:16], in_=SM["EC"][:, 0:16], r=["sEC"], w=["sEC2"])
                    OP("dve", "tensor_copy", out=SM["ES2"][:, 0:16], in_=SM["ES"][:, 0:16], r=["sES"], w=["sES2"])
            p.barrier()
            OP("dve", "memset", ap=INIT[:], constant=0.0, w=["sINIT"])
            order = list(range(nchunk)) if d == 0 else list(range(NCX // TC - 1, -1, -1)) + list(range(nchunk - 1, NCX // TC - 1, -1))
            allw = lambda pr: ["sW%d_%d" % (pr, gi) for gi in range(16)]
            for ci, m in enumerate(order):
                t0 = m * TC
                cp = ci % 2
                Zc, Wc = Zs[cp], Ws[cp]
                for cc in range(2):
                    par = (ci * 2 + cc) % 2
                    b1, b2, by = PS[0 + par], PS[2 + par], PS[4 + par]
                    k1, k2, ky = "ps%d" % (0 + par), "ps%d" % (2 + par), "ps%d" % (4 + par)
                    za, zb, hc, hs = ZAs[par], ZBs[par], HCs[par], HSs[par]
                    zak, zbk, hck, hsk = "sZA%d" % par, "sZB%d" % par, "sHC%d" % par, "sHS%d" % par
                    zk = "sZ%d_%d" % (cp, cc)
                    for gg in range(8):
                        gi = 8 * cc + gg
                        MM(b1[:, gg * TC:(gg + 1) * TC], BP[0][:, gi, :], UT[:, cc, t0:t0 + TC], True, True, ["sBP0", "sUT"], [k1])
                        MM(b2[:, gg * TC:(gg + 1) * TC], BP[1][:, gi, :], UT[:, cc, t0:t0 + TC], True, True, ["sBP1", "sUT"], [k2])
                    gsl = slice(8 * cc, 8 * cc + 8)
                    OP("dve", "tensor_tensor", out=za[:], in0=b1[:, :].rearrange("p (g n) -> p g n", n=TC), in1=TAB[0][:, gsl, :],
                       op=ALU.mult, r=[k1, "sTAB0"], w=[zak])
                    OP("dve", "tensor_tensor", out=zb[:], in0=b2[:, :].rearrange("p (g n) -> p g n", n=TC), in1=TAB[1][:, gsl, :],
                       op=ALU.mult, r=[k2, "sTAB1"], w=[zbk])
                    OP("pool", "tensor_tensor", out=Zc[:, gsl, :], in0=za[:], in1=zb[:], op=ALU.add, r=[zak, zbk], w=[zk])
                    wks = ["sW%d_%d" % (cp, 8 * cc + gg) for gg in range(8)]
                    for gg in range(8):
                        gi = 8 * cc + gg
                        q = 16 * d + gi
                        rho_b = SM["RHO"][:, q:q + 1].to_broadcast([128, TC])
                        if d == 0:
                            zin, wout = Zc[:, gi, :], Wc[:, gi, :]
                        else:
                            zin, wout = Zc[:, gi, ::-1], Wc[:, gi, ::-1]
                        OP("dve", "tensor_tensor_scan", out=wout, data0=rho_b, data1=zin, initial=INIT[:, gi:gi + 1], op0=ALU.mult,
                           op1=ALU.add, r=[zk, "sRHO", "sINIT"], w=[wks[gg]])
                    OP("pool", "tensor_tensor", out=hc[:], in0=Wc[:, gsl, :], in1=TAB[2][:, gsl, :], op=ALU.mult, r=wks + ["sTAB2"],
                       w=[hck])
                    OP("pool", "tensor_tensor", out=hs[:], in0=Wc[:, gsl, :], in1=TAB[3][:, gsl, :], op=ALU.mult, r=wks + ["sTAB3"],
                       w=[hsk])
                    for gg in range(8):
                        gi = 8 * cc + gg
                        MM(by[:, 0:TC], CP[0][:, gi, :], hc[:, gg, :], gg == 0, False, ["sCP0", hck], [ky])
                        MM(by[:, 0:TC], CP[1][:, gi, :], hs[:, gg, :], False, gg == 7, ["sCP1", hsk], [ky])
                    if d == 0:
                        OP("act", "activation", out=YF[:, cc, t0:t0 + TC], in_=by[:, 0:TC], func=AF.Copy, r=[ky], w=["sYF"])
                    else:
                        OP("dve", "scalar_tensor_tensor", out=TMPY[:], in0=UT[:, cc, t0:t0 + TC], scalar=DSK[:, l, cc:cc + 1],
                           in1=YF[:, cc, t0:t0 + TC], op0=ALU.mult, op1=ALU.add, r=["sUT", "sDSK", "sYF"], w=["sTMPY"])
                        OP("dve", "tensor_tensor", out=YF[:, cc, t0:t0 + TC], in0=TMPY[:], in1=by[:, 0:TC], op=ALU.add,
                           r=["sTMPY", ky], w=["sYF"])
                ecol = TC - 1 if d == 0 else 0
                OP("dve", "tensor_copy", out=ENDS[:], in_=Wc[:, :, ecol], r=allw(cp), w=["sENDS"])
                MM(PS[6][:, 0:16], JM[:], ENDS[:], True, True, ["sJM", "sENDS"], ["ps6"])
                OP("dve", "tensor_tensor", out=RT1[:], in0=ENDS[:], in1=SM["EC2"][:, 0:16], op=ALU.mult, r=["sENDS", "sEC2"], w=["sRT1"])
                OP("dve", "tensor_tensor", out=ENDS[:], in0=PS[6][:, 0:16], in1=SM["ES2"][:, 0:16], op=ALU.mult, r=["ps6", "sES2"],
                   w=["sENDS"])
                OP("dve", "tensor_tensor", out=INIT[:], in0=RT1[:], in1=ENDS[:], op=ALU.add, r=["sRT1", "sENDS"], w=["sINIT"])
            p.barrier()
        for (a, b) in TB:
            w = b - a
            ZT = ZA[:, :, :].rearrange("p a b -> p (a b)")
            for kc in range(2):
                OP("act", "activation", out=HC[:, :, :].rearrange("p a b -> p (a b)")[:, :w] if kc == 0 else
                   HS[:, :, :].rearrange("p a b -> p (a b)")[:, :w], in_=YF[:, kc, a:b], func=AF.Gelu_apprx_tanh, r=["sYF"],
                   w=["sHC0" if kc == 0 else "sHS0"])
            zt = [HC[:, :, :].rearrange("p a b -> p (a b)"), HS[:, :, :].rearrange("p a b -> p (a b)")]
            for oc in range(2):
                for kc in range(2):
                    MM(PS[7][:, :w], WGL[:, kc, oc * 128:(oc + 1) * 128], zt[kc][:, :w], kc == 0, kc == 1,
                       ["sWGL", "sHC0", "sHS0"], ["ps7"])
                OP("act", "activation", out=ZT[:, :w], in_=PS[7][:, :w], func=AF.Sigmoid, r=["ps7"], w=["sZA0"])
                OP("dve", "tensor_tensor", out=YT[:, oc, a:b], in0=zt[oc][:, :w], in1=ZT[:, :w], op=ALU.mult,
                   r=["sHC0", "sHS0", "sZA0"], w=["YT"])


def gla_mixer(g, bi, l):
    nc, p, OP, MM, DMA = g.nc, g.p, g.OP, g.MM, g.DMA
    PS, HT, YT = g.PS, g.HT, g.YT
    wkey = g.wkeys("w_in_bf", l, 8)
    with contextlib.ExitStack() as sm:
        def sb(name, shape, dt):
            return sm.enter_context(nc.sbuf_tensor(UN() + name, list(shape), dt))
        QT = sb("gQT", [128, NT], BF16)
        KT = sb("gKT", [128, NT], BF16)
        SR = sb("gSR", [128, NT], BF16)
        KTK = sb("gKTK", [128, 18, 128], BF16)
        VTK = sb("gVTK", [128, 18, 128], BF16)
        OF = sb("gOF", [128, NT], F32)
        GA = [sb("gGA%d" % d, [32, NT], BF16) for d in range(2)]
        WG = sb("gWG", [32, 2, 256], BF16)
        TRI = sb("gTRI", [128, 2, 128], F32)
        BD = sb("gBD", [128, 128], BF16)
        GN = sb("gGN", [128, LD[0]], F32)
        ONE = sb("gONE", [128, 1], F32)
        EPS_ = sb("gEPS", [128, 1], F32)
        WB = [sb("gWB%d" % i, [128, 8, 128], BF16) for i in range(2)]
        WGB = sb("gWGB", [128, 8, 32], BF16)
        NEG = sb("gNEG", [128, 128], F32)
        EX = sb("gEX", [128, 128], F32)
        E1 = sb("gE1", [128, 128], F32)
        E2 = sb("gE2", [128, 128], F32)
        E2T = sb("gE2T", [128, 128], F32)
        QD = sb("gQD", [128, 128], BF16)
        KD = sb("gKD", [128, 128], BF16)
        KDT = sb("gKDT", [128, 128], BF16)
        AM = [sb("gAM%d" % i, [128, 128], BF16) for i in range(2)]
        S = sb("gS", [128, 64], F32)
        SB_ = sb("gSB", [128, 64], BF16)
        SQ = sb("gSQ", [128, 512], BF16)
        RS = sb("gRS", [128, 512], F32)
        OP("dve", "memset", ap=ONE[:], constant=1.0, w=["gONE"])
        OP("dve", "memset", ap=EPS_[:], constant=EPS, w=["gEPS"])
        DMA("sp", TRI[:], g.dram["tri"], w=["gTRI"])
        DMA("sp", BD[:], g.dram["bd64"], w=["gBD"])
        DMA("sp", GN[:], g.dram["gnT"], w=["gGN"])
        for d in range(2):
            DMA("pool", WG[0:16, d, :], g.dram["gla_w_gate2"][l, d], w=["gWG"])
            DMA("pool", WG[16:17, d, :], g.dram["gla_b_gate"][l, d:d + 1, :], w=["gWG"])
        wb_it = [0]

        def load_w(c0):
            i = wb_it[0] % 2
            wb_it[0] += 1
            DMA("sp", WB[i][:], g.w_in_bf[l, :, c0:c0 + 128].rearrange("(k p) c -> p k c", p=128), r=wkey, w=["gWB%d" % i])
            return WB[i], "gWB%d" % i

        DMA("sp", WGB[:], g.w_in_bf[l, :, C_GF:C_GF + 32].rearrange("(k p) c -> p k c", p=128), r=wkey, w=["gWGB"])
        for d in range(2):
            OP("pool", "memset", ap=GA[d][:], constant=1.0, w=["gGA%d" % d])
            for (a, b) in TB:
                w = b - a
                for k in range(8):
                    MM(PS[0][0:16, :w], WGB[:, k, 16 * d:16 * d + 16], HT[:, k, a:b], k == 0, k == 7, ["gWGB", "HT"], ["ps0"])
                OP("act", "activation", out=GA[d][0:16, a:b], in_=PS[0][0:16, :w], func=AF.Copy, r=["ps0"], w=["gGA%d" % d])
        for c in range(2):
            for (dst, dk, c0, fn, sc) in ((QT, "gQT", C_GQ + 128 * c, AF.Copy, 0.125), (KT, "gKT", C_GK + 128 * c, AF.Copy, 1.0),
                                          (SR, "gSR", C_GR + 128 * c, AF.Silu, 1.0)):
                w1, w1k = load_w(c0)
                for (a, b) in TB:
                    w = b - a
                    for k in range(8):
                        MM(PS[0][:, :w], w1[:, k, :], HT[:, k, a:b], k == 0, k == 7, [w1k, "HT"], ["ps0"])
                    OP("act", "activation", out=dst[:, a:b], in_=PS[0][:, :w], func=fn, scale=sc, r=["ps0"], w=[dk])
            for (dst, dk, c0) in ((KTK, "gKTK", C_GK + 128 * c), (VTK, "gVTK", C_GV + 128 * c)):
                wv, wvk = load_w(c0)
                for t4 in range(0, 18, 4):
                    nt = min(4, 18 - t4)
                    for ti in range(nt):
                        tt = t4 + ti
                        for k in range(8):
                            MM(PS[1][:, ti * 128:(ti + 1) * 128], HT[:, k, tt * 128:(tt + 1) * 128], wv[:, k, :], k == 0, k == 7,
                               [wvk, "HT"], ["ps1"])
                    OP("act", "activation", out=dst[:, t4:t4 + nt, :],
                       in_=PS[1][:, 0:nt * 128].rearrange("p (t c) -> p t c", c=128), func=AF.Copy, r=["ps1"], w=[dk])
            for d in range(2):
                OP("dve", "memset", ap=S[:], constant=0.0, w=["gS"])
                OP("dve", "memset", ap=SB_[:], constant=0.0, w=["gSB"])
                tiles = list(range(18)) if d == 0 else [1, 0] + list(range(17, 1, -1))
                chunks = (0, 1) if d == 0 else (1, 0)
                for tt in tiles:
                    t0 = tt * 128
                    MM(PS[2][:, 0:128], GA[d][0:17, t0:t0 + 128], WG[0:17, d, 128 * c:128 * c + 128], True, True,
                       ["gGA%d" % d, "gWG"], ["ps2"])
                    OP("act", "activation", out=EX[:], in_=PS[2][:, 0:128], func=AF.Exp, scale=-1.0, r=["ps2"], w=["gEX"])
                    OP("act", "activation", out=NEG[:], in_=EX[:], func=AF.Ln, bias=ONE[:, 0:1], scale=1.0, r=["gEX", "gONE"],
                       w=["gNEG"])
                    MM(PS[3][:, 0:128], NEG[:], TRI[:, d, :], True, True, ["gNEG", "gTRI"], ["ps3"])
                    MM(PS[3][:, 128:256], TRI[:, d, :], NEG[:], True, True, ["gNEG", "gTRI"], ["ps3"])
                    OP("act", "activation", out=E1[:], in_=PS[3][:, 0:128], func=AF.Exp, scale=-1.0 / 16, r=["ps3"], w=["gE1"])
                    OP("act", "activation", out=E2[:], in_=PS[3][:, 0:128], func=AF.Exp, scale=1.0 / 16, r=["ps3"], w=["gE2"])
                    OP("act", "activation", out=E2T[:], in_=PS[3][:, 128:256], func=AF.Exp, scale=1.0 / 16, r=["ps3"], w=["gE2T"])
                    OP("dve", "tensor_tensor", out=QD[:], in0=QT[:, t0:t0 + 128], in1=E1[:], op=ALU.mult, r=["gQT", "gE1"], w=["gQD"])
                    OP("pool", "tensor_tensor", out=KD[:], in0=KT[:, t0:t0 + 128], in1=E2[:], op=ALU.mult, r=["gKT", "gE2"], w=["gKD"])
                    OP("pool", "tensor_tensor", out=KDT[:], in0=KTK[:, tt, :], in1=E2T[:], op=ALU.mult, r=["gKTK", "gE2T"],
                       w=["gKDT"])
                    for hh in range(2):
                        hb = 64 * hh
                        MM(PS[4 + hh][:, 0:128], KD[hb:hb + 64, :], QD[hb:hb + 64, :], True, True, ["gKD", "gQD"], ["ps%d" % (4 + hh)])
                        OP("dve", "tensor_tensor", out=AM[hh][:], in0=PS[4 + hh][:, 0:128], in1=TRI[:, d, :], op=ALU.mult,
                           r=["ps%d" % (4 + hh), "gTRI"], w=["gAM%d" % hh])
                        MM(PS[6][hb:hb + 64, 0:128], VTK[:, tt, hb:hb + 64], AM[hh][:], True, False, ["gVTK", "gAM%d" % hh], ["ps6"])
                    for ch in chunks:
                        cs = 64 * ch
                        for hh in range(2):
                            hb = 64 * hh
                            MM(PS[6][hb:hb + 64, cs:cs + 64], SB_[hb:hb + 64, :], QD[hb:hb + 64, cs:cs + 64], False, True,
                               ["gSB", "gQD"], ["ps6"])
                            MM(PS[7][hb:hb + 64, 0:64], KDT[cs:cs + 64, hb:hb + 64], VTK[cs:cs + 64, tt, hb:hb + 64], True, True,
                               ["gKDT", "gVTK"], ["ps7"])
                        dcol = cs + 63 if d == 0 else cs
                        OP("dve", "tensor_tensor", out=S[:], in0=S[:], in1=PS[7][:, 0:64], op=ALU.add, r=["gS", "ps7"], w=["gS"])
                        OP("dve", "tensor_scalar", out=S[:], in0=S[:], scalar1=E1[:, dcol:dcol + 1], scalar2=None, op0=ALU.mult,
                           r=["gS", "gE1"], w=["gS"])
                        OP("act", "activation", out=SB_[:], in_=S[:], func=AF.Copy, r=["gS"], w=["gSB"])
                    if d == 0:
                        OP("act", "activation", out=OF[:, t0:t0 + 128], in_=PS[6][:, 0:128], func=AF.Copy, r=["ps6"], w=["gOF"])
                    else:
                        OP("dve", "tensor_tensor", out=OF[:, t0:t0 + 128], in0=OF[:, t0:t0 + 128], in1=PS[6][:, 0:128], op=ALU.add,
                           r=["gOF", "ps6"], w=["gOF"])
            for (a, b) in TB:
                w = b - a
                OP("act", "activation", out=SQ[:, :w], in_=OF[:, a:b], func=AF.Square, r=["gOF"], w=["gSQ"])
                MM(PS[0][:, :w], BD[:], SQ[:, :w], True, True, ["gBD", "gSQ"], ["ps0"])
                OP("act", "activation", out=RS[:, :w], in_=PS[0][:, :w], func=AF.Sqrt, scale=1.0 / 64, bias=EPS_[:, 0:1],
                   r=["ps0", "gEPS"], w=["gRS"])
                OP("dve", "reciprocal", out=RS[:, :w], in_=RS[:, :w], r=["gRS"], w=["gRS"])
                OP("dve", "scalar_tensor_tensor", out=RS[:, :w], in0=OF[:, a:b], scalar=GN[:, l:l + 1], in1=RS[:, :w], op0=ALU.mult,
                   op1=ALU.mult, r=["gOF", "gGN", "gRS"], w=["gRS"])
                OP("pool", "tensor_tensor", out=YT[:, 4 + c, a:b], in0=RS[:, :w], in1=SR[:, a:b], op=ALU.mult, r=["gRS", "gSR"],
                   w=["YT"])


def host_prep(inputs):
    f = np.float32
    w_in = np.asarray(inputs["w_in"], f)
    sq = w_in[:, :, C_SQ:C_SQ + 256]
    sk = w_in[:, :, C_SK:C_SK + 128]
    dup = np.concatenate([np.arange(64), np.arange(64), 64 + np.arange(64), 64 + np.arange(64)])
    w_in_ext = np.concatenate([w_in, sq[:, :, _rope_perm(4)], sk[:, :, dup], sk[:, :, _rope_perm(2)][:, :, dup]], axis=2)
    assert w_in_ext.shape[2] == NEXT
    rc, rs = _rope_tables()
    kl = np.arange(128)[:, None]
    ql = np.arange(128)[None, :]
    maskAB = np.stack([(kl <= ql), (ql <= kl)], 1).astype(f)
    gv = np.stack([inputs["g_pre_mix"], inputs["g_post_mix"], inputs["g_pre_ffn"], inputs["g_post_ffn"]], 1)
    gvec = np.ascontiguousarray(np.asarray(gv, f).reshape(DEPTH, 4, 8, 128).transpose(3, 0, 1, 2))
    b_modT = np.ascontiguousarray(np.asarray(inputs["b_mod"], f).reshape(DEPTH, 48, 128).transpose(2, 0, 1))
    convT = np.ascontiguousarray(np.asarray(inputs["ffn_conv"], f).reshape(DEPTH, 3, 22, 128).transpose(3, 0, 1, 2))
    sink = np.asarray(inputs["swa_sink"], f)
    sinkT = np.zeros((128, DEPTH, 2), f)
    for c in range(2):
        sinkT[0:64, :, c] = sink[None, :, 2 * c]
        sinkT[64:128, :, c] = sink[None, :, 2 * c + 1]
    rpb = np.asarray(inputs["na_rpb"], f)
    kc = np.arange(64)[:, None]
    qc = np.arange(64)[None, :]
    dcx = np.clip(kc - qc, -15, 15) + 15
    na_exp = rpb[:, :, ::-1, :][:, :, :, dcx]
    na_exp = np.ascontiguousarray(na_exp.transpose(0, 1, 3, 2, 4)).reshape(DEPTH, 4, 64, 15 * 64)
    ws = np.clip(np.arange(64) - 8, 0, 48)
    mc = ((kc >= ws[None, :]) & (kc < ws[None, :] + 16)).astype(f)
    mcol = np.tile(np.tile(mc[:, None, :], (1, 15, 1)).reshape(64, 960), (2, 1))
    jj = np.arange(128)[:, None]
    ii = np.arange(128)[None, :]
    same = (jj // 64) == (ii // 64)
    tri = np.stack([(same & (jj <= ii)), (same & (jj >= ii))], 1).astype(f)
    bd64 = same.astype(f)
    gnT = np.ascontiguousarray(np.tile(np.asarray(inputs["gla_g_norm"], f).T, (2, 1)))
    L = DEPTH
    lre = np.asarray(inputs["s5_lam_re"], f).reshape(L, 32, 64)
    lim = np.asarray(inputs["s5_lam_im"], f).reshape(L, 32, 64)
    lam = np.stack([lre, lim], 1)
    lamT = np.ascontiguousarray(np.tile(lam.transpose(3, 0, 1, 2), (2, 1, 1, 1)))
    stepT = np.ascontiguousarray(np.broadcast_to(np.asarray(inputs["s5_log_step"], f).reshape(L, 32)[None], (128, L, 32)))
    dskT = np.ascontiguousarray(np.asarray(inputs["s5_d"], f).reshape(L, 2, 128).transpose(2, 0, 1))
    sgn = np.ones((128, 2), f)
    sgn[0:64, 0] = -1.0
    sgn[64:128, 1] = -1.0
    jmat = np.zeros((128, 128), f)
    for sp in range(64):
        jmat[sp + 64, sp] = -1.0
        jmat[sp, sp + 64] = 1.0
    bre = np.asarray(inputs["s5_b_re"], f)
    bim = np.asarray(inputs["s5_b_im"], f)
    cre = np.asarray(inputs["s5_c_re"], f)
    cim = np.asarray(inputs["s5_c_im"], f)
    B1 = np.zeros((L, 2, 16, 128, 128), f)
    B2 = np.zeros((L, 2, 16, 128, 128), f)
    C1 = np.zeros((L, 2, 16, 128, 128), f)
    C2 = np.zeros((L, 2, 16, 128, 128), f)
    for gi in range(16):
        r0 = 16 * (gi % 8)
        B1[:, :, gi, r0:r0 + 16, 0:64] = bre[:, :, gi].transpose(0, 1, 3, 2)
        B1[:, :, gi, r0:r0 + 16, 64:128] = bim[:, :, gi].transpose(0, 1, 3, 2)
        B2[:, :, gi, r0:r0 + 16, 0:64] = bim[:, :, gi].transpose(0, 1, 3, 2)
        B2[:, :, gi, r0:r0 + 16, 64:128] = bre[:, :, gi].transpose(0, 1, 3, 2)
        C1[:, :, gi, 0:64, r0:r0 + 16] = cre[:, :, gi].transpose(0, 1, 3, 2)
        C1[:, :, gi, 64:128, r0:r0 + 16] = cim[:, :, gi].transpose(0, 1, 3, 2)
        C2[:, :, gi, 0:64, r0:r0 + 16] = cim[:, :, gi].transpose(0, 1, 3, 2)
        C2[:, :, gi, 64:128, r0:r0 + 16] = cre[:, :, gi].transpose(0, 1, 3, 2)
    shared = {
        "lamT": lamT, "stepT": stepT, "dskT": dskT, "sgn": sgn, "jmat": jmat, "s5_w_glu": np.asarray(inputs["s5_w_glu"], f),
        "s5B1": B1, "s5B2": B2, "s5C1": C1, "s5C2": C2,
        "tri": tri, "bd64": bd64.astype(ml_dtypes.bfloat16), "gnT": gnT,
        "gla_w_gate2": np.asarray(inputs["gla_w_gate2"], f), "gla_b_gate": np.asarray(inputs["gla_b_gate"], f),
        "w_mod": np.asarray(inputs["w_mod"], f), "b_modT": b_modT, "gvec": gvec, "w_in_ext": np.ascontiguousarray(w_in_ext),
        "w_out": np.asarray(inputs["w_out"], f), "ffn_w_up": np.asarray(inputs["ffn_w_up"], f),
        "ffn_w_down": np.asarray(inputs["ffn_w_down"], f), "convT": convT,
        "ropeC": rc.astype(ml_dtypes.bfloat16), "ropeS": rs.astype(ml_dtypes.bfloat16),
        "maskAB": maskAB.astype(ml_dtypes.bfloat16), "identf": np.eye(128, dtype=f), "sinkT": sinkT,
        "na_exp": na_exp, "mcol": np.ascontiguousarray(mcol),
    }
    x = np.asarray(inputs["x"], f)
    ctx = np.asarray(inputs["ctx"], f)
    c = np.asarray(inputs["c"], f)
    cc = np.asarray(inputs["c_ctx"], f)
    in_maps = []
    for core in range(8):
        b0 = 2 * core
        xcat = np.concatenate([ctx[b0:b0 + 2], x[b0:b0 + 2]], axis=1)
        cs = np.stack([c[b0], c[b0 + 1], cc], 0)
        cTm = np.ascontiguousarray(cs.reshape(3, 8, 128).transpose(2, 1, 0))
        m = dict(shared)
        m["xcat"] = np.ascontiguousarray(xcat)
        m["cT"] = cTm
        in_maps.append(m)
    return in_maps


L_FIRST = ("w_mod", "w_in_ext", "w_out", "ffn_w_up", "ffn_w_down", "na_exp", "s5B1", "s5B2", "s5C1", "s5C2", "s5_w_glu",
           "gla_w_gate2", "gla_b_gate")
L_SECOND = ("b_modT", "gvec", "convT", "sinkT", "lamT", "stepT", "dskT")

FUSED = True


def kernel(**inputs):
    in_maps = host_prep(inputs)
    if FUSED:
        nc = bass.Bass("TRN2", target_bir_lowering=False)
        build(nc)
        res = run_bass_kernel_spmd(nc, in_maps, core_ids=list(range(8)))
        outs = [r["out"] for r in res.results]
        return np.concatenate(outs, axis=0).astype(np.float32)
    cur = [m["xcat"] for m in in_maps]
    for l in range(DEPTH):
        base = {}
        m0 = in_maps[0]
        for k, v in m0.items():
            if k in ("xcat", "cT"):
                continue
            if k in L_FIRST:
                base[k] = np.ascontiguousarray(v[l:l + 1])
            elif k in L_SECOND:
                base[k] = np.ascontiguousarray(v[:, l:l + 1])
            elif k == "gnT":
                base[k] = np.ascontiguousarray(v[:, l:l + 1])
            else:
                base[k] = v
        for bi in range(2):
            maps = []
            for core in range(8):
                m = dict(base)
                m["xcat"] = np.ascontiguousarray(cur[core][[bi, 1 - bi]])
                cTm = in_maps[core]["cT"]
                m["cT"] = np.ascontiguousarray(cTm[:, :, [bi, 1 - bi, 2]])
                maps.append(m)
            nc = bass.Bass("TRN2", target_bir_lowering=False)
            build(nc, nlayers=1, nbatch=1, ldim=1, full_out=True)
            res = run_bass_kernel_spmd(nc, maps, core_ids=list(range(8)))
            for core in range(8):
                new = np.array(cur[core])
                new[bi] = res.results[core]["out"][0]
                cur[core] = new
    outs = [c[:, NCX:, :] for c in cur]
    return np.concatenate(outs, axis=0).astype(np.float32)
```

```python
import contextlib
import math
import numpy as np
import ml_dtypes
import concourse.bass as bass
import concourse.mybir as mybir
from concourse.bass_utils import run_bass_kernel_spmd

F32 = mybir.dt.float32
BF16 = mybir.dt.bfloat16
ALU = mybir.AluOpType
AF = mybir.ActivationFunctionType

ENG = ("pe", "act", "dve", "pool", "sp")
NDMA = 24


class P:
    def __init__(self, nc, same_eng_sync=True):
        self.nc = nc
        self.ops = {e: [] for e in ENG}
        self.cnt = {e: 0 for e in ENG}
        self.waited = {e: {} for e in ENG}
        self.last_w = {}
        self.readers = {}
        self.dma_nextq = {}
        self.dma_cnt = [0] * NDMA
        self.dma_last_tok = [None] * NDMA
        self.same = same_eng_sync
        self.out_toks = []
        self.bar = []

    def barrier(self):
        self.bar = [("e", e, self.cnt[e]) for e in ENG if self.cnt[e]] + \
                   [("d", k, self.dma_cnt[k]) for k in range(NDMA) if self.dma_cnt[k]]

    def _deps(self, eng, reads, writes):
        deps = list(self.bar)
        for k in reads:
            t = self.last_w.get(k)
            if t is not None:
                deps.append(t)
        for k in writes:
            t = self.last_w.get(k)
            if t is not None:
                deps.append(t)
            deps.extend(self.readers.get(k, ()))
        return deps

    def _waits(self, eng, deps):
        w = self.waited[eng]
        best = {}
        for t in deps:
            if t[0] == "e":
                _, e2, idx = t
                if e2 == eng and (not self.same or eng == "pe"):
                    continue
                key = e2
            else:
                key = ("d", t[1])
            if w.get(key, 0) >= t[2]:
                continue
            if key not in best or best[key][2] < t[2]:
                best[key] = t
        for key, t in best.items():
            w[key] = t[2]
        return list(best.values())

    def _record(self, tok, reads, writes):
        for k in reads:
            lst = self.readers.setdefault(k, [])
            lst.append(tok)
            if len(lst) > 64:
                best = {}
                for t in lst:
                    kk = t[:2]
                    if kk not in best or best[kk][2] < t[2]:
                        best[kk] = t
                self.readers[k] = list(best.values())
        for k in writes:
            self.last_w[k] = tok
            self.readers[k] = []

    def op(self, eng, fn, reads=(), writes=()):
        deps = self._deps(eng, reads, writes)
        waits = self._waits(eng, deps)
        self.cnt[eng] += 1
        tok = ("e", eng, self.cnt[eng])
        self.ops[eng].append((waits, fn, ("e", eng)))
        self._record(tok, reads, writes)
        return tok

    def dma(self, q, fn, reads=(), writes=(), is_out=False):
        lo, n = (0, 16) if q == "sp" else (16, NDMA - 16)
        cur = self.dma_nextq.get(q, 0)
        k = lo + cur
        self.dma_nextq[q] = (cur + 1) % n
        deps = self._deps(q, reads, writes)
        if self.dma_last_tok[k] is not None:
            deps.append(self.dma_last_tok[k])
        waits = self._waits(q, deps)
        self.dma_cnt[k] += 16
        tok = ("d", k, self.dma_cnt[k])
        self.dma_last_tok[k] = tok
        self.ops[q].append((waits, fn, ("d", k)))
        self._record(tok, reads, writes)
        if is_out:
            self.out_toks.append(tok)
        return tok

    def emit(self):
        nc = self.nc
        with contextlib.ExitStack() as es:
            esem = {e: es.enter_context(nc.semaphore("s_" + e)) for e in ENG}
            dsem = [es.enter_context(nc.semaphore("d%d" % i)) for i in range(NDMA)]
            fin = list(self.out_toks)
            for e in ENG:
                if self.cnt[e]:
                    fin.append(("e", e, self.cnt[e]))
            for k in range(NDMA):
                if self.dma_cnt[k]:
                    fin.append(("d", k, self.dma_cnt[k]))
            block = es.enter_context(nc.Block())

            def run(eng_name, eng):
                for waits, fn, kind in self.ops[eng_name]:
                    for t in waits:
                        if t[0] == "e":
                            eng.wait_ge(esem[t[1]], t[2])
                        else:
                            eng.wait_ge(dsem[t[1]], t[2])
                    ins = fn(eng)
                    if kind[0] == "e":
                        ins.then_inc(esem[kind[1]], 1)
                    else:
                        ins.then_inc(dsem[kind[1]], 16)

            @block.tensor
            def _(e):
                run("pe", e)

            @block.scalar
            def _(e):
                run("act", e)

            @block.vector
            def _(e):
                run("dve", e)

            @block.gpsimd
            def _(e):
                run("pool", e)

            @block.sync
            def _(e):
                run("sp", e)
                for t in fin:
                    if t[0] == "e":
                        if t[1] != "sp":
                            e.wait_ge(esem[t[1]], t[2])
                    else:
                        e.wait_ge(dsem[t[1]], t[2])


NT = 2304
NCX = 256
NLAT = 2048
D = 1024
DEPTH = 4
LD = [4]
DFF = 2816
EPS = 1e-6
TB = [(0, 256)] + [(256 + 512 * i, 256 + 512 * (i + 1)) for i in range(4)]
C_A, C_NAQ, C_NAK, C_NAV = 0, 256, 512, 768
C_GQ, C_GK, C_GV, C_GF, C_GB, C_GR = 1024, 1280, 1536, 1792, 1808, 1824
C_SQ, C_SK, C_SV = 2080, 2336, 2464
C_SQP, C_SKD, C_SKDP, NEXT = 2592, 2848, 3104, 3360


def _rope_perm(nh):
    idx = np.arange(nh * 64).reshape(nh, 4, 16)
    return idx[:, [1, 0, 3, 2], :].reshape(-1)


def _rope_tables():
    cos = np.ones((64, NT), np.float32)
    sin = np.zeros((64, NT), np.float32)
    t = np.arange(NLAT)
    pos = (t // 64, t % 64)
    inv = 10000.0 ** (-np.arange(0, 32, 2, dtype=np.float32) / 32)
    for half in range(2):
        ang = pos[half].astype(np.float32)[None, :] * inv[:, None]
        c, s = np.cos(ang), np.sin(ang)
        b = 32 * half
        cos[b:b + 16, NCX:] = c
        cos[b + 16:b + 32, NCX:] = c
        sin[b:b + 16, NCX:] = -s
        sin[b + 16:b + 32, NCX:] = s
    return np.concatenate([cos, cos], 0), np.concatenate([sin, sin], 0)


class Ctx:
    pass


_UN = [0]


def UN():
    _UN[0] += 1
    return "t%d_" % _UN[0]


def build(nc, dbg=None, nlayers=DEPTH, nbatch=2, mixers=("s5", "na", "gla", "swa"), ldim=DEPTH, full_out=False):
    LD[0] = ldim
    _UN[0] = 0
    p = P(nc)
    g = Ctx()
    g.p, g.nc = p, nc
    dram = {}

    def din(name, shape, dt=F32):
        dram[name] = nc.dram_tensor(name, list(shape), dt, kind="ExternalInput").ap()
        return dram[name]

    xcat = din("xcat", [2, NT, D])
    cT = din("cT", [128, 8, 3])
    w_mod = din("w_mod", [LD[0], D, 6 * D])
    b_modT = din("b_modT", [128, LD[0], 48])
    gvec = din("gvec", [128, LD[0], 4, 8])
    w_in = din("w_in_ext", [LD[0], D, NEXT])
    w_out = din("w_out", [LD[0], D, D])
    w_up = din("ffn_w_up", [LD[0], D, 2 * DFF])
    w_down = din("ffn_w_down", [LD[0], DFF, D])
    convT = din("convT", [128, LD[0], 3, 22])
    ropeC = din("ropeC", [128, NT], BF16)
    ropeS = din("ropeS", [128, NT], BF16)
    maskAB = din("maskAB", [128, 2, 128], BF16)
    identf = din("identf", [128, 128])
    sinkT = din("sinkT", [128, LD[0], 2])
    na_exp = din("na_exp", [LD[0], 4, 64, 15 * 64])
    mcol = din("mcol", [128, 15 * 64])
    din("tri", [128, 2, 128])
    din("bd64", [128, 128], BF16)
    din("gnT", [128, LD[0]])
    din("gla_w_gate2", [LD[0], 2, 16, 256])
    din("gla_b_gate", [LD[0], 2, 256])
    din("lamT", [128, LD[0], 2, 32])
    din("stepT", [128, LD[0], 32])
    din("dskT", [128, LD[0], 2])
    din("sgn", [128, 2])
    din("jmat", [128, 128])
    din("s5_w_glu", [LD[0], 256, 256])
    for nm in ("s5B1", "s5B2", "s5C1", "s5C2"):
        din(nm, [LD[0], 2, 16, 128, 128])
        dram[nm + "_bf"] = nc.dram_tensor(nm + "_bf", [LD[0], 2, 16, 128, 128], BF16).ap()
    g.dram = dram
    out = nc.dram_tensor("out", [2, NT if full_out else NLAT, D], F32, kind="ExternalOutput").ap()
    dbg_aps = {}
    if dbg:
        for name, shape in dbg.items():
            dbg_aps[name] = nc.dram_tensor("dbg_" + name, list(shape), F32, kind="ExternalOutput").ap()

    w_in_bf = nc.dram_tensor("w_in_bf", [LD[0], D, NEXT], BF16).ap()
    w_out_bf = nc.dram_tensor("w_out_bf", [LD[0], D, D], BF16).ap()
    w_up_bf = nc.dram_tensor("w_up_bf", [LD[0], D, 2 * DFF], BF16).ap()
    w_down_bf = nc.dram_tensor("w_down_bf", [LD[0], DFF, D], BF16).ap()

    def wkeys(key, l, n):
        return ["%s%d_%d" % (key, l, i) for i in range(n)]
    g.wkeys = wkeys
    g._uid = [0]

    def OP(eng, method, r=(), w=(), **kw):
        return p.op(eng, lambda e, kw=kw, method=method: getattr(e, method)(**kw), reads=r, writes=w)

    def MM(out_, lhsT, rhs, start, stop, r, w):
        return p.op("pe", lambda e: e.matmul(out_, lhsT=lhsT, rhs=rhs, start=start, stop=stop), reads=r, writes=w)

    def DMA(q, out_, in_, r=(), w=(), is_out=False, **kw):
        return p.dma(q, lambda e, kw=kw: e.dma_start(out=out_, in_=in_, **kw), reads=r, writes=w, is_out=is_out)

    g.OP, g.MM, g.DMA = OP, MM, DMA

    for l in range(nlayers):
        for (src, dst, rows, key) in ((w_in, w_in_bf, D, "w_in_bf"), (w_out, w_out_bf, D, "w_out_bf"),
                                      (w_up, w_up_bf, D, "w_up_bf"), (w_down, w_down_bf, DFF, "w_down_bf")):
            for r0 in range(0, rows, 128):
                DMA("pool", dst[l, r0:r0 + 128, :], src[l, r0:r0 + 128, :], w=["%s%d_%d" % (key, l, r0 // 128)],
                    max_dma_last_dim=4096)

    for l in range(nlayers):
        for nm in ("s5B1", "s5B2", "s5C1", "s5C2"):
            for d in range(2):
                DMA("pool", dram[nm + "_bf"][l, d].rearrange("g p s -> (g p) s"), dram[nm][l, d].rearrange("g p s -> (g p) s"),
                    w=["%s_bf%d" % (nm, l)] if d == 1 else ["%s_bf%d_d0" % (nm, l)], max_dma_last_dim=4096)

    es = contextlib.ExitStack()

    def sb(name, shape, dt):
        return es.enter_context(nc.sbuf_tensor(UN() + name, list(shape), dt))

    MODT = sb("MODT", [128, LD[0], 48, 3], F32)
    DER = sb("DER", [128, LD[0], 3, 6, 8], F32)
    GV = sb("GV", [128, LD[0], 4, 8], F32)
    CONV = sb("CONV", [128, LD[0], 3, 22], F32)
    ONESB = sb("ONESB", [128, 128], BF16)
    IDF = sb("IDF", [128, 128], F32)
    MAB = sb("MAB", [128, 2, 128], BF16)
    ESINK = sb("ESINK", [128, LD[0], 2], F32)
    PS = [es.enter_context(nc.psum_tensor("ps%d" % i, [128, 512], F32)) for i in range(8)]
    g.PS = PS

    OP("dve", "memset", ap=ONESB[:], constant=1.0, w=["ONESB"])
    DMA("sp", IDF[:], identf, w=["IDF"])
    DMA("sp", MAB[:], maskAB, w=["MAB"])
    DMA("sp", GV[:], gvec, w=["GV"])
    DMA("sp", CONV[:], convT, w=["CONV"])
    DMA("sp", ESINK[:], sinkT, w=["ESINK"])
    OP("act", "activation", out=ESINK[:], in_=ESINK[:], func=AF.Exp, r=["ESINK"], w=["ESINK"])

    with contextlib.ExitStack() as s1:
        SCT = s1.enter_context(nc.sbuf_tensor(UN() + "SCT", [128, 8, 3], F32))
        BM = s1.enter_context(nc.sbuf_tensor(UN() + "BM", [128, LD[0], 48], F32))
        WM = [s1.enter_context(nc.sbuf_tensor(UN() + "WM%d" % i, [128, 8, 512], F32)) for i in range(2)]
        DMA("sp", SCT[:], cT, w=["SCT"])
        DMA("sp", BM[:], b_modT, w=["BM"])
        OP("act", "activation", out=SCT[:], in_=SCT[:], func=AF.Silu, r=["SCT"], w=["SCT"])
        it = 0
        for l in range(nlayers):
            for cg in range(12):
                wb = WM[it % 2]
                wk = "WM%d" % (it % 2)
                it += 1
                DMA("sp", wb[:], w_mod[l, :, cg * 512:(cg + 1) * 512].rearrange("(k p) c -> p k c", p=128), w=[wk])
                for fc in range(4):
                    f = cg * 4 + fc
                    for k in range(8):
                        MM(PS[0][:, f * 3:f * 3 + 3], wb[:, k, fc * 128:(fc + 1) * 128], SCT[:, k, :], k == 0, k == 7,
                           [wk, "SCT"], ["ps0"])
            for j in range(3):
                OP("dve", "tensor_tensor", out=MODT[:, l, :, j], in0=PS[0][:, 0:144].rearrange("p (f j) -> p f j", j=3)[:, :, j],
                   in1=BM[:, l, :], op=ALU.add, r=["ps0", "BM"], w=["MODT"])
            for j in range(3):
                OP("dve", "scalar_tensor_tensor", out=DER[:, l, j, 0, :], in0=MODT[:, l, 8:16, j], scalar=1.0, in1=GV[:, l, 0, :],
                   op0=ALU.add, op1=ALU.mult, r=["MODT", "GV"], w=["DER"])
                OP("dve", "tensor_copy", out=DER[:, l, j, 1, :], in_=MODT[:, l, 0:8, j], r=["MODT"], w=["DER"])
                OP("dve", "tensor_tensor", out=DER[:, l, j, 2, :], in0=MODT[:, l, 16:24, j], in1=GV[:, l, 1, :], op=ALU.mult,
                   r=["MODT", "GV"], w=["DER"])
                OP("dve", "scalar_tensor_tensor", out=DER[:, l, j, 3, :], in0=MODT[:, l, 32:40, j], scalar=1.0, in1=GV[:, l, 2, :],
                   op0=ALU.add, op1=ALU.mult, r=["MODT", "GV"], w=["DER"])
                OP("dve", "tensor_copy", out=DER[:, l, j, 4, :], in_=MODT[:, l, 24:32, j], r=["MODT"], w=["DER"])
                OP("dve", "tensor_tensor", out=DER[:, l, j, 5, :], in0=MODT[:, l, 40:48, j], in1=GV[:, l, 3, :], op=ALU.mult,
                   r=["MODT", "GV"], w=["DER"])
    p.barrier()

    X = sb("X", [128, 8, NT], F32)
    g.X, g.DER, g.ONESB, g.MAB, g.ESINK, g.CONV = X, DER, ONESB, MAB, ESINK, CONV
    g.w_in_bf, g.w_out_bf, g.w_up_bf, g.w_down_bf = w_in_bf, w_out_bf, w_up_bf, w_down_bf
    g.ropeC, g.ropeS, g.na_exp, g.mcol = ropeC, ropeS, na_exp, mcol
    g.dbg_aps = dbg_aps

    def dump(name, ap, keys):
        if name in dbg_aps:
            DMA("pool", dbg_aps[name], ap, r=keys, is_out=True, max_dma_last_dim=2048)
    g.dump = dump

    for bi in range(nbatch):
        with contextlib.ExitStack() as s2:
            XS = [s2.enter_context(nc.sbuf_tensor(UN() + "XS%d" % i, [128, D], F32)) for i in range(2)]
            for tt in range(18):
                xs, xk = XS[tt % 2], "XS%d" % (tt % 2)
                DMA("sp", xs[:], xcat[bi, tt * 128:(tt + 1) * 128, :], w=[xk])
                for hh in range(2):
                    bank = PS[hh]
                    for kk in range(4):
                        k = hh * 4 + kk
                        p.op("pe", lambda e, o=bank[:, kk * 128:(kk + 1) * 128], i=xs[:, k * 128:(k + 1) * 128]:
                             e.transpose(out=o, in_=i, identity=IDF[:]), reads=[xk, "IDF"], writes=["ps%d" % hh])
                    if hh == 0:
                        OP("act", "activation", out=X[:, 0:4, tt * 128:(tt + 1) * 128],
                           in_=bank[:, :].rearrange("p (k t) -> p k t", t=128), func=AF.Copy, r=["ps0"], w=["X"])
                    else:
                        OP("dve", "tensor_copy", out=X[:, 4:8, tt * 128:(tt + 1) * 128],
                           in_=bank[:, :].rearrange("p (k t) -> p k t", t=128), r=["ps1"], w=["X"])
        p.barrier()
        for l in range(nlayers):
            layer(g, bi, l, (l == DEPTH - 1) and not full_out, mixers)
        with contextlib.ExitStack() as s3:
            OS_ = [s3.enter_context(nc.sbuf_tensor(UN() + "OST%d" % i, [128, D], F32)) for i in range(2)]
            for tt in range(18 if full_out else 16):
                ot, ok = OS_[tt % 2], "OST%d" % (tt % 2)
                t0 = (0 if full_out else NCX) + tt * 128
                for hh in range(2):
                    bank = PS[hh]
                    for kk in range(4):
                        k = hh * 4 + kk
                        p.op("pe", lambda e, o=bank[:, kk * 128:(kk + 1) * 128], i=X[:, k, t0:t0 + 128]:
                             e.transpose(out=o, in_=i, identity=IDF[:]), reads=["X", "IDF"], writes=["ps%d" % hh])
                    if hh == 0:
                        OP("act", "activation", out=ot[:, 0:512], in_=bank[:, :], func=AF.Copy, r=["ps0"], w=[ok])
                    else:
                        OP("dve", "tensor_copy", out=ot[:, 512:1024], in_=bank[:, :], r=["ps1"], w=[ok])
                DMA("sp", out[bi, tt * 128:(tt + 1) * 128, :], ot[:], r=[ok], is_out=True)
        p.barrier()

    p.emit()
    es.close()
    return nc


def rms_stats(g, src_fn, nk, w, SQ, RS, psb, src_keys):
    OP, MM = g.OP, g.MM
    for k in range(nk):
        OP("act", "activation", out=SQ[:, k, :w], in_=src_fn(k), func=AF.Square, r=src_keys, w=["SQ"])
    for k in range(nk):
        MM(g.PS[psb][:, :w], g.ONESB[:], SQ[:, k, :w], k == 0, k == nk - 1, ["SQ", "ONESB"], ["ps%d" % psb])
    OP("act", "activation", out=RS[:, :w], in_=g.PS[psb][:, :w], func=AF.Sqrt, scale=1.0 / D, bias=g.EPSC[:, 0:1],
       r=["ps%d" % psb, "EPSC"], w=["RS"])
    OP("dve", "reciprocal", out=RS[:, :w], in_=RS[:, :w], r=["RS"], w=["RS"])


def layer(g, bi, l, last, mixers):
    nc, p, OP, MM, DMA = g.nc, g.p, g.OP, g.MM, g.DMA
    X, DER, PS = g.X, g.DER, g.PS
    with contextlib.ExitStack() as sl:
        def sb(name, shape, dt):
            return sl.enter_context(nc.sbuf_tensor(UN() + name, list(shape), dt))
        YT = sb("YT", [128, 8, NT], BF16)
        g.YT = YT
        EPSC = sb("EPSC", [128, 1], F32)
        g.EPSC = EPSC
        OP("dve", "memset", ap=EPSC[:], constant=EPS, w=["EPSC"])
        with contextlib.ExitStack() as sh:
            HT = sh.enter_context(nc.sbuf_tensor(UN() + "HT", [128, 8, NT], BF16))
            g.HT = HT
            with contextlib.ExitStack() as sa:
                SQ = sa.enter_context(nc.sbuf_tensor(UN() + "SQ", [128, 8, 512], BF16))
                RS = sa.enter_context(nc.sbuf_tensor(UN() + "RS", [128, 512], F32))
                TMP = [sa.enter_context(nc.sbuf_tensor(UN() + "TMPa%d" % i, [128, 512], F32)) for i in range(2)]
                for (a, b) in TB:
                    w = b - a
                    j = 2 if a < NCX else bi
                    rms_stats(g, lambda k: X[:, k, a:b], 8, w, SQ, RS, 0, ["X"])
                    for k in range(8):
                        tm, tk = TMP[k % 2], "TMPa%d" % (k % 2)
                        OP("dve", "tensor_tensor", out=tm[:, :w], in0=X[:, k, a:b], in1=RS[:, :w], op=ALU.mult,
                           r=["X", "RS"], w=[tk])
                        OP("act", "activation", out=HT[:, k, a:b], in_=tm[:, :w], func=AF.Identity,
                           scale=DER[:, l, j, 0, k:k + 1], bias=DER[:, l, j, 1, k:k + 1], r=[tk, "DER"], w=["HT"])
            p.barrier()
            g.dump("HT%d_%d" % (bi, l), HT[:], ["HT"])
            for nm, chs in (("s5", (0, 1)), ("na", (2, 3)), ("gla", (4, 5)), ("swa", (6, 7))):
                if nm not in mixers:
                    OP("pool", "memset", ap=YT[:, chs[0]:chs[1] + 1, :], constant=0.0, w=["YT"])
            if "s5" in mixers:
                s5_project(g, bi, l)
                p.barrier()
            if "swa" in mixers:
                attn_mixer(g, bi, l, "swa")
                p.barrier()
            if "na" in mixers:
                attn_mixer(g, bi, l, "na")
                p.barrier()
            if "gla" in mixers:
                gla_mixer(g, bi, l)
                p.barrier()
        p.barrier()
        if "s5" in mixers:
            s5_main(g, bi, l)
            p.barrier()
        g.dump("YT%d_%d" % (bi, l), YT[:], ["YT"])
        with contextlib.ExitStack() as sc:
            WO = sc.enter_context(nc.sbuf_tensor(UN() + "WO", [128, 8, D], BF16))
            OS_ = sc.enter_context(nc.sbuf_tensor(UN() + "OS", [128, 8, 512], F32))
            SQ = sc.enter_context(nc.sbuf_tensor(UN() + "SQ", [128, 8, 512], BF16))
            RS = sc.enter_context(nc.sbuf_tensor(UN() + "RS", [128, 512], F32))
            TMP = [sc.enter_context(nc.sbuf_tensor(UN() + "TMPc%d" % i, [128, 512], F32)) for i in range(2)]
            DMA("sp", WO[:], g.w_out_bf[l].rearrange("(k p) c -> p k c", p=128), r=g.wkeys("w_out_bf", l, 8), w=["WO"])
            for (a, b) in TB:
                if last and a < NCX:
                    continue
                w = b - a
                j = 2 if a < NCX else bi
                for dc in range(8):
                    bank = 1 + dc % 2
                    for k in range(8):
                        MM(PS[bank][:, :w], WO[:, k, dc * 128:(dc + 1) * 128], YT[:, k, a:b], k == 0, k == 7,
                           ["WO", "YT"], ["ps%d" % bank])
                    OP("dve", "tensor_copy", out=OS_[:, dc, :w], in_=PS[bank][:, :w], r=["ps%d" % bank], w=["OS%d" % dc])
                rms_stats(g, lambda k: OS_[:, k, :w], 8, w, SQ, RS, 0, ["OS%d" % k for k in range(8)])
                for k in range(8):
                    tm, tk = TMP[k % 2], "TMPc%d" % (k % 2)
                    OP("pool", "tensor_tensor", out=tm[:, :w], in0=OS_[:, k, :w], in1=RS[:, :w], op=ALU.mult,
                       r=["OS%d" % k, "RS"], w=[tk])
                    OP("dve", "scalar_tensor_tensor", out=X[:, k, a:b], in0=tm[:, :w], scalar=DER[:, l, j, 2, k:k + 1],
                       in1=X[:, k, a:b], op0=ALU.mult, op1=ALU.add, r=[tk, "DER", "X"], w=["X"])
    p.barrier()
    g.dump("X1_%d_%d" % (bi, l), X[:], ["X"])
    ffn(g, bi, l, last)
    p.barrier()
    g.dump("X2_%d_%d" % (bi, l), X[:], ["X"])


def ffn_blocks():
    blks = [(0, NCX, 0, NCX)]
    for i in range(5):
        oa = NCX + 410 * i
        ob = min(NCX + 410 * (i + 1), NT)
        blks.append((max(oa - 1, NCX), min(ob + 1, NT), oa, ob))
    return blks


def ffn(g, bi, l, last):
    nc, p, OP, MM, DMA = g.nc, g.p, g.OP, g.MM, g.DMA
    X, DER, PS = g.X, g.DER, g.PS
    with contextlib.ExitStack() as sf:
        def sb(name, shape, dt):
            return sf.enter_context(nc.sbuf_tensor(UN() + name, list(shape), dt))
        EPSC = sb("EPSC", [128, 1], F32)
        g.EPSC = EPSC
        OP("dve", "memset", ap=EPSC[:], constant=EPS, w=["EPSC"])
        HBs = [sb("HB%d" % i, [128, 8, 512], BF16) for i in range(2)]
        GB = sb("GB", [128, 22, 512], BF16)
        SQ = sb("SQ", [128, 8, 512], BF16)
        RS = sb("RS", [128, 512], F32)
        OS_ = sb("OS", [128, 8, 512], F32)
        TMP = [sb("TMPf%d" % i, [128, 512], F32) for i in range(2)]
        GS = [sb("GS%d" % i, [128, 514], F32) for i in range(2)]
        CV = [sb("CV%d" % i, [128, 512], F32) for i in range(2)]
        U1 = [sb("U1%d" % i, [128, 512], F32) for i in range(2)]
        WU = [sb("WU%d" % i, [128, 8, 256], BF16) for i in range(3)]
        WD = [sb("WD%d" % i, [128, 22, 128], BF16) for i in range(2)]
        wu_it = 0
        wd_it = 0
        blks = [bk for bk in ffn_blocks() if not (last and bk[0] < NCX)]

        def make_hb(bidx):
            (ca, cb, oa, ob) = blks[bidx]
            w = cb - ca
            j = 2 if ca < NCX else bi
            HB, hbk = HBs[bidx % 2], "HB%d" % (bidx % 2)
            rms_stats(g, lambda k: X[:, k, ca:cb], 8, w, SQ, RS, 0, ["X"])
            for k in range(8):
                tm, tk = TMP[k % 2], "TMPf%d" % (k % 2)
                OP("dve", "tensor_tensor", out=tm[:, :w], in0=X[:, k, ca:cb], in1=RS[:, :w], op=ALU.mult,
                   r=["X", "RS"], w=[tk])
                OP("act", "activation", out=HB[:, k, :w], in_=tm[:, :w], func=AF.Identity,
                   scale=DER[:, l, j, 3, k:k + 1], bias=DER[:, l, j, 4, k:k + 1], r=[tk, "DER"], w=[hbk])

        make_hb(0)
        for bidx, (ca, cb, oa, ob) in enumerate(blks):
            w = cb - ca
            wo = ob - oa
            off = oa - ca
            j = 2 if ca < NCX else bi
            HB, hbk = HBs[bidx % 2], "HB%d" % (bidx % 2)
            if bidx + 1 < len(blks):
                make_hb(bidx + 1)
            for jc in range(22):
                wu, wuk = WU[wu_it % 3], "WU%d" % (wu_it % 3)
                wu_it += 1
                DMA("sp", wu[:, :, 0:128], g.w_up_bf[l, :, jc * 128:(jc + 1) * 128].rearrange("(k p) c -> p k c", p=128),
                    r=g.wkeys("w_up_bf", l, 8), w=[wuk])
                DMA("sp", wu[:, :, 128:256],
                    g.w_up_bf[l, :, DFF + jc * 128:DFF + (jc + 1) * 128].rearrange("(k p) c -> p k c", p=128),
                    r=g.wkeys("w_up_bf", l, 8), w=[wuk])
                bg, bv = 1 + 2 * (jc % 2), 2 + 2 * (jc % 2)
                for k in range(8):
                    MM(PS[bg][:, :w], wu[:, k, 0:128], HB[:, k, :w], k == 0, k == 7, [wuk, hbk], ["ps%d" % bg])
                for k in range(8):
                    MM(PS[bv][:, :w], wu[:, k, 128:256], HB[:, k, :w], k == 0, k == 7, [wuk, hbk], ["ps%d" % bv])
                gs, gk = GS[jc % 2], "GS%d" % (jc % 2)
                cv, ck = CV[jc % 2], "CV%d" % (jc % 2)
                u1, uk = U1[jc % 2], "U1%d" % (jc % 2)
                OP("pool", "memset", ap=gs[:, 0:1], constant=0.0, w=[gk])
                OP("pool", "memset", ap=gs[:, w + 1:w + 2], constant=0.0, w=[gk])
                OP("act", "activation", out=gs[:, 1:w + 1], in_=PS[bg][:, :w], func=AF.Copy, r=["ps%d" % bg], w=[gk])
                s = 1 + off
                OP("pool", "tensor_scalar", out=cv[:, :wo], in0=gs[:, s - 1:s - 1 + wo], scalar1=g.CONV[:, l, 0, jc:jc + 1],
                   scalar2=0.0, op0=ALU.mult, op1=ALU.add, r=[gk, "CONV"], w=[ck])
                OP("dve", "scalar_tensor_tensor", out=cv[:, :wo], in0=gs[:, s:s + wo], scalar=g.CONV[:, l, 1, jc:jc + 1],
                   in1=cv[:, :wo], op0=ALU.mult, op1=ALU.add, r=[gk, "CONV", ck], w=[ck])
                OP("dve", "scalar_tensor_tensor", out=cv[:, :wo], in0=gs[:, s + 1:s + 1 + wo], scalar=g.CONV[:, l, 2, jc:jc + 1],
                   in1=cv[:, :wo], op0=ALU.mult, op1=ALU.add, r=[gk, "CONV", ck], w=[ck])
                OP("act", "activation", out=u1[:, :wo], in_=cv[:, :wo], func=AF.Gelu_apprx_tanh, r=[ck], w=[uk])
                OP("dve", "tensor_tensor", out=GB[:, jc, :wo], in0=PS[bv][:, off:off + wo], in1=u1[:, :wo], op=ALU.mult,
                   r=["ps%d" % bv, uk], w=["GB"])
            for dc in range(8):
                wd, wdk = WD[wd_it % 2], "WD%d" % (wd_it % 2)
                wd_it += 1
                DMA("sp", wd[:], g.w_down_bf[l, :, dc * 128:(dc + 1) * 128].rearrange("(k p) c -> p k c", p=128),
                    r=g.wkeys("w_down_bf", l, 22), w=[wdk])
                bank = 5 + dc % 2
                for jc in range(22):
                    MM(PS[bank][:, :wo], wd[:, jc, :], GB[:, jc, :wo], jc == 0, jc == 21, [wdk, "GB"], ["ps%d" % bank])
                OP("dve", "tensor_copy", out=OS_[:, dc, :wo], in_=PS[bank][:, :wo], r=["ps%d" % bank], w=["OS%d" % dc])
            rms_stats(g, lambda k: OS_[:, k, :wo], 8, wo, SQ, RS, 0, ["OS%d" % k for k in range(8)])
            for k in range(8):
                tm, tk = TMP[k % 2], "TMPf%d" % (k % 2)
                OP("pool", "tensor_tensor", out=tm[:, :wo], in0=OS_[:, k, :wo], in1=RS[:, :wo], op=ALU.mult,
                   r=["OS%d" % k, "RS"], w=[tk])
                OP("dve", "scalar_tensor_tensor", out=X[:, k, oa:ob], in0=tm[:, :wo], scalar=DER[:, l, j, 5, k:k + 1],
                   in1=X[:, k, oa:ob], op0=ALU.mult, op1=ALU.add, r=[tk, "DER", "X"], w=["X"])


def na_rows(kr):
    rs = [r for r in range(32) if min(max(r - 4, 0), 24) <= kr <= min(max(r - 4, 0), 24) + 7]
    assert rs == list(range(rs[0], rs[-1] + 1))
    return rs[0], rs[-1] + 1


def attn_mixer(g, bi, l, kind):
    nc, p, OP, MM, DMA = g.nc, g.p, g.OP, g.MM, g.DMA
    PS, HT, YT = g.PS, g.HT, g.YT
    wkey = g.wkeys("w_in_bf", l, 8)
    swa = kind == "swa"
    with contextlib.ExitStack() as sm:
        def sb(name, shape, dt):
            return sm.enter_context(nc.sbuf_tensor(UN() + name, list(shape), dt))
        QT = sb("QT", [128, NT], BF16)
        KT = sb("KT", [128, NT], BF16)
        VT = sb("VT", [128, 18, 128], BF16)
        WB = [sb("WB%d" % i, [128, 8, 128], BF16) for i in range(3)]
        T1 = sb("T1", [128, 512], F32)
        T2 = sb("T2", [128, 512], F32)
        PT = [sb("PT%d" % i, [128, 512], BF16) for i in range(2)]
        REC = sb("REC", [128, 512], F32)
        if swa:
            RC = sb("RC", [128, NT], BF16)
            RSN = sb("RSN", [128, NT], BF16)
            DMA("sp", RC[:], g.ropeC, w=["RC"])
            DMA("sp", RSN[:], g.ropeS, w=["RSN"])
        else:
            UT = sb("UT", [128, 2, 960], BF16)
            UF = sb("UF", [128, 960], F32)
            MC = sb("MC", [128, 960], F32)
            DMA("sp", MC[:], g.mcol, w=["MC"])
        wb_it = [0]

        def load_w(c0):
            i = wb_it[0] % 3
            wb_it[0] += 1
            DMA("sp", WB[i][:], g.w_in_bf[l, :, c0:c0 + 128].rearrange("(k p) c -> p k c", p=128), r=wkey, w=["WB%d" % i])
            return WB[i], "WB%d" % i

        for c in range(2):
            if swa:
                cq, cqp, ck, ckp, cv_ = C_SQ + 128 * c, C_SQP + 128 * c, C_SKD + 128 * c, C_SKDP + 128 * c, C_SV
            else:
                cq, ck, cv_ = C_NAQ + 128 * c, C_NAK + 128 * c, C_NAV + 128 * c
            for (dst, dk, c1, c2) in ((QT, "QT", cq, cqp if swa else None), (KT, "KT", ck, ckp if swa else None)):
                w1, w1k = load_w(c1)
                if swa:
                    w2, w2k = load_w(c2)
                for (a, b) in TB:
                    w = b - a
                    for k in range(8):
                        MM(PS[0][:, :w], w1[:, k, :], HT[:, k, a:b], k == 0, k == 7, [w1k, "HT"], ["ps0"])
                    if swa:
                        for k in range(8):
                            MM(PS[1][:, :w], w2[:, k, :], HT[:, k, a:b], k == 0, k == 7, [w2k, "HT"], ["ps1"])
                        OP("dve", "tensor_tensor", out=T1[:, :w], in0=PS[0][:, :w], in1=RC[:, a:b], op=ALU.mult,
                           r=["ps0", "RC"], w=["T1"])
                        OP("dve", "tensor_tensor", out=T2[:, :w], in0=PS[1][:, :w], in1=RSN[:, a:b], op=ALU.mult,
                           r=["ps1", "RSN"], w=["T2"])
                        OP("pool", "tensor_tensor", out=dst[:, a:b], in0=T1[:, :w], in1=T2[:, :w], op=ALU.add,
                           r=["T1", "T2"], w=[dk])
                    else:
                        OP("act", "activation", out=dst[:, a:b], in_=PS[0][:, :w], func=AF.Copy, r=["ps0"], w=[dk])
            if (not swa) or c == 0:
                wv, wvk = load_w(cv_)
                for t4 in range(0, 18, 4):
                    nt = min(4, 18 - t4)
                    for ti in range(nt):
                        tt = t4 + ti
                        for k in range(8):
                            MM(PS[2][:, ti * 128:(ti + 1) * 128], HT[:, k, tt * 128:(tt + 1) * 128], wv[:, k, :], k == 0, k == 7,
                               [wvk, "HT"], ["ps2"])
                    OP("act", "activation", out=VT[:, t4:t4 + nt, :],
                       in_=PS[2][:, 0:nt * 128].rearrange("p (t c) -> p t c", c=128), func=AF.Copy, r=["ps2"], w=["VT"])
            if not swa:
                for hh in range(2):
                    for half in range(2):
                        DMA("sp", UF[half * 64:(half + 1) * 64, :], g.na_exp[l, 2 * c + hh], w=["UF"])
                    OP("act", "activation", out=UF[:], in_=UF[:], func=AF.Exp, r=["UF"], w=["UF"])
                    OP("dve", "tensor_tensor", out=UT[:, hh, :], in0=UF[:], in1=MC[:], op=ALU.mult, r=["UF", "MC"], w=["UT"])
            for (qa, qb) in TB:
                qw = qb - qa
                for hh in range(2):
                    h = 2 * c + hh
                    hb = 64 * hh
                    items = []
                    for kc in range(2):
                        items.append((kc * 128, 128, 0, kc, qa, qb, None))
                    if qa >= NCX:
                        if swa:
                            for kb in range(16):
                                ka = NCX + 128 * kb
                                a_ = max(qa, ka - 128)
                                b_ = min(qb, ka + 256)
                                if a_ < b_:
                                    items.append((ka, 128, 0, 2 + kb, a_, b_, ("swa", ka)))
                        else:
                            for kr in range(32):
                                r0, r1 = na_rows(kr)
                                a_ = max(qa, NCX + 64 * r0)
                                b_ = min(qb, NCX + 64 * r1)
                                if a_ < b_:
                                    items.append((NCX + 64 * kr, 64, 64 * (kr % 2), 2 + kr // 2, a_, b_, ("na", kr)))
                    vc0 = 64 * (h // 2) if swa else 64 * hh
                    def s_mm(ii):
                        (ka, nk, pb, vt, a_, b_, post) = items[ii]
                        n = b_ - a_
                        sbank = 3 + ii % 2
                        MM(PS[sbank][pb:pb + nk, :n], KT[hb:hb + 64, ka:ka + nk], QT[hb:hb + 64, a_:b_], True, True,
                           ["KT", "QT"], ["ps%d" % sbank])

                    for ii, (ka, nk, pb, vt, a_, b_, post) in enumerate(items):
                        n = b_ - a_
                        sbank = 3 + ii % 2
                        pt, ptk = PT[ii % 2], "PT%d" % (ii % 2)
                        s_mm(ii)
                        OP("act", "activation", out=pt[pb:pb + nk, :n], in_=PS[sbank][pb:pb + nk, :n], func=AF.Exp, scale=0.125,
                           r=["ps%d" % sbank], w=[ptk])
                        if post is not None and post[0] == "swa":
                            kst = post[1]
                            if a_ < kst:
                                OP("pool", "tensor_tensor", out=pt[:, 0:128], in0=pt[:, 0:128], in1=g.MAB[:, 0, :], op=ALU.mult,
                                   r=[ptk, "MAB"], w=[ptk])
                            if b_ > kst + 128:
                                o_ = kst + 128 - a_
                                OP("pool", "tensor_tensor", out=pt[:, o_:o_ + 128], in0=pt[:, o_:o_ + 128], in1=g.MAB[:, 1, :],
                                   op=ALU.mult, r=[ptk, "MAB"], w=[ptk])
                        elif post is not None:
                            kr = post[1]
                            ra = (a_ - NCX) // 64
                            i0 = ra - kr + 7
                            nr = n // 64
                            OP("pool", "tensor_tensor", out=pt[pb:pb + nk, :n], in0=pt[pb:pb + nk, :n],
                               in1=UT[pb:pb + nk, hh, i0 * 64:(i0 + nr) * 64], op=ALU.mult, r=[ptk, "UT"], w=[ptk])
                        MM(PS[5][hb:hb + 64, a_ - qa:b_ - qa], VT[pb:pb + nk, vt, vc0:vc0 + 64], pt[pb:pb + nk, :n], ii == 0,
                           ii == len(items) - 1, ["VT", ptk], ["ps5"])
                        MM(PS[6][hb:hb + 64, a_ - qa:b_ - qa], g.ONESB[pb:pb + nk, 0:64], pt[pb:pb + nk, :n], ii == 0,
                           ii == len(items) - 1, ["ONESB", ptk], ["ps6"])
                if swa:
                    OP("dve", "tensor_scalar", out=REC[:, :qw], in0=PS[6][:, :qw], scalar1=g.ESINK[:, l, c:c + 1], scalar2=None,
                       op0=ALU.add, r=["ps6", "ESINK"], w=["REC"])
                    OP("dve", "reciprocal", out=REC[:, :qw], in_=REC[:, :qw], r=["REC"], w=["REC"])
                else:
                    OP("dve", "reciprocal", out=REC[:, :qw], in_=PS[6][:, :qw], r=["ps6"], w=["REC"])
                yc = (6 if swa else 2) + c
                OP("dve", "tensor_tensor", out=YT[:, yc, qa:qb], in0=PS[5][:, :qw], in1=REC[:, :qw], op=ALU.mult,
                   r=["ps5", "REC"], w=["YT"])


TC = 64


def s5_project(g, bi, l):
    nc, p, OP, MM, DMA = g.nc, g.p, g.OP, g.MM, g.DMA
    PS, HT, YT = g.PS, g.HT, g.YT
    wkey = g.wkeys("w_in_bf", l, 8)
    with contextlib.ExitStack() as sm:
        WA = [sm.enter_context(nc.sbuf_tensor(UN() + "sWA%d" % i, [128, 8, 128], BF16)) for i in range(2)]
        for cc in range(2):
            wa, wak = WA[cc], "sWA%d" % cc
            DMA("sp", wa[:], g.w_in_bf[l, :, C_A + 128 * cc:C_A + 128 * cc + 128].rearrange("(k p) c -> p k c", p=128), r=wkey,
                w=[wak])
            for (a, b) in TB:
                w = b - a
                for k in range(8):
                    MM(PS[7][:, :w], wa[:, k, :], HT[:, k, a:b], k == 0, k == 7, [wak, "HT"], ["ps7"])
                OP("act", "activation", out=YT[:, cc, a:b], in_=PS[7][:, :w], func=AF.Copy, r=["ps7"], w=["sUT"])


def s5_main(g, bi, l):
    nc, p, OP, MM, DMA = g.nc, g.p, g.OP, g.MM, g.DMA
    PS, YT = g.PS, g.YT
    wkey = g.wkeys("w_in_bf", l, 8)
    PI = math.pi
    with contextlib.ExitStack() as sm:
        def sb(name, shape, dt):
            return sm.enter_context(nc.sbuf_tensor(UN() + name, list(shape), dt))
        UT = YT[:, 0:2, :]
        YF = sb("sYF", [128, 2, NT], BF16)
        BP = [sb("sBP%d" % i, [128, 16, 128], BF16) for i in range(2)]
        CP = [sb("sCP%d" % i, [128, 16, 128], BF16) for i in range(2)]
        TAB = [sb("sTAB%d" % i, [128, 16, TC], BF16) for i in range(4)]
        Zs = [sb("sZ%d" % i, [128, 16, TC], F32) for i in range(2)]
        Ws = [sb("sW%d" % i, [128, 16, TC], F32) for i in range(2)]
        ZAs = [sb("sZA%d" % i, [128, 8, TC], F32) for i in range(2)]
        ZBs = [sb("sZB%d" % i, [128, 8, TC], F32) for i in range(2)]
        HCs = [sb("sHC%d" % i, [128, 8, TC], BF16) for i in range(2)]
        HSs = [sb("sHS%d" % i, [128, 8, TC], BF16) for i in range(2)]
        Z, W, ZA, ZB, HC, HS = Zs[0], Ws[0], ZAs[0], ZBs[0], HCs[0], HSs[0]
        WGL = sb("sWGL", [128, 2, 256], BF16)
        LAM = sb("sLAM", [128, 2, 32], F32)
        STP = sb("sSTP", [128, 32], F32)
        DSK = sb("sDSK", [128, LD[0], 2], F32)
        SGN = sb("sSGN", [128, 2], F32)
        JM = sb("sJM", [128, 128], F32)
        HPI = sb("sHPI", [128, 1], F32)
        sm_names = ["RHO", "TH", "M", "SH", "CH", "SN", "CS", "LBR", "LBI", "DEN", "KR", "KI", "T0", "T1", "EC", "ES", "EC2", "ES2"]
        SM = {n: sb("s" + n, [128, 32], F32) for n in sm_names}
        INIT = sb("sINIT", [128, 16], F32)
        ENDS = sb("sENDS", [128, 16], F32)
        RT1 = sb("sRT1", [128, 16], F32)
        TMPYs = [sb("sTMPY%d" % i, [128, TC], F32) for i in range(2)]
        DMA("sp", LAM[:], g.dram["lamT"][:, l], w=["sLAM"])
        DMA("sp", STP[:], g.dram["stepT"][:, l], w=["sSTP"])
        DMA("sp", DSK[:], g.dram["dskT"], w=["sDSK"])
        DMA("sp", SGN[:], g.dram["sgn"], w=["sSGN"])
        DMA("sp", JM[:], g.dram["jmat"], w=["sJM"])
        DMA("pool", WGL[:], g.dram["s5_w_glu"][l].rearrange("(k p) c -> p k c", p=128), w=["sWGL"])
        OP("dve", "memset", ap=HPI[:], constant=PI / 2, w=["sHPI"])

        def V(eng, method, outn, r, **kw):
            OP(eng, method, r=["s" + x for x in r], w=["s" + outn], **kw)

        def TT(outn, an, bn, op):
            V("dve", "tensor_tensor", outn, [an, bn], out=SM[outn][:], in0=SM[an][:], in1=SM[bn][:], op=op)

        OP("act", "activation", out=STP[:], in_=STP[:], func=AF.Exp, r=["sSTP"], w=["sSTP"])
        OP("dve", "tensor_tensor", out=SM["T0"][:], in0=LAM[:, 0, :], in1=STP[:], op=ALU.mult, r=["sLAM", "sSTP"], w=["sT0"])
        OP("act", "activation", out=SM["RHO"][:], in_=SM["T0"][:], func=AF.Exp, r=["sT0"], w=["sRHO"])
        OP("dve", "tensor_tensor", out=SM["TH"][:], in0=LAM[:, 1, :], in1=STP[:], op=ALU.mult, r=["sLAM", "sSTP"], w=["sTH"])
        for _ in range(5):
            V("dve", "tensor_scalar", "M", ["TH"], out=SM["M"][:], in0=SM["TH"][:], scalar1=PI, scalar2=-2 * PI, op0=ALU.is_gt,
              op1=ALU.mult)
            TT("TH", "TH", "M", ALU.add)
        for _ in range(2):
            V("dve", "tensor_scalar", "M", ["TH"], out=SM["M"][:], in0=SM["TH"][:], scalar1=-PI, scalar2=2 * PI, op0=ALU.is_lt,
              op1=ALU.mult)
            TT("TH", "TH", "M", ALU.add)
        OP("act", "activation", out=SM["SH"][:], in_=SM["TH"][:], func=AF.Sin, scale=0.5, r=["sTH"], w=["sSH"])
        OP("act", "activation", out=SM["CH"][:], in_=SM["TH"][:], func=AF.Sin, scale=0.5, bias=HPI[:, 0:1], r=["sTH", "sHPI"],
           w=["sCH"])
        TT("SN", "SH", "CH", ALU.mult)
        V("dve", "tensor_scalar", "SN", ["SN"], out=SM["SN"][:], in0=SM["SN"][:], scalar1=2.0, scalar2=None, op0=ALU.mult)
        TT("T0", "CH", "CH", ALU.mult)
        TT("T1", "SH", "SH", ALU.mult)
        TT("CS", "T0", "T1", ALU.subtract)
        TT("LBR", "RHO", "CS", ALU.mult)
        TT("LBI", "RHO", "SN", ALU.mult)
        V("dve", "tensor_scalar", "LBR", ["LBR"], out=SM["LBR"][:], in0=SM["LBR"][:], scalar1=-1.0, scalar2=None, op0=ALU.add)
        OP("dve", "tensor_tensor", out=SM["T0"][:], in0=LAM[:, 0, :], in1=LAM[:, 0, :], op=ALU.mult, r=["sLAM"], w=["sT0"])
        OP("dve", "tensor_tensor", out=SM["T1"][:], in0=LAM[:, 1, :], in1=LAM[:, 1, :], op=ALU.mult, r=["sLAM"], w=["sT1"])
        TT("DEN", "T0", "T1", ALU.add)
        V("dve", "reciprocal", "DEN", ["DEN"], out=SM["DEN"][:], in_=SM["DEN"][:])
        OP("dve", "tensor_tensor", out=SM["T0"][:], in0=SM["LBR"][:], in1=LAM[:, 0, :], op=ALU.mult, r=["sLBR", "sLAM"], w=["sT0"])
        OP("dve", "tensor_tensor", out=SM["T1"][:], in0=SM["LBI"][:], in1=LAM[:, 1, :], op=ALU.mult, r=["sLBI", "sLAM"], w=["sT1"])
        TT("KR", "T0", "T1", ALU.add)
        TT("KR", "KR", "DEN", ALU.mult)
        OP("dve", "tensor_tensor", out=SM["T0"][:], in0=SM["LBI"][:], in1=LAM[:, 0, :], op=ALU.mult, r=["sLBI", "sLAM"], w=["sT0"])
        OP("dve", "tensor_tensor", out=SM["T1"][:], in0=SM["LBR"][:], in1=LAM[:, 1, :], op=ALU.mult, r=["sLBR", "sLAM"], w=["sT1"])
        TT("KI", "T0", "T1", ALU.subtract)
        TT("KI", "KI", "DEN", ALU.mult)

        nchunk = NT // TC
        for d in range(2):
            qs = slice(16 * d, 16 * d + 16)
            for i, nm in enumerate(("s5B1", "s5B2")):
                DMA("sp", BP[i][:], g.dram[nm + "_bf"][l, d].rearrange("g p s -> p g s"), r=["%s_bf%d" % (nm, l), "%s_bf%d_d0" % (nm, l)], w=["sBP%d" % i])
            for i, nm in enumerate(("s5C1", "s5C2")):
                DMA("sp", CP[i][:], g.dram[nm + "_bf"][l, d].rearrange("g p s -> p g s"), r=["%s_bf%d" % (nm, l), "%s_bf%d_d0" % (nm, l)], w=["sCP%d" % i])
            for which in range(2):
                i0 = 0 if d == 0 else TC - 1
                if which == 0:
                    OP("dve", "tensor_copy", out=Z[:, :, i0], in_=SM["KR"][:, qs], r=["sKR"], w=["sZ0"])
                    OP("dve", "tensor_copy", out=W[:, :, i0], in_=SM["KI"][:, qs], r=["sKI"], w=["sW0"])
                else:
                    OP("dve", "memset", ap=Z[:, :, i0:i0 + 1], constant=1.0, w=["sZ0"])
                    OP("dve", "memset", ap=W[:, :, i0:i0 + 1], constant=0.0, w=["sW0"])
                OP("dve", "tensor_copy", out=SM["EC"][:, 0:16], in_=SM["CS"][:, qs], r=["sCS"], w=["sEC"])
                if which == 0:
                    OP("dve", "tensor_scalar", out=SM["ES"][:, 0:16], in0=SM["SN"][:, qs], scalar1=-1.0, scalar2=None, op0=ALU.mult,
                       r=["sSN"], w=["sES"])
                else:
                    OP("dve", "tensor_copy", out=SM["ES"][:, 0:16], in_=SM["SN"][:, qs], r=["sSN"], w=["sES"])
                n = 1
                while n < TC:
                    if d == 0:
                        src, dst = slice(0, n), slice(n, 2 * n)
                    else:
                        src, dst = slice(TC - n, TC), slice(TC - 2 * n, TC - n)
                    ecb = SM["EC"][:, 0:16].unsqueeze(2).to_broadcast([128, 16, n])
                    esb = SM["ES"][:, 0:16].unsqueeze(2).to_broadcast([128, 16, n])
                    OP("dve", "tensor_tensor", out=ZA[:, :, :].rearrange("p a b -> p (a b)")[:, 0:16 * n].rearrange("p (g n) -> p g n", n=n),
                       in0=Z[:, :, src], in1=ecb, op=ALU.mult, r=["sZ0", "sEC"], w=["sZA0"])
                    OP("dve", "tensor_tensor", out=ZB[:, :, :].rearrange("p a b -> p (a b)")[:, 0:16 * n].rearrange("p (g n) -> p g n", n=n),
                       in0=W[:, :, src], in1=esb, op=ALU.mult, r=["sW0", "sES"], w=["sZB0"])
                    OP("dve", "tensor_tensor", out=Z[:, :, dst],
                       in0=ZA[:, :, :].rearrange("p a b -> p (a b)")[:, 0:16 * n].rearrange("p (g n) -> p g n", n=n),
                       in1=ZB[:, :, :].rearrange("p a b -> p (a b)")[:, 0:16 * n].rearrange("p (g n) -> p g n", n=n),
                       op=ALU.subtract, r=["sZA0", "sZB0"], w=["sZ0"])
                    OP("dve", "tensor_tensor", out=ZA[:, :, :].rearrange("p a b -> p (a b)")[:, 0:16 * n].rearrange("p (g n) -> p g n", n=n),
                       in0=W[:, :, src], in1=ecb, op=ALU.mult, r=["sW0", "sEC"], w=["sZA0"])
                    OP("dve", "tensor_tensor", out=ZB[:, :, :].rearrange("p a b -> p (a b)")[:, 0:16 * n].rearrange("p (g n) -> p g n", n=n),
                       in0=Z[:, :, src], in1=esb, op=ALU.mult, r=["sZ0", "sES"], w=["sZB0"])
                    OP("dve", "tensor_tensor", out=W[:, :, dst],
                       in0=ZA[:, :, :].rearrange("p a b -> p (a b)")[:, 0:16 * n].rearrange("p (g n) -> p g n", n=n),
                       in1=ZB[:, :, :].rearrange("p a b -> p (a b)")[:, 0:16 * n].rearrange("p (g n) -> p g n", n=n),
                       op=ALU.add, r=["sZA0", "sZB0"], w=["sW0"])
                    OP("dve", "tensor_tensor", out=SM["EC2"][:, 0:16], in0=SM["EC"][:, 0:16], in1=SM["EC"][:, 0:16], op=ALU.mult,
                       r=["sEC"], w=["sEC2"])
                    OP("dve", "tensor_tensor", out=SM["ES2"][:, 0:16], in0=SM["ES"][:, 0:16], in1=SM["ES"][:, 0:16], op=ALU.mult,
                       r=["sES"], w=["sES2"])
                    OP("dve", "tensor_tensor", out=SM["ES"][:, 0:16], in0=SM["ES"][:, 0:16], in1=SM["EC"][:, 0:16], op=ALU.mult,
                       r=["sES", "sEC"], w=["sES"])
                    OP("dve", "tensor_scalar", out=SM["ES"][:, 0:16], in0=SM["ES"][:, 0:16], scalar1=2.0, scalar2=None, op0=ALU.mult,
                       r=["sES"], w=["sES"])
                    OP("dve", "tensor_tensor", out=SM["EC"][:, 0:16], in0=SM["EC2"][:, 0:16], in1=SM["ES2"][:, 0:16], op=ALU.subtract,
                       r=["sEC2", "sES2"], w=["sEC"])
                    n *= 2
                if which == 0:
                    OP("dve", "tensor_copy", out=TAB[0][:], in_=Z[:], r=["sZ0"], w=["sTAB0"])
                    OP("dve", "tensor_scalar", out=TAB[1][:], in0=W[:], scalar1=SGN[:, 0:1], scalar2=None, op0=ALU.mult,
                       r=["sW0", "sSGN"], w=["sTAB1"])
                else:
                    OP("dve", "tensor_scalar", out=TAB[2][:], in0=Z[:], scalar1=SGN[:, 1:2], scalar2=None, op0=ALU.mult,
                       r=["sZ0", "sSGN"], w=["sTAB2"])
                    OP("dve", "tensor_scalar", out=TAB[3][:], in0=W[:], scalar1=-1.0, scalar2=None, op0=ALU.mult,
                       r=["sW0"], w=["sTAB3"])
                    OP("dve", "tensor_copy", out=SM["EC2"][:, 0:16], in_=SM["EC"][:, 0:16], r=["sEC"], w=["sEC2"])
                    OP("dve", "tensor_copy", out=SM["ES2"][:, 0:16], in_=SM["ES"][:, 0:16], r=["sES"], w=["sES2"])
            p.barrier()
            OP("dve", "memset", ap=INIT[:], constant=0.0, w=["sINIT"])
            order = list(range(nchunk)) if d == 0 else list(range(NCX // TC - 1, -1, -1)) + list(range(nchunk - 1, NCX // TC - 1, -1))
            allw = lambda pr: ["sW%d_%d" % (pr, gi) for gi in range(16)]
            def stage1(ci):
                m = order[ci]
                t0 = m * TC
                cp = ci % 2
                Zc = Zs[cp]
                for cc in range(2):
                    par = cc
                    b1, b2 = PS[0 + par], PS[2 + par]
                    k1, k2 = "ps%d" % (0 + par), "ps%d" % (2 + par)
                    za, zb = ZAs[par], ZBs[par]
                    zak, zbk = "sZA%d" % par, "sZB%d" % par
                    zk = "sZ%d_%d" % (cp, cc)
                    for gg in range(8):
                        gi = 8 * cc + gg
                        MM(b1[:, gg * TC:(gg + 1) * TC], BP[0][:, gi, :], UT[:, cc, t0:t0 + TC], True, True, ["sBP0", "sUT"], [k1])
                        MM(b2[:, gg * TC:(gg + 1) * TC], BP[1][:, gi, :], UT[:, cc, t0:t0 + TC], True, True, ["sBP1", "sUT"], [k2])
                    gsl = slice(8 * cc, 8 * cc + 8)
                    OP("dve", "tensor_tensor", out=za[:], in0=b1[:, :].rearrange("p (g n) -> p g n", n=TC), in1=TAB[0][:, gsl, :],
                       op=ALU.mult, r=[k1, "sTAB0"], w=[zak])
                    OP("dve", "tensor_tensor", out=zb[:], in0=b2[:, :].rearrange("p (g n) -> p g n", n=TC), in1=TAB[1][:, gsl, :],
                       op=ALU.mult, r=[k2, "sTAB1"], w=[zbk])
                    OP("pool", "tensor_tensor", out=Zc[:, gsl, :], in0=za[:], in1=zb[:], op=ALU.add, r=[zak, zbk], w=[zk])

            def stage2(ci):
                m = order[ci]
                t0 = m * TC
                cp = ci % 2
                Zc, Wc = Zs[cp], Ws[cp]
                for cc in range(2):
                    par = cc
                    by, ky = PS[4 + par], "ps%d" % (4 + par)
                    hc, hs = HCs[par], HSs[par]
                    hck, hsk = "sHC%d" % par, "sHS%d" % par
                    zk = "sZ%d_%d" % (cp, cc)
                    gsl = slice(8 * cc, 8 * cc + 8)
                    wks = ["sW%d_%d" % (cp, 8 * cc + gg) for gg in range(8)]
                    for gg in range(8):
                        gi = 8 * cc + gg
                        q = 16 * d + gi
                        rho_b = SM["RHO"][:, q:q + 1].to_broadcast([128, TC])
                        if d == 0:
                            zin, wout = Zc[:, gi, :], Wc[:, gi, :]
                        else:
                            zin, wout = Zc[:, gi, ::-1], Wc[:, gi, ::-1]
                        OP("dve", "tensor_tensor_scan", out=wout, data0=rho_b, data1=zin, initial=INIT[:, gi:gi + 1], op0=ALU.mult,
                           op1=ALU.add, r=[zk, "sRHO", "sINIT"], w=[wks[gg]])
                    OP("pool", "tensor_tensor", out=hc[:], in0=Wc[:, gsl, :], in1=TAB[2][:, gsl, :], op=ALU.mult, r=wks + ["sTAB2"],
                       w=[hck])
                    OP("pool", "tensor_tensor", out=hs[:], in0=Wc[:, gsl, :], in1=TAB[3][:, gsl, :], op=ALU.mult, r=wks + ["sTAB3"],
                       w=[hsk])
                    for gg in range(8):
                        gi = 8 * cc + gg
                        MM(by[:, 0:TC], CP[0][:, gi, :], hc[:, gg, :], gg == 0, False, ["sCP0", hck], [ky])
                        MM(by[:, 0:TC], CP[1][:, gi, :], hs[:, gg, :], False, gg == 7, ["sCP1", hsk], [ky])
                    if d == 0:
                        OP("act", "activation", out=YF[:, cc, t0:t0 + TC], in_=by[:, 0:TC], func=AF.Copy, r=[ky], w=["sYF"])
                    else:
                        tmy, tmk = TMPYs[par], "sTMPY%d" % par
                        OP("dve", "scalar_tensor_tensor", out=tmy[:], in0=UT[:, cc, t0:t0 + TC], scalar=DSK[:, l, cc:cc + 1],
                           in1=YF[:, cc, t0:t0 + TC], op0=ALU.mult, op1=ALU.add, r=["sUT", "sDSK", "sYF"], w=[tmk])
                        OP("dve", "tensor_tensor", out=YF[:, cc, t0:t0 + TC], in0=tmy[:], in1=by[:, 0:TC], op=ALU.add,
                           r=[tmk, ky], w=["sYF"])
                ecol = TC - 1 if d == 0 else 0
                OP("dve", "tensor_copy", out=ENDS[:], in_=Wc[:, :, ecol], r=allw(cp), w=["sENDS"])
                MM(PS[6][:, 0:16], JM[:], ENDS[:], True, True, ["sJM", "sENDS"], ["ps6"])
                OP("dve", "tensor_tensor", out=RT1[:], in0=ENDS[:], in1=SM["EC2"][:, 0:16], op=ALU.mult, r=["sENDS", "sEC2"], w=["sRT1"])
                OP("dve", "tensor_tensor", out=ENDS[:], in0=PS[6][:, 0:16], in1=SM["ES2"][:, 0:16], op=ALU.mult, r=["ps6", "sES2"],
                   w=["sENDS"])
                OP("dve", "tensor_tensor", out=INIT[:], in0=RT1[:], in1=ENDS[:], op=ALU.add, r=["sRT1", "sENDS"], w=["sINIT"])

            stage1(0)
            for ci in range(len(order)):
                if ci + 1 < len(order):
                    stage1(ci + 1)
                stage2(ci)
            p.barrier()
        for (a, b) in TB:
            w = b - a
            ZT = ZA[:, :, :].rearrange("p a b -> p (a b)")
            for kc in range(2):
                OP("act", "activation", out=HC[:, :, :].rearrange("p a b -> p (a b)")[:, :w] if kc == 0 else
                   HS[:, :, :].rearrange("p a b -> p (a b)")[:, :w], in_=YF[:, kc, a:b], func=AF.Gelu_apprx_tanh, r=["sYF"],
                   w=["sHC0" if kc == 0 else "sHS0"])
            zt = [HC[:, :, :].rearrange("p a b -> p (a b)"), HS[:, :, :].rearrange("p a b -> p (a b)")]
            for oc in range(2):
                for kc in range(2):
                    MM(PS[7][:, :w], WGL[:, kc, oc * 128:(oc + 1) * 128], zt[kc][:, :w], kc == 0, kc == 1,
                       ["sWGL", "sHC0", "sHS0"], ["ps7"])
                OP("act", "activation", out=ZT[:, :w], in_=PS[7][:, :w], func=AF.Sigmoid, r=["ps7"], w=["sZA0"])
                OP("dve", "tensor_tensor", out=YT[:, oc, a:b], in0=zt[oc][:, :w], in1=ZT[:, :w], op=ALU.mult,
                   r=["sHC0", "sHS0", "sZA0"], w=["YT"])


def gla_mixer(g, bi, l):
    nc, p, OP, MM, DMA = g.nc, g.p, g.OP, g.MM, g.DMA
    PS, HT, YT = g.PS, g.HT, g.YT
    wkey = g.wkeys("w_in_bf", l, 8)
    with contextlib.ExitStack() as sm:
        def sb(name, shape, dt):
            return sm.enter_context(nc.sbuf_tensor(UN() + name, list(shape), dt))
        QT = sb("gQT", [128, NT], BF16)
        KT = sb("gKT", [128, NT], BF16)
        SR = sb("gSR", [128, NT], BF16)
        KTK = sb("gKTK", [128, 18, 128], BF16)
        VTK = sb("gVTK", [128, 18, 128], BF16)
        OF = sb("gOF", [128, NT], F32)
        GA = [sb("gGA%d" % d, [32, NT], BF16) for d in range(2)]
        WG = sb("gWG", [32, 2, 256], BF16)
        TRI = sb("gTRI", [128, 2, 128], F32)
        BD = sb("gBD", [128, 128], BF16)
        GN = sb("gGN", [128, LD[0]], F32)
        ONE = sb("gONE", [128, 1], F32)
        EPS_ = sb("gEPS", [128, 1], F32)
        WB = [sb("gWB%d" % i, [128, 8, 128], BF16) for i in range(2)]
        WGB = sb("gWGB", [128, 8, 32], BF16)
        NEG = sb("gNEG", [128, 128], F32)
        EX = sb("gEX", [128, 128], F32)
        E1 = sb("gE1", [128, 128], F32)
        E2 = sb("gE2", [128, 128], F32)
        E2T = sb("gE2T", [128, 128], F32)
        QD = sb("gQD", [128, 128], BF16)
        KD = sb("gKD", [128, 128], BF16)
        KDT = sb("gKDT", [128, 128], BF16)
        AM = [sb("gAM%d" % i, [128, 128], BF16) for i in range(2)]
        S = sb("gS", [128, 64], F32)
        SB_ = sb("gSB", [128, 64], BF16)
        SQ = sb("gSQ", [128, 512], BF16)
        RS = sb("gRS", [128, 512], F32)
        OP("dve", "memset", ap=ONE[:], constant=1.0, w=["gONE"])
        OP("dve", "memset", ap=EPS_[:], constant=EPS, w=["gEPS"])
        DMA("sp", TRI[:], g.dram["tri"], w=["gTRI"])
        DMA("sp", BD[:], g.dram["bd64"], w=["gBD"])
        DMA("sp", GN[:], g.dram["gnT"], w=["gGN"])
        for d in range(2):
            DMA("pool", WG[0:16, d, :], g.dram["gla_w_gate2"][l, d], w=["gWG"])
            DMA("pool", WG[16:17, d, :], g.dram["gla_b_gate"][l, d:d + 1, :], w=["gWG"])
        wb_it = [0]

        def load_w(c0):
            i = wb_it[0] % 2
            wb_it[0] += 1
            DMA("sp", WB[i][:], g.w_in_bf[l, :, c0:c0 + 128].rearrange("(k p) c -> p k c", p=128), r=wkey, w=["gWB%d" % i])
            return WB[i], "gWB%d" % i

        DMA("sp", WGB[:], g.w_in_bf[l, :, C_GF:C_GF + 32].rearrange("(k p) c -> p k c", p=128), r=wkey, w=["gWGB"])
        for d in range(2):
            OP("pool", "memset", ap=GA[d][:], constant=1.0, w=["gGA%d" % d])
            for (a, b) in TB:
                w = b - a
                for k in range(8):
                    MM(PS[0][0:16, :w], WGB[:, k, 16 * d:16 * d + 16], HT[:, k, a:b], k == 0, k == 7, ["gWGB", "HT"], ["ps0"])
                OP("act", "activation", out=GA[d][0:16, a:b], in_=PS[0][0:16, :w], func=AF.Copy, r=["ps0"], w=["gGA%d" % d])
        for c in range(2):
            for (dst, dk, c0, fn, sc) in ((QT, "gQT", C_GQ + 128 * c, AF.Copy, 0.125), (KT, "gKT", C_GK + 128 * c, AF.Copy, 1.0),
                                          (SR, "gSR", C_GR + 128 * c, AF.Silu, 1.0)):
                w1, w1k = load_w(c0)
                for (a, b) in TB:
                    w = b - a
                    for k in range(8):
                        MM(PS[0][:, :w], w1[:, k, :], HT[:, k, a:b], k == 0, k == 7, [w1k, "HT"], ["ps0"])
                    OP("act", "activation", out=dst[:, a:b], in_=PS[0][:, :w], func=fn, scale=sc, r=["ps0"], w=[dk])
            for (dst, dk, c0) in ((KTK, "gKTK", C_GK + 128 * c), (VTK, "gVTK", C_GV + 128 * c)):
                wv, wvk = load_w(c0)
                for t4 in range(0, 18, 4):
                    nt = min(4, 18 - t4)
                    for ti in range(nt):
                        tt = t4 + ti
                        for k in range(8):
                            MM(PS[1][:, ti * 128:(ti + 1) * 128], HT[:, k, tt * 128:(tt + 1) * 128], wv[:, k, :], k == 0, k == 7,
                               [wvk, "HT"], ["ps1"])
                    OP("act", "activation", out=dst[:, t4:t4 + nt, :],
                       in_=PS[1][:, 0:nt * 128].rearrange("p (t c) -> p t c", c=128), func=AF.Copy, r=["ps1"], w=[dk])
            for d in range(2):
                OP("dve", "memset", ap=S[:], constant=0.0, w=["gS"])
                OP("dve", "memset", ap=SB_[:], constant=0.0, w=["gSB"])
                tiles = list(range(18)) if d == 0 else [1, 0] + list(range(17, 1, -1))
                chunks = (0, 1) if d == 0 else (1, 0)
                for tt in tiles:
                    t0 = tt * 128
                    MM(PS[2][:, 0:128], GA[d][0:17, t0:t0 + 128], WG[0:17, d, 128 * c:128 * c + 128], True, True,
                       ["gGA%d" % d, "gWG"], ["ps2"])
                    OP("act", "activation", out=EX[:], in_=PS[2][:, 0:128], func=AF.Exp, scale=-1.0, r=["ps2"], w=["gEX"])
                    OP("act", "activation", out=NEG[:], in_=EX[:], func=AF.Ln, bias=ONE[:, 0:1], scale=1.0, r=["gEX", "gONE"],
                       w=["gNEG"])
                    MM(PS[3][:, 0:128], NEG[:], TRI[:, d, :], True, True, ["gNEG", "gTRI"], ["ps3"])
                    MM(PS[3][:, 128:256], TRI[:, d, :], NEG[:], True, True, ["gNEG", "gTRI"], ["ps3"])
                    OP("act", "activation", out=E1[:], in_=PS[3][:, 0:128], func=AF.Exp, scale=-1.0 / 16, r=["ps3"], w=["gE1"])
                    OP("act", "activation", out=E2[:], in_=PS[3][:, 0:128], func=AF.Exp, scale=1.0 / 16, r=["ps3"], w=["gE2"])
                    OP("act", "activation", out=E2T[:], in_=PS[3][:, 128:256], func=AF.Exp, scale=1.0 / 16, r=["ps3"], w=["gE2T"])
                    OP("dve", "tensor_tensor", out=QD[:], in0=QT[:, t0:t0 + 128], in1=E1[:], op=ALU.mult, r=["gQT", "gE1"], w=["gQD"])
                    OP("pool", "tensor_tensor", out=KD[:], in0=KT[:, t0:t0 + 128], in1=E2[:], op=ALU.mult, r=["gKT", "gE2"], w=["gKD"])
                    OP("pool", "tensor_tensor", out=KDT[:], in0=KTK[:, tt, :], in1=E2T[:], op=ALU.mult, r=["gKTK", "gE2T"],
                       w=["gKDT"])
                    for hh in range(2):
                        hb = 64 * hh
                        MM(PS[4 + hh][:, 0:128], KD[hb:hb + 64, :], QD[hb:hb + 64, :], True, True, ["gKD", "gQD"], ["ps%d" % (4 + hh)])
                        OP("dve", "tensor_tensor", out=AM[hh][:], in0=PS[4 + hh][:, 0:128], in1=TRI[:, d, :], op=ALU.mult,
                           r=["ps%d" % (4 + hh), "gTRI"], w=["gAM%d" % hh])
                        MM(PS[6][hb:hb + 64, 0:128], VTK[:, tt, hb:hb + 64], AM[hh][:], True, False, ["gVTK", "gAM%d" % hh], ["ps6"])
                    for ch in chunks:
                        cs = 64 * ch
                        for hh in range(2):
                            hb = 64 * hh
                            MM(PS[6][hb:hb + 64, cs:cs + 64], SB_[hb:hb + 64, :], QD[hb:hb + 64, cs:cs + 64], False, True,
                               ["gSB", "gQD"], ["ps6"])
                            MM(PS[7][hb:hb + 64, 0:64], KDT[cs:cs + 64, hb:hb + 64], VTK[cs:cs + 64, tt, hb:hb + 64], True, True,
                               ["gKDT", "gVTK"], ["ps7"])
                        dcol = cs + 63 if d == 0 else cs
                        OP("dve", "tensor_tensor", out=S[:], in0=S[:], in1=PS[7][:, 0:64], op=ALU.add, r=["gS", "ps7"], w=["gS"])
                        OP("dve", "tensor_scalar", out=S[:], in0=S[:], scalar1=E1[:, dcol:dcol + 1], scalar2=None, op0=ALU.mult,
                           r=["gS", "gE1"], w=["gS"])
                        OP("act", "activation", out=SB_[:], in_=S[:], func=AF.Copy, r=["gS"], w=["gSB"])
                    if d == 0:
                        OP("act", "activation", out=OF[:, t0:t0 + 128], in_=PS[6][:, 0:128], func=AF.Copy, r=["ps6"], w=["gOF"])
                    else:
                        OP("dve", "tensor_tensor", out=OF[:, t0:t0 + 128], in0=OF[:, t0:t0 + 128], in1=PS[6][:, 0:128], op=ALU.add,
                           r=["gOF", "ps6"], w=["gOF"])
            for (a, b) in TB:
                w = b - a
                OP("act", "activation", out=SQ[:, :w], in_=OF[:, a:b], func=AF.Square, r=["gOF"], w=["gSQ"])
                MM(PS[0][:, :w], BD[:], SQ[:, :w], True, True, ["gBD", "gSQ"], ["ps0"])
                OP("act", "activation", out=RS[:, :w], in_=PS[0][:, :w], func=AF.Sqrt, scale=1.0 / 64, bias=EPS_[:, 0:1],
                   r=["ps0", "gEPS"], w=["gRS"])
                OP("dve", "reciprocal", out=RS[:, :w], in_=RS[:, :w], r=["gRS"], w=["gRS"])
                OP("dve", "scalar_tensor_tensor", out=RS[:, :w], in0=OF[:, a:b], scalar=GN[:, l:l + 1], in1=RS[:, :w], op0=ALU.mult,
                   op1=ALU.mult, r=["gOF", "gGN", "gRS"], w=["gRS"])
                OP("pool", "tensor_tensor", out=YT[:, 4 + c, a:b], in0=RS[:, :w], in1=SR[:, a:b], op=ALU.mult, r=["gRS", "gSR"],
                   w=["YT"])


def host_prep(inputs):
    f = np.float32
    w_in = np.asarray(inputs["w_in"], f)
    sq = w_in[:, :, C_SQ:C_SQ + 256]
    sk = w_in[:, :, C_SK:C_SK + 128]
    dup = np.concatenate([np.arange(64), np.arange(64), 64 + np.arange(64), 64 + np.arange(64)])
    w_in_ext = np.concatenate([w_in, sq[:, :, _rope_perm(4)], sk[:, :, dup], sk[:, :, _rope_perm(2)][:, :, dup]], axis=2)
    assert w_in_ext.shape[2] == NEXT
    rc, rs = _rope_tables()
    kl = np.arange(128)[:, None]
    ql = np.arange(128)[None, :]
    maskAB = np.stack([(kl <= ql), (ql <= kl)], 1).astype(f)
    gv = np.stack([inputs["g_pre_mix"], inputs["g_post_mix"], inputs["g_pre_ffn"], inputs["g_post_ffn"]], 1)
    gvec = np.ascontiguousarray(np.asarray(gv, f).reshape(DEPTH, 4, 8, 128).transpose(3, 0, 1, 2))
    b_modT = np.ascontiguousarray(np.asarray(inputs["b_mod"], f).reshape(DEPTH, 48, 128).transpose(2, 0, 1))
    convT = np.ascontiguousarray(np.asarray(inputs["ffn_conv"], f).reshape(DEPTH, 3, 22, 128).transpose(3, 0, 1, 2))
    sink = np.asarray(inputs["swa_sink"], f)
    sinkT = np.zeros((128, DEPTH, 2), f)
    for c in range(2):
        sinkT[0:64, :, c] = sink[None, :, 2 * c]
        sinkT[64:128, :, c] = sink[None, :, 2 * c + 1]
    rpb = np.asarray(inputs["na_rpb"], f)
    kc = np.arange(64)[:, None]
    qc = np.arange(64)[None, :]
    dcx = np.clip(kc - qc, -15, 15) + 15
    na_exp = rpb[:, :, ::-1, :][:, :, :, dcx]
    na_exp = np.ascontiguousarray(na_exp.transpose(0, 1, 3, 2, 4)).reshape(DEPTH, 4, 64, 15 * 64)
    ws = np.clip(np.arange(64) - 8, 0, 48)
    mc = ((kc >= ws[None, :]) & (kc < ws[None, :] + 16)).astype(f)
    mcol = np.tile(np.tile(mc[:, None, :], (1, 15, 1)).reshape(64, 960), (2, 1))
    jj = np.arange(128)[:, None]
    ii = np.arange(128)[None, :]
    same = (jj // 64) == (ii // 64)
    tri = np.stack([(same & (jj <= ii)), (same & (jj >= ii))], 1).astype(f)
    bd64 = same.astype(f)
    gnT = np.ascontiguousarray(np.tile(np.asarray(inputs["gla_g_norm"], f).T, (2, 1)))
    L = DEPTH
    lre = np.asarray(inputs["s5_lam_re"], f).reshape(L, 32, 64)
    lim = np.asarray(inputs["s5_lam_im"], f).reshape(L, 32, 64)
    lam = np.stack([lre, lim], 1)
    lamT = np.ascontiguousarray(np.tile(lam.transpose(3, 0, 1, 2), (2, 1, 1, 1)))
    stepT = np.ascontiguousarray(np.broadcast_to(np.asarray(inputs["s5_log_step"], f).reshape(L, 32)[None], (128, L, 32)))
    dskT = np.ascontiguousarray(np.asarray(inputs["s5_d"], f).reshape(L, 2, 128).transpose(2, 0, 1))
    sgn = np.ones((128, 2), f)
    sgn[0:64, 0] = -1.0
    sgn[64:128, 1] = -1.0
    jmat = np.zeros((128, 128), f)
    for sp in range(64):
        jmat[sp + 64, sp] = -1.0
        jmat[sp, sp + 64] = 1.0
    bre = np.asarray(inputs["s5_b_re"], f)
    bim = np.asarray(inputs["s5_b_im"], f)
    cre = np.asarray(inputs["s5_c_re"], f)
    cim = np.asarray(inputs["s5_c_im"], f)
    B1 = np.zeros((L, 2, 16, 128, 128), f)
    B2 = np.zeros((L, 2, 16, 128, 128), f)
    C1 = np.zeros((L, 2, 16, 128, 128), f)
    C2 = np.zeros((L, 2, 16, 128, 128), f)
    for gi in range(16):
        r0 = 16 * (gi % 8)
        B1[:, :, gi, r0:r0 + 16, 0:64] = bre[:, :, gi].transpose(0, 1, 3, 2)
        B1[:, :, gi, r0:r0 + 16, 64:128] = bim[:, :, gi].transpose(0, 1, 3, 2)
        B2[:, :, gi, r0:r0 + 16, 0:64] = bim[:, :, gi].transpose(0, 1, 3, 2)
        B2[:, :, gi, r0:r0 + 16, 64:128] = bre[:, :, gi].transpose(0, 1, 3, 2)
        C1[:, :, gi, 0:64, r0:r0 + 16] = cre[:, :, gi].transpose(0, 1, 3, 2)
        C1[:, :, gi, 64:128, r0:r0 + 16] = cim[:, :, gi].transpose(0, 1, 3, 2)
        C2[:, :, gi, 0:64, r0:r0 + 16] = cim[:, :, gi].transpose(0, 1, 3, 2)
        C2[:, :, gi, 64:128, r0:r0 + 16] = cre[:, :, gi].transpose(0, 1, 3, 2)
    shared = {
        "lamT": lamT, "stepT": stepT, "dskT": dskT, "sgn": sgn, "jmat": jmat, "s5_w_glu": np.asarray(inputs["s5_w_glu"], f),
        "s5B1": B1, "s5B2": B2, "s5C1": C1, "s5C2": C2,
        "tri": tri, "bd64": bd64.astype(ml_dtypes.bfloat16), "gnT": gnT,
        "gla_w_gate2": np.asarray(inputs["gla_w_gate2"], f), "gla_b_gate": np.asarray(inputs["gla_b_gate"], f),
        "w_mod": np.asarray(inputs["w_mod"], f), "b_modT": b_modT, "gvec": gvec, "w_in_ext": np.ascontiguousarray(w_in_ext),
        "w_out": np.asarray(inputs["w_out"], f), "ffn_w_up": np.asarray(inputs["ffn_w_up"], f),
        "ffn_w_down": np.asarray(inputs["ffn_w_down"], f), "convT": convT,
        "ropeC": rc.astype(ml_dtypes.bfloat16), "ropeS": rs.astype(ml_dtypes.bfloat16),
        "maskAB": maskAB.astype(ml_dtypes.bfloat16), "identf": np.eye(128, dtype=f), "sinkT": sinkT,
        "na_exp": na_exp, "mcol": np.ascontiguousarray(mcol),
    }
    x = np.asarray(inputs["x"], f)
    ctx = np.asarray(inputs["ctx"], f)
    c = np.asarray(inputs["c"], f)
    cc = np.asarray(inputs["c_ctx"], f)
    in_maps = []
    for core in range(8):
        b0 = 2 * core
        xcat = np.concatenate([ctx[b0:b0 + 2], x[b0:b0 + 2]], axis=1)
        cs = np.stack([c[b0], c[b0 + 1], cc], 0)
        cTm = np.ascontiguousarray(cs.reshape(3, 8, 128).transpose(2, 1, 0))
        m = dict(shared)
        m["xcat"] = np.ascontiguousarray(xcat)
        m["cT"] = cTm
        in_maps.append(m)
    return in_maps


L_FIRST = ("w_mod", "w_in_ext", "w_out", "ffn_w_up", "ffn_w_down", "na_exp", "s5B1", "s5B2", "s5C1", "s5C2", "s5_w_glu",
           "gla_w_gate2", "gla_b_gate")
L_SECOND = ("b_modT", "gvec", "convT", "sinkT", "lamT", "stepT", "dskT")

FUSED = True


def kernel(**inputs):
    in_maps = host_prep(inputs)
    if FUSED:
        nc = bass.Bass("TRN2", target_bir_lowering=False)
        build(nc)
        res = run_bass_kernel_spmd(nc, in_maps, core_ids=list(range(8)))
        outs = [r["out"] for r in res.results]
        return np.concatenate(outs, axis=0).astype(np.float32)
    cur = [m["xcat"] for m in in_maps]
    for l in range(DEPTH):
        base = {}
        m0 = in_maps[0]
        for k, v in m0.items():
            if k in ("xcat", "cT"):
                continue
            if k in L_FIRST:
                base[k] = np.ascontiguousarray(v[l:l + 1])
            elif k in L_SECOND:
                base[k] = np.ascontiguousarray(v[:, l:l + 1])
            elif k == "gnT":
                base[k] = np.ascontiguousarray(v[:, l:l + 1])
            else:
                base[k] = v
        for bi in range(2):
            maps = []
            for core in range(8):
                m = dict(base)
                m["xcat"] = np.ascontiguousarray(cur[core][[bi, 1 - bi]])
                cTm = in_maps[core]["cT"]
                m["cT"] = np.ascontiguousarray(cTm[:, :, [bi, 1 - bi, 2]])
                maps.append(m)
            nc = bass.Bass("TRN2", target_bir_lowering=False)
            build(nc, nlayers=1, nbatch=1, ldim=1, full_out=True)
            res = run_bass_kernel_spmd(nc, maps, core_ids=list(range(8)))
            for core in range(8):
                new = np.array(cur[core])
                new[bi] = res.results[core]["out"][0]
                cur[core] = new
    outs = [c[:, NCX:, :] for c in cur]
    return np.concatenate(outs, axis=0).astype(np.float32)
```

```python
import contextlib
import math
import numpy as np
import ml_dtypes
import concourse.bass as bass
import concourse.mybir as mybir
from concourse.bass_utils import run_bass_kernel_spmd

F32 = mybir.dt.float32
BF16 = mybir.dt.bfloat16
ALU = mybir.AluOpType
AF = mybir.ActivationFunctionType

ENG = ("pe", "act", "dve", "pool", "sp")
NDMA = 24


class P:
    def __init__(self, nc, same_eng_sync=True):
        self.nc = nc
        self.ops = {e: [] for e in ENG}
        self.cnt = {e: 0 for e in ENG}
        self.waited = {e: {} for e in ENG}
        self.last_w = {}
        self.readers = {}
        self.dma_nextq = {}
        self.dma_cnt = [0] * NDMA
        self.dma_last_tok = [None] * NDMA
        self.same = same_eng_sync
        self.out_toks = []
        self.bar = []

    def barrier(self):
        self.bar = [("e", e, self.cnt[e]) for e in ENG if self.cnt[e]] + \
                   [("d", k, self.dma_cnt[k]) for k in range(NDMA) if self.dma_cnt[k]]

    def _deps(self, eng, reads, writes):
        deps = list(self.bar)
        for k in reads:
            t = self.last_w.get(k)
            if t is not None:
                deps.append(t)
        for k in writes:
            t = self.last_w.get(k)
            if t is not None:
                deps.append(t)
            deps.extend(self.readers.get(k, ()))
        return deps

    def _waits(self, eng, deps):
        w = self.waited[eng]
        best = {}
        for t in deps:
            if t[0] == "e":
                _, e2, idx = t
                if e2 == eng and (not self.same or eng == "pe"):
                    continue
                key = e2
            else:
                key = ("d", t[1])
            if w.get(key, 0) >= t[2]:
                continue
            if key not in best or best[key][2] < t[2]:
                best[key] = t
        for key, t in best.items():
            w[key] = t[2]
        return list(best.values())

    def _record(self, tok, reads, writes):
        for k in reads:
            lst = self.readers.setdefault(k, [])
            lst.append(tok)
            if len(lst) > 64:
                best = {}
                for t in lst:
                    kk = t[:2]
                    if kk not in best or best[kk][2] < t[2]:
                        best[kk] = t
                self.readers[k] = list(best.values())
        for k in writes:
            self.last_w[k] = tok
            self.readers[k] = []

    def op(self, eng, fn, reads=(), writes=()):
        deps = self._deps(eng, reads, writes)
        waits = self._waits(eng, deps)
        self.cnt[eng] += 1
        tok = ("e", eng, self.cnt[eng])
        self.ops[eng].append((waits, fn, ("e", eng)))
        self._record(tok, reads, writes)
        return tok

    def dma(self, q, fn, reads=(), writes=(), is_out=False):
        lo, n = (0, 16) if q == "sp" else (16, NDMA - 16)
        cur = self.dma_nextq.get(q, 0)
        k = lo + cur
        self.dma_nextq[q] = (cur + 1) % n
        deps = self._deps(q, reads, writes)
        if self.dma_last_tok[k] is not None:
            deps.append(self.dma_last_tok[k])
        waits = self._waits(q, deps)
        self.dma_cnt[k] += 16
        tok = ("d", k, self.dma_cnt[k])
        self.dma_last_tok[k] = tok
        self.ops[q].append((waits, fn, ("d", k)))
        self._record(tok, reads, writes)
        if is_out:
            self.out_toks.append(tok)
        return tok

    def emit(self):
        nc = self.nc
        with contextlib.ExitStack() as es:
            esem = {e: es.enter_context(nc.semaphore("s_" + e)) for e in ENG}
            dsem = [es.enter_context(nc.semaphore("d%d" % i)) for i in range(NDMA)]
            fin = list(self.out_toks)
            for e in ENG:
                if self.cnt[e]:
                    fin.append(("e", e, self.cnt[e]))
            for k in range(NDMA):
                if self.dma_cnt[k]:
                    fin.append(("d", k, self.dma_cnt[k]))
            block = es.enter_context(nc.Block())

            def run(eng_name, eng):
                for waits, fn, kind in self.ops[eng_name]:
                    for t in waits:
                        if t[0] == "e":
                            eng.wait_ge(esem[t[1]], t[2])
                        else:
                            eng.wait_ge(dsem[t[1]], t[2])
                    ins = fn(eng)
                    if kind[0] == "e":
                        ins.then_inc(esem[kind[1]], 1)
                    else:
                        ins.then_inc(dsem[kind[1]], 16)

            @block.tensor
            def _(e):
                run("pe", e)

            @block.scalar
            def _(e):
                run("act", e)

            @block.vector
            def _(e):
                run("dve", e)

            @block.gpsimd
            def _(e):
                run("pool", e)

            @block.sync
            def _(e):
                run("sp", e)
                for t in fin:
                    if t[0] == "e":
                        if t[1] != "sp":
                            e.wait_ge(esem[t[1]], t[2])
                    else:
                        e.wait_ge(dsem[t[1]], t[2])


NT = 2304
NCX = 256
NLAT = 2048
D = 1024
DEPTH = 4
LD = [4]
DFF = 2816
EPS = 1e-6
TB = [(0, 256)] + [(256 + 512 * i, 256 + 512 * (i + 1)) for i in range(4)]
C_A, C_NAQ, C_NAK, C_NAV = 0, 256, 512, 768
C_GQ, C_GK, C_GV, C_GF, C_GB, C_GR = 1024, 1280, 1536, 1792, 1808, 1824
C_SQ, C_SK, C_SV = 2080, 2336, 2464
C_SQP, C_SKD, C_SKDP, NEXT = 2592, 2848, 3104, 3360


def _rope_perm(nh):
    idx = np.arange(nh * 64).reshape(nh, 4, 16)
    return idx[:, [1, 0, 3, 2], :].reshape(-1)


def _rope_tables():
    cos = np.ones((64, NT), np.float32)
    sin = np.zeros((64, NT), np.float32)
    t = np.arange(NLAT)
    pos = (t // 64, t % 64)
    inv = 10000.0 ** (-np.arange(0, 32, 2, dtype=np.float32) / 32)
    for half in range(2):
        ang = pos[half].astype(np.float32)[None, :] * inv[:, None]
        c, s = np.cos(ang), np.sin(ang)
        b = 32 * half
        cos[b:b + 16, NCX:] = c
        cos[b + 16:b + 32, NCX:] = c
        sin[b:b + 16, NCX:] = -s
        sin[b + 16:b + 32, NCX:] = s
    return np.concatenate([cos, cos], 0), np.concatenate([sin, sin], 0)


class Ctx:
    pass


_UN = [0]


def UN():
    _UN[0] += 1
    return "t%d_" % _UN[0]


def build(nc, dbg=None, nlayers=DEPTH, nbatch=2, mixers=("s5", "na", "gla", "swa"), ldim=DEPTH, full_out=False):
    LD[0] = ldim
    _UN[0] = 0
    p = P(nc)
    g = Ctx()
    g.p, g.nc = p, nc
    dram = {}

    def din(name, shape, dt=F32):
        dram[name] = nc.dram_tensor(name, list(shape), dt, kind="ExternalInput").ap()
        return dram[name]

    xcat = din("xcat", [2, NT, D])
    cT = din("cT", [128, 8, 3])
    w_mod = din("w_mod", [LD[0], D, 6 * D])
    b_modT = din("b_modT", [128, LD[0], 48])
    gvec = din("gvec", [128, LD[0], 4, 8])
    w_in = din("w_in_ext", [LD[0], D, NEXT])
    w_out = din("w_out", [LD[0], D, D])
    w_up = din("ffn_w_up", [LD[0], D, 2 * DFF])
    w_down = din("ffn_w_down", [LD[0], DFF, D])
    convT = din("convT", [128, LD[0], 3, 22])
    ropeC = din("ropeC", [128, NT], BF16)
    ropeS = din("ropeS", [128, NT], BF16)
    maskAB = din("maskAB", [128, 2, 128], BF16)
    identf = din("identf", [128, 128])
    sinkT = din("sinkT", [128, LD[0], 2])
    na_exp = din("na_exp", [LD[0], 4, 64, 15 * 64])
    mcol = din("mcol", [128, 15 * 64])
    din("tri", [128, 2, 128])
    din("bd64", [128, 128], BF16)
    din("gnT", [128, LD[0]])
    din("gla_w_gate2", [LD[0], 2, 16, 256])
    din("gla_b_gate", [LD[0], 2, 256])
    din("lamT", [128, LD[0], 2, 32])
    din("stepT", [128, LD[0], 32])
    din("dskT", [128, LD[0], 2])
    din("sgn", [128, 2])
    din("jmat", [128, 128])
    din("s5_w_glu", [LD[0], 256, 256])
    for nm in ("s5B1", "s5B2", "s5C1", "s5C2"):
        din(nm, [LD[0], 2, 16, 128, 128])
        dram[nm + "_bf"] = nc.dram_tensor(nm + "_bf", [LD[0], 2, 16, 128, 128], BF16).ap()
    g.dram = dram
    out = nc.dram_tensor("out", [2, NT if full_out else NLAT, D], F32, kind="ExternalOutput").ap()
    dbg_aps = {}
    if dbg:
        for name, shape in dbg.items():
            dbg_aps[name] = nc.dram_tensor("dbg_" + name, list(shape), F32, kind="ExternalOutput").ap()

    w_in_bf = nc.dram_tensor("w_in_bf", [LD[0], D, NEXT], BF16).ap()
    w_out_bf = nc.dram_tensor("w_out_bf", [LD[0], D, D], BF16).ap()
    w_up_bf = nc.dram_tensor("w_up_bf", [LD[0], D, 2 * DFF], BF16).ap()
    w_down_bf = nc.dram_tensor("w_down_bf", [LD[0], DFF, D], BF16).ap()

    def wkeys(key, l, n):
        return ["%s%d_%d" % (key, l, i) for i in range(n)]
    g.wkeys = wkeys
    g._uid = [0]

    def OP(eng, method, r=(), w=(), **kw):
        return p.op(eng, lambda e, kw=kw, method=method: getattr(e, method)(**kw), reads=r, writes=w)

    def MM(out_, lhsT, rhs, start, stop, r, w):
        return p.op("pe", lambda e: e.matmul(out_, lhsT=lhsT, rhs=rhs, start=start, stop=stop), reads=r, writes=w)

    def DMA(q, out_, in_, r=(), w=(), is_out=False, **kw):
        return p.dma(q, lambda e, kw=kw: e.dma_start(out=out_, in_=in_, **kw), reads=r, writes=w, is_out=is_out)

    g.OP, g.MM, g.DMA = OP, MM, DMA

    for l in range(nlayers):
        for (src, dst, rows, key) in ((w_in, w_in_bf, D, "w_in_bf"), (w_out, w_out_bf, D, "w_out_bf"),
                                      (w_up, w_up_bf, D, "w_up_bf"), (w_down, w_down_bf, DFF, "w_down_bf")):
            for r0 in range(0, rows, 128):
                DMA("pool", dst[l, r0:r0 + 128, :], src[l, r0:r0 + 128, :], w=["%s%d_%d" % (key, l, r0 // 128)],
                    max_dma_last_dim=4096)

    for l in range(nlayers):
        for nm in ("s5B1", "s5B2", "s5C1", "s5C2"):
            for d in range(2):
                DMA("pool", dram[nm + "_bf"][l, d].rearrange("g p s -> (g p) s"), dram[nm][l, d].rearrange("g p s -> (g p) s"),
                    w=["%s_bf%d" % (nm, l)] if d == 1 else ["%s_bf%d_d0" % (nm, l)], max_dma_last_dim=4096)

    es = contextlib.ExitStack()

    def sb(name, shape, dt):
        return es.enter_context(nc.sbuf_tensor(UN() + name, list(shape), dt))

    MODT = sb("MODT", [128, LD[0], 48, 3], F32)
    DER = sb("DER", [128, LD[0], 3, 6, 8], F32)
    GV = sb("GV", [128, LD[0], 4, 8], F32)
    CONV = sb("CONV", [128, LD[0], 3, 22], F32)
    ONESB = sb("ONESB", [128, 128], BF16)
    IDF = sb("IDF", [128, 128], F32)
    MAB = sb("MAB", [128, 2, 128], BF16)
    ESINK = sb("ESINK", [128, LD[0], 2], F32)
    PS = [es.enter_context(nc.psum_tensor("ps%d" % i, [128, 512], F32)) for i in range(8)]
    g.PS = PS

    OP("dve", "memset", ap=ONESB[:], constant=1.0, w=["ONESB"])
    DMA("sp", IDF[:], identf, w=["IDF"])
    DMA("sp", MAB[:], maskAB, w=["MAB"])
    DMA("sp", GV[:], gvec, w=["GV"])
    DMA("sp", CONV[:], convT, w=["CONV"])
    DMA("sp", ESINK[:], sinkT, w=["ESINK"])
    OP("act", "activation", out=ESINK[:], in_=ESINK[:], func=AF.Exp, r=["ESINK"], w=["ESINK"])

    with contextlib.ExitStack() as s1:
        SCT = s1.enter_context(nc.sbuf_tensor(UN() + "SCT", [128, 8, 3], F32))
        BM = s1.enter_context(nc.sbuf_tensor(UN() + "BM", [128, LD[0], 48], F32))
        WM = [s1.enter_context(nc.sbuf_tensor(UN() + "WM%d" % i, [128, 8, 512], F32)) for i in range(2)]
        DMA("sp", SCT[:], cT, w=["SCT"])
        DMA("sp", BM[:], b_modT, w=["BM"])
        OP("act", "activation", out=SCT[:], in_=SCT[:], func=AF.Silu, r=["SCT"], w=["SCT"])
        it = 0
        for l in range(nlayers):
            for cg in range(12):
                wb = WM[it % 2]
                wk = "WM%d" % (it % 2)
                it += 1
                DMA("sp", wb[:], w_mod[l, :, cg * 512:(cg + 1) * 512].rearrange("(k p) c -> p k c", p=128), w=[wk])
                for fc in range(4):
                    f = cg * 4 + fc
                    for k in range(8):
                        MM(PS[0][:, f * 3:f * 3 + 3], wb[:, k, fc * 128:(fc + 1) * 128], SCT[:, k, :], k == 0, k == 7,
                           [wk, "SCT"], ["ps0"])
            for j in range(3):
                OP("dve", "tensor_tensor", out=MODT[:, l, :, j], in0=PS[0][:, 0:144].rearrange("p (f j) -> p f j", j=3)[:, :, j],
                   in1=BM[:, l, :], op=ALU.add, r=["ps0", "BM"], w=["MODT"])
            for j in range(3):
                OP("dve", "scalar_tensor_tensor", out=DER[:, l, j, 0, :], in0=MODT[:, l, 8:16, j], scalar=1.0, in1=GV[:, l, 0, :],
                   op0=ALU.add, op1=ALU.mult, r=["MODT", "GV"], w=["DER"])
                OP("dve", "tensor_copy", out=DER[:, l, j, 1, :], in_=MODT[:, l, 0:8, j], r=["MODT"], w=["DER"])
                OP("dve", "tensor_tensor", out=DER[:, l, j, 2, :], in0=MODT[:, l, 16:24, j], in1=GV[:, l, 1, :], op=ALU.mult,
                   r=["MODT", "GV"], w=["DER"])
                OP("dve", "scalar_tensor_tensor", out=DER[:, l, j, 3, :], in0=MODT[:, l, 32:40, j], scalar=1.0, in1=GV[:, l, 2, :],
                   op0=ALU.add, op1=ALU.mult, r=["MODT", "GV"], w=["DER"])
                OP("dve", "tensor_copy", out=DER[:, l, j, 4, :], in_=MODT[:, l, 24:32, j], r=["MODT"], w=["DER"])
                OP("dve", "tensor_tensor", out=DER[:, l, j, 5, :], in0=MODT[:, l, 40:48, j], in1=GV[:, l, 3, :], op=ALU.mult,
                   r=["MODT", "GV"], w=["DER"])
    p.barrier()

    X = sb("X", [128, 8, NT], F32)
    g.X, g.DER, g.ONESB, g.MAB, g.ESINK, g.CONV = X, DER, ONESB, MAB, ESINK, CONV
    g.w_in_bf, g.w_out_bf, g.w_up_bf, g.w_down_bf = w_in_bf, w_out_bf, w_up_bf, w_down_bf
    g.ropeC, g.ropeS, g.na_exp, g.mcol = ropeC, ropeS, na_exp, mcol
    g.dbg_aps = dbg_aps

    def dump(name, ap, keys):
        if name in dbg_aps:
            DMA("pool", dbg_aps[name], ap, r=keys, is_out=True, max_dma_last_dim=2048)
    g.dump = dump

    for bi in range(nbatch):
        with contextlib.ExitStack() as s2:
            XS = [s2.enter_context(nc.sbuf_tensor(UN() + "XS%d" % i, [128, D], F32)) for i in range(2)]
            for tt in range(18):
                xs, xk = XS[tt % 2], "XS%d" % (tt % 2)
                DMA("sp", xs[:], xcat[bi, tt * 128:(tt + 1) * 128, :], w=[xk])
                for hh in range(2):
                    bank = PS[hh]
                    for kk in range(4):
                        k = hh * 4 + kk
                        p.op("pe", lambda e, o=bank[:, kk * 128:(kk + 1) * 128], i=xs[:, k * 128:(k + 1) * 128]:
                             e.transpose(out=o, in_=i, identity=IDF[:]), reads=[xk, "IDF"], writes=["ps%d" % hh])
                    if hh == 0:
                        OP("act", "activation", out=X[:, 0:4, tt * 128:(tt + 1) * 128],
                           in_=bank[:, :].rearrange("p (k t) -> p k t", t=128), func=AF.Copy, r=["ps0"], w=["X"])
                    else:
                        OP("dve", "tensor_copy", out=X[:, 4:8, tt * 128:(tt + 1) * 128],
                           in_=bank[:, :].rearrange("p (k t) -> p k t", t=128), r=["ps1"], w=["X"])
        p.barrier()
        for l in range(nlayers):
            layer(g, bi, l, (l == DEPTH - 1) and not full_out, mixers)
        with contextlib.ExitStack() as s3:
            OS_ = [s3.enter_context(nc.sbuf_tensor(UN() + "OST%d" % i, [128, D], F32)) for i in range(2)]
            for tt in range(18 if full_out else 16):
                ot, ok = OS_[tt % 2], "OST%d" % (tt % 2)
                t0 = (0 if full_out else NCX) + tt * 128
                for hh in range(2):
                    bank = PS[hh]
                    for kk in range(4):
                        k = hh * 4 + kk
                        p.op("pe", lambda e, o=bank[:, kk * 128:(kk + 1) * 128], i=X[:, k, t0:t0 + 128]:
                             e.transpose(out=o, in_=i, identity=IDF[:]), reads=["X", "IDF"], writes=["ps%d" % hh])
                    if hh == 0:
                        OP("act", "activation", out=ot[:, 0:512], in_=bank[:, :], func=AF.Copy, r=["ps0"], w=[ok])
                    else:
                        OP("dve", "tensor_copy", out=ot[:, 512:1024], in_=bank[:, :], r=["ps1"], w=[ok])
                DMA("sp", out[bi, tt * 128:(tt + 1) * 128, :], ot[:], r=[ok], is_out=True)
        p.barrier()

    p.emit()
    es.close()
    return nc


def rms_stats(g, src_fn, nk, w, SQ, RS, psb, src_keys):
    OP, MM = g.OP, g.MM
    for k in range(nk):
        OP("act", "activation", out=SQ[:, k, :w], in_=src_fn(k), func=AF.Square, r=src_keys, w=["SQ"])
    for k in range(nk):
        MM(g.PS[psb][:, :w], g.ONESB[:], SQ[:, k, :w], k == 0, k == nk - 1, ["SQ", "ONESB"], ["ps%d" % psb])
    OP("act", "activation", out=RS[:, :w], in_=g.PS[psb][:, :w], func=AF.Sqrt, scale=1.0 / D, bias=g.EPSC[:, 0:1],
       r=["ps%d" % psb, "EPSC"], w=["RS"])
    OP("dve", "reciprocal", out=RS[:, :w], in_=RS[:, :w], r=["RS"], w=["RS"])


def layer(g, bi, l, last, mixers):
    nc, p, OP, MM, DMA = g.nc, g.p, g.OP, g.MM, g.DMA
    X, DER, PS = g.X, g.DER, g.PS
    with contextlib.ExitStack() as sl:
        def sb(name, shape, dt):
            return sl.enter_context(nc.sbuf_tensor(UN() + name, list(shape), dt))
        YT = sb("YT", [128, 8, NT], BF16)
        g.YT = YT
        EPSC = sb("EPSC", [128, 1], F32)
        g.EPSC = EPSC
        OP("dve", "memset", ap=EPSC[:], constant=EPS, w=["EPSC"])
        with contextlib.ExitStack() as sh:
            HT = sh.enter_context(nc.sbuf_tensor(UN() + "HT", [128, 8, NT], BF16))
            g.HT = HT
            with contextlib.ExitStack() as sa:
                SQ = sa.enter_context(nc.sbuf_tensor(UN() + "SQ", [128, 8, 512], BF16))
                RS = sa.enter_context(nc.sbuf_tensor(UN() + "RS", [128, 512], F32))
                TMP = [sa.enter_context(nc.sbuf_tensor(UN() + "TMPa%d" % i, [128, 512], F32)) for i in range(2)]
                for (a, b) in TB:
                    w = b - a
                    j = 2 if a < NCX else bi
                    rms_stats(g, lambda k: X[:, k, a:b], 8, w, SQ, RS, 0, ["X"])
                    for k in range(8):
                        tm, tk = TMP[k % 2], "TMPa%d" % (k % 2)
                        OP("dve", "tensor_tensor", out=tm[:, :w], in0=X[:, k, a:b], in1=RS[:, :w], op=ALU.mult,
                           r=["X", "RS"], w=[tk])
                        OP("act", "activation", out=HT[:, k, a:b], in_=tm[:, :w], func=AF.Identity,
                           scale=DER[:, l, j, 0, k:k + 1], bias=DER[:, l, j, 1, k:k + 1], r=[tk, "DER"], w=["HT"])
            p.barrier()
            g.dump("HT%d_%d" % (bi, l), HT[:], ["HT"])
            for nm, chs in (("s5", (0, 1)), ("na", (2, 3)), ("gla", (4, 5)), ("swa", (6, 7))):
                if nm not in mixers:
                    OP("pool", "memset", ap=YT[:, chs[0]:chs[1] + 1, :], constant=0.0, w=["YT"])
            if "s5" in mixers:
                s5_project(g, bi, l)
                p.barrier()
            if "swa" in mixers:
                attn_mixer(g, bi, l, "swa")
                p.barrier()
            if "na" in mixers:
                attn_mixer(g, bi, l, "na")
                p.barrier()
            if "gla" in mixers:
                gla_mixer(g, bi, l)
                p.barrier()
        p.barrier()
        if "s5" in mixers:
            s5_main(g, bi, l)
            p.barrier()
        g.dump("YT%d_%d" % (bi, l), YT[:], ["YT"])
        with contextlib.ExitStack() as sc:
            WO = sc.enter_context(nc.sbuf_tensor(UN() + "WO", [128, 8, D], BF16))
            OS_ = sc.enter_context(nc.sbuf_tensor(UN() + "OS", [128, 8, 512], F32))
            SQ = sc.enter_context(nc.sbuf_tensor(UN() + "SQ", [128, 8, 512], BF16))
            RS = sc.enter_context(nc.sbuf_tensor(UN() + "RS", [128, 512], F32))
            TMP = [sc.enter_context(nc.sbuf_tensor(UN() + "TMPc%d" % i, [128, 512], F32)) for i in range(2)]
            DMA("sp", WO[:], g.w_out_bf[l].rearrange("(k p) c -> p k c", p=128), r=g.wkeys("w_out_bf", l, 8), w=["WO"])
            for (a, b) in TB:
                if last and a < NCX:
                    continue
                w = b - a
                j = 2 if a < NCX else bi
                for dc in range(8):
                    bank = 1 + dc % 2
                    for k in range(8):
                        MM(PS[bank][:, :w], WO[:, k, dc * 128:(dc + 1) * 128], YT[:, k, a:b], k == 0, k == 7,
                           ["WO", "YT"], ["ps%d" % bank])
                    OP("dve", "tensor_copy", out=OS_[:, dc, :w], in_=PS[bank][:, :w], r=["ps%d" % bank], w=["OS%d" % dc])
                rms_stats(g, lambda k: OS_[:, k, :w], 8, w, SQ, RS, 0, ["OS%d" % k for k in range(8)])
                for k in range(8):
                    tm, tk = TMP[k % 2], "TMPc%d" % (k % 2)
                    OP("pool", "tensor_tensor", out=tm[:, :w], in0=OS_[:, k, :w], in1=RS[:, :w], op=ALU.mult,
                       r=["OS%d" % k, "RS"], w=[tk])
                    OP("dve", "scalar_tensor_tensor", out=X[:, k, a:b], in0=tm[:, :w], scalar=DER[:, l, j, 2, k:k + 1],
                       in1=X[:, k, a:b], op0=ALU.mult, op1=ALU.add, r=[tk, "DER", "X"], w=["X"])
    p.barrier()
    g.dump("X1_%d_%d" % (bi, l), X[:], ["X"])
    ffn(g, bi, l, last)
    p.barrier()
    g.dump("X2_%d_%d" % (bi, l), X[:], ["X"])


def ffn_blocks():
    blks = [(0, NCX, 0, NCX)]
    for i in range(5):
        oa = NCX + 410 * i
        ob = min(NCX + 410 * (i + 1), NT)
        blks.append((max(oa - 1, NCX), min(ob + 1, NT), oa, ob))
    return blks


def ffn(g, bi, l, last):
    nc, p, OP, MM, DMA = g.nc, g.p, g.OP, g.MM, g.DMA
    X, DER, PS = g.X, g.DER, g.PS
    with contextlib.ExitStack() as sf:
        def sb(name, shape, dt):
            return sf.enter_context(nc.sbuf_tensor(UN() + name, list(shape), dt))
        EPSC = sb("EPSC", [128, 1], F32)
        g.EPSC = EPSC
        OP("dve", "memset", ap=EPSC[:], constant=EPS, w=["EPSC"])
        HBs = [sb("HB%d" % i, [128, 8, 512], BF16) for i in range(2)]
        GB = sb("GB", [128, 22, 512], BF16)
        SQ = sb("SQ", [128, 8, 512], BF16)
        RS = sb("RS", [128, 512], F32)
        OS_ = sb("OS", [128, 8, 512], F32)
        TMP = [sb("TMPf%d" % i, [128, 512], F32) for i in range(2)]
        GS = [sb("GS%d" % i, [128, 514], F32) for i in range(2)]
        CV = [sb("CV%d" % i, [128, 512], F32) for i in range(2)]
        U1 = [sb("U1%d" % i, [128, 512], F32) for i in range(2)]
        WU = [sb("WU%d" % i, [128, 8, 256], BF16) for i in range(3)]
        WD = [sb("WD%d" % i, [128, 22, 128], BF16) for i in range(2)]
        wu_it = 0
        wd_it = 0
        blks = [bk for bk in ffn_blocks() if not (last and bk[0] < NCX)]

        def make_hb(bidx):
            (ca, cb, oa, ob) = blks[bidx]
            w = cb - ca
            j = 2 if ca < NCX else bi
            HB, hbk = HBs[bidx % 2], "HB%d" % (bidx % 2)
            rms_stats(g, lambda k: X[:, k, ca:cb], 8, w, SQ, RS, 0, ["X"])
            for k in range(8):
                tm, tk = TMP[k % 2], "TMPf%d" % (k % 2)
                OP("dve", "tensor_tensor", out=tm[:, :w], in0=X[:, k, ca:cb], in1=RS[:, :w], op=ALU.mult,
                   r=["X", "RS"], w=[tk])
                OP("act", "activation", out=HB[:, k, :w], in_=tm[:, :w], func=AF.Identity,
                   scale=DER[:, l, j, 3, k:k + 1], bias=DER[:, l, j, 4, k:k + 1], r=[tk, "DER"], w=[hbk])

        make_hb(0)
        for bidx, (ca, cb, oa, ob) in enumerate(blks):
            w = cb - ca
            wo = ob - oa
            off = oa - ca
            j = 2 if ca < NCX else bi
            HB, hbk = HBs[bidx % 2], "HB%d" % (bidx % 2)
            if bidx + 1 < len(blks):
                make_hb(bidx + 1)
            for jc in range(22):
                wu, wuk = WU[wu_it % 3], "WU%d" % (wu_it % 3)
                wu_it += 1
                DMA("sp", wu[:, :, 0:128], g.w_up_bf[l, :, jc * 128:(jc + 1) * 128].rearrange("(k p) c -> p k c", p=128),
                    r=g.wkeys("w_up_bf", l, 8), w=[wuk])
                DMA("sp", wu[:, :, 128:256],
                    g.w_up_bf[l, :, DFF + jc * 128:DFF + (jc + 1) * 128].rearrange("(k p) c -> p k c", p=128),
                    r=g.wkeys("w_up_bf", l, 8), w=[wuk])
                bg, bv = (1, 2)[jc % 2], (3, 4, 7, 5)[jc % 4]
                for k in range(8):
                    MM(PS[bg][:, :w], wu[:, k, 0:128], HB[:, k, :w], k == 0, k == 7, [wuk, hbk], ["ps%d" % bg])
                for k in range(8):
                    MM(PS[bv][:, :w], wu[:, k, 128:256], HB[:, k, :w], k == 0, k == 7, [wuk, hbk], ["ps%d" % bv])
                gs, gk = GS[jc % 2], "GS%d" % (jc % 2)
                cv, ck = CV[jc % 2], "CV%d" % (jc % 2)
                u1, uk = U1[jc % 2], "U1%d" % (jc % 2)
                OP("pool", "memset", ap=gs[:, 0:1], constant=0.0, w=[gk])
                OP("pool", "memset", ap=gs[:, w + 1:w + 2], constant=0.0, w=[gk])
                OP("act", "activation", out=gs[:, 1:w + 1], in_=PS[bg][:, :w], func=AF.Copy, r=["ps%d" % bg], w=[gk])
                s = 1 + off
                OP("pool", "tensor_scalar", out=cv[:, :wo], in0=gs[:, s - 1:s - 1 + wo], scalar1=g.CONV[:, l, 0, jc:jc + 1],
                   scalar2=0.0, op0=ALU.mult, op1=ALU.add, r=[gk, "CONV"], w=[ck])
                OP("dve", "scalar_tensor_tensor", out=cv[:, :wo], in0=gs[:, s:s + wo], scalar=g.CONV[:, l, 1, jc:jc + 1],
                   in1=cv[:, :wo], op0=ALU.mult, op1=ALU.add, r=[gk, "CONV", ck], w=[ck])
                OP("dve", "scalar_tensor_tensor", out=cv[:, :wo], in0=gs[:, s + 1:s + 1 + wo], scalar=g.CONV[:, l, 2, jc:jc + 1],
                   in1=cv[:, :wo], op0=ALU.mult, op1=ALU.add, r=[gk, "CONV", ck], w=[ck])
                OP("act", "activation", out=u1[:, :wo], in_=cv[:, :wo], func=AF.Gelu_apprx_tanh, r=[ck], w=[uk])
                OP("dve", "tensor_tensor", out=GB[:, jc, :wo], in0=PS[bv][:, off:off + wo], in1=u1[:, :wo], op=ALU.mult,
                   r=["ps%d" % bv, uk], w=["GB"])
            for dc in range(8):
                wd, wdk = WD[wd_it % 2], "WD%d" % (wd_it % 2)
                wd_it += 1
                DMA("sp", wd[:], g.w_down_bf[l, :, dc * 128:(dc + 1) * 128].rearrange("(k p) c -> p k c", p=128),
                    r=g.wkeys("w_down_bf", l, 22), w=[wdk])
                bank = (6, 0)[dc % 2]
                for jc in range(22):
                    MM(PS[bank][:, :wo], wd[:, jc, :], GB[:, jc, :wo], jc == 0, jc == 21, [wdk, "GB"], ["ps%d" % bank])
                OP("dve", "tensor_copy", out=OS_[:, dc, :wo], in_=PS[bank][:, :wo], r=["ps%d" % bank], w=["OS%d" % dc])
            rms_stats(g, lambda k: OS_[:, k, :wo], 8, wo, SQ, RS, 0, ["OS%d" % k for k in range(8)])
            for k in range(8):
                tm, tk = TMP[k % 2], "TMPf%d" % (k % 2)
                OP("pool", "tensor_tensor", out=tm[:, :wo], in0=OS_[:, k, :wo], in1=RS[:, :wo], op=ALU.mult,
                   r=["OS%d" % k, "RS"], w=[tk])
                OP("dve", "scalar_tensor_tensor", out=X[:, k, oa:ob], in0=tm[:, :wo], scalar=DER[:, l, j, 5, k:k + 1],
                   in1=X[:, k, oa:ob], op0=ALU.mult, op1=ALU.add, r=[tk, "DER", "X"], w=["X"])


def na_rows(kr):
    rs = [r for r in range(32) if min(max(r - 4, 0), 24) <= kr <= min(max(r - 4, 0), 24) + 7]
    assert rs == list(range(rs[0], rs[-1] + 1))
    return rs[0], rs[-1] + 1


def attn_mixer(g, bi, l, kind):
    nc, p, OP, MM, DMA = g.nc, g.p, g.OP, g.MM, g.DMA
    PS, HT, YT = g.PS, g.HT, g.YT
    wkey = g.wkeys("w_in_bf", l, 8)
    swa = kind == "swa"
    with contextlib.ExitStack() as sm:
        def sb(name, shape, dt):
            return sm.enter_context(nc.sbuf_tensor(UN() + name, list(shape), dt))
        QT = sb("QT", [128, NT], BF16)
        KT = sb("KT", [128, NT], BF16)
        VT = sb("VT", [128, 18, 128], BF16)
        WB = [sb("WB%d" % i, [128, 8, 128], BF16) for i in range(3)]
        T1 = sb("T1", [128, 512], F32)
        T2 = sb("T2", [128, 512], F32)
        PT = [sb("PT%d" % i, [128, 512], BF16) for i in range(2)]
        REC = sb("REC", [128, 512], F32)
        if swa:
            RC = sb("RC", [128, NT], BF16)
            RSN = sb("RSN", [128, NT], BF16)
            DMA("sp", RC[:], g.ropeC, w=["RC"])
            DMA("sp", RSN[:], g.ropeS, w=["RSN"])
        else:
            UT = sb("UT", [128, 2, 960], BF16)
            UF = sb("UF", [128, 960], F32)
            MC = sb("MC", [128, 960], F32)
            DMA("sp", MC[:], g.mcol, w=["MC"])
        wb_it = [0]

        def load_w(c0):
            i = wb_it[0] % 3
            wb_it[0] += 1
            DMA("sp", WB[i][:], g.w_in_bf[l, :, c0:c0 + 128].rearrange("(k p) c -> p k c", p=128), r=wkey, w=["WB%d" % i])
            return WB[i], "WB%d" % i

        for c in range(2):
            if swa:
                cq, cqp, ck, ckp, cv_ = C_SQ + 128 * c, C_SQP + 128 * c, C_SKD + 128 * c, C_SKDP + 128 * c, C_SV
            else:
                cq, ck, cv_ = C_NAQ + 128 * c, C_NAK + 128 * c, C_NAV + 128 * c
            for (dst, dk, c1, c2) in ((QT, "QT", cq, cqp if swa else None), (KT, "KT", ck, ckp if swa else None)):
                w1, w1k = load_w(c1)
                if swa:
                    w2, w2k = load_w(c2)
                for (a, b) in TB:
                    w = b - a
                    for k in range(8):
                        MM(PS[0][:, :w], w1[:, k, :], HT[:, k, a:b], k == 0, k == 7, [w1k, "HT"], ["ps0"])
                    if swa:
                        for k in range(8):
                            MM(PS[1][:, :w], w2[:, k, :], HT[:, k, a:b], k == 0, k == 7, [w2k, "HT"], ["ps1"])
                        OP("dve", "tensor_tensor", out=T1[:, :w], in0=PS[0][:, :w], in1=RC[:, a:b], op=ALU.mult,
                           r=["ps0", "RC"], w=["T1"])
                        OP("dve", "tensor_tensor", out=T2[:, :w], in0=PS[1][:, :w], in1=RSN[:, a:b], op=ALU.mult,
                           r=["ps1", "RSN"], w=["T2"])
                        OP("pool", "tensor_tensor", out=dst[:, a:b], in0=T1[:, :w], in1=T2[:, :w], op=ALU.add,
                           r=["T1", "T2"], w=[dk])
                    else:
                        OP("act", "activation", out=dst[:, a:b], in_=PS[0][:, :w], func=AF.Copy, r=["ps0"], w=[dk])
            if (not swa) or c == 0:
                wv, wvk = load_w(cv_)
                for t4 in range(0, 18, 4):
                    nt = min(4, 18 - t4)
                    for ti in range(nt):
                        tt = t4 + ti
                        for k in range(8):
                            MM(PS[2][:, ti * 128:(ti + 1) * 128], HT[:, k, tt * 128:(tt + 1) * 128], wv[:, k, :], k == 0, k == 7,
                               [wvk, "HT"], ["ps2"])
                    OP("act", "activation", out=VT[:, t4:t4 + nt, :],
                       in_=PS[2][:, 0:nt * 128].rearrange("p (t c) -> p t c", c=128), func=AF.Copy, r=["ps2"], w=["VT"])
            if not swa:
                for hh in range(2):
                    for half in range(2):
                        DMA("sp", UF[half * 64:(half + 1) * 64, :], g.na_exp[l, 2 * c + hh], w=["UF"])
                    OP("act", "activation", out=UF[:], in_=UF[:], func=AF.Exp, r=["UF"], w=["UF"])
                    OP("dve", "tensor_tensor", out=UT[:, hh, :], in0=UF[:], in1=MC[:], op=ALU.mult, r=["UF", "MC"], w=["UT"])
            for (qa, qb) in TB:
                qw = qb - qa
                for hh in range(2):
                    h = 2 * c + hh
                    hb = 64 * hh
                    items = []
                    for kc in range(2):
                        items.append((kc * 128, 128, 0, kc, qa, qb, None))
                    if qa >= NCX:
                        if swa:
                            for kb in range(16):
                                ka = NCX + 128 * kb
                                a_ = max(qa, ka - 128)
                                b_ = min(qb, ka + 256)
                                if a_ < b_:
                                    items.append((ka, 128, 0, 2 + kb, a_, b_, ("swa", ka)))
                        else:
                            for kr in range(32):
                                r0, r1 = na_rows(kr)
                                a_ = max(qa, NCX + 64 * r0)
                                b_ = min(qb, NCX + 64 * r1)
                                if a_ < b_:
                                    items.append((NCX + 64 * kr, 64, 64 * (kr % 2), 2 + kr // 2, a_, b_, ("na", kr)))
                    vc0 = 64 * (h // 2) if swa else 64 * hh
                    def s_mm(ii):
                        (ka, nk, pb, vt, a_, b_, post) = items[ii]
                        n = b_ - a_
                        sbank = 3 + ii % 2
                        MM(PS[sbank][pb:pb + nk, :n], KT[hb:hb + 64, ka:ka + nk], QT[hb:hb + 64, a_:b_], True, True,
                           ["KT", "QT"], ["ps%d" % sbank])

                    for ii, (ka, nk, pb, vt, a_, b_, post) in enumerate(items):
                        n = b_ - a_
                        sbank = 3 + ii % 2
                        pt, ptk = PT[ii % 2], "PT%d" % (ii % 2)
                        s_mm(ii)
                        OP("act", "activation", out=pt[pb:pb + nk, :n], in_=PS[sbank][pb:pb + nk, :n], func=AF.Exp, scale=0.125,
                           r=["ps%d" % sbank], w=[ptk])
                        if post is not None and post[0] == "swa":
                            kst = post[1]
                            if a_ < kst:
                                OP("pool", "tensor_tensor", out=pt[:, 0:128], in0=pt[:, 0:128], in1=g.MAB[:, 0, :], op=ALU.mult,
                                   r=[ptk, "MAB"], w=[ptk])
                            if b_ > kst + 128:
                                o_ = kst + 128 - a_
                                OP("pool", "tensor_tensor", out=pt[:, o_:o_ + 128], in0=pt[:, o_:o_ + 128], in1=g.MAB[:, 1, :],
                                   op=ALU.mult, r=[ptk, "MAB"], w=[ptk])
                        elif post is not None:
                            kr = post[1]
                            ra = (a_ - NCX) // 64
                            i0 = ra - kr + 7
                            nr = n // 64
                            OP("pool", "tensor_tensor", out=pt[pb:pb + nk, :n], in0=pt[pb:pb + nk, :n],
                               in1=UT[pb:pb + nk, hh, i0 * 64:(i0 + nr) * 64], op=ALU.mult, r=[ptk, "UT"], w=[ptk])
                        MM(PS[5][hb:hb + 64, a_ - qa:b_ - qa], VT[pb:pb + nk, vt, vc0:vc0 + 64], pt[pb:pb + nk, :n], ii == 0,
                           ii == len(items) - 1, ["VT", ptk], ["ps5"])
                        MM(PS[6][hb:hb + 64, a_ - qa:b_ - qa], g.ONESB[pb:pb + nk, 0:64], pt[pb:pb + nk, :n], ii == 0,
                           ii == len(items) - 1, ["ONESB", ptk], ["ps6"])
                if swa:
                    OP("dve", "tensor_scalar", out=REC[:, :qw], in0=PS[6][:, :qw], scalar1=g.ESINK[:, l, c:c + 1], scalar2=None,
                       op0=ALU.add, r=["ps6", "ESINK"], w=["REC"])
                    OP("dve", "reciprocal", out=REC[:, :qw], in_=REC[:, :qw], r=["REC"], w=["REC"])
                else:
                    OP("dve", "reciprocal", out=REC[:, :qw], in_=PS[6][:, :qw], r=["ps6"], w=["REC"])
                yc = (6 if swa else 2) + c
                OP("dve", "tensor_tensor", out=YT[:, yc, qa:qb], in0=PS[5][:, :qw], in1=REC[:, :qw], op=ALU.mult,
                   r=["ps5", "REC"], w=["YT"])


TC = 64


def s5_project(g, bi, l):
    nc, p, OP, MM, DMA = g.nc, g.p, g.OP, g.MM, g.DMA
    PS, HT, YT = g.PS, g.HT, g.YT
    wkey = g.wkeys("w_in_bf", l, 8)
    with contextlib.ExitStack() as sm:
        WA = [sm.enter_context(nc.sbuf_tensor(UN() + "sWA%d" % i, [128, 8, 128], BF16)) for i in range(2)]
        for cc in range(2):
            wa, wak = WA[cc], "sWA%d" % cc
            DMA("sp", wa[:], g.w_in_bf[l, :, C_A + 128 * cc:C_A + 128 * cc + 128].rearrange("(k p) c -> p k c", p=128), r=wkey,
                w=[wak])
            for (a, b) in TB:
                w = b - a
                for k in range(8):
                    MM(PS[7][:, :w], wa[:, k, :], HT[:, k, a:b], k == 0, k == 7, [wak, "HT"], ["ps7"])
                OP("act", "activation", out=YT[:, cc, a:b], in_=PS[7][:, :w], func=AF.Copy, r=["ps7"], w=["sUT"])


def s5_main(g, bi, l):
    nc, p, OP, MM, DMA = g.nc, g.p, g.OP, g.MM, g.DMA
    PS, YT = g.PS, g.YT
    wkey = g.wkeys("w_in_bf", l, 8)
    PI = math.pi
    with contextlib.ExitStack() as sm:
        def sb(name, shape, dt):
            return sm.enter_context(nc.sbuf_tensor(UN() + name, list(shape), dt))
        UT = YT[:, 0:2, :]
        YF = sb("sYF", [128, 2, NT], BF16)
        BP = [sb("sBP%d" % i, [128, 16, 128], BF16) for i in range(2)]
        CP = [sb("sCP%d" % i, [128, 16, 128], BF16) for i in range(2)]
        TAB = [sb("sTAB%d" % i, [128, 16, TC], BF16) for i in range(4)]
        Zs = [sb("sZ%d" % i, [128, 16, TC], F32) for i in range(2)]
        Ws = [sb("sW%d" % i, [128, 16, TC], F32) for i in range(2)]
        ZAs = [sb("sZA%d" % i, [128, 8, TC], F32) for i in range(2)]
        ZBs = [sb("sZB%d" % i, [128, 8, TC], F32) for i in range(2)]
        HCs = [sb("sHC%d" % i, [128, 8, TC], BF16) for i in range(2)]
        HSs = [sb("sHS%d" % i, [128, 8, TC], BF16) for i in range(2)]
        Z, W, ZA, ZB, HC, HS = Zs[0], Ws[0], ZAs[0], ZBs[0], HCs[0], HSs[0]
        WGL = sb("sWGL", [128, 2, 256], BF16)
        LAM = sb("sLAM", [128, 2, 32], F32)
        STP = sb("sSTP", [128, 32], F32)
        DSK = sb("sDSK", [128, LD[0], 2], F32)
        SGN = sb("sSGN", [128, 2], F32)
        JM = sb("sJM", [128, 128], F32)
        HPI = sb("sHPI", [128, 1], F32)
        sm_names = ["RHO", "TH", "M", "SH", "CH", "SN", "CS", "LBR", "LBI", "DEN", "KR", "KI", "T0", "T1", "EC", "ES", "EC2", "ES2"]
        SM = {n: sb("s" + n, [128, 32], F32) for n in sm_names}
        INIT = sb("sINIT", [128, 16], F32)
        ENDS = sb("sENDS", [128, 16], F32)
        RT1 = sb("sRT1", [128, 16], F32)
        TMPYs = [sb("sTMPY%d" % i, [128, TC], F32) for i in range(2)]
        DMA("sp", LAM[:], g.dram["lamT"][:, l], w=["sLAM"])
        DMA("sp", STP[:], g.dram["stepT"][:, l], w=["sSTP"])
        DMA("sp", DSK[:], g.dram["dskT"], w=["sDSK"])
        DMA("sp", SGN[:], g.dram["sgn"], w=["sSGN"])
        DMA("sp", JM[:], g.dram["jmat"], w=["sJM"])
        DMA("pool", WGL[:], g.dram["s5_w_glu"][l].rearrange("(k p) c -> p k c", p=128), w=["sWGL"])
        OP("dve", "memset", ap=HPI[:], constant=PI / 2, w=["sHPI"])

        def V(eng, method, outn, r, **kw):
            OP(eng, method, r=["s" + x for x in r], w=["s" + outn], **kw)

        def TT(outn, an, bn, op):
            V("dve", "tensor_tensor", outn, [an, bn], out=SM[outn][:], in0=SM[an][:], in1=SM[bn][:], op=op)

        OP("act", "activation", out=STP[:], in_=STP[:], func=AF.Exp, r=["sSTP"], w=["sSTP"])
        OP("dve", "tensor_tensor", out=SM["T0"][:], in0=LAM[:, 0, :], in1=STP[:], op=ALU.mult, r=["sLAM", "sSTP"], w=["sT0"])
        OP("act", "activation", out=SM["RHO"][:], in_=SM["T0"][:], func=AF.Exp, r=["sT0"], w=["sRHO"])
        OP("dve", "tensor_tensor", out=SM["TH"][:], in0=LAM[:, 1, :], in1=STP[:], op=ALU.mult, r=["sLAM", "sSTP"], w=["sTH"])
        for _ in range(5):
            V("dve", "tensor_scalar", "M", ["TH"], out=SM["M"][:], in0=SM["TH"][:], scalar1=PI, scalar2=-2 * PI, op0=ALU.is_gt,
              op1=ALU.mult)
            TT("TH", "TH", "M", ALU.add)
        for _ in range(2):
            V("dve", "tensor_scalar", "M", ["TH"], out=SM["M"][:], in0=SM["TH"][:], scalar1=-PI, scalar2=2 * PI, op0=ALU.is_lt,
              op1=ALU.mult)
            TT("TH", "TH", "M", ALU.add)
        OP("act", "activation", out=SM["SH"][:], in_=SM["TH"][:], func=AF.Sin, scale=0.5, r=["sTH"], w=["sSH"])
        OP("act", "activation", out=SM["CH"][:], in_=SM["TH"][:], func=AF.Sin, scale=0.5, bias=HPI[:, 0:1], r=["sTH", "sHPI"],
           w=["sCH"])
        TT("SN", "SH", "CH", ALU.mult)
        V("dve", "tensor_scalar", "SN", ["SN"], out=SM["SN"][:], in0=SM["SN"][:], scalar1=2.0, scalar2=None, op0=ALU.mult)
        TT("T0", "CH", "CH", ALU.mult)
        TT("T1", "SH", "SH", ALU.mult)
        TT("CS", "T0", "T1", ALU.subtract)
        TT("LBR", "RHO", "CS", ALU.mult)
        TT("LBI", "RHO", "SN", ALU.mult)
        V("dve", "tensor_scalar", "LBR", ["LBR"], out=SM["LBR"][:], in0=SM["LBR"][:], scalar1=-1.0, scalar2=None, op0=ALU.add)
        OP("dve", "tensor_tensor", out=SM["T0"][:], in0=LAM[:, 0, :], in1=LAM[:, 0, :], op=ALU.mult, r=["sLAM"], w=["sT0"])
        OP("dve", "tensor_tensor", out=SM["T1"][:], in0=LAM[:, 1, :], in1=LAM[:, 1, :], op=ALU.mult, r=["sLAM"], w=["sT1"])
        TT("DEN", "T0", "T1", ALU.add)
        V("dve", "reciprocal", "DEN", ["DEN"], out=SM["DEN"][:], in_=SM["DEN"][:])
        OP("dve", "tensor_tensor", out=SM["T0"][:], in0=SM["LBR"][:], in1=LAM[:, 0, :], op=ALU.mult, r=["sLBR", "sLAM"], w=["sT0"])
        OP("dve", "tensor_tensor", out=SM["T1"][:], in0=SM["LBI"][:], in1=LAM[:, 1, :], op=ALU.mult, r=["sLBI", "sLAM"], w=["sT1"])
        TT("KR", "T0", "T1", ALU.add)
        TT("KR", "KR", "DEN", ALU.mult)
        OP("dve", "tensor_tensor", out=SM["T0"][:], in0=SM["LBI"][:], in1=LAM[:, 0, :], op=ALU.mult, r=["sLBI", "sLAM"], w=["sT0"])
        OP("dve", "tensor_tensor", out=SM["T1"][:], in0=SM["LBR"][:], in1=LAM[:, 1, :], op=ALU.mult, r=["sLBR", "sLAM"], w=["sT1"])
        TT("KI", "T0", "T1", ALU.subtract)
        TT("KI", "KI", "DEN", ALU.mult)

        nchunk = NT // TC
        for d in range(2):
            qs = slice(16 * d, 16 * d + 16)
            for i, nm in enumerate(("s5B1", "s5B2")):
                DMA("sp", BP[i][:], g.dram[nm + "_bf"][l, d].rearrange("g p s -> p g s"), r=["%s_bf%d" % (nm, l), "%s_bf%d_d0" % (nm, l)], w=["sBP%d" % i])
            for i, nm in enumerate(("s5C1", "s5C2")):
                DMA("sp", CP[i][:], g.dram[nm + "_bf"][l, d].rearrange("g p s -> p g s"), r=["%s_bf%d" % (nm, l), "%s_bf%d_d0" % (nm, l)], w=["sCP%d" % i])
            for which in range(2):
                i0 = 0 if d == 0 else TC - 1
                if which == 0:
                    OP("dve", "tensor_copy", out=Z[:, :, i0], in_=SM["KR"][:, qs], r=["sKR"], w=["sZ0"])
                    OP("dve", "tensor_copy", out=W[:, :, i0], in_=SM["KI"][:, qs], r=["sKI"], w=["sW0"])
                else:
                    OP("dve", "memset", ap=Z[:, :, i0:i0 + 1], constant=1.0, w=["sZ0"])
                    OP("dve", "memset", ap=W[:, :, i0:i0 + 1], constant=0.0, w=["sW0"])
                OP("dve", "tensor_copy", out=SM["EC"][:, 0:16], in_=SM["CS"][:, qs], r=["sCS"], w=["sEC"])
                if which == 0:
                    OP("dve", "tensor_scalar", out=SM["ES"][:, 0:16], in0=SM["SN"][:, qs], scalar1=-1.0, scalar2=None, op0=ALU.mult,
                       r=["sSN"], w=["sES"])
                else:
                    OP("dve", "tensor_copy", out=SM["ES"][:, 0:16], in_=SM["SN"][:, qs], r=["sSN"], w=["sES"])
                n = 1
                while n < TC:
                    if d == 0:
                        src, dst = slice(0, n), slice(n, 2 * n)
                    else:
                        src, dst = slice(TC - n, TC), slice(TC - 2 * n, TC - n)
                    ecb = SM["EC"][:, 0:16].unsqueeze(2).to_broadcast([128, 16, n])
                    esb = SM["ES"][:, 0:16].unsqueeze(2).to_broadcast([128, 16, n])
                    OP("dve", "tensor_tensor", out=ZA[:, :, :].rearrange("p a b -> p (a b)")[:, 0:16 * n].rearrange("p (g n) -> p g n", n=n),
                       in0=Z[:, :, src], in1=ecb, op=ALU.mult, r=["sZ0", "sEC"], w=["sZA0"])
                    OP("dve", "tensor_tensor", out=ZB[:, :, :].rearrange("p a b -> p (a b)")[:, 0:16 * n].rearrange("p (g n) -> p g n", n=n),
                       in0=W[:, :, src], in1=esb, op=ALU.mult, r=["sW0", "sES"], w=["sZB0"])
                    OP("dve", "tensor_tensor", out=Z[:, :, dst],
                       in0=ZA[:, :, :].rearrange("p a b -> p (a b)")[:, 0:16 * n].rearrange("p (g n) -> p g n", n=n),
                       in1=ZB[:, :, :].rearrange("p a b -> p (a b)")[:, 0:16 * n].rearrange("p (g n) -> p g n", n=n),
                       op=ALU.subtract, r=["sZA0", "sZB0"], w=["sZ0"])
                    OP("dve", "tensor_tensor", out=ZA[:, :, :].rearrange("p a b -> p (a b)")[:, 0:16 * n].rearrange("p (g n) -> p g n", n=n),
                       in0=W[:, :, src], in1=ecb, op=ALU.mult, r=["sW0", "sEC"], w=["sZA0"])
                    OP("dve", "tensor_tensor", out=ZB[:, :, :].rearrange("p a b -> p (a b)")[:, 0:16 * n].rearrange("p (g n) -> p g n", n=n),
                       in0=Z[:, :, src], in1=esb, op=ALU.mult, r=["sZ0", "sES"], w=["sZB0"])
                    OP("dve", "tensor_tensor", out=W[:, :, dst],
                       in0=ZA[:, :, :].rearrange("p a b -> p (a b)")[:, 0:16 * n].rearrange("p (g n) -> p g n", n=n),
                       in1=ZB[:, :, :].rearrange("p a b -> p (a b)")[:, 0:16 * n].rearrange("p (g n) -> p g n", n=n),
                       op=ALU.add, r=["sZA0", "sZB0"], w=["sW0"])
                    OP("dve", "tensor_tensor", out=SM["EC2"][:, 0:16], in0=SM["EC"][:, 0:16], in1=SM["EC"][:, 0:16], op=ALU.mult,
                       r=["sEC"], w=["sEC2"])
                    OP("dve", "tensor_tensor", out=SM["ES2"][:, 0:16], in0=SM["ES"][:, 0:16], in1=SM["ES"][:, 0:16], op=ALU.mult,
                       r=["sES"], w=["sES2"])
                    OP("dve", "tensor_tensor", out=SM["ES"][:, 0:16], in0=SM["ES"][:, 0:16], in1=SM["EC"][:, 0:16], op=ALU.mult,
                       r=["sES", "sEC"], w=["sES"])
                    OP("dve", "tensor_scalar", out=SM["ES"][:, 0:16], in0=SM["ES"][:, 0:16], scalar1=2.0, scalar2=None, op0=ALU.mult,
                       r=["sES"], w=["sES"])
                    OP("dve", "tensor_tensor", out=SM["EC"][:, 0:16], in0=SM["EC2"][:, 0:16], in1=SM["ES2"][:, 0:16], op=ALU.subtract,
                       r=["sEC2", "sES2"], w=["sEC"])
                    n *= 2
                if which == 0:
                    OP("dve", "tensor_copy", out=TAB[0][:], in_=Z[:], r=["sZ0"], w=["sTAB0"])
                    OP("dve", "tensor_scalar", out=TAB[1][:], in0=W[:], scalar1=SGN[:, 0:1], scalar2=None, op0=ALU.mult,
                       r=["sW0", "sSGN"], w=["sTAB1"])
                else:
                    OP("dve", "tensor_scalar", out=TAB[2][:], in0=Z[:], scalar1=SGN[:, 1:2], scalar2=None, op0=ALU.mult,
                       r=["sZ0", "sSGN"], w=["sTAB2"])
                    OP("dve", "tensor_scalar", out=TAB[3][:], in0=W[:], scalar1=-1.0, scalar2=None, op0=ALU.mult,
                       r=["sW0"], w=["sTAB3"])
                    OP("dve", "tensor_copy", out=SM["EC2"][:, 0:16], in_=SM["EC"][:, 0:16], r=["sEC"], w=["sEC2"])
                    OP("dve", "tensor_copy", out=SM["ES2"][:, 0:16], in_=SM["ES"][:, 0:16], r=["sES"], w=["sES2"])
            p.barrier()
            OP("dve", "memset", ap=INIT[:], constant=0.0, w=["sINIT"])
            order = list(range(nchunk)) if d == 0 else list(range(NCX // TC - 1, -1, -1)) + list(range(nchunk - 1, NCX // TC - 1, -1))
            allw = lambda pr: ["sW%d_%d" % (pr, gi) for gi in range(16)]
            def stage1(ci):
                m = order[ci]
                t0 = m * TC
                cp = ci % 2
                Zc = Zs[cp]
                for cc in range(2):
                    par = cc
                    b1, b2 = PS[0 + par], PS[2 + par]
                    k1, k2 = "ps%d" % (0 + par), "ps%d" % (2 + par)
                    za, zb = ZAs[par], ZBs[par]
                    zak, zbk = "sZA%d" % par, "sZB%d" % par
                    zk = "sZ%d_%d" % (cp, cc)
                    for gg in range(8):
                        gi = 8 * cc + gg
                        MM(b1[:, gg * TC:(gg + 1) * TC], BP[0][:, gi, :], UT[:, cc, t0:t0 + TC], True, True, ["sBP0", "sUT"], [k1])
                        MM(b2[:, gg * TC:(gg + 1) * TC], BP[1][:, gi, :], UT[:, cc, t0:t0 + TC], True, True, ["sBP1", "sUT"], [k2])
                    gsl = slice(8 * cc, 8 * cc + 8)
                    OP("dve", "tensor_tensor", out=za[:], in0=b1[:, :].rearrange("p (g n) -> p g n", n=TC), in1=TAB[0][:, gsl, :],
                       op=ALU.mult, r=[k1, "sTAB0"], w=[zak])
                    OP("dve", "tensor_tensor", out=zb[:], in0=b2[:, :].rearrange("p (g n) -> p g n", n=TC), in1=TAB[1][:, gsl, :],
                       op=ALU.mult, r=[k2, "sTAB1"], w=[zbk])
                    OP("pool", "tensor_tensor", out=Zc[:, gsl, :], in0=za[:], in1=zb[:], op=ALU.add, r=[zak, zbk], w=[zk])

            def stage2(ci):
                m = order[ci]
                t0 = m * TC
                cp = ci % 2
                Zc, Wc = Zs[cp], Ws[cp]
                for cc in range(2):
                    par = cc
                    by, ky = PS[4 + par], "ps%d" % (4 + par)
                    hc, hs = HCs[par], HSs[par]
                    hck, hsk = "sHC%d" % par, "sHS%d" % par
                    zk = "sZ%d_%d" % (cp, cc)
                    gsl = slice(8 * cc, 8 * cc + 8)
                    wks = ["sW%d_%d" % (cp, 8 * cc + gg) for gg in range(8)]
                    for gg in range(8):
                        gi = 8 * cc + gg
                        q = 16 * d + gi
                        rho_b = SM["RHO"][:, q:q + 1].to_broadcast([128, TC])
                        if d == 0:
                            zin, wout = Zc[:, gi, :], Wc[:, gi, :]
                        else:
                            zin, wout = Zc[:, gi, ::-1], Wc[:, gi, ::-1]
                        OP("dve", "tensor_tensor_scan", out=wout, data0=rho_b, data1=zin, initial=INIT[:, gi:gi + 1], op0=ALU.mult,
                           op1=ALU.add, r=[zk, "sRHO", "sINIT"], w=[wks[gg]])
                    OP("pool", "tensor_tensor", out=hc[:], in0=Wc[:, gsl, :], in1=TAB[2][:, gsl, :], op=ALU.mult, r=wks + ["sTAB2"],
                       w=[hck])
                    OP("pool", "tensor_tensor", out=hs[:], in0=Wc[:, gsl, :], in1=TAB[3][:, gsl, :], op=ALU.mult, r=wks + ["sTAB3"],
                       w=[hsk])
                    for gg in range(8):
                        gi = 8 * cc + gg
                        MM(by[:, 0:TC], CP[0][:, gi, :], hc[:, gg, :], gg == 0, False, ["sCP0", hck], [ky])
                        MM(by[:, 0:TC], CP[1][:, gi, :], hs[:, gg, :], False, gg == 7, ["sCP1", hsk], [ky])
                    if d == 0:
                        OP("act", "activation", out=YF[:, cc, t0:t0 + TC], in_=by[:, 0:TC], func=AF.Copy, r=[ky], w=["sYF"])
                    else:
                        tmy, tmk = TMPYs[par], "sTMPY%d" % par
                        OP("dve", "scalar_tensor_tensor", out=tmy[:], in0=UT[:, cc, t0:t0 + TC], scalar=DSK[:, l, cc:cc + 1],
                           in1=YF[:, cc, t0:t0 + TC], op0=ALU.mult, op1=ALU.add, r=["sUT", "sDSK", "sYF"], w=[tmk])
                        OP("dve", "tensor_tensor", out=YF[:, cc, t0:t0 + TC], in0=tmy[:], in1=by[:, 0:TC], op=ALU.add,
                           r=[tmk, ky], w=["sYF"])
                ecol = TC - 1 if d == 0 else 0
                OP("dve", "tensor_copy", out=ENDS[:], in_=Wc[:, :, ecol], r=allw(cp), w=["sENDS"])
                MM(PS[6][:, 0:16], JM[:], ENDS[:], True, True, ["sJM", "sENDS"], ["ps6"])
                OP("dve", "tensor_tensor", out=RT1[:], in0=ENDS[:], in1=SM["EC2"][:, 0:16], op=ALU.mult, r=["sENDS", "sEC2"], w=["sRT1"])
                OP("dve", "tensor_tensor", out=ENDS[:], in0=PS[6][:, 0:16], in1=SM["ES2"][:, 0:16], op=ALU.mult, r=["ps6", "sES2"],
                   w=["sENDS"])
                OP("dve", "tensor_tensor", out=INIT[:], in0=RT1[:], in1=ENDS[:], op=ALU.add, r=["sRT1", "sENDS"], w=["sINIT"])

            stage1(0)
            for ci in range(len(order)):
                if ci + 1 < len(order):
                    stage1(ci + 1)
                stage2(ci)
            p.barrier()
        for (a, b) in TB:
            w = b - a
            ZT = ZA[:, :, :].rearrange("p a b -> p (a b)")
            for kc in range(2):
                OP("act", "activation", out=HC[:, :, :].rearrange("p a b -> p (a b)")[:, :w] if kc == 0 else
                   HS[:, :, :].rearrange("p a b -> p (a b)")[:, :w], in_=YF[:, kc, a:b], func=AF.Gelu_apprx_tanh, r=["sYF"],
                   w=["sHC0" if kc == 0 else "sHS0"])
            zt = [HC[:, :, :].rearrange("p a b -> p (a b)"), HS[:, :, :].rearrange("p a b -> p (a b)")]
            for oc in range(2):
                for kc in range(2):
                    MM(PS[7][:, :w], WGL[:, kc, oc * 128:(oc + 1) * 128], zt[kc][:, :w], kc == 0, kc == 1,
                       ["sWGL", "sHC0", "sHS0"], ["ps7"])
                OP("act", "activation", out=ZT[:, :w], in_=PS[7][:, :w], func=AF.Sigmoid, r=["ps7"], w=["sZA0"])
                OP("dve", "tensor_tensor", out=YT[:, oc, a:b], in0=zt[oc][:, :w], in1=ZT[:, :w], op=ALU.mult,
                   r=["sHC0", "sHS0", "sZA0"], w=["YT"])


def gla_mixer(g, bi, l):
    nc, p, OP, MM, DMA = g.nc, g.p, g.OP, g.MM, g.DMA
    PS, HT, YT = g.PS, g.HT, g.YT
    wkey = g.wkeys("w_in_bf", l, 8)
    with contextlib.ExitStack() as sm:
        def sb(name, shape, dt):
            return sm.enter_context(nc.sbuf_tensor(UN() + name, list(shape), dt))
        QT = sb("gQT", [128, NT], BF16)
        KT = sb("gKT", [128, NT], BF16)
        SR = sb("gSR", [128, NT], BF16)
        KTK = sb("gKTK", [128, 18, 128], BF16)
        VTK = sb("gVTK", [128, 18, 128], BF16)
        OF = sb("gOF", [128, NT], F32)
        GA = [sb("gGA%d" % d, [32, NT], BF16) for d in range(2)]
        WG = sb("gWG", [32, 2, 256], BF16)
        TRI = sb("gTRI", [128, 2, 128], F32)
        BD = sb("gBD", [128, 128], BF16)
        GN = sb("gGN", [128, LD[0]], F32)
        ONE = sb("gONE", [128, 1], F32)
        EPS_ = sb("gEPS", [128, 1], F32)
        WB = [sb("gWB%d" % i, [128, 8, 128], BF16) for i in range(2)]
        WGB = sb("gWGB", [128, 8, 32], BF16)
        NEG = sb("gNEG", [128, 128], F32)
        EX = sb("gEX", [128, 128], F32)
        E1s = [sb("gE1%d" % i, [128, 128], F32) for i in range(2)]
        OT = sb("gOT", [128, 128], F32)
        E2 = sb("gE2", [128, 128], F32)
        E2T = sb("gE2T", [128, 128], F32)
        QDs = [sb("gQD%d" % i, [128, 128], BF16) for i in range(2)]
        KD = sb("gKD", [128, 128], BF16)
        KDTs = [sb("gKDT%d" % i, [128, 128], BF16) for i in range(2)]
        AMs = [[sb("gAM%d_%d" % (i, hh), [128, 128], BF16) for hh in range(2)] for i in range(2)]
        S = sb("gS", [128, 64], F32)
        SB_ = sb("gSB", [128, 64], BF16)
        SQ = sb("gSQ", [128, 512], BF16)
        RS = sb("gRS", [128, 512], F32)
        OP("dve", "memset", ap=ONE[:], constant=1.0, w=["gONE"])
        OP("dve", "memset", ap=EPS_[:], constant=EPS, w=["gEPS"])
        DMA("sp", TRI[:], g.dram["tri"], w=["gTRI"])
        DMA("sp", BD[:], g.dram["bd64"], w=["gBD"])
        DMA("sp", GN[:], g.dram["gnT"], w=["gGN"])
        for d in range(2):
            DMA("pool", WG[0:16, d, :], g.dram["gla_w_gate2"][l, d], w=["gWG"])
            DMA("pool", WG[16:17, d, :], g.dram["gla_b_gate"][l, d:d + 1, :], w=["gWG"])
        wb_it = [0]

        def load_w(c0):
            i = wb_it[0] % 2
            wb_it[0] += 1
            DMA("sp", WB[i][:], g.w_in_bf[l, :, c0:c0 + 128].rearrange("(k p) c -> p k c", p=128), r=wkey, w=["gWB%d" % i])
            return WB[i], "gWB%d" % i

        DMA("sp", WGB[:], g.w_in_bf[l, :, C_GF:C_GF + 32].rearrange("(k p) c -> p k c", p=128), r=wkey, w=["gWGB"])
        for d in range(2):
            OP("pool", "memset", ap=GA[d][:], constant=1.0, w=["gGA%d" % d])
            for (a, b) in TB:
                w = b - a
                for k in range(8):
                    MM(PS[0][0:16, :w], WGB[:, k, 16 * d:16 * d + 16], HT[:, k, a:b], k == 0, k == 7, ["gWGB", "HT"], ["ps0"])
                OP("act", "activation", out=GA[d][0:16, a:b], in_=PS[0][0:16, :w], func=AF.Copy, r=["ps0"], w=["gGA%d" % d])
        for c in range(2):
            for (dst, dk, c0, fn, sc) in ((QT, "gQT", C_GQ + 128 * c, AF.Copy, 0.125), (KT, "gKT", C_GK + 128 * c, AF.Copy, 1.0),
                                          (SR, "gSR", C_GR + 128 * c, AF.Silu, 1.0)):
                w1, w1k = load_w(c0)
                for (a, b) in TB:
                    w = b - a
                    for k in range(8):
                        MM(PS[0][:, :w], w1[:, k, :], HT[:, k, a:b], k == 0, k == 7, [w1k, "HT"], ["ps0"])
                    OP("act", "activation", out=dst[:, a:b], in_=PS[0][:, :w], func=fn, scale=sc, r=["ps0"], w=[dk])
            for (dst, dk, c0) in ((KTK, "gKTK", C_GK + 128 * c), (VTK, "gVTK", C_GV + 128 * c)):
                wv, wvk = load_w(c0)
                for t4 in range(0, 18, 4):
                    nt = min(4, 18 - t4)
                    for ti in range(nt):
                        tt = t4 + ti
                        for k in range(8):
                            MM(PS[1][:, ti * 128:(ti + 1) * 128], HT[:, k, tt * 128:(tt + 1) * 128], wv[:, k, :], k == 0, k == 7,
                               [wvk, "HT"], ["ps1"])
                    OP("act", "activation", out=dst[:, t4:t4 + nt, :],
                       in_=PS[1][:, 0:nt * 128].rearrange("p (t c) -> p t c", c=128), func=AF.Copy, r=["ps1"], w=[dk])
            for d in range(2):
                OP("dve", "memset", ap=S[:], constant=0.0, w=["gS"])
                OP("dve", "memset", ap=SB_[:], constant=0.0, w=["gSB"])
                tiles = list(range(18)) if d == 0 else [1, 0] + list(range(17, 1, -1))
                chunks = (0, 1) if d == 0 else (1, 0)
                def prep(i):
                    tt = tiles[i]
                    t0 = tt * 128
                    pr = i % 2
                    e1, qd, kdt = E1s[pr], QDs[pr], KDTs[pr]
                    e1k, qdk, kdtk = "gE1%d" % pr, "gQD%d" % pr, "gKDT%d" % pr
                    MM(PS[2][:, 0:128], GA[d][0:17, t0:t0 + 128], WG[0:17, d, 128 * c:128 * c + 128], True, True,
                       ["gGA%d" % d, "gWG"], ["ps2"])
                    OP("act", "activation", out=EX[:], in_=PS[2][:, 0:128], func=AF.Exp, scale=-1.0, r=["ps2"], w=["gEX"])
                    OP("act", "activation", out=NEG[:], in_=EX[:], func=AF.Ln, bias=ONE[:, 0:1], scale=1.0, r=["gEX", "gONE"],
                       w=["gNEG"])
                    MM(PS[3][:, 0:128], NEG[:], TRI[:, d, :], True, True, ["gNEG", "gTRI"], ["ps3"])
                    MM(PS[3][:, 128:256], TRI[:, d, :], NEG[:], True, True, ["gNEG", "gTRI"], ["ps3"])
                    OP("act", "activation", out=e1[:], in_=PS[3][:, 0:128], func=AF.Exp, scale=-1.0 / 16, r=["ps3"], w=[e1k])
                    OP("act", "activation", out=E2[:], in_=PS[3][:, 0:128], func=AF.Exp, scale=1.0 / 16, r=["ps3"], w=["gE2"])
                    OP("act", "activation", out=E2T[:], in_=PS[3][:, 128:256], func=AF.Exp, scale=1.0 / 16, r=["ps3"], w=["gE2T"])
                    OP("dve", "tensor_tensor", out=qd[:], in0=QT[:, t0:t0 + 128], in1=e1[:], op=ALU.mult, r=["gQT", e1k], w=[qdk])
                    OP("pool", "tensor_tensor", out=KD[:], in0=KT[:, t0:t0 + 128], in1=E2[:], op=ALU.mult, r=["gKT", "gE2"], w=["gKD"])
                    OP("pool", "tensor_tensor", out=kdt[:], in0=KTK[:, tt, :], in1=E2T[:], op=ALU.mult, r=["gKTK", "gE2T"],
                       w=[kdtk])
                    for hh in range(2):
                        hb = 64 * hh
                        am, amk = AMs[pr][hh], "gAM%d_%d" % (pr, hh)
                        MM(PS[4 + hh][:, 0:128], KD[hb:hb + 64, :], qd[hb:hb + 64, :], True, True, ["gKD", qdk], ["ps%d" % (4 + hh)])
                        OP("dve", "tensor_tensor", out=am[:], in0=PS[4 + hh][:, 0:128], in1=TRI[:, d, :], op=ALU.mult,
                           r=["ps%d" % (4 + hh), "gTRI"], w=[amk])

                def recur(i):
                    tt = tiles[i]
                    t0 = tt * 128
                    pr = i % 2
                    e1, qd, kdt = E1s[pr], QDs[pr], KDTs[pr]
                    e1k, qdk, kdtk = "gE1%d" % pr, "gQD%d" % pr, "gKDT%d" % pr
                    for hh in range(2):
                        hb = 64 * hh
                        am, amk = AMs[pr][hh], "gAM%d_%d" % (pr, hh)
                        MM(PS[6][hb:hb + 64, 0:128], VTK[:, tt, hb:hb + 64], am[:], True, False, ["gVTK", amk], ["ps6"])
                    for ch in chunks:
                        cs = 64 * ch
                        for hh in range(2):
                            hb = 64 * hh
                            MM(PS[6][hb:hb + 64, cs:cs + 64], SB_[hb:hb + 64, :], qd[hb:hb + 64, cs:cs + 64], False, True,
                               ["gSB", qdk], ["ps6"])
                            MM(PS[7][hb:hb + 64, 0:64], kdt[cs:cs + 64, hb:hb + 64], VTK[cs:cs + 64, tt, hb:hb + 64], True, True,
                               [kdtk, "gVTK"], ["ps7"])
                        dcol = cs + 63 if d == 0 else cs
                        OP("dve", "tensor_tensor", out=S[:], in0=S[:], in1=PS[7][:, 0:64], op=ALU.add, r=["gS", "ps7"], w=["gS"])
                        OP("dve", "tensor_scalar", out=S[:], in0=S[:], scalar1=e1[:, dcol:dcol + 1], scalar2=None, op0=ALU.mult,
                           r=["gS", e1k], w=["gS"])
                        OP("act", "activation", out=SB_[:], in_=S[:], func=AF.Copy, r=["gS"], w=["gSB"])
                    if d == 0:
                        OP("act", "activation", out=OF[:, t0:t0 + 128], in_=PS[6][:, 0:128], func=AF.Copy, r=["ps6"], w=["gOF"])
                    else:
                        OP("pool", "tensor_copy", out=OT[:], in_=OF[:, t0:t0 + 128], r=["gOF"], w=["gOT"])
                        OP("dve", "tensor_tensor", out=OF[:, t0:t0 + 128], in0=OT[:], in1=PS[6][:, 0:128], op=ALU.add,
                           r=["gOT", "ps6"], w=["gOF"])

                prep(0)
                for i in range(len(tiles)):
                    if i + 1 < len(tiles):
                        prep(i + 1)
                    recur(i)
            for (a, b) in TB:
                w = b - a
                OP("act", "activation", out=SQ[:, :w], in_=OF[:, a:b], func=AF.Square, r=["gOF"], w=["gSQ"])
                MM(PS[0][:, :w], BD[:], SQ[:, :w], True, True, ["gBD", "gSQ"], ["ps0"])
                OP("act", "activation", out=RS[:, :w], in_=PS[0][:, :w], func=AF.Sqrt, scale=1.0 / 64, bias=EPS_[:, 0:1],
                   r=["ps0", "gEPS"], w=["gRS"])
                OP("dve", "reciprocal", out=RS[:, :w], in_=RS[:, :w], r=["gRS"], w=["gRS"])
                OP("dve", "scalar_tensor_tensor", out=RS[:, :w], in0=OF[:, a:b], scalar=GN[:, l:l + 1], in1=RS[:, :w], op0=ALU.mult,
                   op1=ALU.mult, r=["gOF", "gGN", "gRS"], w=["gRS"])
                OP("pool", "tensor_tensor", out=YT[:, 4 + c, a:b], in0=RS[:, :w], in1=SR[:, a:b], op=ALU.mult, r=["gRS", "gSR"],
                   w=["YT"])


def host_prep(inputs):
    f = np.float32
    w_in = np.asarray(inputs["w_in"], f)
    sq = w_in[:, :, C_SQ:C_SQ + 256]
    sk = w_in[:, :, C_SK:C_SK + 128]
    dup = np.concatenate([np.arange(64), np.arange(64), 64 + np.arange(64), 64 + np.arange(64)])
    w_in_ext = np.concatenate([w_in, sq[:, :, _rope_perm(4)], sk[:, :, dup], sk[:, :, _rope_perm(2)][:, :, dup]], axis=2)
    assert w_in_ext.shape[2] == NEXT
    rc, rs = _rope_tables()
    kl = np.arange(128)[:, None]
    ql = np.arange(128)[None, :]
    maskAB = np.stack([(kl <= ql), (ql <= kl)], 1).astype(f)
    gv = np.stack([inputs["g_pre_mix"], inputs["g_post_mix"], inputs["g_pre_ffn"], inputs["g_post_ffn"]], 1)
    gvec = np.ascontiguousarray(np.asarray(gv, f).reshape(DEPTH, 4, 8, 128).transpose(3, 0, 1, 2))
    b_modT = np.ascontiguousarray(np.asarray(inputs["b_mod"], f).reshape(DEPTH, 48, 128).transpose(2, 0, 1))
    convT = np.ascontiguousarray(np.asarray(inputs["ffn_conv"], f).reshape(DEPTH, 3, 22, 128).transpose(3, 0, 1, 2))
    sink = np.asarray(inputs["swa_sink"], f)
    sinkT = np.zeros((128, DEPTH, 2), f)
    for c in range(2):
        sinkT[0:64, :, c] = sink[None, :, 2 * c]
        sinkT[64:128, :, c] = sink[None, :, 2 * c + 1]
    rpb = np.asarray(inputs["na_rpb"], f)
    kc = np.arange(64)[:, None]
    qc = np.arange(64)[None, :]
    dcx = np.clip(kc - qc, -15, 15) + 15
    na_exp = rpb[:, :, ::-1, :][:, :, :, dcx]
    na_exp = np.ascontiguousarray(na_exp.transpose(0, 1, 3, 2, 4)).reshape(DEPTH, 4, 64, 15 * 64)
    ws = np.clip(np.arange(64) - 8, 0, 48)
    mc = ((kc >= ws[None, :]) & (kc < ws[None, :] + 16)).astype(f)
    mcol = np.tile(np.tile(mc[:, None, :], (1, 15, 1)).reshape(64, 960), (2, 1))
    jj = np.arange(128)[:, None]
    ii = np.arange(128)[None, :]
    same = (jj // 64) == (ii // 64)
    tri = np.stack([(same & (jj <= ii)), (same & (jj >= ii))], 1).astype(f)
    bd64 = same.astype(f)
    gnT = np.ascontiguousarray(np.tile(np.asarray(inputs["gla_g_norm"], f).T, (2, 1)))
    L = DEPTH
    lre = np.asarray(inputs["s5_lam_re"], f).reshape(L, 32, 64)
    lim = np.asarray(inputs["s5_lam_im"], f).reshape(L, 32, 64)
    lam = np.stack([lre, lim], 1)
    lamT = np.ascontiguousarray(np.tile(lam.transpose(3, 0, 1, 2), (2, 1, 1, 1)))
    stepT = np.ascontiguousarray(np.broadcast_to(np.asarray(inputs["s5_log_step"], f).reshape(L, 32)[None], (128, L, 32)))
    dskT = np.ascontiguousarray(np.asarray(inputs["s5_d"], f).reshape(L, 2, 128).transpose(2, 0, 1))
    sgn = np.ones((128, 2), f)
    sgn[0:64, 0] = -1.0
    sgn[64:128, 1] = -1.0
    jmat = np.zeros((128, 128), f)
    for sp in range(64):
        jmat[sp + 64, sp] = -1.0
        jmat[sp, sp + 64] = 1.0
    bre = np.asarray(inputs["s5_b_re"], f)
    bim = np.asarray(inputs["s5_b_im"], f)
    cre = np.asarray(inputs["s5_c_re"], f)
    cim = np.asarray(inputs["s5_c_im"], f)
    B1 = np.zeros((L, 2, 16, 128, 128), f)
    B2 = np.zeros((L, 2, 16, 128, 128), f)
    C1 = np.zeros((L, 2, 16, 128, 128), f)
    C2 = np.zeros((L, 2, 16, 128, 128), f)
    for gi in range(16):
        r0 = 16 * (gi % 8)
        B1[:, :, gi, r0:r0 + 16, 0:64] = bre[:, :, gi].transpose(0, 1, 3, 2)
        B1[:, :, gi, r0:r0 + 16, 64:128] = bim[:, :, gi].transpose(0, 1, 3, 2)
        B2[:, :, gi, r0:r0 + 16, 0:64] = bim[:, :, gi].transpose(0, 1, 3, 2)
        B2[:, :, gi, r0:r0 + 16, 64:128] = bre[:, :, gi].transpose(0, 1, 3, 2)
        C1[:, :, gi, 0:64, r0:r0 + 16] = cre[:, :, gi].transpose(0, 1, 3, 2)
        C1[:, :, gi, 64:128, r0:r0 + 16] = cim[:, :, gi].transpose(0, 1, 3, 2)
        C2[:, :, gi, 0:64, r0:r0 + 16] = cim[:, :, gi].transpose(0, 1, 3, 2)
        C2[:, :, gi, 64:128, r0:r0 + 16] = cre[:, :, gi].transpose(0, 1, 3, 2)
    shared = {
        "lamT": lamT, "stepT": stepT, "dskT": dskT, "sgn": sgn, "jmat": jmat, "s5_w_glu": np.asarray(inputs["s5_w_glu"], f),
        "s5B1": B1, "s5B2": B2, "s5C1": C1, "s5C2": C2,
        "tri": tri, "bd64": bd64.astype(ml_dtypes.bfloat16), "gnT": gnT,
        "gla_w_gate2": np.asarray(inputs["gla_w_gate2"], f), "gla_b_gate": np.asarray(inputs["gla_b_gate"], f),
        "w_mod": np.asarray(inputs["w_mod"], f), "b_modT": b_modT, "gvec": gvec, "w_in_ext": np.ascontiguousarray(w_in_ext),
        "w_out": np.asarray(inputs["w_out"], f), "ffn_w_up": np.asarray(inputs["ffn_w_up"], f),
        "ffn_w_down": np.asarray(inputs["ffn_w_down"], f), "convT": convT,
        "ropeC": rc.astype(ml_dtypes.bfloat16), "ropeS": rs.astype(ml_dtypes.bfloat16),
        "maskAB": maskAB.astype(ml_dtypes.bfloat16), "identf": np.eye(128, dtype=f), "sinkT": sinkT,
        "na_exp": na_exp, "mcol": np.ascontiguousarray(mcol),
    }
    x = np.asarray(inputs["x"], f)
    ctx = np.asarray(inputs["ctx"], f)
    c = np.asarray(inputs["c"], f)
    cc = np.asarray(inputs["c_ctx"], f)
    in_maps = []
    for core in range(8):
        b0 = 2 * core
        xcat = np.concatenate([ctx[b0:b0 + 2], x[b0:b0 + 2]], axis=1)
        cs = np.stack([c[b0], c[b0 + 1], cc], 0)
        cTm = np.ascontiguousarray(cs.reshape(3, 8, 128).transpose(2, 1, 0))
        m = dict(shared)
        m["xcat"] = np.ascontiguousarray(xcat)
        m["cT"] = cTm
        in_maps.append(m)
    return in_maps


L_FIRST = ("w_mod", "w_in_ext", "w_out", "ffn_w_up", "ffn_w_down", "na_exp", "s5B1", "s5B2", "s5C1", "s5C2", "s5_w_glu",
           "gla_w_gate2", "gla_b_gate")
L_SECOND = ("b_modT", "gvec", "convT", "sinkT", "lamT", "stepT", "dskT")

FUSED = True


def kernel(**inputs):
    in_maps = host_prep(inputs)
    if FUSED:
        nc = bass.Bass("TRN2", target_bir_lowering=False)
        build(nc)
        res = run_bass_kernel_spmd(nc, in_maps, core_ids=list(range(8)))
        outs = [r["out"] for r in res.results]
        return np.concatenate(outs, axis=0).astype(np.float32)
    cur = [m["xcat"] for m in in_maps]
    for l in range(DEPTH):
        base = {}
        m0 = in_maps[0]
        for k, v in m0.items():
            if k in ("xcat", "cT"):
                continue
            if k in L_FIRST:
                base[k] = np.ascontiguousarray(v[l:l + 1])
            elif k in L_SECOND:
                base[k] = np.ascontiguousarray(v[:, l:l + 1])
            elif k == "gnT":
                base[k] = np.ascontiguousarray(v[:, l:l + 1])
            else:
                base[k] = v
        for bi in range(2):
            maps = []
            for core in range(8):
                m = dict(base)
                m["xcat"] = np.ascontiguousarray(cur[core][[bi, 1 - bi]])
                cTm = in_maps[core]["cT"]
                m["cT"] = np.ascontiguousarray(cTm[:, :, [bi, 1 - bi, 2]])
                maps.append(m)
            nc = bass.Bass("TRN2", target_bir_lowering=False)
            build(nc, nlayers=1, nbatch=1, ldim=1, full_out=True)
            res = run_bass_kernel_spmd(nc, maps, core_ids=list(range(8)))
            for core in range(8):
                new = np.array(cur[core])
                new[bi] = res.results[core]["out"][0]
                cur[core] = new
    outs = [c[:, NCX:, :] for c in cur]
    return np.concatenate(outs, axis=0).astype(np.float32)
```

```python
import contextlib
import math
import numpy as np
import ml_dtypes
import concourse.bass as bass
import concourse.mybir as mybir
from concourse.bass_utils import run_bass_kernel_spmd

F32 = mybir.dt.float32
BF16 = mybir.dt.bfloat16
ALU = mybir.AluOpType
AF = mybir.ActivationFunctionType

ENG = ("pe", "act", "dve", "pool", "sp")
NDMA = 24


class P:
    def __init__(self, nc, same_eng_sync=True):
        self.nc = nc
        self.ops = {e: [] for e in ENG}
        self.cnt = {e: 0 for e in ENG}
        self.waited = {e: {} for e in ENG}
        self.last_w = {}
        self.readers = {}
        self.dma_nextq = {}
        self.dma_cnt = [0] * NDMA
        self.dma_last_tok = [None] * NDMA
        self.same = same_eng_sync
        self.out_toks = []
        self.bar = []

    def barrier(self):
        self.bar = [("e", e, self.cnt[e]) for e in ENG if self.cnt[e]] + \
                   [("d", k, self.dma_cnt[k]) for k in range(NDMA) if self.dma_cnt[k]]

    def _deps(self, eng, reads, writes):
        deps = list(self.bar)
        for k in reads:
            t = self.last_w.get(k)
            if t is not None:
                deps.append(t)
        for k in writes:
            t = self.last_w.get(k)
            if t is not None:
                deps.append(t)
            deps.extend(self.readers.get(k, ()))
        return deps

    def _waits(self, eng, deps):
        w = self.waited[eng]
        best = {}
        for t in deps:
            if t[0] == "e":
                _, e2, idx = t
                if e2 == eng and (not self.same or eng == "pe"):
                    continue
                key = e2
            else:
                key = ("d", t[1])
            if w.get(key, 0) >= t[2]:
                continue
            if key not in best or best[key][2] < t[2]:
                best[key] = t
        for key, t in best.items():
            w[key] = t[2]
        return list(best.values())

    def _record(self, tok, reads, writes):
        for k in reads:
            lst = self.readers.setdefault(k, [])
            lst.append(tok)
            if len(lst) > 64:
                best = {}
                for t in lst:
                    kk = t[:2]
                    if kk not in best or best[kk][2] < t[2]:
                        best[kk] = t
                self.readers[k] = list(best.values())
        for k in writes:
            self.last_w[k] = tok
            self.readers[k] = []

    def op(self, eng, fn, reads=(), writes=()):
        deps = self._deps(eng, reads, writes)
        waits = self._waits(eng, deps)
        self.cnt[eng] += 1
        tok = ("e", eng, self.cnt[eng])
        self.ops[eng].append((waits, fn, ("e", eng)))
        self._record(tok, reads, writes)
        return tok

    def dma(self, q, fn, reads=(), writes=(), is_out=False):
        lo, n = (0, 16) if q == "sp" else (16, NDMA - 16)
        cur = self.dma_nextq.get(q, 0)
        k = lo + cur
        self.dma_nextq[q] = (cur + 1) % n
        deps = self._deps(q, reads, writes)
        if self.dma_last_tok[k] is not None:
            deps.append(self.dma_last_tok[k])
        waits = self._waits(q, deps)
        self.dma_cnt[k] += 16
        tok = ("d", k, self.dma_cnt[k])
        self.dma_last_tok[k] = tok
        self.ops[q].append((waits, fn, ("d", k)))
        self._record(tok, reads, writes)
        if is_out:
            self.out_toks.append(tok)
        return tok

    def emit(self):
        nc = self.nc
        with contextlib.ExitStack() as es:
            esem = {e: es.enter_context(nc.semaphore("s_" + e)) for e in ENG}
            dsem = [es.enter_context(nc.semaphore("d%d" % i)) for i in range(NDMA)]
            fin = list(self.out_toks)
            for e in ENG:
                if self.cnt[e]:
                    fin.append(("e", e, self.cnt[e]))
            for k in range(NDMA):
                if self.dma_cnt[k]:
                    fin.append(("d", k, self.dma_cnt[k]))
            block = es.enter_context(nc.Block())

            def run(eng_name, eng):
                for waits, fn, kind in self.ops[eng_name]:
                    for t in waits:
                        if t[0] == "e":
                            eng.wait_ge(esem[t[1]], t[2])
                        else:
                            eng.wait_ge(dsem[t[1]], t[2])
                    ins = fn(eng)
                    if kind[0] == "e":
                        ins.then_inc(esem[kind[1]], 1)
                    else:
                        ins.then_inc(dsem[kind[1]], 16)

            @block.tensor
            def _(e):
                run("pe", e)

            @block.scalar
            def _(e):
                run("act", e)

            @block.vector
            def _(e):
                run("dve", e)

            @block.gpsimd
            def _(e):
                run("pool", e)

            @block.sync
            def _(e):
                run("sp", e)
                for t in fin:
                    if t[0] == "e":
                        if t[1] != "sp":
                            e.wait_ge(esem[t[1]], t[2])
                    else:
                        e.wait_ge(dsem[t[1]], t[2])


NT = 2304
NCX = 256
NLAT = 2048
D = 1024
DEPTH = 4
LD = [4]
DFF = 2816
EPS = 1e-6
TB = [(0, 256)] + [(256 + 512 * i, 256 + 512 * (i + 1)) for i in range(4)]
C_A, C_NAQ, C_NAK, C_NAV = 0, 256, 512, 768
C_GQ, C_GK, C_GV, C_GF, C_GB, C_GR = 1024, 1280, 1536, 1792, 1808, 1824
C_SQ, C_SK, C_SV = 2080, 2336, 2464
C_SQP, C_SKD, C_SKDP, NEXT = 2592, 2848, 3104, 3360


def _rope_perm(nh):
    idx = np.arange(nh * 64).reshape(nh, 4, 16)
    return idx[:, [1, 0, 3, 2], :].reshape(-1)


def _rope_tables():
    cos = np.ones((64, NT), np.float32)
    sin = np.zeros((64, NT), np.float32)
    t = np.arange(NLAT)
    pos = (t // 64, t % 64)
    inv = 10000.0 ** (-np.arange(0, 32, 2, dtype=np.float32) / 32)
    for half in range(2):
        ang = pos[half].astype(np.float32)[None, :] * inv[:, None]
        c, s = np.cos(ang), np.sin(ang)
        b = 32 * half
        cos[b:b + 16, NCX:] = c
        cos[b + 16:b + 32, NCX:] = c
        sin[b:b + 16, NCX:] = -s
        sin[b + 16:b + 32, NCX:] = s
    return np.concatenate([cos, cos], 0), np.concatenate([sin, sin], 0)


class Ctx:
    pass


_UN = [0]


def UN():
    _UN[0] += 1
    return "t%d_" % _UN[0]


def build(nc, dbg=None, nlayers=DEPTH, nbatch=2, mixers=("s5", "na", "gla", "swa"), ldim=DEPTH, full_out=False):
    LD[0] = ldim
    _UN[0] = 0
    p = P(nc)
    g = Ctx()
    g.p, g.nc = p, nc
    dram = {}

    def din(name, shape, dt=F32):
        dram[name] = nc.dram_tensor(name, list(shape), dt, kind="ExternalInput").ap()
        return dram[name]

    xcat = din("xcat", [2, NT, D])
    cT = din("cT", [128, 8, 3])
    w_mod = din("w_mod", [LD[0], D, 6 * D])
    b_modT = din("b_modT", [128, LD[0], 48])
    gvec = din("gvec", [128, LD[0], 4, 8])
    w_in = din("w_in_ext", [LD[0], D, NEXT])
    w_out = din("w_out", [LD[0], D, D])
    w_up = din("ffn_w_up", [LD[0], D, 2 * DFF])
    w_down = din("ffn_w_down", [LD[0], DFF, D])
    convT = din("convT", [128, LD[0], 3, 22])
    ropeC = din("ropeC", [128, NT], BF16)
    ropeS = din("ropeS", [128, NT], BF16)
    maskAB = din("maskAB", [128, 2, 128], BF16)
    identf = din("identf", [128, 128])
    sinkT = din("sinkT", [128, LD[0], 2])
    na_exp = din("na_exp", [LD[0], 4, 64, 15 * 64])
    mcol = din("mcol", [128, 15 * 64])
    din("tri", [128, 2, 128])
    din("bd64", [128, 128], BF16)
    din("gnT", [128, LD[0]])
    din("gla_w_gate2", [LD[0], 2, 16, 256])
    din("gla_b_gate", [LD[0], 2, 256])
    din("lamT", [128, LD[0], 2, 32])
    din("stepT", [128, LD[0], 32])
    din("dskT", [128, LD[0], 2])
    din("sgn", [128, 2])
    din("jmat", [128, 128])
    din("s5_w_glu", [LD[0], 256, 256])
    for nm in ("s5B1", "s5B2", "s5C1", "s5C2"):
        din(nm, [LD[0], 2, 16, 128, 128])
        dram[nm + "_bf"] = nc.dram_tensor(nm + "_bf", [LD[0], 2, 16, 128, 128], BF16).ap()
    g.dram = dram
    out = nc.dram_tensor("out", [2, NT if full_out else NLAT, D], F32, kind="ExternalOutput").ap()
    dbg_aps = {}
    if dbg:
        for name, shape in dbg.items():
            dbg_aps[name] = nc.dram_tensor("dbg_" + name, list(shape), F32, kind="ExternalOutput").ap()

    w_in_bf = nc.dram_tensor("w_in_bf", [LD[0], D, NEXT], BF16).ap()
    w_out_bf = nc.dram_tensor("w_out_bf", [LD[0], D, D], BF16).ap()
    w_up_bf = nc.dram_tensor("w_up_bf", [LD[0], D, 2 * DFF], BF16).ap()
    w_down_bf = nc.dram_tensor("w_down_bf", [LD[0], DFF, D], BF16).ap()

    def wkeys(key, l, n):
        return ["%s%d_%d" % (key, l, i) for i in range(n)]
    g.wkeys = wkeys
    g._uid = [0]

    def OP(eng, method, r=(), w=(), **kw):
        return p.op(eng, lambda e, kw=kw, method=method: getattr(e, method)(**kw), reads=r, writes=w)

    def MM(out_, lhsT, rhs, start, stop, r, w):
        return p.op("pe", lambda e: e.matmul(out_, lhsT=lhsT, rhs=rhs, start=start, stop=stop), reads=r, writes=w)

    def DMA(q, out_, in_, r=(), w=(), is_out=False, **kw):
        return p.dma(q, lambda e, kw=kw: e.dma_start(out=out_, in_=in_, **kw), reads=r, writes=w, is_out=is_out)

    g.OP, g.MM, g.DMA = OP, MM, DMA

    for l in range(nlayers):
        for (src, dst, rows, key) in ((w_in, w_in_bf, D, "w_in_bf"), (w_out, w_out_bf, D, "w_out_bf"),
                                      (w_up, w_up_bf, D, "w_up_bf"), (w_down, w_down_bf, DFF, "w_down_bf")):
            for r0 in range(0, rows, 128):
                DMA("pool", dst[l, r0:r0 + 128, :], src[l, r0:r0 + 128, :], w=["%s%d_%d" % (key, l, r0 // 128)],
                    max_dma_last_dim=4096)

    for l in range(nlayers):
        for nm in ("s5B1", "s5B2", "s5C1", "s5C2"):
            for d in range(2):
                DMA("pool", dram[nm + "_bf"][l, d].rearrange("g p s -> (g p) s"), dram[nm][l, d].rearrange("g p s -> (g p) s"),
                    w=["%s_bf%d" % (nm, l)] if d == 1 else ["%s_bf%d_d0" % (nm, l)], max_dma_last_dim=4096)

    es = contextlib.ExitStack()

    def sb(name, shape, dt):
        return es.enter_context(nc.sbuf_tensor(UN() + name, list(shape), dt))

    MODT = sb("MODT", [128, LD[0], 48, 3], F32)
    DER = sb("DER", [128, LD[0], 3, 6, 8], F32)
    GV = sb("GV", [128, LD[0], 4, 8], F32)
    CONV = sb("CONV", [128, LD[0], 3, 22], F32)
    ONESB = sb("ONESB", [128, 128], BF16)
    IDF = sb("IDF", [128, 128], F32)
    MAB = sb("MAB", [128, 2, 128], BF16)
    ESINK = sb("ESINK", [128, LD[0], 2], F32)
    PS = [es.enter_context(nc.psum_tensor("ps%d" % i, [128, 512], F32)) for i in range(8)]
    g.PS = PS

    OP("dve", "memset", ap=ONESB[:], constant=1.0, w=["ONESB"])
    DMA("sp", IDF[:], identf, w=["IDF"])
    DMA("sp", MAB[:], maskAB, w=["MAB"])
    DMA("sp", GV[:], gvec, w=["GV"])
    DMA("sp", CONV[:], convT, w=["CONV"])
    DMA("sp", ESINK[:], sinkT, w=["ESINK"])
    OP("act", "activation", out=ESINK[:], in_=ESINK[:], func=AF.Exp, r=["ESINK"], w=["ESINK"])

    with contextlib.ExitStack() as s1:
        SCT = s1.enter_context(nc.sbuf_tensor(UN() + "SCT", [128, 8, 3], F32))
        BM = s1.enter_context(nc.sbuf_tensor(UN() + "BM", [128, LD[0], 48], F32))
        WM = [s1.enter_context(nc.sbuf_tensor(UN() + "WM%d" % i, [128, 8, 512], F32)) for i in range(2)]
        DMA("sp", SCT[:], cT, w=["SCT"])
        DMA("sp", BM[:], b_modT, w=["BM"])
        OP("act", "activation", out=SCT[:], in_=SCT[:], func=AF.Silu, r=["SCT"], w=["SCT"])
        it = 0
        for l in range(nlayers):
            for cg in range(12):
                wb = WM[it % 2]
                wk = "WM%d" % (it % 2)
                it += 1
                DMA("sp", wb[:], w_mod[l, :, cg * 512:(cg + 1) * 512].rearrange("(k p) c -> p k c", p=128), w=[wk])
                for fc in range(4):
                    f = cg * 4 + fc
                    for k in range(8):
                        MM(PS[0][:, f * 3:f * 3 + 3], wb[:, k, fc * 128:(fc + 1) * 128], SCT[:, k, :], k == 0, k == 7,
                           [wk, "SCT"], ["ps0"])
            for j in range(3):
                OP("dve", "tensor_tensor", out=MODT[:, l, :, j], in0=PS[0][:, 0:144].rearrange("p (f j) -> p f j", j=3)[:, :, j],
                   in1=BM[:, l, :], op=ALU.add, r=["ps0", "BM"], w=["MODT"])
            for j in range(3):
                OP("dve", "scalar_tensor_tensor", out=DER[:, l, j, 0, :], in0=MODT[:, l, 8:16, j], scalar=1.0, in1=GV[:, l, 0, :],
                   op0=ALU.add, op1=ALU.mult, r=["MODT", "GV"], w=["DER"])
                OP("dve", "tensor_copy", out=DER[:, l, j, 1, :], in_=MODT[:, l, 0:8, j], r=["MODT"], w=["DER"])
                OP("dve", "tensor_tensor", out=DER[:, l, j, 2, :], in0=MODT[:, l, 16:24, j], in1=GV[:, l, 1, :], op=ALU.mult,
                   r=["MODT", "GV"], w=["DER"])
                OP("dve", "scalar_tensor_tensor", out=DER[:, l, j, 3, :], in0=MODT[:, l, 32:40, j], scalar=1.0, in1=GV[:, l, 2, :],
                   op0=ALU.add, op1=ALU.mult, r=["MODT", "GV"], w=["DER"])
                OP("dve", "tensor_copy", out=DER[:, l, j, 4, :], in_=MODT[:, l, 24:32, j], r=["MODT"], w=["DER"])
                OP("dve", "tensor_tensor", out=DER[:, l, j, 5, :], in0=MODT[:, l, 40:48, j], in1=GV[:, l, 3, :], op=ALU.mult,
                   r=["MODT", "GV"], w=["DER"])
    p.barrier()

    X = sb("X", [128, 8, NT], F32)
    g.X, g.DER, g.ONESB, g.MAB, g.ESINK, g.CONV = X, DER, ONESB, MAB, ESINK, CONV
    g.w_in_bf, g.w_out_bf, g.w_up_bf, g.w_down_bf = w_in_bf, w_out_bf, w_up_bf, w_down_bf
    g.ropeC, g.ropeS, g.na_exp, g.mcol = ropeC, ropeS, na_exp, mcol
    g.dbg_aps = dbg_aps

    def dump(name, ap, keys):
        if name in dbg_aps:
            DMA("pool", dbg_aps[name], ap, r=keys, is_out=True, max_dma_last_dim=2048)
    g.dump = dump

    for bi in range(nbatch):
        with contextlib.ExitStack() as s2:
            XS = [s2.enter_context(nc.sbuf_tensor(UN() + "XS%d" % i, [128, D], F32)) for i in range(2)]
            for tt in range(18):
                xs, xk = XS[tt % 2], "XS%d" % (tt % 2)
                DMA("sp", xs[:], xcat[bi, tt * 128:(tt + 1) * 128, :], w=[xk])
                for hh in range(2):
                    bank = PS[hh]
                    for kk in range(4):
                        k = hh * 4 + kk
                        p.op("pe", lambda e, o=bank[:, kk * 128:(kk + 1) * 128], i=xs[:, k * 128:(k + 1) * 128]:
                             e.transpose(out=o, in_=i, identity=IDF[:]), reads=[xk, "IDF"], writes=["ps%d" % hh])
                    if hh == 0:
                        OP("act", "activation", out=X[:, 0:4, tt * 128:(tt + 1) * 128],
                           in_=bank[:, :].rearrange("p (k t) -> p k t", t=128), func=AF.Copy, r=["ps0"], w=["X"])
                    else:
                        OP("dve", "tensor_copy", out=X[:, 4:8, tt * 128:(tt + 1) * 128],
                           in_=bank[:, :].rearrange("p (k t) -> p k t", t=128), r=["ps1"], w=["X"])
        p.barrier()
        for l in range(nlayers):
            layer(g, bi, l, (l == DEPTH - 1) and not full_out, mixers)
        with contextlib.ExitStack() as s3:
            OS_ = [s3.enter_context(nc.sbuf_tensor(UN() + "OST%d" % i, [128, D], F32)) for i in range(2)]
            for tt in range(18 if full_out else 16):
                ot, ok = OS_[tt % 2], "OST%d" % (tt % 2)
                t0 = (0 if full_out else NCX) + tt * 128
                for hh in range(2):
                    bank = PS[hh]
                    for kk in range(4):
                        k = hh * 4 + kk
                        p.op("pe", lambda e, o=bank[:, kk * 128:(kk + 1) * 128], i=X[:, k, t0:t0 + 128]:
                             e.transpose(out=o, in_=i, identity=IDF[:]), reads=["X", "IDF"], writes=["ps%d" % hh])
                    if hh == 0:
                        OP("act", "activation", out=ot[:, 0:512], in_=bank[:, :], func=AF.Copy, r=["ps0"], w=[ok])
                    else:
                        OP("dve", "tensor_copy", out=ot[:, 512:1024], in_=bank[:, :], r=["ps1"], w=[ok])
                DMA("sp", out[bi, tt * 128:(tt + 1) * 128, :], ot[:], r=[ok], is_out=True)
        p.barrier()

    p.emit()
    es.close()
    return nc


def rms_stats(g, src_fn, nk, w, SQ, RS, psb, src_keys):
    OP, MM = g.OP, g.MM
    for k in range(nk):
        OP("act", "activation", out=SQ[:, k, :w], in_=src_fn(k), func=AF.Square, r=src_keys, w=["SQ"])
    for k in range(nk):
        MM(g.PS[psb][:, :w], g.ONESB[:], SQ[:, k, :w], k == 0, k == nk - 1, ["SQ", "ONESB"], ["ps%d" % psb])
    OP("act", "activation", out=RS[:, :w], in_=g.PS[psb][:, :w], func=AF.Sqrt, scale=1.0 / D, bias=g.EPSC[:, 0:1],
       r=["ps%d" % psb, "EPSC"], w=["RS"])
    OP("dve", "reciprocal", out=RS[:, :w], in_=RS[:, :w], r=["RS"], w=["RS"])


def layer(g, bi, l, last, mixers):
    nc, p, OP, MM, DMA = g.nc, g.p, g.OP, g.MM, g.DMA
    X, DER, PS = g.X, g.DER, g.PS
    with contextlib.ExitStack() as sl:
        def sb(name, shape, dt):
            return sl.enter_context(nc.sbuf_tensor(UN() + name, list(shape), dt))
        YT = sb("YT", [128, 8, NT], BF16)
        g.YT = YT
        EPSC = sb("EPSC", [128, 1], F32)
        g.EPSC = EPSC
        OP("dve", "memset", ap=EPSC[:], constant=EPS, w=["EPSC"])
        with contextlib.ExitStack() as sh:
            HT = sh.enter_context(nc.sbuf_tensor(UN() + "HT", [128, 8, NT], BF16))
            g.HT = HT
            with contextlib.ExitStack() as sa:
                SQ = sa.enter_context(nc.sbuf_tensor(UN() + "SQ", [128, 8, 512], BF16))
                RS = sa.enter_context(nc.sbuf_tensor(UN() + "RS", [128, 512], F32))
                TMP = [sa.enter_context(nc.sbuf_tensor(UN() + "TMPa%d" % i, [128, 512], F32)) for i in range(2)]
                for (a, b) in TB:
                    w = b - a
                    j = 2 if a < NCX else bi
                    rms_stats(g, lambda k: X[:, k, a:b], 8, w, SQ, RS, 0, ["X"])
                    for k in range(8):
                        tm, tk = TMP[k % 2], "TMPa%d" % (k % 2)
                        OP("dve", "tensor_tensor", out=tm[:, :w], in0=X[:, k, a:b], in1=RS[:, :w], op=ALU.mult,
                           r=["X", "RS"], w=[tk])
                        OP("act", "activation", out=HT[:, k, a:b], in_=tm[:, :w], func=AF.Identity,
                           scale=DER[:, l, j, 0, k:k + 1], bias=DER[:, l, j, 1, k:k + 1], r=[tk, "DER"], w=["HT"])
            p.barrier()
            g.dump("HT%d_%d" % (bi, l), HT[:], ["HT"])
            for nm, chs in (("s5", (0, 1)), ("na", (2, 3)), ("gla", (4, 5)), ("swa", (6, 7))):
                if nm not in mixers:
                    OP("pool", "memset", ap=YT[:, chs[0]:chs[1] + 1, :], constant=0.0, w=["YT"])
            if "s5" in mixers:
                s5_project(g, bi, l)
                p.barrier()
            if "swa" in mixers:
                attn_mixer(g, bi, l, "swa")
                p.barrier()
            if "na" in mixers:
                attn_mixer(g, bi, l, "na")
                p.barrier()
            if "gla" in mixers:
                gla_mixer(g, bi, l)
                p.barrier()
        p.barrier()
        if "s5" in mixers:
            s5_main(g, bi, l)
            p.barrier()
        g.dump("YT%d_%d" % (bi, l), YT[:], ["YT"])
        with contextlib.ExitStack() as sc:
            WO = sc.enter_context(nc.sbuf_tensor(UN() + "WO", [128, 8, D], BF16))
            OS_ = sc.enter_context(nc.sbuf_tensor(UN() + "OS", [128, 8, 512], F32))
            SQ = sc.enter_context(nc.sbuf_tensor(UN() + "SQ", [128, 8, 512], BF16))
            RS = sc.enter_context(nc.sbuf_tensor(UN() + "RS", [128, 512], F32))
            TMP = [sc.enter_context(nc.sbuf_tensor(UN() + "TMPc%d" % i, [128, 512], F32)) for i in range(2)]
            DMA("sp", WO[:], g.w_out_bf[l].rearrange("(k p) c -> p k c", p=128), r=g.wkeys("w_out_bf", l, 8), w=["WO"])
            for (a, b) in TB:
                if last and a < NCX:
                    continue
                w = b - a
                j = 2 if a < NCX else bi
                for dc in range(8):
                    bank = 1 + dc % 2
                    for k in range(8):
                        MM(PS[bank][:, :w], WO[:, k, dc * 128:(dc + 1) * 128], YT[:, k, a:b], k == 0, k == 7,
                           ["WO", "YT"], ["ps%d" % bank])
                    OP("dve", "tensor_copy", out=OS_[:, dc, :w], in_=PS[bank][:, :w], r=["ps%d" % bank], w=["OS%d" % dc])
                rms_stats(g, lambda k: OS_[:, k, :w], 8, w, SQ, RS, 0, ["OS%d" % k for k in range(8)])
                for k in range(8):
                    tm, tk = TMP[k % 2], "TMPc%d" % (k % 2)
                    OP("pool", "tensor_tensor", out=tm[:, :w], in0=OS_[:, k, :w], in1=RS[:, :w], op=ALU.mult,
                       r=["OS%d" % k, "RS"], w=[tk])
                    OP("dve", "scalar_tensor_tensor", out=X[:, k, a:b], in0=tm[:, :w], scalar=DER[:, l, j, 2, k:k + 1],
                       in1=X[:, k, a:b], op0=ALU.mult, op1=ALU.add, r=[tk, "DER", "X"], w=["X"])
    p.barrier()
    g.dump("X1_%d_%d" % (bi, l), X[:], ["X"])
    ffn(g, bi, l, last)
    p.barrier()
    g.dump("X2_%d_%d" % (bi, l), X[:], ["X"])


def ffn_blocks():
    blks = [(0, NCX, 0, NCX)]
    for i in range(5):
        oa = NCX + 410 * i
        ob = min(NCX + 410 * (i + 1), NT)
        blks.append((max(oa - 1, NCX), min(ob + 1, NT), oa, ob))
    return blks


def ffn(g, bi, l, last):
    nc, p, OP, MM, DMA = g.nc, g.p, g.OP, g.MM, g.DMA
    X, DER, PS = g.X, g.DER, g.PS
    with contextlib.ExitStack() as sf:
        def sb(name, shape, dt):
            return sf.enter_context(nc.sbuf_tensor(UN() + name, list(shape), dt))
        EPSC = sb("EPSC", [128, 1], F32)
        g.EPSC = EPSC
        OP("dve", "memset", ap=EPSC[:], constant=EPS, w=["EPSC"])
        HBs = [sb("HB%d" % i, [128, 8, 512], BF16) for i in range(2)]
        GB = sb("GB", [128, 22, 512], BF16)
        SQ = sb("SQ", [128, 8, 512], BF16)
        RS = sb("RS", [128, 512], F32)
        OS_ = sb("OS", [128, 8, 512], F32)
        TMP = [sb("TMPf%d" % i, [128, 512], F32) for i in range(2)]
        GS = [sb("GS%d" % i, [128, 514], F32) for i in range(2)]
        CV = [sb("CV%d" % i, [128, 512], F32) for i in range(2)]
        U1 = [sb("U1%d" % i, [128, 512], F32) for i in range(2)]
        WU = [sb("WU%d" % i, [128, 8, 256], BF16) for i in range(4)]
        WD = [sb("WD%d" % i, [128, 22, 128], BF16) for i in range(2)]
        wu_it = 0
        wd_it = 0
        blks = [bk for bk in ffn_blocks() if not (last and bk[0] < NCX)]

        def make_hb(bidx):
            (ca, cb, oa, ob) = blks[bidx]
            w = cb - ca
            j = 2 if ca < NCX else bi
            HB, hbk = HBs[bidx % 2], "HB%d" % (bidx % 2)
            rms_stats(g, lambda k: X[:, k, ca:cb], 8, w, SQ, RS, 0, ["X"])
            for k in range(8):
                tm, tk = TMP[k % 2], "TMPf%d" % (k % 2)
                OP("dve", "tensor_tensor", out=tm[:, :w], in0=X[:, k, ca:cb], in1=RS[:, :w], op=ALU.mult,
                   r=["X", "RS"], w=[tk])
                OP("act", "activation", out=HB[:, k, :w], in_=tm[:, :w], func=AF.Identity,
                   scale=DER[:, l, j, 3, k:k + 1], bias=DER[:, l, j, 4, k:k + 1], r=[tk, "DER"], w=[hbk])

        make_hb(0)
        for bidx, (ca, cb, oa, ob) in enumerate(blks):
            w = cb - ca
            wo = ob - oa
            off = oa - ca
            j = 2 if ca < NCX else bi
            HB, hbk = HBs[bidx % 2], "HB%d" % (bidx % 2)
            if bidx + 1 < len(blks):
                make_hb(bidx + 1)
            for jc in range(22):
                wu, wuk = WU[wu_it % 4], "WU%d" % (wu_it % 4)
                wu_it += 1
                DMA("sp", wu[:, :, 0:128], g.w_up_bf[l, :, jc * 128:(jc + 1) * 128].rearrange("(k p) c -> p k c", p=128),
                    r=g.wkeys("w_up_bf", l, 8), w=[wuk + "g"])
                DMA("sp", wu[:, :, 128:256],
                    g.w_up_bf[l, :, DFF + jc * 128:DFF + (jc + 1) * 128].rearrange("(k p) c -> p k c", p=128),
                    r=g.wkeys("w_up_bf", l, 8), w=[wuk + "v"])
                bg, bv = (1, 2)[jc % 2], (3, 4, 7, 5)[jc % 4]
                for k in range(8):
                    MM(PS[bg][:, :w], wu[:, k, 0:128], HB[:, k, :w], k == 0, k == 7, [wuk + "g", hbk], ["ps%d" % bg])
                for k in range(8):
                    MM(PS[bv][:, :w], wu[:, k, 128:256], HB[:, k, :w], k == 0, k == 7, [wuk + "v", hbk], ["ps%d" % bv])
                gs, gk = GS[jc % 2], "GS%d" % (jc % 2)
                cv, ck = CV[jc % 2], "CV%d" % (jc % 2)
                u1, uk = U1[jc % 2], "U1%d" % (jc % 2)
                OP("pool", "memset", ap=gs[:, 0:1], constant=0.0, w=[gk])
                OP("pool", "memset", ap=gs[:, w + 1:w + 2], constant=0.0, w=[gk])
                OP("act", "activation", out=gs[:, 1:w + 1], in_=PS[bg][:, :w], func=AF.Copy, r=["ps%d" % bg], w=[gk])
                s = 1 + off
                OP("pool", "tensor_scalar", out=cv[:, :wo], in0=gs[:, s - 1:s - 1 + wo], scalar1=g.CONV[:, l, 0, jc:jc + 1],
                   scalar2=0.0, op0=ALU.mult, op1=ALU.add, r=[gk, "CONV"], w=[ck])
                OP("dve", "scalar_tensor_tensor", out=cv[:, :wo], in0=gs[:, s:s + wo], scalar=g.CONV[:, l, 1, jc:jc + 1],
                   in1=cv[:, :wo], op0=ALU.mult, op1=ALU.add, r=[gk, "CONV", ck], w=[ck])
                OP("dve", "scalar_tensor_tensor", out=cv[:, :wo], in0=gs[:, s + 1:s + 1 + wo], scalar=g.CONV[:, l, 2, jc:jc + 1],
                   in1=cv[:, :wo], op0=ALU.mult, op1=ALU.add, r=[gk, "CONV", ck], w=[ck])
                OP("act", "activation", out=u1[:, :wo], in_=cv[:, :wo], func=AF.Gelu_apprx_tanh, r=[ck], w=[uk])
                OP("dve", "tensor_tensor", out=GB[:, jc, :wo], in0=PS[bv][:, off:off + wo], in1=u1[:, :wo], op=ALU.mult,
                   r=["ps%d" % bv, uk], w=["GB"])
            for dc in range(8):
                wd, wdk = WD[wd_it % 2], "WD%d" % (wd_it % 2)
                wd_it += 1
                DMA("sp", wd[:], g.w_down_bf[l, :, dc * 128:(dc + 1) * 128].rearrange("(k p) c -> p k c", p=128),
                    r=g.wkeys("w_down_bf", l, 22), w=[wdk])
                bank = (6, 0)[dc % 2]
                for jc in range(22):
                    MM(PS[bank][:, :wo], wd[:, jc, :], GB[:, jc, :wo], jc == 0, jc == 21, [wdk, "GB"], ["ps%d" % bank])
                OP("dve", "tensor_copy", out=OS_[:, dc, :wo], in_=PS[bank][:, :wo], r=["ps%d" % bank], w=["OS%d" % dc])
            rms_stats(g, lambda k: OS_[:, k, :wo], 8, wo, SQ, RS, 0, ["OS%d" % k for k in range(8)])
            for k in range(8):
                tm, tk = TMP[k % 2], "TMPf%d" % (k % 2)
                OP("pool", "tensor_tensor", out=tm[:, :wo], in0=OS_[:, k, :wo], in1=RS[:, :wo], op=ALU.mult,
                   r=["OS%d" % k, "RS"], w=[tk])
                OP("dve", "scalar_tensor_tensor", out=X[:, k, oa:ob], in0=tm[:, :wo], scalar=DER[:, l, j, 5, k:k + 1],
                   in1=X[:, k, oa:ob], op0=ALU.mult, op1=ALU.add, r=[tk, "DER", "X"], w=["X"])


def na_rows(kr):
    rs = [r for r in range(32) if min(max(r - 4, 0), 24) <= kr <= min(max(r - 4, 0), 24) + 7]
    assert rs == list(range(rs[0], rs[-1] + 1))
    return rs[0], rs[-1] + 1


def attn_mixer(g, bi, l, kind):
    nc, p, OP, MM, DMA = g.nc, g.p, g.OP, g.MM, g.DMA
    PS, HT, YT = g.PS, g.HT, g.YT
    wkey = g.wkeys("w_in_bf", l, 8)
    swa = kind == "swa"
    with contextlib.ExitStack() as sm:
        def sb(name, shape, dt):
            return sm.enter_context(nc.sbuf_tensor(UN() + name, list(shape), dt))
        QT = sb("QT", [128, NT], BF16)
        KT = sb("KT", [128, NT], BF16)
        VT = sb("VT", [128, 18, 128], BF16)
        WB = [sb("WB%d" % i, [128, 8, 128], BF16) for i in range(3)]
        T1 = sb("T1", [128, 512], F32)
        T2 = sb("T2", [128, 512], F32)
        PT = [sb("PT%d" % i, [128, 512], BF16) for i in range(2)]
        REC = sb("REC", [128, 512], F32)
        if swa:
            RC = sb("RC", [128, NT], BF16)
            RSN = sb("RSN", [128, NT], BF16)
            DMA("sp", RC[:], g.ropeC, w=["RC"])
            DMA("sp", RSN[:], g.ropeS, w=["RSN"])
        else:
            UT = sb("UT", [128, 2, 960], BF16)
            UF = sb("UF", [128, 960], F32)
            MC = sb("MC", [128, 960], F32)
            DMA("sp", MC[:], g.mcol, w=["MC"])
        wb_it = [0]

        def load_w(c0):
            i = wb_it[0] % 3
            wb_it[0] += 1
            DMA("sp", WB[i][:], g.w_in_bf[l, :, c0:c0 + 128].rearrange("(k p) c -> p k c", p=128), r=wkey, w=["WB%d" % i])
            return WB[i], "WB%d" % i

        for c in range(2):
            if swa:
                cq, cqp, ck, ckp, cv_ = C_SQ + 128 * c, C_SQP + 128 * c, C_SKD + 128 * c, C_SKDP + 128 * c, C_SV
            else:
                cq, ck, cv_ = C_NAQ + 128 * c, C_NAK + 128 * c, C_NAV + 128 * c
            for (dst, dk, c1, c2) in ((QT, "QT", cq, cqp if swa else None), (KT, "KT", ck, ckp if swa else None)):
                w1, w1k = load_w(c1)
                if swa:
                    w2, w2k = load_w(c2)
                for (a, b) in TB:
                    w = b - a
                    for k in range(8):
                        MM(PS[0][:, :w], w1[:, k, :], HT[:, k, a:b], k == 0, k == 7, [w1k, "HT"], ["ps0"])
                    if swa:
                        for k in range(8):
                            MM(PS[1][:, :w], w2[:, k, :], HT[:, k, a:b], k == 0, k == 7, [w2k, "HT"], ["ps1"])
                        OP("dve", "tensor_tensor", out=T1[:, :w], in0=PS[0][:, :w], in1=RC[:, a:b], op=ALU.mult,
                           r=["ps0", "RC"], w=["T1"])
                        OP("dve", "tensor_tensor", out=T2[:, :w], in0=PS[1][:, :w], in1=RSN[:, a:b], op=ALU.mult,
                           r=["ps1", "RSN"], w=["T2"])
                        OP("pool", "tensor_tensor", out=dst[:, a:b], in0=T1[:, :w], in1=T2[:, :w], op=ALU.add,
                           r=["T1", "T2"], w=[dk])
                    else:
                        OP("act", "activation", out=dst[:, a:b], in_=PS[0][:, :w], func=AF.Copy, r=["ps0"], w=[dk])
            if (not swa) or c == 0:
                wv, wvk = load_w(cv_)
                for t4 in range(0, 18, 4):
                    nt = min(4, 18 - t4)
                    for ti in range(nt):
                        tt = t4 + ti
                        for k in range(8):
                            MM(PS[2][:, ti * 128:(ti + 1) * 128], HT[:, k, tt * 128:(tt + 1) * 128], wv[:, k, :], k == 0, k == 7,
                               [wvk, "HT"], ["ps2"])
                    OP("act", "activation", out=VT[:, t4:t4 + nt, :],
                       in_=PS[2][:, 0:nt * 128].rearrange("p (t c) -> p t c", c=128), func=AF.Copy, r=["ps2"], w=["VT"])
            if not swa:
                for hh in range(2):
                    for half in range(2):
                        DMA("sp", UF[half * 64:(half + 1) * 64, :], g.na_exp[l, 2 * c + hh], w=["UF"])
                    OP("act", "activation", out=UF[:], in_=UF[:], func=AF.Exp, r=["UF"], w=["UF"])
                    OP("dve", "tensor_tensor", out=UT[:, hh, :], in0=UF[:], in1=MC[:], op=ALU.mult, r=["UF", "MC"], w=["UT"])
            for (qa, qb) in TB:
                qw = qb - qa
                for hh in range(2):
                    h = 2 * c + hh
                    hb = 64 * hh
                    items = []
                    for kc in range(2):
                        items.append((kc * 128, 128, 0, kc, qa, qb, None))
                    if qa >= NCX:
                        if swa:
                            for kb in range(16):
                                ka = NCX + 128 * kb
                                a_ = max(qa, ka - 128)
                                b_ = min(qb, ka + 256)
                                if a_ < b_:
                                    items.append((ka, 128, 0, 2 + kb, a_, b_, ("swa", ka)))
                        else:
                            for kr in range(32):
                                r0, r1 = na_rows(kr)
                                a_ = max(qa, NCX + 64 * r0)
                                b_ = min(qb, NCX + 64 * r1)
                                if a_ < b_:
                                    items.append((NCX + 64 * kr, 64, 64 * (kr % 2), 2 + kr // 2, a_, b_, ("na", kr)))
                    vc0 = 64 * (h // 2) if swa else 64 * hh
                    def s_mm(ii):
                        (ka, nk, pb, vt, a_, b_, post) = items[ii]
                        n = b_ - a_
                        sbank = 3 + ii % 2
                        MM(PS[sbank][pb:pb + nk, :n], KT[hb:hb + 64, ka:ka + nk], QT[hb:hb + 64, a_:b_], True, True,
                           ["KT", "QT"], ["ps%d" % sbank])

                    for ii, (ka, nk, pb, vt, a_, b_, post) in enumerate(items):
                        n = b_ - a_
                        sbank = 3 + ii % 2
                        pt, ptk = PT[ii % 2], "PT%d" % (ii % 2)
                        s_mm(ii)
                        OP("act", "activation", out=pt[pb:pb + nk, :n], in_=PS[sbank][pb:pb + nk, :n], func=AF.Exp, scale=0.125,
                           r=["ps%d" % sbank], w=[ptk])
                        if post is not None and post[0] == "swa":
                            kst = post[1]
                            if a_ < kst:
                                OP("pool", "tensor_tensor", out=pt[:, 0:128], in0=pt[:, 0:128], in1=g.MAB[:, 0, :], op=ALU.mult,
                                   r=[ptk, "MAB"], w=[ptk])
                            if b_ > kst + 128:
                                o_ = kst + 128 - a_
                                OP("pool", "tensor_tensor", out=pt[:, o_:o_ + 128], in0=pt[:, o_:o_ + 128], in1=g.MAB[:, 1, :],
                                   op=ALU.mult, r=[ptk, "MAB"], w=[ptk])
                        elif post is not None:
                            kr = post[1]
                            ra = (a_ - NCX) // 64
                            i0 = ra - kr + 7
                            nr = n // 64
                            OP("pool", "tensor_tensor", out=pt[pb:pb + nk, :n], in0=pt[pb:pb + nk, :n],
                               in1=UT[pb:pb + nk, hh, i0 * 64:(i0 + nr) * 64], op=ALU.mult, r=[ptk, "UT"], w=[ptk])
                        MM(PS[5][hb:hb + 64, a_ - qa:b_ - qa], VT[pb:pb + nk, vt, vc0:vc0 + 64], pt[pb:pb + nk, :n], ii == 0,
                           ii == len(items) - 1, ["VT", ptk], ["ps5"])
                        MM(PS[6][hb:hb + 64, a_ - qa:b_ - qa], g.ONESB[pb:pb + nk, 0:64], pt[pb:pb + nk, :n], ii == 0,
                           ii == len(items) - 1, ["ONESB", ptk], ["ps6"])
                if swa:
                    OP("dve", "tensor_scalar", out=REC[:, :qw], in0=PS[6][:, :qw], scalar1=g.ESINK[:, l, c:c + 1], scalar2=None,
                       op0=ALU.add, r=["ps6", "ESINK"], w=["REC"])
                    OP("dve", "reciprocal", out=REC[:, :qw], in_=REC[:, :qw], r=["REC"], w=["REC"])
                else:
                    OP("dve", "reciprocal", out=REC[:, :qw], in_=PS[6][:, :qw], r=["ps6"], w=["REC"])
                yc = (6 if swa else 2) + c
                OP("dve", "tensor_tensor", out=YT[:, yc, qa:qb], in0=PS[5][:, :qw], in1=REC[:, :qw], op=ALU.mult,
                   r=["ps5", "REC"], w=["YT"])


TC = 64


def s5_project(g, bi, l):
    nc, p, OP, MM, DMA = g.nc, g.p, g.OP, g.MM, g.DMA
    PS, HT, YT = g.PS, g.HT, g.YT
    wkey = g.wkeys("w_in_bf", l, 8)
    with contextlib.ExitStack() as sm:
        WA = [sm.enter_context(nc.sbuf_tensor(UN() + "sWA%d" % i, [128, 8, 128], BF16)) for i in range(2)]
        for cc in range(2):
            wa, wak = WA[cc], "sWA%d" % cc
            DMA("sp", wa[:], g.w_in_bf[l, :, C_A + 128 * cc:C_A + 128 * cc + 128].rearrange("(k p) c -> p k c", p=128), r=wkey,
                w=[wak])
            for (a, b) in TB:
                w = b - a
                for k in range(8):
                    MM(PS[7][:, :w], wa[:, k, :], HT[:, k, a:b], k == 0, k == 7, [wak, "HT"], ["ps7"])
                OP("act", "activation", out=YT[:, cc, a:b], in_=PS[7][:, :w], func=AF.Copy, r=["ps7"], w=["sUT"])


def s5_main(g, bi, l):
    nc, p, OP, MM, DMA = g.nc, g.p, g.OP, g.MM, g.DMA
    PS, YT = g.PS, g.YT
    wkey = g.wkeys("w_in_bf", l, 8)
    PI = math.pi
    with contextlib.ExitStack() as sm:
        def sb(name, shape, dt):
            return sm.enter_context(nc.sbuf_tensor(UN() + name, list(shape), dt))
        UT = YT[:, 0:2, :]
        YF = sb("sYF", [128, 2, NT], BF16)
        BP = [sb("sBP%d" % i, [128, 16, 128], BF16) for i in range(2)]
        CP = [sb("sCP%d" % i, [128, 16, 128], BF16) for i in range(2)]
        TAB = [sb("sTAB%d" % i, [128, 16, TC], BF16) for i in range(4)]
        Zs = [sb("sZ%d" % i, [128, 16, TC], F32) for i in range(2)]
        Ws = [sb("sW%d" % i, [128, 16, TC], F32) for i in range(2)]
        ZAs = [sb("sZA%d" % i, [128, 8, TC], F32) for i in range(2)]
        ZBs = [sb("sZB%d" % i, [128, 8, TC], F32) for i in range(2)]
        HCs = [sb("sHC%d" % i, [128, 8, TC], BF16) for i in range(2)]
        HSs = [sb("sHS%d" % i, [128, 8, TC], BF16) for i in range(2)]
        Z, W, ZA, ZB, HC, HS = Zs[0], Ws[0], ZAs[0], ZBs[0], HCs[0], HSs[0]
        WGL = sb("sWGL", [128, 2, 256], BF16)
        LAM = sb("sLAM", [128, 2, 32], F32)
        STP = sb("sSTP", [128, 32], F32)
        DSK = sb("sDSK", [128, LD[0], 2], F32)
        SGN = sb("sSGN", [128, 2], F32)
        JM = sb("sJM", [128, 128], F32)
        HPI = sb("sHPI", [128, 1], F32)
        sm_names = ["RHO", "TH", "M", "SH", "CH", "SN", "CS", "LBR", "LBI", "DEN", "KR", "KI", "T0", "T1", "EC", "ES", "EC2", "ES2"]
        SM = {n: sb("s" + n, [128, 32], F32) for n in sm_names}
        INIT = sb("sINIT", [128, 16], F32)
        ENDS = sb("sENDS", [128, 16], F32)
        RT1 = sb("sRT1", [128, 16], F32)
        TMPYs = [sb("sTMPY%d" % i, [128, TC], F32) for i in range(2)]
        DMA("sp", LAM[:], g.dram["lamT"][:, l], w=["sLAM"])
        DMA("sp", STP[:], g.dram["stepT"][:, l], w=["sSTP"])
        DMA("sp", DSK[:], g.dram["dskT"], w=["sDSK"])
        DMA("sp", SGN[:], g.dram["sgn"], w=["sSGN"])
        DMA("sp", JM[:], g.dram["jmat"], w=["sJM"])
        DMA("pool", WGL[:], g.dram["s5_w_glu"][l].rearrange("(k p) c -> p k c", p=128), w=["sWGL"])
        OP("dve", "memset", ap=HPI[:], constant=PI / 2, w=["sHPI"])

        def V(eng, method, outn, r, **kw):
            OP(eng, method, r=["s" + x for x in r], w=["s" + outn], **kw)

        def TT(outn, an, bn, op):
            V("dve", "tensor_tensor", outn, [an, bn], out=SM[outn][:], in0=SM[an][:], in1=SM[bn][:], op=op)

        OP("act", "activation", out=STP[:], in_=STP[:], func=AF.Exp, r=["sSTP"], w=["sSTP"])
        OP("dve", "tensor_tensor", out=SM["T0"][:], in0=LAM[:, 0, :], in1=STP[:], op=ALU.mult, r=["sLAM", "sSTP"], w=["sT0"])
        OP("act", "activation", out=SM["RHO"][:], in_=SM["T0"][:], func=AF.Exp, r=["sT0"], w=["sRHO"])
        OP("dve", "tensor_tensor", out=SM["TH"][:], in0=LAM[:, 1, :], in1=STP[:], op=ALU.mult, r=["sLAM", "sSTP"], w=["sTH"])
        for _ in range(5):
            V("dve", "tensor_scalar", "M", ["TH"], out=SM["M"][:], in0=SM["TH"][:], scalar1=PI, scalar2=-2 * PI, op0=ALU.is_gt,
              op1=ALU.mult)
            TT("TH", "TH", "M", ALU.add)
        for _ in range(2):
            V("dve", "tensor_scalar", "M", ["TH"], out=SM["M"][:], in0=SM["TH"][:], scalar1=-PI, scalar2=2 * PI, op0=ALU.is_lt,
              op1=ALU.mult)
            TT("TH", "TH", "M", ALU.add)
        OP("act", "activation", out=SM["SH"][:], in_=SM["TH"][:], func=AF.Sin, scale=0.5, r=["sTH"], w=["sSH"])
        OP("act", "activation", out=SM["CH"][:], in_=SM["TH"][:], func=AF.Sin, scale=0.5, bias=HPI[:, 0:1], r=["sTH", "sHPI"],
           w=["sCH"])
        TT("SN", "SH", "CH", ALU.mult)
        V("dve", "tensor_scalar", "SN", ["SN"], out=SM["SN"][:], in0=SM["SN"][:], scalar1=2.0, scalar2=None, op0=ALU.mult)
        TT("T0", "CH", "CH", ALU.mult)
        TT("T1", "SH", "SH", ALU.mult)
        TT("CS", "T0", "T1", ALU.subtract)
        TT("LBR", "RHO", "CS", ALU.mult)
        TT("LBI", "RHO", "SN", ALU.mult)
        V("dve", "tensor_scalar", "LBR", ["LBR"], out=SM["LBR"][:], in0=SM["LBR"][:], scalar1=-1.0, scalar2=None, op0=ALU.add)
        OP("dve", "tensor_tensor", out=SM["T0"][:], in0=LAM[:, 0, :], in1=LAM[:, 0, :], op=ALU.mult, r=["sLAM"], w=["sT0"])
        OP("dve", "tensor_tensor", out=SM["T1"][:], in0=LAM[:, 1, :], in1=LAM[:, 1, :], op=ALU.mult, r=["sLAM"], w=["sT1"])
        TT("DEN", "T0", "T1", ALU.add)
        V("dve", "reciprocal", "DEN", ["DEN"], out=SM["DEN"][:], in_=SM["DEN"][:])
        OP("dve", "tensor_tensor", out=SM["T0"][:], in0=SM["LBR"][:], in1=LAM[:, 0, :], op=ALU.mult, r=["sLBR", "sLAM"], w=["sT0"])
        OP("dve", "tensor_tensor", out=SM["T1"][:], in0=SM["LBI"][:], in1=LAM[:, 1, :], op=ALU.mult, r=["sLBI", "sLAM"], w=["sT1"])
        TT("KR", "T0", "T1", ALU.add)
        TT("KR", "KR", "DEN", ALU.mult)
        OP("dve", "tensor_tensor", out=SM["T0"][:], in0=SM["LBI"][:], in1=LAM[:, 0, :], op=ALU.mult, r=["sLBI", "sLAM"], w=["sT0"])
        OP("dve", "tensor_tensor", out=SM["T1"][:], in0=SM["LBR"][:], in1=LAM[:, 1, :], op=ALU.mult, r=["sLBR", "sLAM"], w=["sT1"])
        TT("KI", "T0", "T1", ALU.subtract)
        TT("KI", "KI", "DEN", ALU.mult)

        nchunk = NT // TC
        for d in range(2):
            qs = slice(16 * d, 16 * d + 16)
            for i, nm in enumerate(("s5B1", "s5B2")):
                DMA("sp", BP[i][:], g.dram[nm + "_bf"][l, d].rearrange("g p s -> p g s"), r=["%s_bf%d" % (nm, l), "%s_bf%d_d0" % (nm, l)], w=["sBP%d" % i])
            for i, nm in enumerate(("s5C1", "s5C2")):
                DMA("sp", CP[i][:], g.dram[nm + "_bf"][l, d].rearrange("g p s -> p g s"), r=["%s_bf%d" % (nm, l), "%s_bf%d_d0" % (nm, l)], w=["sCP%d" % i])
            for which in range(2):
                i0 = 0 if d == 0 else TC - 1
                if which == 0:
                    OP("dve", "tensor_copy", out=Z[:, :, i0], in_=SM["KR"][:, qs], r=["sKR"], w=["sZ0"])
                    OP("dve", "tensor_copy", out=W[:, :, i0], in_=SM["KI"][:, qs], r=["sKI"], w=["sW0"])
                else:
                    OP("dve", "memset", ap=Z[:, :, i0:i0 + 1], constant=1.0, w=["sZ0"])
                    OP("dve", "memset", ap=W[:, :, i0:i0 + 1], constant=0.0, w=["sW0"])
                OP("dve", "tensor_copy", out=SM["EC"][:, 0:16], in_=SM["CS"][:, qs], r=["sCS"], w=["sEC"])
                if which == 0:
                    OP("dve", "tensor_scalar", out=SM["ES"][:, 0:16], in0=SM["SN"][:, qs], scalar1=-1.0, scalar2=None, op0=ALU.mult,
                       r=["sSN"], w=["sES"])
                else:
                    OP("dve", "tensor_copy", out=SM["ES"][:, 0:16], in_=SM["SN"][:, qs], r=["sSN"], w=["sES"])
                n = 1
                while n < TC:
                    if d == 0:
                        src, dst = slice(0, n), slice(n, 2 * n)
                    else:
                        src, dst = slice(TC - n, TC), slice(TC - 2 * n, TC - n)
                    ecb = SM["EC"][:, 0:16].unsqueeze(2).to_broadcast([128, 16, n])
                    esb = SM["ES"][:, 0:16].unsqueeze(2).to_broadcast([128, 16, n])
                    OP("dve", "tensor_tensor", out=ZA[:, :, :].rearrange("p a b -> p (a b)")[:, 0:16 * n].rearrange("p (g n) -> p g n", n=n),
                       in0=Z[:, :, src], in1=ecb, op=ALU.mult, r=["sZ0", "sEC"], w=["sZA0"])
                    OP("dve", "tensor_tensor", out=ZB[:, :, :].rearrange("p a b -> p (a b)")[:, 0:16 * n].rearrange("p (g n) -> p g n", n=n),
                       in0=W[:, :, src], in1=esb, op=ALU.mult, r=["sW0", "sES"], w=["sZB0"])
                    OP("dve", "tensor_tensor", out=Z[:, :, dst],
                       in0=ZA[:, :, :].rearrange("p a b -> p (a b)")[:, 0:16 * n].rearrange("p (g n) -> p g n", n=n),
                       in1=ZB[:, :, :].rearrange("p a b -> p (a b)")[:, 0:16 * n].rearrange("p (g n) -> p g n", n=n),
                       op=ALU.subtract, r=["sZA0", "sZB0"], w=["sZ0"])
                    OP("dve", "tensor_tensor", out=ZA[:, :, :].rearrange("p a b -> p (a b)")[:, 0:16 * n].rearrange("p (g n) -> p g n", n=n),
                       in0=W[:, :, src], in1=ecb, op=ALU.mult, r=["sW0", "sEC"], w=["sZA0"])
                    OP("dve", "tensor_tensor", out=ZB[:, :, :].rearrange("p a b -> p (a b)")[:, 0:16 * n].rearrange("p (g n) -> p g n", n=n),
                       in0=Z[:, :, src], in1=esb, op=ALU.mult, r=["sZ0", "sES"], w=["sZB0"])
                    OP("dve", "tensor_tensor", out=W[:, :, dst],
                       in0=ZA[:, :, :].rearrange("p a b -> p (a b)")[:, 0:16 * n].rearrange("p (g n) -> p g n", n=n),
                       in1=ZB[:, :, :].rearrange("p a b -> p (a b)")[:, 0:16 * n].rearrange("p (g n) -> p g n", n=n),
                       op=ALU.add, r=["sZA0", "sZB0"], w=["sW0"])
                    OP("dve", "tensor_tensor", out=SM["EC2"][:, 0:16], in0=SM["EC"][:, 0:16], in1=SM["EC"][:, 0:16], op=ALU.mult,
                       r=["sEC"], w=["sEC2"])
                    OP("dve", "tensor_tensor", out=SM["ES2"][:, 0:16], in0=SM["ES"][:, 0:16], in1=SM["ES"][:, 0:16], op=ALU.mult,
                       r=["sES"], w=["sES2"])
                    OP("dve", "tensor_tensor", out=SM["ES"][:, 0:16], in0=SM["ES"][:, 0:16], in1=SM["EC"][:, 0:16], op=ALU.mult,
                       r=["sES", "sEC"], w=["sES"])
                    OP("dve", "tensor_scalar", out=SM["ES"][:, 0:16], in0=SM["ES"][:, 0:16], scalar1=2.0, scalar2=None, op0=ALU.mult,
                       r=["sES"], w=["sES"])
                    OP("dve", "tensor_tensor", out=SM["EC"][:, 0:16], in0=SM["EC2"][:, 0:16], in1=SM["ES2"][:, 0:16], op=ALU.subtract,
                       r=["sEC2", "sES2"], w=["sEC"])
                    n *= 2
                if which == 0:
                    OP("dve", "tensor_copy", out=TAB[0][:], in_=Z[:], r=["sZ0"], w=["sTAB0"])
                    OP("dve", "tensor_scalar", out=TAB[1][:], in0=W[:], scalar1=SGN[:, 0:1], scalar2=None, op0=ALU.mult,
                       r=["sW0", "sSGN"], w=["sTAB1"])
                else:
                    OP("dve", "tensor_scalar", out=TAB[2][:], in0=Z[:], scalar1=SGN[:, 1:2], scalar2=None, op0=ALU.mult,
                       r=["sZ0", "sSGN"], w=["sTAB2"])
                    OP("dve", "tensor_scalar", out=TAB[3][:], in0=W[:], scalar1=-1.0, scalar2=None, op0=ALU.mult,
                       r=["sW0"], w=["sTAB3"])
                    OP("dve", "tensor_copy", out=SM["EC2"][:, 0:16], in_=SM["EC"][:, 0:16], r=["sEC"], w=["sEC2"])
                    OP("dve", "tensor_copy", out=SM["ES2"][:, 0:16], in_=SM["ES"][:, 0:16], r=["sES"], w=["sES2"])
            p.barrier()
            OP("dve", "memset", ap=INIT[:], constant=0.0, w=["sINIT"])
            order = list(range(nchunk)) if d == 0 else list(range(NCX // TC - 1, -1, -1)) + list(range(nchunk - 1, NCX // TC - 1, -1))
            allw = lambda pr: ["sW%d_%d" % (pr, gi) for gi in range(16)]
            def stage1(ci):
                m = order[ci]
                t0 = m * TC
                cp = ci % 2
                Zc = Zs[cp]
                for cc in range(2):
                    par = cc
                    b1, b2 = PS[0 + par], PS[2 + par]
                    k1, k2 = "ps%d" % (0 + par), "ps%d" % (2 + par)
                    za, zb = ZAs[par], ZBs[par]
                    zak, zbk = "sZA%d" % par, "sZB%d" % par
                    zk = "sZ%d_%d" % (cp, cc)
                    for gg in range(8):
                        gi = 8 * cc + gg
                        MM(b1[:, gg * TC:(gg + 1) * TC], BP[0][:, gi, :], UT[:, cc, t0:t0 + TC], True, True, ["sBP0", "sUT"], [k1])
                        MM(b2[:, gg * TC:(gg + 1) * TC], BP[1][:, gi, :], UT[:, cc, t0:t0 + TC], True, True, ["sBP1", "sUT"], [k2])
                    gsl = slice(8 * cc, 8 * cc + 8)
                    OP("dve", "tensor_tensor", out=za[:], in0=b1[:, :].rearrange("p (g n) -> p g n", n=TC), in1=TAB[0][:, gsl, :],
                       op=ALU.mult, r=[k1, "sTAB0"], w=[zak])
                    OP("dve", "tensor_tensor", out=zb[:], in0=b2[:, :].rearrange("p (g n) -> p g n", n=TC), in1=TAB[1][:, gsl, :],
                       op=ALU.mult, r=[k2, "sTAB1"], w=[zbk])
                    OP("pool", "tensor_tensor", out=Zc[:, gsl, :], in0=za[:], in1=zb[:], op=ALU.add, r=[zak, zbk], w=[zk])

            def stage2(ci):
                m = order[ci]
                t0 = m * TC
                cp = ci % 2
                Zc, Wc = Zs[cp], Ws[cp]
                for cc in range(2):
                    par = cc
                    by, ky = PS[4 + par], "ps%d" % (4 + par)
                    hc, hs = HCs[par], HSs[par]
                    hck, hsk = "sHC%d" % par, "sHS%d" % par
                    zk = "sZ%d_%d" % (cp, cc)
                    gsl = slice(8 * cc, 8 * cc + 8)
                    wks = ["sW%d_%d" % (cp, 8 * cc + gg) for gg in range(8)]
                    for gg in range(8):
                        gi = 8 * cc + gg
                        q = 16 * d + gi
                        rho_b = SM["RHO"][:, q:q + 1].to_broadcast([128, TC])
                        if d == 0:
                            zin, wout = Zc[:, gi, :], Wc[:, gi, :]
                        else:
                            zin, wout = Zc[:, gi, ::-1], Wc[:, gi, ::-1]
                        OP("dve", "tensor_tensor_scan", out=wout, data0=rho_b, data1=zin, initial=INIT[:, gi:gi + 1], op0=ALU.mult,
                           op1=ALU.add, r=[zk, "sRHO", "sINIT"], w=[wks[gg]])
                    OP("pool", "tensor_tensor", out=hc[:], in0=Wc[:, gsl, :], in1=TAB[2][:, gsl, :], op=ALU.mult, r=wks + ["sTAB2"],
                       w=[hck])
                    OP("pool", "tensor_tensor", out=hs[:], in0=Wc[:, gsl, :], in1=TAB[3][:, gsl, :], op=ALU.mult, r=wks + ["sTAB3"],
                       w=[hsk])
                    for gg in range(8):
                        gi = 8 * cc + gg
                        MM(by[:, 0:TC], CP[0][:, gi, :], hc[:, gg, :], gg == 0, False, ["sCP0", hck], [ky])
                        MM(by[:, 0:TC], CP[1][:, gi, :], hs[:, gg, :], False, gg == 7, ["sCP1", hsk], [ky])
                    if d == 0:
                        OP("act", "activation", out=YF[:, cc, t0:t0 + TC], in_=by[:, 0:TC], func=AF.Copy, r=[ky], w=["sYF"])
                    else:
                        tmy, tmk = TMPYs[par], "sTMPY%d" % par
                        OP("dve", "scalar_tensor_tensor", out=tmy[:], in0=UT[:, cc, t0:t0 + TC], scalar=DSK[:, l, cc:cc + 1],
                           in1=YF[:, cc, t0:t0 + TC], op0=ALU.mult, op1=ALU.add, r=["sUT", "sDSK", "sYF"], w=[tmk])
                        OP("dve", "tensor_tensor", out=YF[:, cc, t0:t0 + TC], in0=tmy[:], in1=by[:, 0:TC], op=ALU.add,
                           r=[tmk, ky], w=["sYF"])
                ecol = TC - 1 if d == 0 else 0
                OP("dve", "tensor_copy", out=ENDS[:], in_=Wc[:, :, ecol], r=allw(cp), w=["sENDS"])
                MM(PS[6][:, 0:16], JM[:], ENDS[:], True, True, ["sJM", "sENDS"], ["ps6"])
                OP("dve", "tensor_tensor", out=RT1[:], in0=ENDS[:], in1=SM["EC2"][:, 0:16], op=ALU.mult, r=["sENDS", "sEC2"], w=["sRT1"])
                OP("dve", "tensor_tensor", out=ENDS[:], in0=PS[6][:, 0:16], in1=SM["ES2"][:, 0:16], op=ALU.mult, r=["ps6", "sES2"],
                   w=["sENDS"])
                OP("dve", "tensor_tensor", out=INIT[:], in0=RT1[:], in1=ENDS[:], op=ALU.add, r=["sRT1", "sENDS"], w=["sINIT"])

            stage1(0)
            for ci in range(len(order)):
                if ci + 1 < len(order):
                    stage1(ci + 1)
                stage2(ci)
            p.barrier()
        for (a, b) in TB:
            w = b - a
            ZT = ZA[:, :, :].rearrange("p a b -> p (a b)")
            for kc in range(2):
                OP("act", "activation", out=HC[:, :, :].rearrange("p a b -> p (a b)")[:, :w] if kc == 0 else
                   HS[:, :, :].rearrange("p a b -> p (a b)")[:, :w], in_=YF[:, kc, a:b], func=AF.Gelu_apprx_tanh, r=["sYF"],
                   w=["sHC0" if kc == 0 else "sHS0"])
            zt = [HC[:, :, :].rearrange("p a b -> p (a b)"), HS[:, :, :].rearrange("p a b -> p (a b)")]
            for oc in range(2):
                for kc in range(2):
                    MM(PS[7][:, :w], WGL[:, kc, oc * 128:(oc + 1) * 128], zt[kc][:, :w], kc == 0, kc == 1,
                       ["sWGL", "sHC0", "sHS0"], ["ps7"])
                OP("act", "activation", out=ZT[:, :w], in_=PS[7][:, :w], func=AF.Sigmoid, r=["ps7"], w=["sZA0"])
                OP("dve", "tensor_tensor", out=YT[:, oc, a:b], in0=zt[oc][:, :w], in1=ZT[:, :w], op=ALU.mult,
                   r=["sHC0", "sHS0", "sZA0"], w=["YT"])


def gla_mixer(g, bi, l):
    nc, p, OP, MM, DMA = g.nc, g.p, g.OP, g.MM, g.DMA
    PS, HT, YT = g.PS, g.HT, g.YT
    wkey = g.wkeys("w_in_bf", l, 8)
    with contextlib.ExitStack() as sm:
        def sb(name, shape, dt):
            return sm.enter_context(nc.sbuf_tensor(UN() + name, list(shape), dt))
        QT = sb("gQT", [128, NT], BF16)
        KT = sb("gKT", [128, NT], BF16)
        SR = sb("gSR", [128, NT], BF16)
        KTK = sb("gKTK", [128, 18, 128], BF16)
        VTK = sb("gVTK", [128, 18, 128], BF16)
        OF = sb("gOF", [128, NT], F32)
        GA = [sb("gGA%d" % d, [32, NT], BF16) for d in range(2)]
        WG = sb("gWG", [32, 2, 256], BF16)
        TRI = sb("gTRI", [128, 2, 128], F32)
        BD = sb("gBD", [128, 128], BF16)
        GN = sb("gGN", [128, LD[0]], F32)
        ONE = sb("gONE", [128, 1], F32)
        EPS_ = sb("gEPS", [128, 1], F32)
        WB = [sb("gWB%d" % i, [128, 8, 128], BF16) for i in range(2)]
        WGB = sb("gWGB", [128, 8, 32], BF16)
        NEG = sb("gNEG", [128, 128], F32)
        EX = sb("gEX", [128, 128], F32)
        E1s = [sb("gE1%d" % i, [128, 128], F32) for i in range(2)]
        OT = sb("gOT", [128, 128], F32)
        E2 = sb("gE2", [128, 128], F32)
        E2T = sb("gE2T", [128, 128], F32)
        QDs = [sb("gQD%d" % i, [128, 128], BF16) for i in range(2)]
        KD = sb("gKD", [128, 128], BF16)
        KDTs = [sb("gKDT%d" % i, [128, 128], BF16) for i in range(2)]
        AMs = [[sb("gAM%d_%d" % (i, hh), [128, 128], BF16) for hh in range(2)] for i in range(2)]
        S = sb("gS", [128, 64], F32)
        SB_ = sb("gSB", [128, 64], BF16)
        SQ = sb("gSQ", [128, 512], BF16)
        RS = sb("gRS", [128, 512], F32)
        OP("dve", "memset", ap=ONE[:], constant=1.0, w=["gONE"])
        OP("dve", "memset", ap=EPS_[:], constant=EPS, w=["gEPS"])
        DMA("sp", TRI[:], g.dram["tri"], w=["gTRI"])
        DMA("sp", BD[:], g.dram["bd64"], w=["gBD"])
        DMA("sp", GN[:], g.dram["gnT"], w=["gGN"])
        for d in range(2):
            DMA("pool", WG[0:16, d, :], g.dram["gla_w_gate2"][l, d], w=["gWG"])
            DMA("pool", WG[16:17, d, :], g.dram["gla_b_gate"][l, d:d + 1, :], w=["gWG"])
        wb_it = [0]

        def load_w(c0):
            i = wb_it[0] % 2
            wb_it[0] += 1
            DMA("sp", WB[i][:], g.w_in_bf[l, :, c0:c0 + 128].rearrange("(k p) c -> p k c", p=128), r=wkey, w=["gWB%d" % i])
            return WB[i], "gWB%d" % i

        DMA("sp", WGB[:], g.w_in_bf[l, :, C_GF:C_GF + 32].rearrange("(k p) c -> p k c", p=128), r=wkey, w=["gWGB"])
        for d in range(2):
            OP("pool", "memset", ap=GA[d][:], constant=1.0, w=["gGA%d" % d])
            for (a, b) in TB:
                w = b - a
                for k in range(8):
                    MM(PS[0][0:16, :w], WGB[:, k, 16 * d:16 * d + 16], HT[:, k, a:b], k == 0, k == 7, ["gWGB", "HT"], ["ps0"])
                OP("act", "activation", out=GA[d][0:16, a:b], in_=PS[0][0:16, :w], func=AF.Copy, r=["ps0"], w=["gGA%d" % d])
        for c in range(2):
            for (dst, dk, c0, fn, sc) in ((QT, "gQT", C_GQ + 128 * c, AF.Copy, 0.125), (KT, "gKT", C_GK + 128 * c, AF.Copy, 1.0),
                                          (SR, "gSR", C_GR + 128 * c, AF.Silu, 1.0)):
                w1, w1k = load_w(c0)
                for (a, b) in TB:
                    w = b - a
                    for k in range(8):
                        MM(PS[0][:, :w], w1[:, k, :], HT[:, k, a:b], k == 0, k == 7, [w1k, "HT"], ["ps0"])
                    OP("act", "activation", out=dst[:, a:b], in_=PS[0][:, :w], func=fn, scale=sc, r=["ps0"], w=[dk])
            for (dst, dk, c0) in ((KTK, "gKTK", C_GK + 128 * c), (VTK, "gVTK", C_GV + 128 * c)):
                wv, wvk = load_w(c0)
                for t4 in range(0, 18, 4):
                    nt = min(4, 18 - t4)
                    for ti in range(nt):
                        tt = t4 + ti
                        for k in range(8):
                            MM(PS[1][:, ti * 128:(ti + 1) * 128], HT[:, k, tt * 128:(tt + 1) * 128], wv[:, k, :], k == 0, k == 7,
                               [wvk, "HT"], ["ps1"])
                    OP("act", "activation", out=dst[:, t4:t4 + nt, :],
                       in_=PS[1][:, 0:nt * 128].rearrange("p (t c) -> p t c", c=128), func=AF.Copy, r=["ps1"], w=[dk])
            for d in range(2):
                OP("dve", "memset", ap=S[:], constant=0.0, w=["gS"])
                OP("dve", "memset", ap=SB_[:], constant=0.0, w=["gSB"])
                tiles = list(range(18)) if d == 0 else [1, 0] + list(range(17, 1, -1))
                chunks = (0, 1) if d == 0 else (1, 0)
                def prep(i):
                    tt = tiles[i]
                    t0 = tt * 128
                    pr = i % 2
                    e1, qd, kdt = E1s[pr], QDs[pr], KDTs[pr]
                    e1k, qdk, kdtk = "gE1%d" % pr, "gQD%d" % pr, "gKDT%d" % pr
                    MM(PS[2][:, 0:128], GA[d][0:17, t0:t0 + 128], WG[0:17, d, 128 * c:128 * c + 128], True, True,
                       ["gGA%d" % d, "gWG"], ["ps2"])
                    OP("act", "activation", out=EX[:], in_=PS[2][:, 0:128], func=AF.Exp, scale=-1.0, r=["ps2"], w=["gEX"])
                    OP("act", "activation", out=NEG[:], in_=EX[:], func=AF.Ln, bias=ONE[:, 0:1], scale=1.0, r=["gEX", "gONE"],
                       w=["gNEG"])
                    MM(PS[3][:, 0:128], NEG[:], TRI[:, d, :], True, True, ["gNEG", "gTRI"], ["ps3"])
                    MM(PS[3][:, 128:256], TRI[:, d, :], NEG[:], True, True, ["gNEG", "gTRI"], ["ps3"])
                    OP("act", "activation", out=e1[:], in_=PS[3][:, 0:128], func=AF.Exp, scale=-1.0 / 16, r=["ps3"], w=[e1k])
                    OP("act", "activation", out=E2[:], in_=PS[3][:, 0:128], func=AF.Exp, scale=1.0 / 16, r=["ps3"], w=["gE2"])
                    OP("act", "activation", out=E2T[:], in_=PS[3][:, 128:256], func=AF.Exp, scale=1.0 / 16, r=["ps3"], w=["gE2T"])
                    OP("dve", "tensor_tensor", out=qd[:], in0=QT[:, t0:t0 + 128], in1=e1[:], op=ALU.mult, r=["gQT", e1k], w=[qdk])
                    OP("pool", "tensor_tensor", out=KD[:], in0=KT[:, t0:t0 + 128], in1=E2[:], op=ALU.mult, r=["gKT", "gE2"], w=["gKD"])
                    OP("pool", "tensor_tensor", out=kdt[:], in0=KTK[:, tt, :], in1=E2T[:], op=ALU.mult, r=["gKTK", "gE2T"],
                       w=[kdtk])
                    for hh in range(2):
                        hb = 64 * hh
                        am, amk = AMs[pr][hh], "gAM%d_%d" % (pr, hh)
                        MM(PS[4 + hh][:, 0:128], KD[hb:hb + 64, :], qd[hb:hb + 64, :], True, True, ["gKD", qdk], ["ps%d" % (4 + hh)])
                        OP("dve", "tensor_tensor", out=am[:], in0=PS[4 + hh][:, 0:128], in1=TRI[:, d, :], op=ALU.mult,
                           r=["ps%d" % (4 + hh), "gTRI"], w=[amk])

                def recur(i):
                    tt = tiles[i]
                    t0 = tt * 128
                    pr = i % 2
                    e1, qd, kdt = E1s[pr], QDs[pr], KDTs[pr]
                    e1k, qdk, kdtk = "gE1%d" % pr, "gQD%d" % pr, "gKDT%d" % pr
                    for hh in range(2):
                        hb = 64 * hh
                        am, amk = AMs[pr][hh], "gAM%d_%d" % (pr, hh)
                        MM(PS[6][hb:hb + 64, 0:128], VTK[:, tt, hb:hb + 64], am[:], True, False, ["gVTK", amk], ["ps6"])
                    for ch in chunks:
                        cs = 64 * ch
                        for hh in range(2):
                            hb = 64 * hh
                            MM(PS[6][hb:hb + 64, cs:cs + 64], SB_[hb:hb + 64, :], qd[hb:hb + 64, cs:cs + 64], False, True,
                               ["gSB", qdk], ["ps6"])
                            MM(PS[7][hb:hb + 64, 0:64], kdt[cs:cs + 64, hb:hb + 64], VTK[cs:cs + 64, tt, hb:hb + 64], True, True,
                               [kdtk, "gVTK"], ["ps7"])
                        dcol = cs + 63 if d == 0 else cs
                        OP("dve", "tensor_tensor", out=S[:], in0=S[:], in1=PS[7][:, 0:64], op=ALU.add, r=["gS", "ps7"], w=["gS"])
                        OP("dve", "tensor_scalar", out=S[:], in0=S[:], scalar1=e1[:, dcol:dcol + 1], scalar2=None, op0=ALU.mult,
                           r=["gS", e1k], w=["gS"])
                        OP("act", "activation", out=SB_[:], in_=S[:], func=AF.Copy, r=["gS"], w=["gSB"])
                    if d == 0:
                        OP("act", "activation", out=OF[:, t0:t0 + 128], in_=PS[6][:, 0:128], func=AF.Copy, r=["ps6"], w=["gOF"])
                    else:
                        OP("pool", "tensor_copy", out=OT[:], in_=OF[:, t0:t0 + 128], r=["gOF"], w=["gOT"])
                        OP("dve", "tensor_tensor", out=OF[:, t0:t0 + 128], in0=OT[:], in1=PS[6][:, 0:128], op=ALU.add,
                           r=["gOT", "ps6"], w=["gOF"])

                prep(0)
                for i in range(len(tiles)):
                    if i + 1 < len(tiles):
                        prep(i + 1)
                    recur(i)
            for (a, b) in TB:
                w = b - a
                OP("act", "activation", out=SQ[:, :w], in_=OF[:, a:b], func=AF.Square, r=["gOF"], w=["gSQ"])
                MM(PS[0][:, :w], BD[:], SQ[:, :w], True, True, ["gBD", "gSQ"], ["ps0"])
                OP("act", "activation", out=RS[:, :w], in_=PS[0][:, :w], func=AF.Sqrt, scale=1.0 / 64, bias=EPS_[:, 0:1],
                   r=["ps0", "gEPS"], w=["gRS"])
                OP("dve", "reciprocal", out=RS[:, :w], in_=RS[:, :w], r=["gRS"], w=["gRS"])
                OP("dve", "scalar_tensor_tensor", out=RS[:, :w], in0=OF[:, a:b], scalar=GN[:, l:l + 1], in1=RS[:, :w], op0=ALU.mult,
                   op1=ALU.mult, r=["gOF", "gGN", "gRS"], w=["gRS"])
                OP("pool", "tensor_tensor", out=YT[:, 4 + c, a:b], in0=RS[:, :w], in1=SR[:, a:b], op=ALU.mult, r=["gRS", "gSR"],
                   w=["YT"])


def host_prep(inputs):
    f = np.float32
    w_in = np.asarray(inputs["w_in"], f)
    sq = w_in[:, :, C_SQ:C_SQ + 256]
    sk = w_in[:, :, C_SK:C_SK + 128]
    dup = np.concatenate([np.arange(64), np.arange(64), 64 + np.arange(64), 64 + np.arange(64)])
    w_in_ext = np.concatenate([w_in, sq[:, :, _rope_perm(4)], sk[:, :, dup], sk[:, :, _rope_perm(2)][:, :, dup]], axis=2)
    assert w_in_ext.shape[2] == NEXT
    rc, rs = _rope_tables()
    kl = np.arange(128)[:, None]
    ql = np.arange(128)[None, :]
    maskAB = np.stack([(kl <= ql), (ql <= kl)], 1).astype(f)
    gv = np.stack([inputs["g_pre_mix"], inputs["g_post_mix"], inputs["g_pre_ffn"], inputs["g_post_ffn"]], 1)
    gvec = np.ascontiguousarray(np.asarray(gv, f).reshape(DEPTH, 4, 8, 128).transpose(3, 0, 1, 2))
    b_modT = np.ascontiguousarray(np.asarray(inputs["b_mod"], f).reshape(DEPTH, 48, 128).transpose(2, 0, 1))
    convT = np.ascontiguousarray(np.asarray(inputs["ffn_conv"], f).reshape(DEPTH, 3, 22, 128).transpose(3, 0, 1, 2))
    sink = np.asarray(inputs["swa_sink"], f)
    sinkT = np.zeros((128, DEPTH, 2), f)
    for c in range(2):
        sinkT[0:64, :, c] = sink[None, :, 2 * c]
        sinkT[64:128, :, c] = sink[None, :, 2 * c + 1]
    rpb = np.asarray(inputs["na_rpb"], f)
    kc = np.arange(64)[:, None]
    qc = np.arange(64)[None, :]
    dcx = np.clip(kc - qc, -15, 15) + 15
    na_exp = rpb[:, :, ::-1, :][:, :, :, dcx]
    na_exp = np.ascontiguousarray(na_exp.transpose(0, 1, 3, 2, 4)).reshape(DEPTH, 4, 64, 15 * 64)
    ws = np.clip(np.arange(64) - 8, 0, 48)
    mc = ((kc >= ws[None, :]) & (kc < ws[None, :] + 16)).astype(f)
    mcol = np.tile(np.tile(mc[:, None, :], (1, 15, 1)).reshape(64, 960), (2, 1))
    jj = np.arange(128)[:, None]
    ii = np.arange(128)[None, :]
    same = (jj // 64) == (ii // 64)
    tri = np.stack([(same & (jj <= ii)), (same & (jj >= ii))], 1).astype(f)
    bd64 = same.astype(f)
    gnT = np.ascontiguousarray(np.tile(np.asarray(inputs["gla_g_norm"], f).T, (2, 1)))
    L = DEPTH
    lre = np.asarray(inputs["s5_lam_re"], f).reshape(L, 32, 64)
    lim = np.asarray(inputs["s5_lam_im"], f).reshape(L, 32, 64)
    lam = np.stack([lre, lim], 1)
    lamT = np.ascontiguousarray(np.tile(lam.transpose(3, 0, 1, 2), (2, 1, 1, 1)))
    stepT = np.ascontiguousarray(np.broadcast_to(np.asarray(inputs["s5_log_step"], f).reshape(L, 32)[None], (128, L, 32)))
    dskT = np.ascontiguousarray(np.asarray(inputs["s5_d"], f).reshape(L, 2, 128).transpose(2, 0, 1))
    sgn = np.ones((128, 2), f)
    sgn[0:64, 0] = -1.0
    sgn[64:128, 1] = -1.0
    jmat = np.zeros((128, 128), f)
    for sp in range(64):
        jmat[sp + 64, sp] = -1.0
        jmat[sp, sp + 64] = 1.0
    bre = np.asarray(inputs["s5_b_re"], f)
    bim = np.asarray(inputs["s5_b_im"], f)
    cre = np.asarray(inputs["s5_c_re"], f)
    cim = np.asarray(inputs["s5_c_im"], f)
    B1 = np.zeros((L, 2, 16, 128, 128), f)
    B2 = np.zeros((L, 2, 16, 128, 128), f)
    C1 = np.zeros((L, 2, 16, 128, 128), f)
    C2 = np.zeros((L, 2, 16, 128, 128), f)
    for gi in range(16):
        r0 = 16 * (gi % 8)
        B1[:, :, gi, r0:r0 + 16, 0:64] = bre[:, :, gi].transpose(0, 1, 3, 2)
        B1[:, :, gi, r0:r0 + 16, 64:128] = bim[:, :, gi].transpose(0, 1, 3, 2)
        B2[:, :, gi, r0:r0 + 16, 0:64] = bim[:, :, gi].transpose(0, 1, 3, 2)
        B2[:, :, gi, r0:r0 + 16, 64:128] = bre[:, :, gi].transpose(0, 1, 3, 2)
        C1[:, :, gi, 0:64, r0:r0 + 16] = cre[:, :, gi].transpose(0, 1, 3, 2)
        C1[:, :, gi, 64:128, r0:r0 + 16] = cim[:, :, gi].transpose(0, 1, 3, 2)
        C2[:, :, gi, 0:64, r0:r0 + 16] = cim[:, :, gi].transpose(0, 1, 3, 2)
        C2[:, :, gi, 64:128, r0:r0 + 16] = cre[:, :, gi].transpose(0, 1, 3, 2)
    shared = {
        "lamT": lamT, "stepT": stepT, "dskT": dskT, "sgn": sgn, "jmat": jmat, "s5_w_glu": np.asarray(inputs["s5_w_glu"], f),
        "s5B1": B1, "s5B2": B2, "s5C1": C1, "s5C2": C2,
        "tri": tri, "bd64": bd64.astype(ml_dtypes.bfloat16), "gnT": gnT,
        "gla_w_gate2": np.asarray(inputs["gla_w_gate2"], f), "gla_b_gate": np.asarray(inputs["gla_b_gate"], f),
        "w_mod": np.asarray(inputs["w_mod"], f), "b_modT": b_modT, "gvec": gvec, "w_in_ext": np.ascontiguousarray(w_in_ext),
        "w_out": np.asarray(inputs["w_out"], f), "ffn_w_up": np.asarray(inputs["ffn_w_up"], f),
        "ffn_w_down": np.asarray(inputs["ffn_w_down"], f), "convT": convT,
        "ropeC": rc.astype(ml_dtypes.bfloat16), "ropeS": rs.astype(ml_dtypes.bfloat16),
        "maskAB": maskAB.astype(ml_dtypes.bfloat16), "identf": np.eye(128, dtype=f), "sinkT": sinkT,
        "na_exp": na_exp, "mcol": np.ascontiguousarray(mcol),
    }
    x = np.asarray(inputs["x"], f)
    ctx = np.asarray(inputs["ctx"], f)
    c = np.asarray(inputs["c"], f)
    cc = np.asarray(inputs["c_ctx"], f)
    in_maps = []
    for core in range(8):
        b0 = 2 * core
        xcat = np.concatenate([ctx[b0:b0 + 2], x[b0:b0 + 2]], axis=1)
        cs = np.stack([c[b0], c[b0 + 1], cc], 0)
        cTm = np.ascontiguousarray(cs.reshape(3, 8, 128).transpose(2, 1, 0))
        m = dict(shared)
        m["xcat"] = np.ascontiguousarray(xcat)
        m["cT"] = cTm
        in_maps.append(m)
    return in_maps


L_FIRST = ("w_mod", "w_in_ext", "w_out", "ffn_w_up", "ffn_w_down", "na_exp", "s5B1", "s5B2", "s5C1", "s5C2", "s5_w_glu",
           "gla_w_gate2", "gla_b_gate")
L_SECOND = ("b_modT", "gvec", "convT", "sinkT", "lamT", "stepT", "dskT")

FUSED = True


def kernel(**inputs):
    in_maps = host_prep(inputs)
    if FUSED:
        nc = bass.Bass("TRN2", target_bir_lowering=False)
        build(nc)
        res = run_bass_kernel_spmd(nc, in_maps, core_ids=list(range(8)))
        outs = [r["out"] for r in res.results]
        return np.concatenate(outs, axis=0).astype(np.float32)
    cur = [m["xcat"] for m in in_maps]
    for l in range(DEPTH):
        base = {}
        m0 = in_maps[0]
        for k, v in m0.items():
            if k in ("xcat", "cT"):
                continue
            if k in L_FIRST:
                base[k] = np.ascontiguousarray(v[l:l + 1])
            elif k in L_SECOND:
                base[k] = np.ascontiguousarray(v[:, l:l + 1])
            elif k == "gnT":
                base[k] = np.ascontiguousarray(v[:, l:l + 1])
            else:
                base[k] = v
        for bi in range(2):
            maps = []
            for core in range(8):
                m = dict(base)
                m["xcat"] = np.ascontiguousarray(cur[core][[bi, 1 - bi]])
                cTm = in_maps[core]["cT"]
                m["cT"] = np.ascontiguousarray(cTm[:, :, [bi, 1 - bi, 2]])
                maps.append(m)
            nc = bass.Bass("TRN2", target_bir_lowering=False)
            build(nc, nlayers=1, nbatch=1, ldim=1, full_out=True)
            res = run_bass_kernel_spmd(nc, maps, core_ids=list(range(8)))
            for core in range(8):
                new = np.array(cur[core])
                new[bi] = res.results[core]["out"][0]
                cur[core] = new
    outs = [c[:, NCX:, :] for c in cur]
    return np.concatenate(outs, axis=0).astype(np.float32)
```

```python
import contextlib
import math
import numpy as np
import ml_dtypes
import concourse.bass as bass
import concourse.mybir as mybir
from concourse.bass_utils import run_bass_kernel_spmd

F32 = mybir.dt.float32
BF16 = mybir.dt.bfloat16
ALU = mybir.AluOpType
AF = mybir.ActivationFunctionType

ENG = ("pe", "act", "dve", "pool", "sp")
NDMA = 24


class P:
    def __init__(self, nc, same_eng_sync=True):
        self.nc = nc
        self.ops = {e: [] for e in ENG}
        self.cnt = {e: 0 for e in ENG}
        self.waited = {e: {} for e in ENG}
        self.last_w = {}
        self.readers = {}
        self.dma_nextq = {}
        self.dma_cnt = [0] * NDMA
        self.dma_last_tok = [None] * NDMA
        self.same = same_eng_sync
        self.out_toks = []
        self.bar = []

    def barrier(self):
        self.bar = [("e", e, self.cnt[e]) for e in ENG if self.cnt[e]] + \
                   [("d", k, self.dma_cnt[k]) for k in range(NDMA) if self.dma_cnt[k]]

    def _deps(self, eng, reads, writes):
        deps = list(self.bar)
        for k in reads:
            t = self.last_w.get(k)
            if t is not None:
                deps.append(t)
        for k in writes:
            t = self.last_w.get(k)
            if t is not None:
                deps.append(t)
            deps.extend(self.readers.get(k, ()))
        return deps

    def _waits(self, eng, deps):
        w = self.waited[eng]
        best = {}
        for t in deps:
            if t[0] == "e":
                _, e2, idx = t
                if e2 == eng and (not self.same or eng == "pe"):
                    continue
                key = e2
            else:
                key = ("d", t[1])
            if w.get(key, 0) >= t[2]:
                continue
            if key not in best or best[key][2] < t[2]:
                best[key] = t
        for key, t in best.items():
            w[key] = t[2]
        return list(best.values())

    def _record(self, tok, reads, writes):
        for k in reads:
            lst = self.readers.setdefault(k, [])
            lst.append(tok)
            if len(lst) > 64:
                best = {}
                for t in lst:
                    kk = t[:2]
                    if kk not in best or best[kk][2] < t[2]:
                        best[kk] = t
                self.readers[k] = list(best.values())
        for k in writes:
            self.last_w[k] = tok
            self.readers[k] = []

    def op(self, eng, fn, reads=(), writes=()):
        deps = self._deps(eng, reads, writes)
        waits = self._waits(eng, deps)
        self.cnt[eng] += 1
        tok = ("e", eng, self.cnt[eng])
        self.ops[eng].append((waits, fn, ("e", eng)))
        self._record(tok, reads, writes)
        return tok

    def dma(self, q, fn, reads=(), writes=(), is_out=False):
        lo, n = (0, 16) if q == "sp" else (16, NDMA - 16)
        cur = self.dma_nextq.get(q, 0)
        k = lo + cur
        self.dma_nextq[q] = (cur + 1) % n
        deps = self._deps(q, reads, writes)
        if self.dma_last_tok[k] is not None:
            deps.append(self.dma_last_tok[k])
        waits = self._waits(q, deps)
        self.dma_cnt[k] += 16
        tok = ("d", k, self.dma_cnt[k])
        self.dma_last_tok[k] = tok
        self.ops[q].append((waits, fn, ("d", k)))
        self._record(tok, reads, writes)
        if is_out:
            self.out_toks.append(tok)
        return tok

    def emit(self):
        nc = self.nc
        with contextlib.ExitStack() as es:
            esem = {e: es.enter_context(nc.semaphore("s_" + e)) for e in ENG}
            dsem = [es.enter_context(nc.semaphore("d%d" % i)) for i in range(NDMA)]
            fin = list(self.out_toks)
            for e in ENG:
                if self.cnt[e]:
                    fin.append(("e", e, self.cnt[e]))
            for k in range(NDMA):
                if self.dma_cnt[k]:
                    fin.append(("d", k, self.dma_cnt[k]))
            block = es.enter_context(nc.Block())

            def run(eng_name, eng):
                for waits, fn, kind in self.ops[eng_name]:
                    for t in waits:
                        if t[0] == "e":
                            eng.wait_ge(esem[t[1]], t[2])
                        else:
                            eng.wait_ge(dsem[t[1]], t[2])
                    ins = fn(eng)
                    if kind[0] == "e":
                        ins.then_inc(esem[kind[1]], 1)
                    else:
                        ins.then_inc(dsem[kind[1]], 16)

            @block.tensor
            def _(e):
                run("pe", e)

            @block.scalar
            def _(e):
                run("act", e)

            @block.vector
            def _(e):
                run("dve", e)

            @block.gpsimd
            def _(e):
                run("pool", e)

            @block.sync
            def _(e):
                run("sp", e)
                for t in fin:
                    if t[0] == "e":
                        if t[1] != "sp":
                            e.wait_ge(esem[t[1]], t[2])
                    else:
                        e.wait_ge(dsem[t[1]], t[2])


NT = 2304
NCX = 256
NLAT = 2048
D = 1024
DEPTH = 4
LD = [4]
DFF = 2816
EPS = 1e-6
TB = [(0, 256)] + [(256 + 512 * i, 256 + 512 * (i + 1)) for i in range(4)]
C_A, C_NAQ, C_NAK, C_NAV = 0, 256, 512, 768
C_GQ, C_GK, C_GV, C_GF, C_GB, C_GR = 1024, 1280, 1536, 1792, 1808, 1824
C_SQ, C_SK, C_SV = 2080, 2336, 2464
C_SQP, C_SKD, C_SKDP, NEXT = 2592, 2848, 3104, 3360


def _rope_perm(nh):
    idx = np.arange(nh * 64).reshape(nh, 4, 16)
    return idx[:, [1, 0, 3, 2], :].reshape(-1)


def _rope_tables():
    cos = np.ones((64, NT), np.float32)
    sin = np.zeros((64, NT), np.float32)
    t = np.arange(NLAT)
    pos = (t // 64, t % 64)
    inv = 10000.0 ** (-np.arange(0, 32, 2, dtype=np.float32) / 32)
    for half in range(2):
        ang = pos[half].astype(np.float32)[None, :] * inv[:, None]
        c, s = np.cos(ang), np.sin(ang)
        b = 32 * half
        cos[b:b + 16, NCX:] = c
        cos[b + 16:b + 32, NCX:] = c
        sin[b:b + 16, NCX:] = -s
        sin[b + 16:b + 32, NCX:] = s
    return np.concatenate([cos, cos], 0), np.concatenate([sin, sin], 0)


class Ctx:
    pass


_UN = [0]


def UN():
    _UN[0] += 1
    return "t%d_" % _UN[0]


def build(nc, dbg=None, nlayers=DEPTH, nbatch=2, mixers=("s5", "na", "gla", "swa"), ldim=DEPTH, full_out=False):
    LD[0] = ldim
    _UN[0] = 0
    p = P(nc)
    g = Ctx()
    g.p, g.nc = p, nc
    dram = {}

    def din(name, shape, dt=F32):
        dram[name] = nc.dram_tensor(name, list(shape), dt, kind="ExternalInput").ap()
        return dram[name]

    xcat = din("xcat", [2, NT, D])
    cT = din("cT", [128, 8, 3])
    w_mod = din("w_mod", [LD[0], D, 6 * D])
    b_modT = din("b_modT", [128, LD[0], 48])
    gvec = din("gvec", [128, LD[0], 4, 8])
    w_in = din("w_in_ext", [LD[0], D, NEXT])
    w_out = din("w_out", [LD[0], D, D])
    w_up = din("ffn_w_up", [LD[0], D, 2 * DFF])
    w_down = din("ffn_w_down", [LD[0], DFF, D])
    convT = din("convT", [128, LD[0], 3, 22])
    ropeC = din("ropeC", [128, NT], BF16)
    ropeS = din("ropeS", [128, NT], BF16)
    maskAB = din("maskAB", [128, 2, 128], BF16)
    identf = din("identf", [128, 128])
    sinkT = din("sinkT", [128, LD[0], 2])
    na_exp = din("na_exp", [LD[0], 4, 64, 15 * 64])
    mcol = din("mcol", [128, 15 * 64])
    din("mc8", [128, 15 * 64])
    din("mcn", [128, 15 * 64])
    din("idb", [128, 64], BF16)
    din("tri", [128, 2, 128])
    din("bd64", [128, 128], BF16)
    din("gnT", [128, LD[0]])
    din("gla_w_gate2", [LD[0], 2, 16, 256])
    din("gla_b_gate", [LD[0], 2, 256])
    din("lamT", [128, LD[0], 2, 32])
    din("stepT", [128, LD[0], 32])
    din("dskT", [128, LD[0], 2])
    din("sgn", [128, 2])
    din("jmat", [128, 128])
    din("s5_w_glu", [LD[0], 256, 256])
    for nm in ("s5B1", "s5B2", "s5C1", "s5C2"):
        din(nm, [LD[0], 2, 16, 128, 128])
        dram[nm + "_bf"] = nc.dram_tensor(nm + "_bf", [LD[0], 2, 16, 128, 128], BF16).ap()
    g.dram = dram
    out = nc.dram_tensor("out", [2, NT if full_out else NLAT, D], F32, kind="ExternalOutput").ap()
    dbg_aps = {}
    if dbg:
        for name, shape in dbg.items():
            dbg_aps[name] = nc.dram_tensor("dbg_" + name, list(shape), F32, kind="ExternalOutput").ap()

    w_in_bf = nc.dram_tensor("w_in_bf", [LD[0], D, NEXT], BF16).ap()
    w_out_bf = nc.dram_tensor("w_out_bf", [LD[0], D, D], BF16).ap()
    w_up_bf = nc.dram_tensor("w_up_bf", [LD[0], D, 2 * DFF], BF16).ap()
    w_down_bf = nc.dram_tensor("w_down_bf", [LD[0], DFF, D], BF16).ap()

    def wkeys(key, l, n):
        return ["%s%d_%d" % (key, l, i) for i in range(n)]
    g.wkeys = wkeys
    g._uid = [0]

    def OP(eng, method, r=(), w=(), **kw):
        return p.op(eng, lambda e, kw=kw, method=method: getattr(e, method)(**kw), reads=r, writes=w)

    def MM(out_, lhsT, rhs, start, stop, r, w):
        return p.op("pe", lambda e: e.matmul(out_, lhsT=lhsT, rhs=rhs, start=start, stop=stop), reads=r, writes=w)

    def DMA(q, out_, in_, r=(), w=(), is_out=False, **kw):
        return p.dma(q, lambda e, kw=kw: e.dma_start(out=out_, in_=in_, **kw), reads=r, writes=w, is_out=is_out)

    g.OP, g.MM, g.DMA = OP, MM, DMA

    for l in range(nlayers):
        for (src, dst, rows, key) in ((w_in, w_in_bf, D, "w_in_bf"), (w_out, w_out_bf, D, "w_out_bf"),
                                      (w_up, w_up_bf, D, "w_up_bf"), (w_down, w_down_bf, DFF, "w_down_bf")):
            for r0 in range(0, rows, 128):
                DMA("pool", dst[l, r0:r0 + 128, :], src[l, r0:r0 + 128, :], w=["%s%d_%d" % (key, l, r0 // 128)],
                    max_dma_last_dim=4096)

    for l in range(nlayers):
        for nm in ("s5B1", "s5B2", "s5C1", "s5C2"):
            for d in range(2):
                DMA("pool", dram[nm + "_bf"][l, d].rearrange("g p s -> (g p) s"), dram[nm][l, d].rearrange("g p s -> (g p) s"),
                    w=["%s_bf%d" % (nm, l)] if d == 1 else ["%s_bf%d_d0" % (nm, l)], max_dma_last_dim=4096)

    es = contextlib.ExitStack()

    def sb(name, shape, dt):
        return es.enter_context(nc.sbuf_tensor(UN() + name, list(shape), dt))

    MODT = sb("MODT", [128, LD[0], 48, 3], F32)
    DER = sb("DER", [128, LD[0], 3, 6, 8], F32)
    GV = sb("GV", [128, LD[0], 4, 8], F32)
    CONV = sb("CONV", [128, LD[0], 3, 22], F32)
    ONESB = sb("ONESB", [128, 128], BF16)
    IDF = sb("IDF", [128, 128], F32)
    MAB = sb("MAB", [128, 2, 128], BF16)
    ESINK = sb("ESINK", [128, LD[0], 2], F32)
    PS = [es.enter_context(nc.psum_tensor("ps%d" % i, [128, 512], F32)) for i in range(8)]
    g.PS = PS

    OP("dve", "memset", ap=ONESB[:], constant=1.0, w=["ONESB"])
    DMA("sp", IDF[:], identf, w=["IDF"])
    DMA("sp", MAB[:], maskAB, w=["MAB"])
    DMA("sp", GV[:], gvec, w=["GV"])
    DMA("sp", CONV[:], convT, w=["CONV"])
    DMA("sp", ESINK[:], sinkT, w=["ESINK"])
    OP("act", "activation", out=ESINK[:], in_=ESINK[:], func=AF.Exp, r=["ESINK"], w=["ESINK"])

    with contextlib.ExitStack() as s1:
        SCT = s1.enter_context(nc.sbuf_tensor(UN() + "SCT", [128, 8, 3], F32))
        BM = s1.enter_context(nc.sbuf_tensor(UN() + "BM", [128, LD[0], 48], F32))
        WM = [s1.enter_context(nc.sbuf_tensor(UN() + "WM%d" % i, [128, 8, 512], F32)) for i in range(2)]
        DMA("sp", SCT[:], cT, w=["SCT"])
        DMA("sp", BM[:], b_modT, w=["BM"])
        OP("act", "activation", out=SCT[:], in_=SCT[:], func=AF.Silu, r=["SCT"], w=["SCT"])
        it = 0
        for l in range(nlayers):
            for cg in range(12):
                wb = WM[it % 2]
                wk = "WM%d" % (it % 2)
                it += 1
                DMA("sp", wb[:], w_mod[l, :, cg * 512:(cg + 1) * 512].rearrange("(k p) c -> p k c", p=128), w=[wk])
                for fc in range(4):
                    f = cg * 4 + fc
                    for k in range(8):
                        MM(PS[0][:, f * 3:f * 3 + 3], wb[:, k, fc * 128:(fc + 1) * 128], SCT[:, k, :], k == 0, k == 7,
                           [wk, "SCT"], ["ps0"])
            for j in range(3):
                OP("dve", "tensor_tensor", out=MODT[:, l, :, j], in0=PS[0][:, 0:144].rearrange("p (f j) -> p f j", j=3)[:, :, j],
                   in1=BM[:, l, :], op=ALU.add, r=["ps0", "BM"], w=["MODT"])
            for j in range(3):
                OP("dve", "scalar_tensor_tensor", out=DER[:, l, j, 0, :], in0=MODT[:, l, 8:16, j], scalar=1.0, in1=GV[:, l, 0, :],
                   op0=ALU.add, op1=ALU.mult, r=["MODT", "GV"], w=["DER"])
                OP("dve", "tensor_copy", out=DER[:, l, j, 1, :], in_=MODT[:, l, 0:8, j], r=["MODT"], w=["DER"])
                OP("dve", "tensor_tensor", out=DER[:, l, j, 2, :], in0=MODT[:, l, 16:24, j], in1=GV[:, l, 1, :], op=ALU.mult,
                   r=["MODT", "GV"], w=["DER"])
                OP("dve", "scalar_tensor_tensor", out=DER[:, l, j, 3, :], in0=MODT[:, l, 32:40, j], scalar=1.0, in1=GV[:, l, 2, :],
                   op0=ALU.add, op1=ALU.mult, r=["MODT", "GV"], w=["DER"])
                OP("dve", "tensor_copy", out=DER[:, l, j, 4, :], in_=MODT[:, l, 24:32, j], r=["MODT"], w=["DER"])
                OP("dve", "tensor_tensor", out=DER[:, l, j, 5, :], in0=MODT[:, l, 40:48, j], in1=GV[:, l, 3, :], op=ALU.mult,
                   r=["MODT", "GV"], w=["DER"])
    p.barrier()

    X = sb("X", [128, 8, NT], F32)
    g.X, g.DER, g.ONESB, g.MAB, g.ESINK, g.CONV = X, DER, ONESB, MAB, ESINK, CONV
    g.w_in_bf, g.w_out_bf, g.w_up_bf, g.w_down_bf = w_in_bf, w_out_bf, w_up_bf, w_down_bf
    g.ropeC, g.ropeS, g.na_exp, g.mcol = ropeC, ropeS, na_exp, mcol
    g.dbg_aps = dbg_aps

    def dump(name, ap, keys):
        if name in dbg_aps:
            DMA("pool", dbg_aps[name], ap, r=keys, is_out=True, max_dma_last_dim=2048)
    g.dump = dump

    for bi in range(nbatch):
        with contextlib.ExitStack() as s2:
            XS = [s2.enter_context(nc.sbuf_tensor(UN() + "XS%d" % i, [128, D], F32)) for i in range(2)]
            for tt in range(18):
                xs, xk = XS[tt % 2], "XS%d" % (tt % 2)
                DMA("sp", xs[:], xcat[bi, tt * 128:(tt + 1) * 128, :], w=[xk])
                for hh in range(2):
                    bank = PS[hh]
                    for kk in range(4):
                        k = hh * 4 + kk
                        p.op("pe", lambda e, o=bank[:, kk * 128:(kk + 1) * 128], i=xs[:, k * 128:(k + 1) * 128]:
                             e.transpose(out=o, in_=i, identity=IDF[:]), reads=[xk, "IDF"], writes=["ps%d" % hh])
                    if hh == 0:
                        OP("act", "activation", out=X[:, 0:4, tt * 128:(tt + 1) * 128],
                           in_=bank[:, :].rearrange("p (k t) -> p k t", t=128), func=AF.Copy, r=["ps0"], w=["X"])
                    else:
                        OP("dve", "tensor_copy", out=X[:, 4:8, tt * 128:(tt + 1) * 128],
                           in_=bank[:, :].rearrange("p (k t) -> p k t", t=128), r=["ps1"], w=["X"])
        p.barrier()
        for l in range(nlayers):
            layer(g, bi, l, (l == DEPTH - 1) and not full_out, mixers)
        with contextlib.ExitStack() as s3:
            OS_ = [s3.enter_context(nc.sbuf_tensor(UN() + "OST%d" % i, [128, D], F32)) for i in range(2)]
            for tt in range(18 if full_out else 16):
                ot, ok = OS_[tt % 2], "OST%d" % (tt % 2)
                t0 = (0 if full_out else NCX) + tt * 128
                for hh in range(2):
                    bank = PS[hh]
                    for kk in range(4):
                        k = hh * 4 + kk
                        p.op("pe", lambda e, o=bank[:, kk * 128:(kk + 1) * 128], i=X[:, k, t0:t0 + 128]:
                             e.transpose(out=o, in_=i, identity=IDF[:]), reads=["X", "IDF"], writes=["ps%d" % hh])
                    if hh == 0:
                        OP("act", "activation", out=ot[:, 0:512], in_=bank[:, :], func=AF.Copy, r=["ps0"], w=[ok])
                    else:
                        OP("dve", "tensor_copy", out=ot[:, 512:1024], in_=bank[:, :], r=["ps1"], w=[ok])
                DMA("sp", out[bi, tt * 128:(tt + 1) * 128, :], ot[:], r=[ok], is_out=True)
        p.barrier()

    p.emit()
    es.close()
    return nc


def rms_stats(g, src_fn, nk, w, SQ, RS, psb, src_keys):
    OP, MM = g.OP, g.MM
    for k in range(nk):
        OP("act", "activation", out=SQ[:, k, :w], in_=src_fn(k), func=AF.Square, r=src_keys, w=["SQ"])
    for k in range(nk):
        MM(g.PS[psb][:, :w], g.ONESB[:], SQ[:, k, :w], k == 0, k == nk - 1, ["SQ", "ONESB"], ["ps%d" % psb])
    OP("act", "activation", out=RS[:, :w], in_=g.PS[psb][:, :w], func=AF.Sqrt, scale=1.0 / D, bias=g.EPSC[:, 0:1],
       r=["ps%d" % psb, "EPSC"], w=["RS"])
    OP("dve", "reciprocal", out=RS[:, :w], in_=RS[:, :w], r=["RS"], w=["RS"])


def layer(g, bi, l, last, mixers):
    nc, p, OP, MM, DMA = g.nc, g.p, g.OP, g.MM, g.DMA
    X, DER, PS = g.X, g.DER, g.PS
    with contextlib.ExitStack() as sl:
        def sb(name, shape, dt):
            return sl.enter_context(nc.sbuf_tensor(UN() + name, list(shape), dt))
        YT = sb("YT", [128, 8, NT], BF16)
        g.YT = YT
        EPSC = sb("EPSC", [128, 1], F32)
        g.EPSC = EPSC
        OP("dve", "memset", ap=EPSC[:], constant=EPS, w=["EPSC"])
        with contextlib.ExitStack() as sh:
            HT = sh.enter_context(nc.sbuf_tensor(UN() + "HT", [128, 8, NT], BF16))
            g.HT = HT
            with contextlib.ExitStack() as sa:
                SQ = sa.enter_context(nc.sbuf_tensor(UN() + "SQ", [128, 8, 512], BF16))
                RS = sa.enter_context(nc.sbuf_tensor(UN() + "RS", [128, 512], F32))
                TMP = [sa.enter_context(nc.sbuf_tensor(UN() + "TMPa%d" % i, [128, 512], F32)) for i in range(2)]
                for (a, b) in TB:
                    w = b - a
                    j = 2 if a < NCX else bi
                    rms_stats(g, lambda k: X[:, k, a:b], 8, w, SQ, RS, 0, ["X"])
                    for k in range(8):
                        tm, tk = TMP[k % 2], "TMPa%d" % (k % 2)
                        OP("dve", "tensor_tensor", out=tm[:, :w], in0=X[:, k, a:b], in1=RS[:, :w], op=ALU.mult,
                           r=["X", "RS"], w=[tk])
                        OP("act", "activation", out=HT[:, k, a:b], in_=tm[:, :w], func=AF.Identity,
                           scale=DER[:, l, j, 0, k:k + 1], bias=DER[:, l, j, 1, k:k + 1], r=[tk, "DER"], w=["HT"])
            p.barrier()
            g.dump("HT%d_%d" % (bi, l), HT[:], ["HT"])
            for nm, chs in (("s5", (0, 1)), ("na", (2, 3)), ("gla", (4, 5)), ("swa", (6, 7))):
                if nm not in mixers:
                    OP("pool", "memset", ap=YT[:, chs[0]:chs[1] + 1, :], constant=0.0, w=["YT"])
            if "s5" in mixers:
                s5_project(g, bi, l)
                p.barrier()
            if "swa" in mixers:
                attn_mixer(g, bi, l, "swa")
                p.barrier()
            if "na" in mixers:
                attn_mixer(g, bi, l, "na")
                p.barrier()
            if "gla" in mixers:
                gla_mixer(g, bi, l)
                p.barrier()
        p.barrier()
        if "s5" in mixers:
            s5_main(g, bi, l)
            p.barrier()
        g.dump("YT%d_%d" % (bi, l), YT[:], ["YT"])
        with contextlib.ExitStack() as sc:
            WO = sc.enter_context(nc.sbuf_tensor(UN() + "WO", [128, 8, D], BF16))
            OS_ = sc.enter_context(nc.sbuf_tensor(UN() + "OS", [128, 8, 512], F32))
            SQ = sc.enter_context(nc.sbuf_tensor(UN() + "SQ", [128, 8, 512], BF16))
            RS = sc.enter_context(nc.sbuf_tensor(UN() + "RS", [128, 512], F32))
            TMP = [sc.enter_context(nc.sbuf_tensor(UN() + "TMPc%d" % i, [128, 512], F32)) for i in range(2)]
            DMA("sp", WO[:], g.w_out_bf[l].rearrange("(k p) c -> p k c", p=128), r=g.wkeys("w_out_bf", l, 8), w=["WO"])
            for (a, b) in TB:
                if last and a < NCX:
                    continue
                w = b - a
                j = 2 if a < NCX else bi
                for dc in range(8):
                    bank = 1 + dc % 2
                    for k in range(8):
                        MM(PS[bank][:, :w], WO[:, k, dc * 128:(dc + 1) * 128], YT[:, k, a:b], k == 0, k == 7,
                           ["WO", "YT"], ["ps%d" % bank])
                    OP("dve", "tensor_copy", out=OS_[:, dc, :w], in_=PS[bank][:, :w], r=["ps%d" % bank], w=["OS%d" % dc])
                rms_stats(g, lambda k: OS_[:, k, :w], 8, w, SQ, RS, 0, ["OS%d" % k for k in range(8)])
                for k in range(8):
                    tm, tk = TMP[k % 2], "TMPc%d" % (k % 2)
                    OP("pool", "tensor_tensor", out=tm[:, :w], in0=OS_[:, k, :w], in1=RS[:, :w], op=ALU.mult,
                       r=["OS%d" % k, "RS"], w=[tk])
                    OP("dve", "scalar_tensor_tensor", out=X[:, k, a:b], in0=tm[:, :w], scalar=DER[:, l, j, 2, k:k + 1],
                       in1=X[:, k, a:b], op0=ALU.mult, op1=ALU.add, r=[tk, "DER", "X"], w=["X"])
    p.barrier()
    g.dump("X1_%d_%d" % (bi, l), X[:], ["X"])
    ffn(g, bi, l, last)
    p.barrier()
    g.dump("X2_%d_%d" % (bi, l), X[:], ["X"])


def ffn_blocks():
    blks = [(0, NCX, 0, NCX)]
    for i in range(5):
        oa = NCX + 410 * i
        ob = min(NCX + 410 * (i + 1), NT)
        blks.append((max(oa - 1, NCX), min(ob + 1, NT), oa, ob))
    return blks


def ffn(g, bi, l, last):
    nc, p, OP, MM, DMA = g.nc, g.p, g.OP, g.MM, g.DMA
    X, DER, PS = g.X, g.DER, g.PS
    with contextlib.ExitStack() as sf:
        def sb(name, shape, dt):
            return sf.enter_context(nc.sbuf_tensor(UN() + name, list(shape), dt))
        EPSC = sb("EPSC", [128, 1], F32)
        g.EPSC = EPSC
        OP("dve", "memset", ap=EPSC[:], constant=EPS, w=["EPSC"])
        HBs = [sb("HB%d" % i, [128, 8, 512], BF16) for i in range(2)]
        GB = sb("GB", [128, 22, 512], BF16)
        SQ = sb("SQ", [128, 8, 512], BF16)
        RS = sb("RS", [128, 512], F32)
        OS_ = sb("OS", [128, 8, 512], F32)
        TMP = [sb("TMPf%d" % i, [128, 512], F32) for i in range(2)]
        GS = [sb("GS%d" % i, [128, 514], F32) for i in range(2)]
        CV = [sb("CV%d" % i, [128, 512], F32) for i in range(2)]
        U1 = [sb("U1%d" % i, [128, 512], F32) for i in range(2)]
        WU = [sb("WU%d" % i, [128, 8, 256], BF16) for i in range(4)]
        WD = [sb("WD%d" % i, [128, 22, 128], BF16) for i in range(2)]
        wu_it = 0
        wd_it = 0
        blks = [bk for bk in ffn_blocks() if not (last and bk[0] < NCX)]

        def make_hb(bidx):
            (ca, cb, oa, ob) = blks[bidx]
            w = cb - ca
            j = 2 if ca < NCX else bi
            HB, hbk = HBs[bidx % 2], "HB%d" % (bidx % 2)
            rms_stats(g, lambda k: X[:, k, ca:cb], 8, w, SQ, RS, 0, ["X"])
            for k in range(8):
                tm, tk = TMP[k % 2], "TMPf%d" % (k % 2)
                OP("dve", "tensor_tensor", out=tm[:, :w], in0=X[:, k, ca:cb], in1=RS[:, :w], op=ALU.mult,
                   r=["X", "RS"], w=[tk])
                OP("act", "activation", out=HB[:, k, :w], in_=tm[:, :w], func=AF.Identity,
                   scale=DER[:, l, j, 3, k:k + 1], bias=DER[:, l, j, 4, k:k + 1], r=[tk, "DER"], w=[hbk])

        make_hb(0)
        for bidx, (ca, cb, oa, ob) in enumerate(blks):
            w = cb - ca
            wo = ob - oa
            off = oa - ca
            j = 2 if ca < NCX else bi
            HB, hbk = HBs[bidx % 2], "HB%d" % (bidx % 2)
            if bidx + 1 < len(blks):
                make_hb(bidx + 1)
            for jc in range(22):
                wu, wuk = WU[wu_it % 4], "WU%d" % (wu_it % 4)
                wu_it += 1
                DMA("sp", wu[:, :, 0:128], g.w_up_bf[l, :, jc * 128:(jc + 1) * 128].rearrange("(k p) c -> p k c", p=128),
                    r=g.wkeys("w_up_bf", l, 8), w=[wuk + "g"])
                DMA("sp", wu[:, :, 128:256],
                    g.w_up_bf[l, :, DFF + jc * 128:DFF + (jc + 1) * 128].rearrange("(k p) c -> p k c", p=128),
                    r=g.wkeys("w_up_bf", l, 8), w=[wuk + "v"])
                bg, bv = (1, 2)[jc % 2], (3, 4, 7, 5)[jc % 4]
                for k in range(8):
                    MM(PS[bg][:, :w], wu[:, k, 0:128], HB[:, k, :w], k == 0, k == 7, [wuk + "g", hbk], ["ps%d" % bg])
                for k in range(8):
                    MM(PS[bv][:, :w], wu[:, k, 128:256], HB[:, k, :w], k == 0, k == 7, [wuk + "v", hbk], ["ps%d" % bv])
                gs, gk = GS[jc % 2], "GS%d" % (jc % 2)
                cv, ck = CV[jc % 2], "CV%d" % (jc % 2)
                u1, uk = U1[jc % 2], "U1%d" % (jc % 2)
                OP("pool", "memset", ap=gs[:, 0:1], constant=0.0, w=[gk])
                OP("pool", "memset", ap=gs[:, w + 1:w + 2], constant=0.0, w=[gk])
                OP("act", "activation", out=gs[:, 1:w + 1], in_=PS[bg][:, :w], func=AF.Copy, r=["ps%d" % bg], w=[gk])
                s = 1 + off
                OP("pool", "tensor_scalar", out=cv[:, :wo], in0=gs[:, s - 1:s - 1 + wo], scalar1=g.CONV[:, l, 0, jc:jc + 1],
                   scalar2=0.0, op0=ALU.mult, op1=ALU.add, r=[gk, "CONV"], w=[ck])
                OP("dve", "scalar_tensor_tensor", out=cv[:, :wo], in0=gs[:, s:s + wo], scalar=g.CONV[:, l, 1, jc:jc + 1],
                   in1=cv[:, :wo], op0=ALU.mult, op1=ALU.add, r=[gk, "CONV", ck], w=[ck])
                OP("dve", "scalar_tensor_tensor", out=cv[:, :wo], in0=gs[:, s + 1:s + 1 + wo], scalar=g.CONV[:, l, 2, jc:jc + 1],
                   in1=cv[:, :wo], op0=ALU.mult, op1=ALU.add, r=[gk, "CONV", ck], w=[ck])
                OP("act", "activation", out=u1[:, :wo], in_=cv[:, :wo], func=AF.Gelu_apprx_tanh, r=[ck], w=[uk])
                OP("dve", "tensor_tensor", out=GB[:, jc, :wo], in0=PS[bv][:, off:off + wo], in1=u1[:, :wo], op=ALU.mult,
                   r=["ps%d" % bv, uk], w=["GB"])
            for dc in range(8):
                wd, wdk = WD[wd_it % 2], "WD%d" % (wd_it % 2)
                wd_it += 1
                DMA("sp", wd[:], g.w_down_bf[l, :, dc * 128:(dc + 1) * 128].rearrange("(k p) c -> p k c", p=128),
                    r=g.wkeys("w_down_bf", l, 22), w=[wdk])
                bank = (6, 0)[dc % 2]
                for jc in range(22):
                    MM(PS[bank][:, :wo], wd[:, jc, :], GB[:, jc, :wo], jc == 0, jc == 21, [wdk, "GB"], ["ps%d" % bank])
                OP("dve", "tensor_copy", out=OS_[:, dc, :wo], in_=PS[bank][:, :wo], r=["ps%d" % bank], w=["OS%d" % dc])
            rms_stats(g, lambda k: OS_[:, k, :wo], 8, wo, SQ, RS, 0, ["OS%d" % k for k in range(8)])
            for k in range(8):
                tm, tk = TMP[k % 2], "TMPf%d" % (k % 2)
                OP("pool", "tensor_tensor", out=tm[:, :wo], in0=OS_[:, k, :wo], in1=RS[:, :wo], op=ALU.mult,
                   r=["OS%d" % k, "RS"], w=[tk])
                OP("dve", "scalar_tensor_tensor", out=X[:, k, oa:ob], in0=tm[:, :wo], scalar=DER[:, l, j, 5, k:k + 1],
                   in1=X[:, k, oa:ob], op0=ALU.mult, op1=ALU.add, r=[tk, "DER", "X"], w=["X"])


def na_rows(kr):
    rs = [r for r in range(32) if min(max(r - 4, 0), 24) <= kr <= min(max(r - 4, 0), 24) + 7]
    assert rs == list(range(rs[0], rs[-1] + 1))
    return rs[0], rs[-1] + 1


def attn_mixer(g, bi, l, kind):
    nc, p, OP, MM, DMA = g.nc, g.p, g.OP, g.MM, g.DMA
    PS, HT, YT = g.PS, g.HT, g.YT
    wkey = g.wkeys("w_in_bf", l, 8)
    swa = kind == "swa"
    with contextlib.ExitStack() as sm:
        def sb(name, shape, dt):
            return sm.enter_context(nc.sbuf_tensor(UN() + name, list(shape), dt))
        QT = sb("QT", [128, NT], BF16)
        KT = sb("KT", [128, NT], BF16)
        VT = sb("VT", [128, 18, 128], BF16)
        WB = [sb("WB%d" % i, [128, 8, 128], BF16) for i in range(3)]
        T1 = sb("T1", [128, 512], F32)
        T2 = sb("T2", [128, 512], F32)
        PT = [sb("PT%d" % i, [128, 512], BF16) for i in range(2)]
        REC = sb("REC", [128, 512], F32)
        if swa:
            RC = sb("RC", [128, NT], BF16)
            RSN = sb("RSN", [128, NT], BF16)
            DMA("sp", RC[:], g.ropeC, w=["RC"])
            DMA("sp", RSN[:], g.ropeS, w=["RSN"])
        else:
            UT = sb("UT", [128, 2, 960], BF16)
            UF = sb("UF", [128, 960], F32)
            MC8 = sb("MC8", [128, 960], F32)
            MCN = sb("MCN", [128, 960], F32)
            IDB = sb("IDB", [128, 64], BF16)
            DMA("sp", MC8[:], g.dram["mc8"], w=["MC8"])
            DMA("sp", MCN[:], g.dram["mcn"], w=["MCN"])
            DMA("sp", IDB[:], g.dram["idb"], w=["IDB"])
        wb_it = [0]

        def load_w(c0):
            i = wb_it[0] % 3
            wb_it[0] += 1
            DMA("sp", WB[i][:], g.w_in_bf[l, :, c0:c0 + 128].rearrange("(k p) c -> p k c", p=128), r=wkey, w=["WB%d" % i])
            return WB[i], "WB%d" % i

        for c in range(2):
            if swa:
                cq, cqp, ck, ckp, cv_ = C_SQ + 128 * c, C_SQP + 128 * c, C_SKD + 128 * c, C_SKDP + 128 * c, C_SV
            else:
                cq, ck, cv_ = C_NAQ + 128 * c, C_NAK + 128 * c, C_NAV + 128 * c
            for (dst, dk, c1, c2) in ((QT, "QT", cq, cqp if swa else None), (KT, "KT", ck, ckp if swa else None)):
                w1, w1k = load_w(c1)
                if swa:
                    w2, w2k = load_w(c2)
                for (a, b) in TB:
                    w = b - a
                    for k in range(8):
                        MM(PS[0][:, :w], w1[:, k, :], HT[:, k, a:b], k == 0, k == 7, [w1k, "HT"], ["ps0"])
                    if swa:
                        for k in range(8):
                            MM(PS[1][:, :w], w2[:, k, :], HT[:, k, a:b], k == 0, k == 7, [w2k, "HT"], ["ps1"])
                        OP("dve", "tensor_tensor", out=T1[:, :w], in0=PS[0][:, :w], in1=RC[:, a:b], op=ALU.mult,
                           r=["ps0", "RC"], w=["T1"])
                        OP("dve", "tensor_tensor", out=T2[:, :w], in0=PS[1][:, :w], in1=RSN[:, a:b], op=ALU.mult,
                           r=["ps1", "RSN"], w=["T2"])
                        OP("pool", "tensor_tensor", out=dst[:, a:b], in0=T1[:, :w], in1=T2[:, :w], op=ALU.add,
                           r=["T1", "T2"], w=[dk])
                    else:
                        OP("act", "activation", out=dst[:, a:b], in_=PS[0][:, :w], func=AF.Copy, r=["ps0"], w=[dk])
            if (not swa) or c == 0:
                wv, wvk = load_w(cv_)
                for t4 in range(0, 18, 4):
                    nt = min(4, 18 - t4)
                    for ti in range(nt):
                        tt = t4 + ti
                        for k in range(8):
                            MM(PS[2][:, ti * 128:(ti + 1) * 128], HT[:, k, tt * 128:(tt + 1) * 128], wv[:, k, :], k == 0, k == 7,
                               [wvk, "HT"], ["ps2"])
                    OP("act", "activation", out=VT[:, t4:t4 + nt, :],
                       in_=PS[2][:, 0:nt * 128].rearrange("p (t c) -> p t c", c=128), func=AF.Copy, r=["ps2"], w=["VT"])
            if not swa:
                for hh in range(2):
                    for half in range(2):
                        DMA("sp", UF[half * 64:(half + 1) * 64, :], g.na_exp[l, 2 * c + hh], w=["UF"])
                    OP("dve", "tensor_tensor", out=UF[:], in0=UF[:], in1=MC8[:], op=ALU.mult, r=["UF", "MC8"], w=["UF"])
                    OP("dve", "tensor_tensor", out=UT[:, hh, :], in0=UF[:], in1=MCN[:], op=ALU.add, r=["UF", "MCN"], w=["UT"])
            for (qa, qb) in TB:
                qw = qb - qa
                for hh in range(2):
                    h = 2 * c + hh
                    hb = 64 * hh
                    items = []
                    for kc in range(2):
                        items.append((kc * 128, 128, 0, kc, qa, qb, None))
                    if qa >= NCX:
                        if swa:
                            for kb in range(16):
                                ka = NCX + 128 * kb
                                a_ = max(qa, ka - 128)
                                b_ = min(qb, ka + 256)
                                if a_ < b_:
                                    items.append((ka, 128, 0, 2 + kb, a_, b_, ("swa", ka)))
                        else:
                            for kr in range(32):
                                r0, r1 = na_rows(kr)
                                a_ = max(qa, NCX + 64 * r0)
                                b_ = min(qb, NCX + 64 * r1)
                                if a_ < b_:
                                    items.append((NCX + 64 * kr, 64, 64 * (kr % 2), 2 + kr // 2, a_, b_, ("na", kr)))
                    vc0 = 64 * (h // 2) if swa else 64 * hh
                    def s_mm(ii):
                        (ka, nk, pb, vt, a_, b_, post) = items[ii]
                        n = b_ - a_
                        sbank = 3 + ii % 2
                        nab = post is not None and post[0] == "na"
                        MM(PS[sbank][pb:pb + nk, :n], KT[hb:hb + 64, ka:ka + nk], QT[hb:hb + 64, a_:b_], True, not nab,
                           ["KT", "QT"], ["ps%d" % sbank])
                        if nab:
                            kr = post[1]
                            i0 = (a_ - NCX) // 64 - kr + 7
                            MM(PS[sbank][pb:pb + nk, :n], IDB[hb:hb + 64, :], UT[hb:hb + 64, hh, i0 * 64:i0 * 64 + n], False, True,
                               ["IDB", "UT"], ["ps%d" % sbank])

                    for ii, (ka, nk, pb, vt, a_, b_, post) in enumerate(items):
                        n = b_ - a_
                        sbank = 3 + ii % 2
                        pt, ptk = PT[ii % 2], "PT%d" % (ii % 2)
                        s_mm(ii)
                        OP("act", "activation", out=pt[pb:pb + nk, :n], in_=PS[sbank][pb:pb + nk, :n], func=AF.Exp, scale=0.125,
                           r=["ps%d" % sbank], w=[ptk])
                        if post is not None and post[0] == "swa":
                            kst = post[1]
                            if a_ < kst:
                                OP("pool", "tensor_tensor", out=pt[:, 0:128], in0=pt[:, 0:128], in1=g.MAB[:, 0, :], op=ALU.mult,
                                   r=[ptk, "MAB"], w=[ptk])
                            if b_ > kst + 128:
                                o_ = kst + 128 - a_
                                OP("pool", "tensor_tensor", out=pt[:, o_:o_ + 128], in0=pt[:, o_:o_ + 128], in1=g.MAB[:, 1, :],
                                   op=ALU.mult, r=[ptk, "MAB"], w=[ptk])
                        MM(PS[5][hb:hb + 64, a_ - qa:b_ - qa], VT[pb:pb + nk, vt, vc0:vc0 + 64], pt[pb:pb + nk, :n], ii == 0,
                           ii == len(items) - 1, ["VT", ptk], ["ps5"])
                        MM(PS[6][hb:hb + 64, a_ - qa:b_ - qa], g.ONESB[pb:pb + nk, 0:64], pt[pb:pb + nk, :n], ii == 0,
                           ii == len(items) - 1, ["ONESB", ptk], ["ps6"])
                if swa:
                    OP("dve", "tensor_scalar", out=REC[:, :qw], in0=PS[6][:, :qw], scalar1=g.ESINK[:, l, c:c + 1], scalar2=None,
                       op0=ALU.add, r=["ps6", "ESINK"], w=["REC"])
                    OP("dve", "reciprocal", out=REC[:, :qw], in_=REC[:, :qw], r=["REC"], w=["REC"])
                else:
                    OP("dve", "reciprocal", out=REC[:, :qw], in_=PS[6][:, :qw], r=["ps6"], w=["REC"])
                yc = (6 if swa else 2) + c
                OP("dve", "tensor_tensor", out=YT[:, yc, qa:qb], in0=PS[5][:, :qw], in1=REC[:, :qw], op=ALU.mult,
                   r=["ps5", "REC"], w=["YT"])


TC = 64


def s5_project(g, bi, l):
    nc, p, OP, MM, DMA = g.nc, g.p, g.OP, g.MM, g.DMA
    PS, HT, YT = g.PS, g.HT, g.YT
    wkey = g.wkeys("w_in_bf", l, 8)
    with contextlib.ExitStack() as sm:
        WA = [sm.enter_context(nc.sbuf_tensor(UN() + "sWA%d" % i, [128, 8, 128], BF16)) for i in range(2)]
        for cc in range(2):
            wa, wak = WA[cc], "sWA%d" % cc
            DMA("sp", wa[:], g.w_in_bf[l, :, C_A + 128 * cc:C_A + 128 * cc + 128].rearrange("(k p) c -> p k c", p=128), r=wkey,
                w=[wak])
            for (a, b) in TB:
                w = b - a
                for k in range(8):
                    MM(PS[7][:, :w], wa[:, k, :], HT[:, k, a:b], k == 0, k == 7, [wak, "HT"], ["ps7"])
                OP("act", "activation", out=YT[:, cc, a:b], in_=PS[7][:, :w], func=AF.Copy, r=["ps7"], w=["sUT"])


def s5_main(g, bi, l):
    nc, p, OP, MM, DMA = g.nc, g.p, g.OP, g.MM, g.DMA
    PS, YT = g.PS, g.YT
    wkey = g.wkeys("w_in_bf", l, 8)
    PI = math.pi
    with contextlib.ExitStack() as sm:
        def sb(name, shape, dt):
            return sm.enter_context(nc.sbuf_tensor(UN() + name, list(shape), dt))
        UT = YT[:, 0:2, :]
        YF = sb("sYF", [128, 2, NT], BF16)
        BP = [sb("sBP%d" % i, [128, 16, 128], BF16) for i in range(2)]
        CP = [sb("sCP%d" % i, [128, 16, 128], BF16) for i in range(2)]
        TAB = [sb("sTAB%d" % i, [128, 16, TC], BF16) for i in range(4)]
        Zs = [sb("sZ%d" % i, [128, 16, TC], F32) for i in range(2)]
        Ws = [sb("sW%d" % i, [128, 16, TC], F32) for i in range(2)]
        ZAs = [sb("sZA%d" % i, [128, 8, TC], F32) for i in range(2)]
        ZBs = [sb("sZB%d" % i, [128, 8, TC], F32) for i in range(2)]
        HCs = [sb("sHC%d" % i, [128, 8, TC], BF16) for i in range(2)]
        HSs = [sb("sHS%d" % i, [128, 8, TC], BF16) for i in range(2)]
        Z, W, ZA, ZB, HC, HS = Zs[0], Ws[0], ZAs[0], ZBs[0], HCs[0], HSs[0]
        WGL = sb("sWGL", [128, 2, 256], BF16)
        LAM = sb("sLAM", [128, 2, 32], F32)
        STP = sb("sSTP", [128, 32], F32)
        DSK = sb("sDSK", [128, LD[0], 2], F32)
        SGN = sb("sSGN", [128, 2], F32)
        JM = sb("sJM", [128, 128], F32)
        HPI = sb("sHPI", [128, 1], F32)
        sm_names = ["RHO", "TH", "M", "SH", "CH", "SN", "CS", "LBR", "LBI", "DEN", "KR", "KI", "T0", "T1", "EC", "ES", "EC2", "ES2"]
        SM = {n: sb("s" + n, [128, 32], F32) for n in sm_names}
        INIT = sb("sINIT", [128, 16], F32)
        ENDS = sb("sENDS", [128, 16], F32)
        RT1 = sb("sRT1", [128, 16], F32)
        TMPYs = [sb("sTMPY%d" % i, [128, TC], F32) for i in range(2)]
        DMA("sp", LAM[:], g.dram["lamT"][:, l], w=["sLAM"])
        DMA("sp", STP[:], g.dram["stepT"][:, l], w=["sSTP"])
        DMA("sp", DSK[:], g.dram["dskT"], w=["sDSK"])
        DMA("sp", SGN[:], g.dram["sgn"], w=["sSGN"])
        DMA("sp", JM[:], g.dram["jmat"], w=["sJM"])
        DMA("pool", WGL[:], g.dram["s5_w_glu"][l].rearrange("(k p) c -> p k c", p=128), w=["sWGL"])
        OP("dve", "memset", ap=HPI[:], constant=PI / 2, w=["sHPI"])

        def V(eng, method, outn, r, **kw):
            OP(eng, method, r=["s" + x for x in r], w=["s" + outn], **kw)

        def TT(outn, an, bn, op):
            V("dve", "tensor_tensor", outn, [an, bn], out=SM[outn][:], in0=SM[an][:], in1=SM[bn][:], op=op)

        OP("act", "activation", out=STP[:], in_=STP[:], func=AF.Exp, r=["sSTP"], w=["sSTP"])
        OP("dve", "tensor_tensor", out=SM["T0"][:], in0=LAM[:, 0, :], in1=STP[:], op=ALU.mult, r=["sLAM", "sSTP"], w=["sT0"])
        OP("act", "activation", out=SM["RHO"][:], in_=SM["T0"][:], func=AF.Exp, r=["sT0"], w=["sRHO"])
        OP("dve", "tensor_tensor", out=SM["TH"][:], in0=LAM[:, 1, :], in1=STP[:], op=ALU.mult, r=["sLAM", "sSTP"], w=["sTH"])
        for _ in range(5):
            V("dve", "tensor_scalar", "M", ["TH"], out=SM["M"][:], in0=SM["TH"][:], scalar1=PI, scalar2=-2 * PI, op0=ALU.is_gt,
              op1=ALU.mult)
            TT("TH", "TH", "M", ALU.add)
        for _ in range(2):
            V("dve", "tensor_scalar", "M", ["TH"], out=SM["M"][:], in0=SM["TH"][:], scalar1=-PI, scalar2=2 * PI, op0=ALU.is_lt,
              op1=ALU.mult)
            TT("TH", "TH", "M", ALU.add)
        OP("act", "activation", out=SM["SH"][:], in_=SM["TH"][:], func=AF.Sin, scale=0.5, r=["sTH"], w=["sSH"])
        OP("act", "activation", out=SM["CH"][:], in_=SM["TH"][:], func=AF.Sin, scale=0.5, bias=HPI[:, 0:1], r=["sTH", "sHPI"],
           w=["sCH"])
        TT("SN", "SH", "CH", ALU.mult)
        V("dve", "tensor_scalar", "SN", ["SN"], out=SM["SN"][:], in0=SM["SN"][:], scalar1=2.0, scalar2=None, op0=ALU.mult)
        TT("T0", "CH", "CH", ALU.mult)
        TT("T1", "SH", "SH", ALU.mult)
        TT("CS", "T0", "T1", ALU.subtract)
        TT("LBR", "RHO", "CS", ALU.mult)
        TT("LBI", "RHO", "SN", ALU.mult)
        V("dve", "tensor_scalar", "LBR", ["LBR"], out=SM["LBR"][:], in0=SM["LBR"][:], scalar1=-1.0, scalar2=None, op0=ALU.add)
        OP("dve", "tensor_tensor", out=SM["T0"][:], in0=LAM[:, 0, :], in1=LAM[:, 0, :], op=ALU.mult, r=["sLAM"], w=["sT0"])
        OP("dve", "tensor_tensor", out=SM["T1"][:], in0=LAM[:, 1, :], in1=LAM[:, 1, :], op=ALU.mult, r=["sLAM"], w=["sT1"])
        TT("DEN", "T0", "T1", ALU.add)
        V("dve", "reciprocal", "DEN", ["DEN"], out=SM["DEN"][:], in_=SM["DEN"][:])
        OP("dve", "tensor_tensor", out=SM["T0"][:], in0=SM["LBR"][:], in1=LAM[:, 0, :], op=ALU.mult, r=["sLBR", "sLAM"], w=["sT0"])
        OP("dve", "tensor_tensor", out=SM["T1"][:], in0=SM["LBI"][:], in1=LAM[:, 1, :], op=ALU.mult, r=["sLBI", "sLAM"], w=["sT1"])
        TT("KR", "T0", "T1", ALU.add)
        TT("KR", "KR", "DEN", ALU.mult)
        OP("dve", "tensor_tensor", out=SM["T0"][:], in0=SM["LBI"][:], in1=LAM[:, 0, :], op=ALU.mult, r=["sLBI", "sLAM"], w=["sT0"])
        OP("dve", "tensor_tensor", out=SM["T1"][:], in0=SM["LBR"][:], in1=LAM[:, 1, :], op=ALU.mult, r=["sLBR", "sLAM"], w=["sT1"])
        TT("KI", "T0", "T1", ALU.subtract)
        TT("KI", "KI", "DEN", ALU.mult)

        nchunk = NT // TC
        for d in range(2):
            qs = slice(16 * d, 16 * d + 16)
            for i, nm in enumerate(("s5B1", "s5B2")):
                DMA("sp", BP[i][:], g.dram[nm + "_bf"][l, d].rearrange("g p s -> p g s"), r=["%s_bf%d" % (nm, l), "%s_bf%d_d0" % (nm, l)], w=["sBP%d" % i])
            for i, nm in enumerate(("s5C1", "s5C2")):
                DMA("sp", CP[i][:], g.dram[nm + "_bf"][l, d].rearrange("g p s -> p g s"), r=["%s_bf%d" % (nm, l), "%s_bf%d_d0" % (nm, l)], w=["sCP%d" % i])
            for which in range(2):
                i0 = 0 if d == 0 else TC - 1
                if which == 0:
                    OP("dve", "tensor_copy", out=Z[:, :, i0], in_=SM["KR"][:, qs], r=["sKR"], w=["sZ0"])
                    OP("dve", "tensor_copy", out=W[:, :, i0], in_=SM["KI"][:, qs], r=["sKI"], w=["sW0"])
                else:
                    OP("dve", "memset", ap=Z[:, :, i0:i0 + 1], constant=1.0, w=["sZ0"])
                    OP("dve", "memset", ap=W[:, :, i0:i0 + 1], constant=0.0, w=["sW0"])
                OP("dve", "tensor_copy", out=SM["EC"][:, 0:16], in_=SM["CS"][:, qs], r=["sCS"], w=["sEC"])
                if which == 0:
                    OP("dve", "tensor_scalar", out=SM["ES"][:, 0:16], in0=SM["SN"][:, qs], scalar1=-1.0, scalar2=None, op0=ALU.mult,
                       r=["sSN"], w=["sES"])
                else:
                    OP("dve", "tensor_copy", out=SM["ES"][:, 0:16], in_=SM["SN"][:, qs], r=["sSN"], w=["sES"])
                n = 1
                while n < TC:
                    if d == 0:
                        src, dst = slice(0, n), slice(n, 2 * n)
                    else:
                        src, dst = slice(TC - n, TC), slice(TC - 2 * n, TC - n)
                    ecb = SM["EC"][:, 0:16].unsqueeze(2).to_broadcast([128, 16, n])
                    esb = SM["ES"][:, 0:16].unsqueeze(2).to_broadcast([128, 16, n])
                    OP("dve", "tensor_tensor", out=ZA[:, :, :].rearrange("p a b -> p (a b)")[:, 0:16 * n].rearrange("p (g n) -> p g n", n=n),
                       in0=Z[:, :, src], in1=ecb, op=ALU.mult, r=["sZ0", "sEC"], w=["sZA0"])
                    OP("dve", "tensor_tensor", out=ZB[:, :, :].rearrange("p a b -> p (a b)")[:, 0:16 * n].rearrange("p (g n) -> p g n", n=n),
                       in0=W[:, :, src], in1=esb, op=ALU.mult, r=["sW0", "sES"], w=["sZB0"])
                    OP("dve", "tensor_tensor", out=Z[:, :, dst],
                       in0=ZA[:, :, :].rearrange("p a b -> p (a b)")[:, 0:16 * n].rearrange("p (g n) -> p g n", n=n),
                       in1=ZB[:, :, :].rearrange("p a b -> p (a b)")[:, 0:16 * n].rearrange("p (g n) -> p g n", n=n),
                       op=ALU.subtract, r=["sZA0", "sZB0"], w=["sZ0"])
                    OP("dve", "tensor_tensor", out=ZA[:, :, :].rearrange("p a b -> p (a b)")[:, 0:16 * n].rearrange("p (g n) -> p g n", n=n),
                       in0=W[:, :, src], in1=ecb, op=ALU.mult, r=["sW0", "sEC"], w=["sZA0"])
                    OP("dve", "tensor_tensor", out=ZB[:, :, :].rearrange("p a b -> p (a b)")[:, 0:16 * n].rearrange("p (g n) -> p g n", n=n),
                       in0=Z[:, :, src], in1=esb, op=ALU.mult, r=["sZ0", "sES"], w=["sZB0"])
                    OP("dve", "tensor_tensor", out=W[:, :, dst],
                       in0=ZA[:, :, :].rearrange("p a b -> p (a b)")[:, 0:16 * n].rearrange("p (g n) -> p g n", n=n),
                       in1=ZB[:, :, :].rearrange("p a b -> p (a b)")[:, 0:16 * n].rearrange("p (g n) -> p g n", n=n),
                       op=ALU.add, r=["sZA0", "sZB0"], w=["sW0"])
                    OP("dve", "tensor_tensor", out=SM["EC2"][:, 0:16], in0=SM["EC"][:, 0:16], in1=SM["EC"][:, 0:16], op=ALU.mult,
                       r=["sEC"], w=["sEC2"])
                    OP("dve", "tensor_tensor", out=SM["ES2"][:, 0:16], in0=SM["ES"][:, 0:16], in1=SM["ES"][:, 0:16], op=ALU.mult,
                       r=["sES"], w=["sES2"])
                    OP("dve", "tensor_tensor", out=SM["ES"][:, 0:16], in0=SM["ES"][:, 0:16], in1=SM["EC"][:, 0:16], op=ALU.mult,
                       r=["sES", "sEC"], w=["sES"])
                    OP("dve", "tensor_scalar", out=SM["ES"][:, 0:16], in0=SM["ES"][:, 0:16], scalar1=2.0, scalar2=None, op0=ALU.mult,
                       r=["sES"], w=["sES"])
                    OP("dve", "tensor_tensor", out=SM["EC"][:, 0:16], in0=SM["EC2"][:, 0:16], in1=SM["ES2"][:, 0:16], op=ALU.subtract,
                       r=["sEC2", "sES2"], w=["sEC"])
                    n *= 2
                if which == 0:
                    OP("dve", "tensor_copy", out=TAB[0][:], in_=Z[:], r=["sZ0"], w=["sTAB0"])
                    OP("dve", "tensor_scalar", out=TAB[1][:], in0=W[:], scalar1=SGN[:, 0:1], scalar2=None, op0=ALU.mult,
                       r=["sW0", "sSGN"], w=["sTAB1"])
                else:
                    OP("dve", "tensor_scalar", out=TAB[2][:], in0=Z[:], scalar1=SGN[:, 1:2], scalar2=None, op0=ALU.mult,
                       r=["sZ0", "sSGN"], w=["sTAB2"])
                    OP("dve", "tensor_scalar", out=TAB[3][:], in0=W[:], scalar1=-1.0, scalar2=None, op0=ALU.mult,
                       r=["sW0"], w=["sTAB3"])
                    OP("dve", "tensor_copy", out=SM["EC2"][:, 0:16], in_=SM["EC"][:, 0:16], r=["sEC"], w=["sEC2"])
                    OP("dve", "tensor_copy", out=SM["ES2"][:, 0:16], in_=SM["ES"][:, 0:16], r=["sES"], w=["sES2"])
            p.barrier()
            OP("dve", "memset", ap=INIT[:], constant=0.0, w=["sINIT"])
            order = list(range(nchunk)) if d == 0 else list(range(NCX // TC - 1, -1, -1)) + list(range(nchunk - 1, NCX // TC - 1, -1))
            allw = lambda pr: ["sW%d_%d" % (pr, gi) for gi in range(16)]
            def stage1(ci):
                m = order[ci]
                t0 = m * TC
                cp = ci % 2
                Zc = Zs[cp]
                for cc in range(2):
                    par = cc
                    b1, b2 = PS[0 + par], PS[2 + par]
                    k1, k2 = "ps%d" % (0 + par), "ps%d" % (2 + par)
                    za, zb = ZAs[par], ZBs[par]
                    zak, zbk = "sZA%d" % par, "sZB%d" % par
                    zk = "sZ%d_%d" % (cp, cc)
                    for gg in range(8):
                        gi = 8 * cc + gg
                        MM(b1[:, gg * TC:(gg + 1) * TC], BP[0][:, gi, :], UT[:, cc, t0:t0 + TC], True, True, ["sBP0", "sUT"], [k1])
                        MM(b2[:, gg * TC:(gg + 1) * TC], BP[1][:, gi, :], UT[:, cc, t0:t0 + TC], True, True, ["sBP1", "sUT"], [k2])
                    gsl = slice(8 * cc, 8 * cc + 8)
                    OP("dve", "tensor_tensor", out=za[:], in0=b1[:, :].rearrange("p (g n) -> p g n", n=TC), in1=TAB[0][:, gsl, :],
                       op=ALU.mult, r=[k1, "sTAB0"], w=[zak])
                    OP("dve", "tensor_tensor", out=zb[:], in0=b2[:, :].rearrange("p (g n) -> p g n", n=TC), in1=TAB[1][:, gsl, :],
                       op=ALU.mult, r=[k2, "sTAB1"], w=[zbk])
                    OP("pool", "tensor_tensor", out=Zc[:, gsl, :], in0=za[:], in1=zb[:], op=ALU.add, r=[zak, zbk], w=[zk])

            def stage2(ci):
                m = order[ci]
                t0 = m * TC
                cp = ci % 2
                Zc, Wc = Zs[cp], Ws[cp]
                for cc in range(2):
                    par = cc
                    by, ky = PS[4 + par], "ps%d" % (4 + par)
                    hc, hs = HCs[par], HSs[par]
                    hck, hsk = "sHC%d" % par, "sHS%d" % par
                    zk = "sZ%d_%d" % (cp, cc)
                    gsl = slice(8 * cc, 8 * cc + 8)
                    wks = ["sW%d_%d" % (cp, 8 * cc + gg) for gg in range(8)]
                    for gg in range(8):
                        gi = 8 * cc + gg
                        q = 16 * d + gi
                        rho_b = SM["RHO"][:, q:q + 1].to_broadcast([128, TC])
                        if d == 0:
                            zin, wout = Zc[:, gi, :], Wc[:, gi, :]
                        else:
                            zin, wout = Zc[:, gi, ::-1], Wc[:, gi, ::-1]
                        OP("dve", "tensor_tensor_scan", out=wout, data0=rho_b, data1=zin, initial=INIT[:, gi:gi + 1], op0=ALU.mult,
                           op1=ALU.add, r=[zk, "sRHO", "sINIT"], w=[wks[gg]])
                    OP("pool", "tensor_tensor", out=hc[:], in0=Wc[:, gsl, :], in1=TAB[2][:, gsl, :], op=ALU.mult, r=wks + ["sTAB2"],
                       w=[hck])
                    OP("pool", "tensor_tensor", out=hs[:], in0=Wc[:, gsl, :], in1=TAB[3][:, gsl, :], op=ALU.mult, r=wks + ["sTAB3"],
                       w=[hsk])
                    for gg in range(8):
                        gi = 8 * cc + gg
                        MM(by[:, 0:TC], CP[0][:, gi, :], hc[:, gg, :], gg == 0, False, ["sCP0", hck], [ky])
                        MM(by[:, 0:TC], CP[1][:, gi, :], hs[:, gg, :], False, gg == 7, ["sCP1", hsk], [ky])
                    if d == 0:
                        OP("act", "activation", out=YF[:, cc, t0:t0 + TC], in_=by[:, 0:TC], func=AF.Copy, r=[ky], w=["sYF"])
                    else:
                        tmy, tmk = TMPYs[par], "sTMPY%d" % par
                        OP("dve", "scalar_tensor_tensor", out=tmy[:], in0=UT[:, cc, t0:t0 + TC], scalar=DSK[:, l, cc:cc + 1],
                           in1=YF[:, cc, t0:t0 + TC], op0=ALU.mult, op1=ALU.add, r=["sUT", "sDSK", "sYF"], w=[tmk])
                        OP("dve", "tensor_tensor", out=YF[:, cc, t0:t0 + TC], in0=tmy[:], in1=by[:, 0:TC], op=ALU.add,
                           r=[tmk, ky], w=["sYF"])
                ecol = TC - 1 if d == 0 else 0
                OP("dve", "tensor_copy", out=ENDS[:], in_=Wc[:, :, ecol], r=allw(cp), w=["sENDS"])
                MM(PS[6][:, 0:16], JM[:], ENDS[:], True, True, ["sJM", "sENDS"], ["ps6"])
                OP("dve", "tensor_tensor", out=RT1[:], in0=ENDS[:], in1=SM["EC2"][:, 0:16], op=ALU.mult, r=["sENDS", "sEC2"], w=["sRT1"])
                OP("dve", "tensor_tensor", out=ENDS[:], in0=PS[6][:, 0:16], in1=SM["ES2"][:, 0:16], op=ALU.mult, r=["ps6", "sES2"],
                   w=["sENDS"])
                OP("dve", "tensor_tensor", out=INIT[:], in0=RT1[:], in1=ENDS[:], op=ALU.add, r=["sRT1", "sENDS"], w=["sINIT"])

            stage1(0)
            for ci in range(len(order)):
                if ci + 1 < len(order):
                    stage1(ci + 1)
                stage2(ci)
            p.barrier()
        for (a, b) in TB:
            w = b - a
            ZT = ZA[:, :, :].rearrange("p a b -> p (a b)")
            for kc in range(2):
                OP("act", "activation", out=HC[:, :, :].rearrange("p a b -> p (a b)")[:, :w] if kc == 0 else
                   HS[:, :, :].rearrange("p a b -> p (a b)")[:, :w], in_=YF[:, kc, a:b], func=AF.Gelu_apprx_tanh, r=["sYF"],
                   w=["sHC0" if kc == 0 else "sHS0"])
            zt = [HC[:, :, :].rearrange("p a b -> p (a b)"), HS[:, :, :].rearrange("p a b -> p (a b)")]
            for oc in range(2):
                for kc in range(2):
                    MM(PS[7][:, :w], WGL[:, kc, oc * 128:(oc + 1) * 128], zt[kc][:, :w], kc == 0, kc == 1,
                       ["sWGL", "sHC0", "sHS0"], ["ps7"])
                OP("act", "activation", out=ZT[:, :w], in_=PS[7][:, :w], func=AF.Sigmoid, r=["ps7"], w=["sZA0"])
                OP("dve", "tensor_tensor", out=YT[:, oc, a:b], in0=zt[oc][:, :w], in1=ZT[:, :w], op=ALU.mult,
                   r=["sHC0", "sHS0", "sZA0"], w=["YT"])


def gla_mixer(g, bi, l):
    nc, p, OP, MM, DMA = g.nc, g.p, g.OP, g.MM, g.DMA
    PS, HT, YT = g.PS, g.HT, g.YT
    wkey = g.wkeys("w_in_bf", l, 8)
    with contextlib.ExitStack() as sm:
        def sb(name, shape, dt):
            return sm.enter_context(nc.sbuf_tensor(UN() + name, list(shape), dt))
        QT = sb("gQT", [128, NT], BF16)
        KT = sb("gKT", [128, NT], BF16)
        SR = sb("gSR", [128, NT], BF16)
        KTK = sb("gKTK", [128, 18, 128], BF16)
        VTK = sb("gVTK", [128, 18, 128], BF16)
        OF = sb("gOF", [128, NT], F32)
        GA = [sb("gGA%d" % d, [32, NT], BF16) for d in range(2)]
        WG = sb("gWG", [32, 2, 256], BF16)
        TRI = sb("gTRI", [128, 2, 128], F32)
        BD = sb("gBD", [128, 128], BF16)
        GN = sb("gGN", [128, LD[0]], F32)
        ONE = sb("gONE", [128, 1], F32)
        EPS_ = sb("gEPS", [128, 1], F32)
        WB = [sb("gWB%d" % i, [128, 8, 128], BF16) for i in range(2)]
        WGB = sb("gWGB", [128, 8, 32], BF16)
        NEG = sb("gNEG", [128, 128], F32)
        EX = sb("gEX", [128, 128], F32)
        E1s = [sb("gE1%d" % i, [128, 128], F32) for i in range(2)]
        OT = sb("gOT", [128, 128], F32)
        E2 = sb("gE2", [128, 128], F32)
        E2T = sb("gE2T", [128, 128], F32)
        QDs = [sb("gQD%d" % i, [128, 128], BF16) for i in range(2)]
        KD = sb("gKD", [128, 128], BF16)
        KDTs = [sb("gKDT%d" % i, [128, 128], BF16) for i in range(2)]
        AMs = [[sb("gAM%d_%d" % (i, hh), [128, 128], BF16) for hh in range(2)] for i in range(2)]
        S = sb("gS", [128, 64], F32)
        SB_ = sb("gSB", [128, 64], BF16)
        SQ = sb("gSQ", [128, 512], BF16)
        RS = sb("gRS", [128, 512], F32)
        OP("dve", "memset", ap=ONE[:], constant=1.0, w=["gONE"])
        OP("dve", "memset", ap=EPS_[:], constant=EPS, w=["gEPS"])
        DMA("sp", TRI[:], g.dram["tri"], w=["gTRI"])
        DMA("sp", BD[:], g.dram["bd64"], w=["gBD"])
        DMA("sp", GN[:], g.dram["gnT"], w=["gGN"])
        for d in range(2):
            DMA("pool", WG[0:16, d, :], g.dram["gla_w_gate2"][l, d], w=["gWG"])
            DMA("pool", WG[16:17, d, :], g.dram["gla_b_gate"][l, d:d + 1, :], w=["gWG"])
        wb_it = [0]

        def load_w(c0):
            i = wb_it[0] % 2
            wb_it[0] += 1
            DMA("sp", WB[i][:], g.w_in_bf[l, :, c0:c0 + 128].rearrange("(k p) c -> p k c", p=128), r=wkey, w=["gWB%d" % i])
            return WB[i], "gWB%d" % i

        DMA("sp", WGB[:], g.w_in_bf[l, :, C_GF:C_GF + 32].rearrange("(k p) c -> p k c", p=128), r=wkey, w=["gWGB"])
        for d in range(2):
            OP("pool", "memset", ap=GA[d][:], constant=1.0, w=["gGA%d" % d])
            for (a, b) in TB:
                w = b - a
                for k in range(8):
                    MM(PS[0][0:16, :w], WGB[:, k, 16 * d:16 * d + 16], HT[:, k, a:b], k == 0, k == 7, ["gWGB", "HT"], ["ps0"])
                OP("act", "activation", out=GA[d][0:16, a:b], in_=PS[0][0:16, :w], func=AF.Copy, r=["ps0"], w=["gGA%d" % d])
        for c in range(2):
            for (dst, dk, c0, fn, sc) in ((QT, "gQT", C_GQ + 128 * c, AF.Copy, 0.125), (KT, "gKT", C_GK + 128 * c, AF.Copy, 1.0),
                                          (SR, "gSR", C_GR + 128 * c, AF.Silu, 1.0)):
                w1, w1k = load_w(c0)
                for (a, b) in TB:
                    w = b - a
                    for k in range(8):
                        MM(PS[0][:, :w], w1[:, k, :], HT[:, k, a:b], k == 0, k == 7, [w1k, "HT"], ["ps0"])
                    OP("act", "activation", out=dst[:, a:b], in_=PS[0][:, :w], func=fn, scale=sc, r=["ps0"], w=[dk])
            for (dst, dk, c0) in ((KTK, "gKTK", C_GK + 128 * c), (VTK, "gVTK", C_GV + 128 * c)):
                wv, wvk = load_w(c0)
                for t4 in range(0, 18, 4):
                    nt = min(4, 18 - t4)
                    for ti in range(nt):
                        tt = t4 + ti
                        for k in range(8):
                            MM(PS[1][:, ti * 128:(ti + 1) * 128], HT[:, k, tt * 128:(tt + 1) * 128], wv[:, k, :], k == 0, k == 7,
                               [wvk, "HT"], ["ps1"])
                    OP("act", "activation", out=dst[:, t4:t4 + nt, :],
                       in_=PS[1][:, 0:nt * 128].rearrange("p (t c) -> p t c", c=128), func=AF.Copy, r=["ps1"], w=[dk])
            for d in range(2):
                OP("dve", "memset", ap=S[:], constant=0.0, w=["gS"])
                OP("dve", "memset", ap=SB_[:], constant=0.0, w=["gSB"])
                tiles = list(range(18)) if d == 0 else [1, 0] + list(range(17, 1, -1))
                chunks = (0, 1) if d == 0 else (1, 0)
                def prep(i):
                    tt = tiles[i]
                    t0 = tt * 128
                    pr = i % 2
                    e1, qd, kdt = E1s[pr], QDs[pr], KDTs[pr]
                    e1k, qdk, kdtk = "gE1%d" % pr, "gQD%d" % pr, "gKDT%d" % pr
                    MM(PS[2][:, 0:128], GA[d][0:17, t0:t0 + 128], WG[0:17, d, 128 * c:128 * c + 128], True, True,
                       ["gGA%d" % d, "gWG"], ["ps2"])
                    OP("act", "activation", out=EX[:], in_=PS[2][:, 0:128], func=AF.Exp, scale=-1.0, r=["ps2"], w=["gEX"])
                    OP("act", "activation", out=NEG[:], in_=EX[:], func=AF.Ln, bias=ONE[:, 0:1], scale=1.0, r=["gEX", "gONE"],
                       w=["gNEG"])
                    MM(PS[3][:, 0:128], NEG[:], TRI[:, d, :], True, True, ["gNEG", "gTRI"], ["ps3"])
                    MM(PS[3][:, 128:256], TRI[:, d, :], NEG[:], True, True, ["gNEG", "gTRI"], ["ps3"])
                    OP("act", "activation", out=e1[:], in_=PS[3][:, 0:128], func=AF.Exp, scale=-1.0 / 16, r=["ps3"], w=[e1k])
                    OP("act", "activation", out=E2[:], in_=PS[3][:, 0:128], func=AF.Exp, scale=1.0 / 16, r=["ps3"], w=["gE2"])
                    OP("act", "activation", out=E2T[:], in_=PS[3][:, 128:256], func=AF.Exp, scale=1.0 / 16, r=["ps3"], w=["gE2T"])
                    OP("dve", "tensor_tensor", out=qd[:], in0=QT[:, t0:t0 + 128], in1=e1[:], op=ALU.mult, r=["gQT", e1k], w=[qdk])
                    OP("pool", "tensor_tensor", out=KD[:], in0=KT[:, t0:t0 + 128], in1=E2[:], op=ALU.mult, r=["gKT", "gE2"], w=["gKD"])
                    OP("pool", "tensor_tensor", out=kdt[:], in0=KTK[:, tt, :], in1=E2T[:], op=ALU.mult, r=["gKTK", "gE2T"],
                       w=[kdtk])
                    for hh in range(2):
                        hb = 64 * hh
                        am, amk = AMs[pr][hh], "gAM%d_%d" % (pr, hh)
                        MM(PS[4 + hh][:, 0:128], KD[hb:hb + 64, :], qd[hb:hb + 64, :], True, True, ["gKD", qdk], ["ps%d" % (4 + hh)])
                        OP("dve", "tensor_tensor", out=am[:], in0=PS[4 + hh][:, 0:128], in1=TRI[:, d, :], op=ALU.mult,
                           r=["ps%d" % (4 + hh), "gTRI"], w=[amk])

                def recur(i):
                    tt = tiles[i]
                    t0 = tt * 128
                    pr = i % 2
                    e1, qd, kdt = E1s[pr], QDs[pr], KDTs[pr]
                    e1k, qdk, kdtk = "gE1%d" % pr, "gQD%d" % pr, "gKDT%d" % pr
                    for hh in range(2):
                        hb = 64 * hh
                        am, amk = AMs[pr][hh], "gAM%d_%d" % (pr, hh)
                        MM(PS[6][hb:hb + 64, 0:128], VTK[:, tt, hb:hb + 64], am[:], True, False, ["gVTK", amk], ["ps6"])
                    for ch in chunks:
                        cs = 64 * ch
                        for hh in range(2):
                            hb = 64 * hh
                            MM(PS[6][hb:hb + 64, cs:cs + 64], SB_[hb:hb + 64, :], qd[hb:hb + 64, cs:cs + 64], False, True,
                               ["gSB", qdk], ["ps6"])
                            MM(PS[7][hb:hb + 64, 0:64], kdt[cs:cs + 64, hb:hb + 64], VTK[cs:cs + 64, tt, hb:hb + 64], True, True,
                               [kdtk, "gVTK"], ["ps7"])
                        dcol = cs + 63 if d == 0 else cs
                        OP("dve", "tensor_tensor", out=S[:], in0=S[:], in1=PS[7][:, 0:64], op=ALU.add, r=["gS", "ps7"], w=["gS"])
                        OP("dve", "tensor_scalar", out=S[:], in0=S[:], scalar1=e1[:, dcol:dcol + 1], scalar2=None, op0=ALU.mult,
                           r=["gS", e1k], w=["gS"])
                        OP("act", "activation", out=SB_[:], in_=S[:], func=AF.Copy, r=["gS"], w=["gSB"])
                    if d == 0:
                        OP("act", "activation", out=OF[:, t0:t0 + 128], in_=PS[6][:, 0:128], func=AF.Copy, r=["ps6"], w=["gOF"])
                    else:
                        OP("pool", "tensor_copy", out=OT[:], in_=OF[:, t0:t0 + 128], r=["gOF"], w=["gOT"])
                        OP("dve", "tensor_tensor", out=OF[:, t0:t0 + 128], in0=OT[:], in1=PS[6][:, 0:128], op=ALU.add,
                           r=["gOT", "ps6"], w=["gOF"])

                prep(0)
                for i in range(len(tiles)):
                    if i + 1 < len(tiles):
                        prep(i + 1)
                    recur(i)
            for (a, b) in TB:
                w = b - a
                OP("act", "activation", out=SQ[:, :w], in_=OF[:, a:b], func=AF.Square, r=["gOF"], w=["gSQ"])
                MM(PS[0][:, :w], BD[:], SQ[:, :w], True, True, ["gBD", "gSQ"], ["ps0"])
                OP("act", "activation", out=RS[:, :w], in_=PS[0][:, :w], func=AF.Sqrt, scale=1.0 / 64, bias=EPS_[:, 0:1],
                   r=["ps0", "gEPS"], w=["gRS"])
                OP("dve", "reciprocal", out=RS[:, :w], in_=RS[:, :w], r=["gRS"], w=["gRS"])
                OP("dve", "scalar_tensor_tensor", out=RS[:, :w], in0=OF[:, a:b], scalar=GN[:, l:l + 1], in1=RS[:, :w], op0=ALU.mult,
                   op1=ALU.mult, r=["gOF", "gGN", "gRS"], w=["gRS"])
                OP("pool", "tensor_tensor", out=YT[:, 4 + c, a:b], in0=RS[:, :w], in1=SR[:, a:b], op=ALU.mult, r=["gRS", "gSR"],
                   w=["YT"])


def host_prep(inputs):
    f = np.float32
    w_in = np.asarray(inputs["w_in"], f)
    sq = w_in[:, :, C_SQ:C_SQ + 256]
    sk = w_in[:, :, C_SK:C_SK + 128]
    dup = np.concatenate([np.arange(64), np.arange(64), 64 + np.arange(64), 64 + np.arange(64)])
    w_in_ext = np.concatenate([w_in, sq[:, :, _rope_perm(4)], sk[:, :, dup], sk[:, :, _rope_perm(2)][:, :, dup]], axis=2)
    assert w_in_ext.shape[2] == NEXT
    rc, rs = _rope_tables()
    kl = np.arange(128)[:, None]
    ql = np.arange(128)[None, :]
    maskAB = np.stack([(kl <= ql), (ql <= kl)], 1).astype(f)
    gv = np.stack([inputs["g_pre_mix"], inputs["g_post_mix"], inputs["g_pre_ffn"], inputs["g_post_ffn"]], 1)
    gvec = np.ascontiguousarray(np.asarray(gv, f).reshape(DEPTH, 4, 8, 128).transpose(3, 0, 1, 2))
    b_modT = np.ascontiguousarray(np.asarray(inputs["b_mod"], f).reshape(DEPTH, 48, 128).transpose(2, 0, 1))
    convT = np.ascontiguousarray(np.asarray(inputs["ffn_conv"], f).reshape(DEPTH, 3, 22, 128).transpose(3, 0, 1, 2))
    sink = np.asarray(inputs["swa_sink"], f)
    sinkT = np.zeros((128, DEPTH, 2), f)
    for c in range(2):
        sinkT[0:64, :, c] = sink[None, :, 2 * c]
        sinkT[64:128, :, c] = sink[None, :, 2 * c + 1]
    rpb = np.asarray(inputs["na_rpb"], f)
    kc = np.arange(64)[:, None]
    qc = np.arange(64)[None, :]
    dcx = np.clip(kc - qc, -15, 15) + 15
    na_exp = rpb[:, :, ::-1, :][:, :, :, dcx]
    na_exp = np.ascontiguousarray(na_exp.transpose(0, 1, 3, 2, 4)).reshape(DEPTH, 4, 64, 15 * 64)
    ws = np.clip(np.arange(64) - 8, 0, 48)
    mc = ((kc >= ws[None, :]) & (kc < ws[None, :] + 16)).astype(f)
    mcol = np.tile(np.tile(mc[:, None, :], (1, 15, 1)).reshape(64, 960), (2, 1))
    jj = np.arange(128)[:, None]
    ii = np.arange(128)[None, :]
    same = (jj // 64) == (ii // 64)
    tri = np.stack([(same & (jj <= ii)), (same & (jj >= ii))], 1).astype(f)
    bd64 = same.astype(f)
    gnT = np.ascontiguousarray(np.tile(np.asarray(inputs["gla_g_norm"], f).T, (2, 1)))
    L = DEPTH
    lre = np.asarray(inputs["s5_lam_re"], f).reshape(L, 32, 64)
    lim = np.asarray(inputs["s5_lam_im"], f).reshape(L, 32, 64)
    lam = np.stack([lre, lim], 1)
    lamT = np.ascontiguousarray(np.tile(lam.transpose(3, 0, 1, 2), (2, 1, 1, 1)))
    stepT = np.ascontiguousarray(np.broadcast_to(np.asarray(inputs["s5_log_step"], f).reshape(L, 32)[None], (128, L, 32)))
    dskT = np.ascontiguousarray(np.asarray(inputs["s5_d"], f).reshape(L, 2, 128).transpose(2, 0, 1))
    sgn = np.ones((128, 2), f)
    sgn[0:64, 0] = -1.0
    sgn[64:128, 1] = -1.0
    jmat = np.zeros((128, 128), f)
    for sp in range(64):
        jmat[sp + 64, sp] = -1.0
        jmat[sp, sp + 64] = 1.0
    bre = np.asarray(inputs["s5_b_re"], f)
    bim = np.asarray(inputs["s5_b_im"], f)
    cre = np.asarray(inputs["s5_c_re"], f)
    cim = np.asarray(inputs["s5_c_im"], f)
    B1 = np.zeros((L, 2, 16, 128, 128), f)
    B2 = np.zeros((L, 2, 16, 128, 128), f)
    C1 = np.zeros((L, 2, 16, 128, 128), f)
    C2 = np.zeros((L, 2, 16, 128, 128), f)
    for gi in range(16):
        r0 = 16 * (gi % 8)
        B1[:, :, gi, r0:r0 + 16, 0:64] = bre[:, :, gi].transpose(0, 1, 3, 2)
        B1[:, :, gi, r0:r0 + 16, 64:128] = bim[:, :, gi].transpose(0, 1, 3, 2)
        B2[:, :, gi, r0:r0 + 16, 0:64] = bim[:, :, gi].transpose(0, 1, 3, 2)
        B2[:, :, gi, r0:r0 + 16, 64:128] = bre[:, :, gi].transpose(0, 1, 3, 2)
        C1[:, :, gi, 0:64, r0:r0 + 16] = cre[:, :, gi].transpose(0, 1, 3, 2)
        C1[:, :, gi, 64:128, r0:r0 + 16] = cim[:, :, gi].transpose(0, 1, 3, 2)
        C2[:, :, gi, 0:64, r0:r0 + 16] = cim[:, :, gi].transpose(0, 1, 3, 2)
        C2[:, :, gi, 64:128, r0:r0 + 16] = cre[:, :, gi].transpose(0, 1, 3, 2)
    shared = {
        "lamT": lamT, "stepT": stepT, "dskT": dskT, "sgn": sgn, "jmat": jmat, "s5_w_glu": np.asarray(inputs["s5_w_glu"], f),
        "s5B1": B1, "s5B2": B2, "s5C1": C1, "s5C2": C2,
        "tri": tri, "bd64": bd64.astype(ml_dtypes.bfloat16), "gnT": gnT,
        "gla_w_gate2": np.asarray(inputs["gla_w_gate2"], f), "gla_b_gate": np.asarray(inputs["gla_b_gate"], f),
        "w_mod": np.asarray(inputs["w_mod"], f), "b_modT": b_modT, "gvec": gvec, "w_in_ext": np.ascontiguousarray(w_in_ext),
        "w_out": np.asarray(inputs["w_out"], f), "ffn_w_up": np.asarray(inputs["ffn_w_up"], f),
        "ffn_w_down": np.asarray(inputs["ffn_w_down"], f), "convT": convT,
        "ropeC": rc.astype(ml_dtypes.bfloat16), "ropeS": rs.astype(ml_dtypes.bfloat16),
        "maskAB": maskAB.astype(ml_dtypes.bfloat16), "identf": np.eye(128, dtype=f), "sinkT": sinkT,
        "na_exp": na_exp, "mcol": np.ascontiguousarray(mcol), "mc8": np.ascontiguousarray(8.0 * mcol),
        "mcn": np.ascontiguousarray((mcol - 1.0) * 240000.0),
        "idb": (np.arange(128)[:, None] % 64 == np.arange(64)[None, :]).astype(ml_dtypes.bfloat16),
    }
    x = np.asarray(inputs["x"], f)
    ctx = np.asarray(inputs["ctx"], f)
    c = np.asarray(inputs["c"], f)
    cc = np.asarray(inputs["c_ctx"], f)
    in_maps = []
    for core in range(8):
        b0 = 2 * core
        xcat = np.concatenate([ctx[b0:b0 + 2], x[b0:b0 + 2]], axis=1)
        cs = np.stack([c[b0], c[b0 + 1], cc], 0)
        cTm = np.ascontiguousarray(cs.reshape(3, 8, 128).transpose(2, 1, 0))
        m = dict(shared)
        m["xcat"] = np.ascontiguousarray(xcat)
        m["cT"] = cTm
        in_maps.append(m)
    return in_maps


L_FIRST = ("w_mod", "w_in_ext", "w_out", "ffn_w_up", "ffn_w_down", "na_exp", "s5B1", "s5B2", "s5C1", "s5C2", "s5_w_glu",
           "gla_w_gate2", "gla_b_gate")
L_SECOND = ("b_modT", "gvec", "convT", "sinkT", "lamT", "stepT", "dskT")

FUSED = True


def kernel(**inputs):
    in_maps = host_prep(inputs)
    if FUSED:
        nc = bass.Bass("TRN2", target_bir_lowering=False)
        build(nc)
        res = run_bass_kernel_spmd(nc, in_maps, core_ids=list(range(8)))
        outs = [r["out"] for r in res.results]
        return np.concatenate(outs, axis=0).astype(np.float32)
    cur = [m["xcat"] for m in in_maps]
    for l in range(DEPTH):
        base = {}
        m0 = in_maps[0]
        for k, v in m0.items():
            if k in ("xcat", "cT"):
                continue
            if k in L_FIRST:
                base[k] = np.ascontiguousarray(v[l:l + 1])
            elif k in L_SECOND:
                base[k] = np.ascontiguousarray(v[:, l:l + 1])
            elif k == "gnT":
                base[k] = np.ascontiguousarray(v[:, l:l + 1])
            else:
                base[k] = v
        for bi in range(2):
            maps = []
            for core in range(8):
                m = dict(base)
                m["xcat"] = np.ascontiguousarray(cur[core][[bi, 1 - bi]])
                cTm = in_maps[core]["cT"]
                m["cT"] = np.ascontiguousarray(cTm[:, :, [bi, 1 - bi, 2]])
                maps.append(m)
            nc = bass.Bass("TRN2", target_bir_lowering=False)
            build(nc, nlayers=1, nbatch=1, ldim=1, full_out=True)
            res = run_bass_kernel_spmd(nc, maps, core_ids=list(range(8)))
            for core in range(8):
                new = np.array(cur[core])
                new[bi] = res.results[core]["out"][0]
                cur[core] = new
    outs = [c[:, NCX:, :] for c in cur]
    return np.concatenate(outs, axis=0).astype(np.float32)
```

```python
import contextlib
import math
import numpy as np
import ml_dtypes
import concourse.bass as bass
import concourse.mybir as mybir
from concourse.bass_utils import run_bass_kernel_spmd

F32 = mybir.dt.float32
BF16 = mybir.dt.bfloat16
ALU = mybir.AluOpType
AF = mybir.ActivationFunctionType

ENG = ("pe", "act", "dve", "pool", "sp")
NDMA = 24


class P:
    def __init__(self, nc, same_eng_sync=True):
        self.nc = nc
        self.ops = {e: [] for e in ENG}
        self.cnt = {e: 0 for e in ENG}
        self.waited = {e: {} for e in ENG}
        self.last_w = {}
        self.readers = {}
        self.dma_nextq = {}
        self.dma_cnt = [0] * NDMA
        self.dma_last_tok = [None] * NDMA
        self.same = same_eng_sync
        self.out_toks = []
        self.bar = []

    def barrier(self):
        self.bar = [("e", e, self.cnt[e]) for e in ENG if self.cnt[e]] + \
                   [("d", k, self.dma_cnt[k]) for k in range(NDMA) if self.dma_cnt[k]]

    def _deps(self, eng, reads, writes):
        deps = list(self.bar)
        for k in reads:
            t = self.last_w.get(k)
            if t is not None:
                deps.append(t)
        for k in writes:
            t = self.last_w.get(k)
            if t is not None:
                deps.append(t)
            deps.extend(self.readers.get(k, ()))
        return deps

    def _waits(self, eng, deps):
        w = self.waited[eng]
        best = {}
        for t in deps:
            if t[0] == "e":
                _, e2, idx = t
                if e2 == eng and (not self.same or eng == "pe"):
                    continue
                key = e2
            else:
                key = ("d", t[1])
            if w.get(key, 0) >= t[2]:
                continue
            if key not in best or best[key][2] < t[2]:
                best[key] = t
        for key, t in best.items():
            w[key] = t[2]
        return list(best.values())

    def _record(self, tok, reads, writes):
        for k in reads:
            lst = self.readers.setdefault(k, [])
            lst.append(tok)
            if len(lst) > 64:
                best = {}
                for t in lst:
                    kk = t[:2]
                    if kk not in best or best[kk][2] < t[2]:
                        best[kk] = t
                self.readers[k] = list(best.values())
        for k in writes:
            self.last_w[k] = tok
            self.readers[k] = []

    def op(self, eng, fn, reads=(), writes=()):
        deps = self._deps(eng, reads, writes)
        waits = self._waits(eng, deps)
        self.cnt[eng] += 1
        tok = ("e", eng, self.cnt[eng])
        self.ops[eng].append((waits, fn, ("e", eng)))
        self._record(tok, reads, writes)
        return tok

    def dma(self, q, fn, reads=(), writes=(), is_out=False):
        lo, n = (0, 16) if q == "sp" else (16, NDMA - 16)
        cur = self.dma_nextq.get(q, 0)
        k = lo + cur
        self.dma_nextq[q] = (cur + 1) % n
        deps = self._deps(q, reads, writes)
        if self.dma_last_tok[k] is not None:
            deps.append(self.dma_last_tok[k])
        waits = self._waits(q, deps)
        self.dma_cnt[k] += 16
        tok = ("d", k, self.dma_cnt[k])
        self.dma_last_tok[k] = tok
        self.ops[q].append((waits, fn, ("d", k)))
        self._record(tok, reads, writes)
        if is_out:
            self.out_toks.append(tok)
        return tok

    def emit(self):
        nc = self.nc
        with contextlib.ExitStack() as es:
            esem = {e: es.enter_context(nc.semaphore("s_" + e)) for e in ENG}
            dsem = [es.enter_context(nc.semaphore("d%d" % i)) for i in range(NDMA)]
            fin = list(self.out_toks)
            for e in ENG:
                if self.cnt[e]:
                    fin.append(("e", e, self.cnt[e]))
            for k in range(NDMA):
                if self.dma_cnt[k]:
                    fin.append(("d", k, self.dma_cnt[k]))
            block = es.enter_context(nc.Block())

            def run(eng_name, eng):
                for waits, fn, kind in self.ops[eng_name]:
                    for t in waits:
                        if t[0] == "e":
                            eng.wait_ge(esem[t[1]], t[2])
                        else:
                            eng.wait_ge(dsem[t[1]], t[2])
                    ins = fn(eng)
                    if kind[0] == "e":
                        ins.then_inc(esem[kind[1]], 1)
                    else:
                        ins.then_inc(dsem[kind[1]], 16)

            @block.tensor
            def _(e):
                run("pe", e)

            @block.scalar
            def _(e):
                run("act", e)

            @block.vector
            def _(e):
                run("dve", e)

            @block.gpsimd
            def _(e):
                run("pool", e)

            @block.sync
            def _(e):
                run("sp", e)
                for t in fin:
                    if t[0] == "e":
                        if t[1] != "sp":
                            e.wait_ge(esem[t[1]], t[2])
                    else:
                        e.wait_ge(dsem[t[1]], t[2])


NT = 2304
NCX = 256
NLAT = 2048
D = 1024
DEPTH = 4
LD = [4]
DFF = 2816
EPS = 1e-6
TB = [(0, 256)] + [(256 + 512 * i, 256 + 512 * (i + 1)) for i in range(4)]
C_A, C_NAQ, C_NAK, C_NAV = 0, 256, 512, 768
C_GQ, C_GK, C_GV, C_GF, C_GB, C_GR = 1024, 1280, 1536, 1792, 1808, 1824
C_SQ, C_SK, C_SV = 2080, 2336, 2464
C_SQP, C_SKD, C_SKDP, NEXT = 2592, 2848, 3104, 3360


def _rope_perm(nh):
    idx = np.arange(nh * 64).reshape(nh, 4, 16)
    return idx[:, [1, 0, 3, 2], :].reshape(-1)


def _rope_tables():
    cos = np.ones((64, NT), np.float32)
    sin = np.zeros((64, NT), np.float32)
    t = np.arange(NLAT)
    pos = (t // 64, t % 64)
    inv = 10000.0 ** (-np.arange(0, 32, 2, dtype=np.float32) / 32)
    for half in range(2):
        ang = pos[half].astype(np.float32)[None, :] * inv[:, None]
        c, s = np.cos(ang), np.sin(ang)
        b = 32 * half
        cos[b:b + 16, NCX:] = c
        cos[b + 16:b + 32, NCX:] = c
        sin[b:b + 16, NCX:] = -s
        sin[b + 16:b + 32, NCX:] = s
    return np.concatenate([cos, cos], 0), np.concatenate([sin, sin], 0)


class Ctx:
    pass


_UN = [0]


def UN():
    _UN[0] += 1
    return "t%d_" % _UN[0]


def build(nc, dbg=None, nlayers=DEPTH, nbatch=2, mixers=("s5", "na", "gla", "swa"), ldim=DEPTH, full_out=False):
    LD[0] = ldim
    _UN[0] = 0
    p = P(nc)
    g = Ctx()
    g.p, g.nc = p, nc
    dram = {}

    def din(name, shape, dt=F32):
        dram[name] = nc.dram_tensor(name, list(shape), dt, kind="ExternalInput").ap()
        return dram[name]

    xcat = din("xcat", [2, NT, D])
    cT = din("cT", [128, 8, 3])
    w_mod = din("w_mod", [LD[0], D, 6 * D])
    b_modT = din("b_modT", [128, LD[0], 48])
    gvec = din("gvec", [128, LD[0], 4, 8])
    w_in = din("w_in_ext", [LD[0], D, NEXT])
    w_out = din("w_out", [LD[0], D, D])
    w_up = din("ffn_w_up", [LD[0], D, 2 * DFF])
    w_down = din("ffn_w_down", [LD[0], DFF, D])
    convT = din("convT", [128, LD[0], 3, 22])
    ropeC = din("ropeC", [128, NT], BF16)
    ropeS = din("ropeS", [128, NT], BF16)
    maskAB = din("maskAB", [128, 2, 128], BF16)
    identf = din("identf", [128, 128])
    sinkT = din("sinkT", [128, LD[0], 2])
    na_exp = din("na_exp", [LD[0], 4, 64, 15 * 64])
    mcol = din("mcol", [128, 15 * 64])
    din("mc8", [128, 15 * 64])
    din("mcn", [128, 15 * 64])
    din("idb", [128, 64], BF16)
    din("tri", [128, 2, 128])
    din("bd64", [128, 128], BF16)
    din("gnT", [128, LD[0]])
    din("gla_w_gate2", [LD[0], 2, 16, 256])
    din("gla_b_gate", [LD[0], 2, 256])
    din("lamT", [128, LD[0], 2, 32])
    din("stepT", [128, LD[0], 32])
    din("dskT", [128, LD[0], 2])
    din("sgn", [128, 2])
    din("jmat", [128, 128])
    din("s5_w_glu", [LD[0], 256, 256])
    for nm in ("s5B1", "s5B2", "s5C1", "s5C2"):
        din(nm, [LD[0], 2, 16, 128, 128])
        dram[nm + "_bf"] = nc.dram_tensor(nm + "_bf", [LD[0], 2, 16, 128, 128], BF16).ap()
    g.dram = dram
    out = nc.dram_tensor("out", [2, NT if full_out else NLAT, D], F32, kind="ExternalOutput").ap()
    dbg_aps = {}
    if dbg:
        for name, shape in dbg.items():
            dbg_aps[name] = nc.dram_tensor("dbg_" + name, list(shape), F32, kind="ExternalOutput").ap()

    w_in_bf = nc.dram_tensor("w_in_bf", [LD[0], D, NEXT], BF16).ap()
    w_out_bf = nc.dram_tensor("w_out_bf", [LD[0], D, D], BF16).ap()
    w_up_bf = nc.dram_tensor("w_up_bf", [LD[0], D, 2 * DFF], BF16).ap()
    w_down_bf = nc.dram_tensor("w_down_bf", [LD[0], DFF, D], BF16).ap()

    def wkeys(key, l, n):
        return ["%s%d_%d" % (key, l, i) for i in range(n)]
    g.wkeys = wkeys
    g._uid = [0]

    def OP(eng, method, r=(), w=(), **kw):
        return p.op(eng, lambda e, kw=kw, method=method: getattr(e, method)(**kw), reads=r, writes=w)

    def MM(out_, lhsT, rhs, start, stop, r, w):
        return p.op("pe", lambda e: e.matmul(out_, lhsT=lhsT, rhs=rhs, start=start, stop=stop), reads=r, writes=w)

    def DMA(q, out_, in_, r=(), w=(), is_out=False, **kw):
        return p.dma(q, lambda e, kw=kw: e.dma_start(out=out_, in_=in_, **kw), reads=r, writes=w, is_out=is_out)

    g.OP, g.MM, g.DMA = OP, MM, DMA

    for l in range(nlayers):
        for (src, dst, rows, key) in ((w_in, w_in_bf, D, "w_in_bf"), (w_out, w_out_bf, D, "w_out_bf"),
                                      (w_up, w_up_bf, D, "w_up_bf"), (w_down, w_down_bf, DFF, "w_down_bf")):
            for r0 in range(0, rows, 128):
                DMA("pool", dst[l, r0:r0 + 128, :], src[l, r0:r0 + 128, :], w=["%s%d_%d" % (key, l, r0 // 128)],
                    max_dma_last_dim=4096)

    for l in range(nlayers):
        for nm in ("s5B1", "s5B2", "s5C1", "s5C2"):
            for d in range(2):
                DMA("pool", dram[nm + "_bf"][l, d].rearrange("g p s -> (g p) s"), dram[nm][l, d].rearrange("g p s -> (g p) s"),
                    w=["%s_bf%d" % (nm, l)] if d == 1 else ["%s_bf%d_d0" % (nm, l)], max_dma_last_dim=4096)

    es = contextlib.ExitStack()

    def sb(name, shape, dt):
        return es.enter_context(nc.sbuf_tensor(UN() + name, list(shape), dt))

    MODT = sb("MODT", [128, LD[0], 48, 3], F32)
    DER = sb("DER", [128, LD[0], 3, 6, 8], F32)
    GV = sb("GV", [128, LD[0], 4, 8], F32)
    CONV = sb("CONV", [128, LD[0], 3, 22], F32)
    ONESB = sb("ONESB", [128, 128], BF16)
    IDF = sb("IDF", [128, 128], F32)
    MAB = sb("MAB", [128, 2, 128], BF16)
    ESINK = sb("ESINK", [128, LD[0], 2], F32)
    PS = [es.enter_context(nc.psum_tensor("ps%d" % i, [128, 512], F32)) for i in range(8)]
    g.PS = PS

    OP("dve", "memset", ap=ONESB[:], constant=1.0, w=["ONESB"])
    DMA("sp", IDF[:], identf, w=["IDF"])
    DMA("sp", MAB[:], maskAB, w=["MAB"])
    DMA("sp", GV[:], gvec, w=["GV"])
    DMA("sp", CONV[:], convT, w=["CONV"])
    DMA("sp", ESINK[:], sinkT, w=["ESINK"])
    OP("act", "activation", out=ESINK[:], in_=ESINK[:], func=AF.Exp, r=["ESINK"], w=["ESINK"])

    with contextlib.ExitStack() as s1:
        SCT = s1.enter_context(nc.sbuf_tensor(UN() + "SCT", [128, 8, 3], F32))
        BM = s1.enter_context(nc.sbuf_tensor(UN() + "BM", [128, LD[0], 48], F32))
        WM = [s1.enter_context(nc.sbuf_tensor(UN() + "WM%d" % i, [128, 8, 512], F32)) for i in range(2)]
        DMA("sp", SCT[:], cT, w=["SCT"])
        DMA("sp", BM[:], b_modT, w=["BM"])
        OP("act", "activation", out=SCT[:], in_=SCT[:], func=AF.Silu, r=["SCT"], w=["SCT"])
        it = 0
        for l in range(nlayers):
            for cg in range(12):
                wb = WM[it % 2]
                wk = "WM%d" % (it % 2)
                it += 1
                DMA("sp", wb[:], w_mod[l, :, cg * 512:(cg + 1) * 512].rearrange("(k p) c -> p k c", p=128), w=[wk])
                for fc in range(4):
                    f = cg * 4 + fc
                    for k in range(8):
                        MM(PS[0][:, f * 3:f * 3 + 3], wb[:, k, fc * 128:(fc + 1) * 128], SCT[:, k, :], k == 0, k == 7,
                           [wk, "SCT"], ["ps0"])
            for j in range(3):
                OP("dve", "tensor_tensor", out=MODT[:, l, :, j], in0=PS[0][:, 0:144].rearrange("p (f j) -> p f j", j=3)[:, :, j],
                   in1=BM[:, l, :], op=ALU.add, r=["ps0", "BM"], w=["MODT"])
            for j in range(3):
                OP("dve", "scalar_tensor_tensor", out=DER[:, l, j, 0, :], in0=MODT[:, l, 8:16, j], scalar=1.0, in1=GV[:, l, 0, :],
                   op0=ALU.add, op1=ALU.mult, r=["MODT", "GV"], w=["DER"])
                OP("dve", "tensor_copy", out=DER[:, l, j, 1, :], in_=MODT[:, l, 0:8, j], r=["MODT"], w=["DER"])
                OP("dve", "tensor_tensor", out=DER[:, l, j, 2, :], in0=MODT[:, l, 16:24, j], in1=GV[:, l, 1, :], op=ALU.mult,
                   r=["MODT", "GV"], w=["DER"])
                OP("dve", "scalar_tensor_tensor", out=DER[:, l, j, 3, :], in0=MODT[:, l, 32:40, j], scalar=1.0, in1=GV[:, l, 2, :],
                   op0=ALU.add, op1=ALU.mult, r=["MODT", "GV"], w=["DER"])
                OP("dve", "tensor_copy", out=DER[:, l, j, 4, :], in_=MODT[:, l, 24:32, j], r=["MODT"], w=["DER"])
                OP("dve", "tensor_tensor", out=DER[:, l, j, 5, :], in0=MODT[:, l, 40:48, j], in1=GV[:, l, 3, :], op=ALU.mult,
                   r=["MODT", "GV"], w=["DER"])
    p.barrier()

    X = sb("X", [128, 8, NT], F32)
    g.X, g.DER, g.ONESB, g.MAB, g.ESINK, g.CONV = X, DER, ONESB, MAB, ESINK, CONV
    g.w_in_bf, g.w_out_bf, g.w_up_bf, g.w_down_bf = w_in_bf, w_out_bf, w_up_bf, w_down_bf
    g.ropeC, g.ropeS, g.na_exp, g.mcol = ropeC, ropeS, na_exp, mcol
    g.dbg_aps = dbg_aps

    def dump(name, ap, keys):
        if name in dbg_aps:
            DMA("pool", dbg_aps[name], ap, r=keys, is_out=True, max_dma_last_dim=2048)
    g.dump = dump

    for bi in range(nbatch):
        with contextlib.ExitStack() as s2:
            XS = [s2.enter_context(nc.sbuf_tensor(UN() + "XS%d" % i, [128, D], F32)) for i in range(2)]
            for tt in range(18):
                xs, xk = XS[tt % 2], "XS%d" % (tt % 2)
                DMA("sp", xs[:], xcat[bi, tt * 128:(tt + 1) * 128, :], w=[xk])
                for hh in range(2):
                    bank = PS[hh]
                    for kk in range(4):
                        k = hh * 4 + kk
                        p.op("pe", lambda e, o=bank[:, kk * 128:(kk + 1) * 128], i=xs[:, k * 128:(k + 1) * 128]:
                             e.transpose(out=o, in_=i, identity=IDF[:]), reads=[xk, "IDF"], writes=["ps%d" % hh])
                    if hh == 0:
                        OP("act", "activation", out=X[:, 0:4, tt * 128:(tt + 1) * 128],
                           in_=bank[:, :].rearrange("p (k t) -> p k t", t=128), func=AF.Copy, r=["ps0"], w=["X"])
                    else:
                        OP("dve", "tensor_copy", out=X[:, 4:8, tt * 128:(tt + 1) * 128],
                           in_=bank[:, :].rearrange("p (k t) -> p k t", t=128), r=["ps1"], w=["X"])
        p.barrier()
        for l in range(nlayers):
            layer(g, bi, l, (l == DEPTH - 1) and not full_out, mixers)
        with contextlib.ExitStack() as s3:
            OS_ = [s3.enter_context(nc.sbuf_tensor(UN() + "OST%d" % i, [128, D], F32)) for i in range(2)]
            for tt in range(18 if full_out else 16):
                ot, ok = OS_[tt % 2], "OST%d" % (tt % 2)
                t0 = (0 if full_out else NCX) + tt * 128
                for hh in range(2):
                    bank = PS[hh]
                    for kk in range(4):
                        k = hh * 4 + kk
                        p.op("pe", lambda e, o=bank[:, kk * 128:(kk + 1) * 128], i=X[:, k, t0:t0 + 128]:
                             e.transpose(out=o, in_=i, identity=IDF[:]), reads=["X", "IDF"], writes=["ps%d" % hh])
                    if hh == 0:
                        OP("act", "activation", out=ot[:, 0:512], in_=bank[:, :], func=AF.Copy, r=["ps0"], w=[ok])
                    else:
                        OP("dve", "tensor_copy", out=ot[:, 512:1024], in_=bank[:, :], r=["ps1"], w=[ok])
                DMA("sp", out[bi, tt * 128:(tt + 1) * 128, :], ot[:], r=[ok], is_out=True)
        p.barrier()

    p.emit()
    es.close()
    return nc


def rms_stats(g, src_fn, nk, w, SQ, RS, psb, src_keys):
    OP, MM = g.OP, g.MM
    for k in range(nk):
        OP("act", "activation", out=SQ[:, k, :w], in_=src_fn(k), func=AF.Square, r=src_keys, w=["SQ"])
    for k in range(nk):
        MM(g.PS[psb][:, :w], g.ONESB[:], SQ[:, k, :w], k == 0, k == nk - 1, ["SQ", "ONESB"], ["ps%d" % psb])
    OP("act", "activation", out=RS[:, :w], in_=g.PS[psb][:, :w], func=AF.Sqrt, scale=1.0 / D, bias=g.EPSC[:, 0:1],
       r=["ps%d" % psb, "EPSC"], w=["RS"])
    OP("dve", "reciprocal", out=RS[:, :w], in_=RS[:, :w], r=["RS"], w=["RS"])


def layer(g, bi, l, last, mixers):
    nc, p, OP, MM, DMA = g.nc, g.p, g.OP, g.MM, g.DMA
    X, DER, PS = g.X, g.DER, g.PS
    with contextlib.ExitStack() as sl:
        def sb(name, shape, dt):
            return sl.enter_context(nc.sbuf_tensor(UN() + name, list(shape), dt))
        YT = sb("YT", [128, 8, NT], BF16)
        g.YT = YT
        EPSC = sb("EPSC", [128, 1], F32)
        g.EPSC = EPSC
        OP("dve", "memset", ap=EPSC[:], constant=EPS, w=["EPSC"])
        with contextlib.ExitStack() as sh:
            HT = sh.enter_context(nc.sbuf_tensor(UN() + "HT", [128, 8, NT], BF16))
            g.HT = HT
            with contextlib.ExitStack() as sa:
                SQ = sa.enter_context(nc.sbuf_tensor(UN() + "SQ", [128, 8, 512], BF16))
                RS = sa.enter_context(nc.sbuf_tensor(UN() + "RS", [128, 512], F32))
                TMP = [sa.enter_context(nc.sbuf_tensor(UN() + "TMPa%d" % i, [128, 512], F32)) for i in range(2)]
                for (a, b) in TB:
                    w = b - a
                    j = 2 if a < NCX else bi
                    rms_stats(g, lambda k: X[:, k, a:b], 8, w, SQ, RS, 0, ["X"])
                    for k in range(8):
                        tm, tk = TMP[k % 2], "TMPa%d" % (k % 2)
                        OP("dve", "tensor_tensor", out=tm[:, :w], in0=X[:, k, a:b], in1=RS[:, :w], op=ALU.mult,
                           r=["X", "RS"], w=[tk])
                        OP("act", "activation", out=HT[:, k, a:b], in_=tm[:, :w], func=AF.Identity,
                           scale=DER[:, l, j, 0, k:k + 1], bias=DER[:, l, j, 1, k:k + 1], r=[tk, "DER"], w=["HT"])
            p.barrier()
            g.dump("HT%d_%d" % (bi, l), HT[:], ["HT"])
            for nm, chs in (("s5", (0, 1)), ("na", (2, 3)), ("gla", (4, 5)), ("swa", (6, 7))):
                if nm not in mixers:
                    OP("pool", "memset", ap=YT[:, chs[0]:chs[1] + 1, :], constant=0.0, w=["YT"])
            if "s5" in mixers:
                s5_project(g, bi, l)
                p.barrier()
            if "swa" in mixers:
                attn_mixer(g, bi, l, "swa")
                p.barrier()
            if "na" in mixers:
                attn_mixer(g, bi, l, "na")
                p.barrier()
            if "gla" in mixers:
                gla_mixer(g, bi, l)
                p.barrier()
        p.barrier()
        if "s5" in mixers:
            s5_main(g, bi, l)
            p.barrier()
        g.dump("YT%d_%d" % (bi, l), YT[:], ["YT"])
        with contextlib.ExitStack() as sc:
            WO = sc.enter_context(nc.sbuf_tensor(UN() + "WO", [128, 8, D], BF16))
            OS_ = sc.enter_context(nc.sbuf_tensor(UN() + "OS", [128, 8, 512], F32))
            SQ = sc.enter_context(nc.sbuf_tensor(UN() + "SQ", [128, 8, 512], BF16))
            RS = sc.enter_context(nc.sbuf_tensor(UN() + "RS", [128, 512], F32))
            TMP = [sc.enter_context(nc.sbuf_tensor(UN() + "TMPc%d" % i, [128, 512], F32)) for i in range(2)]
            DMA("sp", WO[:], g.w_out_bf[l].rearrange("(k p) c -> p k c", p=128), r=g.wkeys("w_out_bf", l, 8), w=["WO"])
            for (a, b) in TB:
                if last and a < NCX:
                    continue
                w = b - a
                j = 2 if a < NCX else bi
                for dc in range(8):
                    bank = 1 + dc % 2
                    for k in range(8):
                        MM(PS[bank][:, :w], WO[:, k, dc * 128:(dc + 1) * 128], YT[:, k, a:b], k == 0, k == 7,
                           ["WO", "YT"], ["ps%d" % bank])
                    OP("dve", "tensor_copy", out=OS_[:, dc, :w], in_=PS[bank][:, :w], r=["ps%d" % bank], w=["OS%d" % dc])
                rms_stats(g, lambda k: OS_[:, k, :w], 8, w, SQ, RS, 0, ["OS%d" % k for k in range(8)])
                for k in range(8):
                    tm, tk = TMP[k % 2], "TMPc%d" % (k % 2)
                    OP("pool", "tensor_tensor", out=tm[:, :w], in0=OS_[:, k, :w], in1=RS[:, :w], op=ALU.mult,
                       r=["OS%d" % k, "RS"], w=[tk])
                    OP("dve", "scalar_tensor_tensor", out=X[:, k, a:b], in0=tm[:, :w], scalar=DER[:, l, j, 2, k:k + 1],
                       in1=X[:, k, a:b], op0=ALU.mult, op1=ALU.add, r=[tk, "DER", "X"], w=["X"])
    p.barrier()
    g.dump("X1_%d_%d" % (bi, l), X[:], ["X"])
    ffn(g, bi, l, last)
    p.barrier()
    g.dump("X2_%d_%d" % (bi, l), X[:], ["X"])


def ffn_blocks():
    blks = [(0, NCX, 0, NCX)]
    for i in range(5):
        oa = NCX + 410 * i
        ob = min(NCX + 410 * (i + 1), NT)
        blks.append((max(oa - 1, NCX), min(ob + 1, NT), oa, ob))
    return blks


def ffn(g, bi, l, last):
    nc, p, OP, MM, DMA = g.nc, g.p, g.OP, g.MM, g.DMA
    X, DER, PS = g.X, g.DER, g.PS
    with contextlib.ExitStack() as sf:
        def sb(name, shape, dt):
            return sf.enter_context(nc.sbuf_tensor(UN() + name, list(shape), dt))
        EPSC = sb("EPSC", [128, 1], F32)
        g.EPSC = EPSC
        OP("dve", "memset", ap=EPSC[:], constant=EPS, w=["EPSC"])
        HBs = [sb("HB%d" % i, [128, 8, 512], BF16) for i in range(2)]
        GB = sb("GB", [128, 22, 512], BF16)
        SQ = sb("SQ", [128, 8, 512], BF16)
        RS = sb("RS", [128, 512], F32)
        OS_ = sb("OS", [128, 8, 512], F32)
        TMP = [sb("TMPf%d" % i, [128, 512], F32) for i in range(2)]
        GS = [sb("GS%d" % i, [128, 514], F32) for i in range(2)]
        CV = [sb("CV%d" % i, [128, 512], F32) for i in range(2)]
        U1 = [sb("U1%d" % i, [128, 512], F32) for i in range(2)]
        WU = [sb("WU%d" % i, [128, 8, 256], BF16) for i in range(4)]
        WD = [sb("WD%d" % i, [128, 22, 128], BF16) for i in range(2)]
        wu_it = 0
        wd_it = 0
        blks = [bk for bk in ffn_blocks() if not (last and bk[0] < NCX)]

        def make_hb(bidx):
            (ca, cb, oa, ob) = blks[bidx]
            w = cb - ca
            j = 2 if ca < NCX else bi
            HB, hbk = HBs[bidx % 2], "HB%d" % (bidx % 2)
            rms_stats(g, lambda k: X[:, k, ca:cb], 8, w, SQ, RS, 0, ["X"])
            for k in range(8):
                tm, tk = TMP[k % 2], "TMPf%d" % (k % 2)
                OP("dve", "tensor_tensor", out=tm[:, :w], in0=X[:, k, ca:cb], in1=RS[:, :w], op=ALU.mult,
                   r=["X", "RS"], w=[tk])
                OP("act", "activation", out=HB[:, k, :w], in_=tm[:, :w], func=AF.Identity,
                   scale=DER[:, l, j, 3, k:k + 1], bias=DER[:, l, j, 4, k:k + 1], r=[tk, "DER"], w=[hbk])

        make_hb(0)
        for bidx, (ca, cb, oa, ob) in enumerate(blks):
            w = cb - ca
            wo = ob - oa
            off = oa - ca
            j = 2 if ca < NCX else bi
            HB, hbk = HBs[bidx % 2], "HB%d" % (bidx % 2)
            if bidx + 1 < len(blks):
                make_hb(bidx + 1)
            for jc in range(22):
                wu, wuk = WU[wu_it % 4], "WU%d" % (wu_it % 4)
                wu_it += 1
                DMA("sp", wu[:, :, 0:128], g.w_up_bf[l, :, jc * 128:(jc + 1) * 128].rearrange("(k p) c -> p k c", p=128),
                    r=g.wkeys("w_up_bf", l, 8), w=[wuk + "g"])
                DMA("sp", wu[:, :, 128:256],
                    g.w_up_bf[l, :, DFF + jc * 128:DFF + (jc + 1) * 128].rearrange("(k p) c -> p k c", p=128),
                    r=g.wkeys("w_up_bf", l, 8), w=[wuk + "v"])
                bg, bv = (1, 2)[jc % 2], (3, 4, 7, 5)[jc % 4]
                for k in range(8):
                    MM(PS[bg][:, :w], wu[:, k, 0:128], HB[:, k, :w], k == 0, k == 7, [wuk + "g", hbk], ["ps%d" % bg])
                for k in range(8):
                    MM(PS[bv][:, :w], wu[:, k, 128:256], HB[:, k, :w], k == 0, k == 7, [wuk + "v", hbk], ["ps%d" % bv])
                gs, gk = GS[jc % 2], "GS%d" % (jc % 2)
                cv, ck = CV[jc % 2], "CV%d" % (jc % 2)
                u1, uk = U1[jc % 2], "U1%d" % (jc % 2)
                OP("pool", "memset", ap=gs[:, 0:1], constant=0.0, w=[gk])
                OP("pool", "memset", ap=gs[:, w + 1:w + 2], constant=0.0, w=[gk])
                OP("act", "activation", out=gs[:, 1:w + 1], in_=PS[bg][:, :w], func=AF.Copy, r=["ps%d" % bg], w=[gk])
                s = 1 + off
                OP("pool", "tensor_scalar", out=cv[:, :wo], in0=gs[:, s - 1:s - 1 + wo], scalar1=g.CONV[:, l, 0, jc:jc + 1],
                   scalar2=0.0, op0=ALU.mult, op1=ALU.add, r=[gk, "CONV"], w=[ck])
                OP("dve", "scalar_tensor_tensor", out=cv[:, :wo], in0=gs[:, s:s + wo], scalar=g.CONV[:, l, 1, jc:jc + 1],
                   in1=cv[:, :wo], op0=ALU.mult, op1=ALU.add, r=[gk, "CONV", ck], w=[ck])
                OP("dve", "scalar_tensor_tensor", out=cv[:, :wo], in0=gs[:, s + 1:s + 1 + wo], scalar=g.CONV[:, l, 2, jc:jc + 1],
                   in1=cv[:, :wo], op0=ALU.mult, op1=ALU.add, r=[gk, "CONV", ck], w=[ck])
                OP("act", "activation", out=u1[:, :wo], in_=cv[:, :wo], func=AF.Gelu_apprx_tanh, r=[ck], w=[uk])
                OP("dve", "tensor_tensor", out=GB[:, jc, :wo], in0=PS[bv][:, off:off + wo], in1=u1[:, :wo], op=ALU.mult,
                   r=["ps%d" % bv, uk], w=["GB"])
            for dc in range(8):
                wd, wdk = WD[wd_it % 2], "WD%d" % (wd_it % 2)
                wd_it += 1
                DMA("sp", wd[:], g.w_down_bf[l, :, dc * 128:(dc + 1) * 128].rearrange("(k p) c -> p k c", p=128),
                    r=g.wkeys("w_down_bf", l, 22), w=[wdk])
                bank = (6, 0)[dc % 2]
                for jc in range(22):
                    MM(PS[bank][:, :wo], wd[:, jc, :], GB[:, jc, :wo], jc == 0, jc == 21, [wdk, "GB"], ["ps%d" % bank])
                OP("dve", "tensor_copy", out=OS_[:, dc, :wo], in_=PS[bank][:, :wo], r=["ps%d" % bank], w=["OS%d" % dc])
            rms_stats(g, lambda k: OS_[:, k, :wo], 8, wo, SQ, RS, 0, ["OS%d" % k for k in range(8)])
            for k in range(8):
                tm, tk = TMP[k % 2], "TMPf%d" % (k % 2)
                OP("pool", "tensor_tensor", out=tm[:, :wo], in0=OS_[:, k, :wo], in1=RS[:, :wo], op=ALU.mult,
                   r=["OS%d" % k, "RS"], w=[tk])
                OP("dve", "scalar_tensor_tensor", out=X[:, k, oa:ob], in0=tm[:, :wo], scalar=DER[:, l, j, 5, k:k + 1],
                   in1=X[:, k, oa:ob], op0=ALU.mult, op1=ALU.add, r=[tk, "DER", "X"], w=["X"])


def na_rows(kr):
    rs = [r for r in range(32) if min(max(r - 4, 0), 24) <= kr <= min(max(r - 4, 0), 24) + 7]
    assert rs == list(range(rs[0], rs[-1] + 1))
    return rs[0], rs[-1] + 1


def attn_mixer(g, bi, l, kind):
    nc, p, OP, MM, DMA = g.nc, g.p, g.OP, g.MM, g.DMA
    PS, HT, YT = g.PS, g.HT, g.YT
    wkey = g.wkeys("w_in_bf", l, 8)
    swa = kind == "swa"
    with contextlib.ExitStack() as sm:
        def sb(name, shape, dt):
            return sm.enter_context(nc.sbuf_tensor(UN() + name, list(shape), dt))
        QT = sb("QT", [128, NT], BF16)
        KT = sb("KT", [128, NT], BF16)
        VT = sb("VT", [128, 18, 128], BF16)
        WB = [sb("WB%d" % i, [128, 8, 128], BF16) for i in range(3)]
        T1 = sb("T1", [128, 512], F32)
        T2 = sb("T2", [128, 512], F32)
        PT = [sb("PT%d" % i, [128, 512], BF16) for i in range(2)]
        REC = sb("REC", [128, 512], F32)
        if swa:
            RC = sb("RC", [128, NT], BF16)
            RSN = sb("RSN", [128, NT], BF16)
            DMA("sp", RC[:], g.ropeC, w=["RC"])
            DMA("sp", RSN[:], g.ropeS, w=["RSN"])
        else:
            UT = sb("UT", [128, 2, 960], BF16)
            UF = sb("UF", [128, 960], F32)
            MC8 = sb("MC8", [128, 960], F32)
            MCN = sb("MCN", [128, 960], F32)
            IDB = sb("IDB", [128, 64], BF16)
            DMA("sp", MC8[:], g.dram["mc8"], w=["MC8"])
            DMA("sp", MCN[:], g.dram["mcn"], w=["MCN"])
            DMA("sp", IDB[:], g.dram["idb"], w=["IDB"])
        wb_it = [0]

        def load_w(c0):
            i = wb_it[0] % 3
            wb_it[0] += 1
            DMA("sp", WB[i][:], g.w_in_bf[l, :, c0:c0 + 128].rearrange("(k p) c -> p k c", p=128), r=wkey, w=["WB%d" % i])
            return WB[i], "WB%d" % i

        for c in range(2):
            if swa:
                cq, cqp, ck, ckp, cv_ = C_SQ + 128 * c, C_SQP + 128 * c, C_SKD + 128 * c, C_SKDP + 128 * c, C_SV
            else:
                cq, ck, cv_ = C_NAQ + 128 * c, C_NAK + 128 * c, C_NAV + 128 * c
            for (dst, dk, c1, c2) in ((QT, "QT", cq, cqp if swa else None), (KT, "KT", ck, ckp if swa else None)):
                w1, w1k = load_w(c1)
                if swa:
                    w2, w2k = load_w(c2)
                for (a, b) in TB:
                    w = b - a
                    for k in range(8):
                        MM(PS[0][:, :w], w1[:, k, :], HT[:, k, a:b], k == 0, k == 7, [w1k, "HT"], ["ps0"])
                    if swa:
                        for k in range(8):
                            MM(PS[1][:, :w], w2[:, k, :], HT[:, k, a:b], k == 0, k == 7, [w2k, "HT"], ["ps1"])
                        OP("dve", "tensor_tensor", out=T1[:, :w], in0=PS[0][:, :w], in1=RC[:, a:b], op=ALU.mult,
                           r=["ps0", "RC"], w=["T1"])
                        OP("dve", "tensor_tensor", out=T2[:, :w], in0=PS[1][:, :w], in1=RSN[:, a:b], op=ALU.mult,
                           r=["ps1", "RSN"], w=["T2"])
                        OP("pool", "tensor_tensor", out=dst[:, a:b], in0=T1[:, :w], in1=T2[:, :w], op=ALU.add,
                           r=["T1", "T2"], w=[dk])
                    else:
                        OP("act", "activation", out=dst[:, a:b], in_=PS[0][:, :w], func=AF.Copy, r=["ps0"], w=[dk])
            if (not swa) or c == 0:
                wv, wvk = load_w(cv_)
                for t4 in range(0, 18, 4):
                    nt = min(4, 18 - t4)
                    for ti in range(nt):
                        tt = t4 + ti
                        for k in range(8):
                            MM(PS[2][:, ti * 128:(ti + 1) * 128], HT[:, k, tt * 128:(tt + 1) * 128], wv[:, k, :], k == 0, k == 7,
                               [wvk, "HT"], ["ps2"])
                    OP("act", "activation", out=VT[:, t4:t4 + nt, :],
                       in_=PS[2][:, 0:nt * 128].rearrange("p (t c) -> p t c", c=128), func=AF.Copy, r=["ps2"], w=["VT"])
            if not swa:
                for hh in range(2):
                    for half in range(2):
                        DMA("sp", UF[half * 64:(half + 1) * 64, :], g.na_exp[l, 2 * c + hh], w=["UF"])
                    OP("dve", "tensor_tensor", out=UF[:], in0=UF[:], in1=MC8[:], op=ALU.mult, r=["UF", "MC8"], w=["UF"])
                    OP("dve", "tensor_tensor", out=UT[:, hh, :], in0=UF[:], in1=MCN[:], op=ALU.add, r=["UF", "MCN"], w=["UT"])
            for (qa, qb) in TB:
                qw = qb - qa
                for hh in range(2):
                    h = 2 * c + hh
                    hb = 64 * hh
                    items = []
                    for kc in range(2):
                        items.append((kc * 128, 128, 0, kc, qa, qb, None))
                    if qa >= NCX:
                        if swa:
                            for kb in range(16):
                                ka = NCX + 128 * kb
                                a_ = max(qa, ka - 128)
                                b_ = min(qb, ka + 256)
                                if a_ < b_:
                                    items.append((ka, 128, 0, 2 + kb, a_, b_, ("swa", ka)))
                        else:
                            for kr in range(32):
                                r0, r1 = na_rows(kr)
                                a_ = max(qa, NCX + 64 * r0)
                                b_ = min(qb, NCX + 64 * r1)
                                if a_ < b_:
                                    items.append((NCX + 64 * kr, 64, 64 * (kr % 2), 2 + kr // 2, a_, b_, ("na", kr)))
                    vc0 = 64 * (h // 2) if swa else 64 * hh
                    def s_mm(ii):
                        (ka, nk, pb, vt, a_, b_, post) = items[ii]
                        n = b_ - a_
                        sbank = 3 + ii % 2
                        nab = post is not None and post[0] == "na"
                        MM(PS[sbank][pb:pb + nk, :n], KT[hb:hb + 64, ka:ka + nk], QT[hb:hb + 64, a_:b_], True, not nab,
                           ["KT", "QT"], ["ps%d" % sbank])
                        if nab:
                            kr = post[1]
                            i0 = (a_ - NCX) // 64 - kr + 7
                            MM(PS[sbank][pb:pb + nk, :n], IDB[hb:hb + 64, :], UT[hb:hb + 64, hh, i0 * 64:i0 * 64 + n], False, True,
                               ["IDB", "UT"], ["ps%d" % sbank])

                    for ii, (ka, nk, pb, vt, a_, b_, post) in enumerate(items):
                        n = b_ - a_
                        sbank = 3 + ii % 2
                        pt, ptk = PT[ii % 2], "PT%d" % (ii % 2)
                        s_mm(ii)
                        OP("act", "activation", out=pt[pb:pb + nk, :n], in_=PS[sbank][pb:pb + nk, :n], func=AF.Exp, scale=0.125,
                           r=["ps%d" % sbank], w=[ptk])
                        if post is not None and post[0] == "swa":
                            kst = post[1]
                            if a_ < kst:
                                OP("pool", "tensor_tensor", out=pt[:, 0:128], in0=pt[:, 0:128], in1=g.MAB[:, 0, :], op=ALU.mult,
                                   r=[ptk, "MAB"], w=[ptk])
                            if b_ > kst + 128:
                                o_ = kst + 128 - a_
                                OP("pool", "tensor_tensor", out=pt[:, o_:o_ + 128], in0=pt[:, o_:o_ + 128], in1=g.MAB[:, 1, :],
                                   op=ALU.mult, r=[ptk, "MAB"], w=[ptk])
                        MM(PS[5][hb:hb + 64, a_ - qa:b_ - qa], VT[pb:pb + nk, vt, vc0:vc0 + 64], pt[pb:pb + nk, :n], ii == 0,
                           ii == len(items) - 1, ["VT", ptk], ["ps5"])
                        MM(PS[6][hb:hb + 64, a_ - qa:b_ - qa], g.ONESB[pb:pb + nk, 0:64], pt[pb:pb + nk, :n], ii == 0,
                           ii == len(items) - 1, ["ONESB", ptk], ["ps6"])
                if swa:
                    OP("dve", "tensor_scalar", out=REC[:, :qw], in0=PS[6][:, :qw], scalar1=g.ESINK[:, l, c:c + 1], scalar2=None,
                       op0=ALU.add, r=["ps6", "ESINK"], w=["REC"])
                    OP("dve", "reciprocal", out=REC[:, :qw], in_=REC[:, :qw], r=["REC"], w=["REC"])
                else:
                    OP("dve", "reciprocal", out=REC[:, :qw], in_=PS[6][:, :qw], r=["ps6"], w=["REC"])
                yc = (6 if swa else 2) + c
                OP("dve", "tensor_tensor", out=YT[:, yc, qa:qb], in0=PS[5][:, :qw], in1=REC[:, :qw], op=ALU.mult,
                   r=["ps5", "REC"], w=["YT"])


TC = 64


def s5_project(g, bi, l):
    nc, p, OP, MM, DMA = g.nc, g.p, g.OP, g.MM, g.DMA
    PS, HT, YT = g.PS, g.HT, g.YT
    wkey = g.wkeys("w_in_bf", l, 8)
    with contextlib.ExitStack() as sm:
        WA = [sm.enter_context(nc.sbuf_tensor(UN() + "sWA%d" % i, [128, 8, 128], BF16)) for i in range(2)]
        for cc in range(2):
            wa, wak = WA[cc], "sWA%d" % cc
            DMA("sp", wa[:], g.w_in_bf[l, :, C_A + 128 * cc:C_A + 128 * cc + 128].rearrange("(k p) c -> p k c", p=128), r=wkey,
                w=[wak])
            for (a, b) in TB:
                w = b - a
                for k in range(8):
                    MM(PS[7][:, :w], wa[:, k, :], HT[:, k, a:b], k == 0, k == 7, [wak, "HT"], ["ps7"])
                OP("act", "activation", out=YT[:, cc, a:b], in_=PS[7][:, :w], func=AF.Copy, r=["ps7"], w=["sUT"])


def s5_main(g, bi, l):
    nc, p, OP, MM, DMA = g.nc, g.p, g.OP, g.MM, g.DMA
    PS, YT = g.PS, g.YT
    wkey = g.wkeys("w_in_bf", l, 8)
    PI = math.pi
    with contextlib.ExitStack() as sm:
        def sb(name, shape, dt):
            return sm.enter_context(nc.sbuf_tensor(UN() + name, list(shape), dt))
        UT = YT[:, 0:2, :]
        YF = sb("sYF", [128, 2, NT], BF16)
        BP = [sb("sBP%d" % i, [128, 16, 128], BF16) for i in range(2)]
        CP = [sb("sCP%d" % i, [128, 16, 128], BF16) for i in range(2)]
        TAB = [sb("sTAB%d" % i, [128, 16, TC], BF16) for i in range(4)]
        Zs = [sb("sZ%d" % i, [128, 16, TC], F32) for i in range(2)]
        Ws = [sb("sW%d" % i, [128, 16, TC], F32) for i in range(2)]
        ZAs = [sb("sZA%d" % i, [128, 8, TC], F32) for i in range(2)]
        ZBs = [sb("sZB%d" % i, [128, 8, TC], F32) for i in range(2)]
        HCs = [sb("sHC%d" % i, [128, 8, TC], BF16) for i in range(2)]
        HSs = [sb("sHS%d" % i, [128, 8, TC], BF16) for i in range(2)]
        Z, W, ZA, ZB, HC, HS = Zs[0], Ws[0], ZAs[0], ZBs[0], HCs[0], HSs[0]
        WGL = sb("sWGL", [128, 2, 256], BF16)
        LAM = sb("sLAM", [128, 2, 32], F32)
        STP = sb("sSTP", [128, 32], F32)
        DSK = sb("sDSK", [128, LD[0], 2], F32)
        SGN = sb("sSGN", [128, 2], F32)
        JM = sb("sJM", [128, 128], F32)
        HPI = sb("sHPI", [128, 1], F32)
        sm_names = ["RHO", "TH", "M", "SH", "CH", "SN", "CS", "LBR", "LBI", "DEN", "KR", "KI", "T0", "T1", "EC", "ES", "EC2", "ES2"]
        SM = {n: sb("s" + n, [128, 32], F32) for n in sm_names}
        INIT = sb("sINIT", [128, 16], F32)
        ENDS = sb("sENDS", [128, 16], F32)
        RT1 = sb("sRT1", [128, 16], F32)
        TMPYs = [sb("sTMPY%d" % i, [128, TC], F32) for i in range(2)]
        DMA("sp", LAM[:], g.dram["lamT"][:, l], w=["sLAM"])
        DMA("sp", STP[:], g.dram["stepT"][:, l], w=["sSTP"])
        DMA("sp", DSK[:], g.dram["dskT"], w=["sDSK"])
        DMA("sp", SGN[:], g.dram["sgn"], w=["sSGN"])
        DMA("sp", JM[:], g.dram["jmat"], w=["sJM"])
        DMA("pool", WGL[:], g.dram["s5_w_glu"][l].rearrange("(k p) c -> p k c", p=128), w=["sWGL"])
        OP("dve", "memset", ap=HPI[:], constant=PI / 2, w=["sHPI"])

        def V(eng, method, outn, r, **kw):
            OP(eng, method, r=["s" + x for x in r], w=["s" + outn], **kw)

        def TT(outn, an, bn, op):
            V("dve", "tensor_tensor", outn, [an, bn], out=SM[outn][:], in0=SM[an][:], in1=SM[bn][:], op=op)

        OP("act", "activation", out=STP[:], in_=STP[:], func=AF.Exp, r=["sSTP"], w=["sSTP"])
        OP("dve", "tensor_tensor", out=SM["T0"][:], in0=LAM[:, 0, :], in1=STP[:], op=ALU.mult, r=["sLAM", "sSTP"], w=["sT0"])
        OP("act", "activation", out=SM["RHO"][:], in_=SM["T0"][:], func=AF.Exp, r=["sT0"], w=["sRHO"])
        OP("dve", "tensor_tensor", out=SM["TH"][:], in0=LAM[:, 1, :], in1=STP[:], op=ALU.mult, r=["sLAM", "sSTP"], w=["sTH"])
        for _ in range(5):
            V("dve", "tensor_scalar", "M", ["TH"], out=SM["M"][:], in0=SM["TH"][:], scalar1=PI, scalar2=-2 * PI, op0=ALU.is_gt,
              op1=ALU.mult)
            TT("TH", "TH", "M", ALU.add)
        for _ in range(2):
            V("dve", "tensor_scalar", "M", ["TH"], out=SM["M"][:], in0=SM["TH"][:], scalar1=-PI, scalar2=2 * PI, op0=ALU.is_lt,
              op1=ALU.mult)
            TT("TH", "TH", "M", ALU.add)
        OP("act", "activation", out=SM["SH"][:], in_=SM["TH"][:], func=AF.Sin, scale=0.5, r=["sTH"], w=["sSH"])
        OP("act", "activation", out=SM["CH"][:], in_=SM["TH"][:], func=AF.Sin, scale=0.5, bias=HPI[:, 0:1], r=["sTH", "sHPI"],
           w=["sCH"])
        TT("SN", "SH", "CH", ALU.mult)
        V("dve", "tensor_scalar", "SN", ["SN"], out=SM["SN"][:], in0=SM["SN"][:], scalar1=2.0, scalar2=None, op0=ALU.mult)
        TT("T0", "CH", "CH", ALU.mult)
        TT("T1", "SH", "SH", ALU.mult)
        TT("CS", "T0", "T1", ALU.subtract)
        TT("LBR", "RHO", "CS", ALU.mult)
        TT("LBI", "RHO", "SN", ALU.mult)
        V("dve", "tensor_scalar", "LBR", ["LBR"], out=SM["LBR"][:], in0=SM["LBR"][:], scalar1=-1.0, scalar2=None, op0=ALU.add)
        OP("dve", "tensor_tensor", out=SM["T0"][:], in0=LAM[:, 0, :], in1=LAM[:, 0, :], op=ALU.mult, r=["sLAM"], w=["sT0"])
        OP("dve", "tensor_tensor", out=SM["T1"][:], in0=LAM[:, 1, :], in1=LAM[:, 1, :], op=ALU.mult, r=["sLAM"], w=["sT1"])
        TT("DEN", "T0", "T1", ALU.add)
        V("dve", "reciprocal", "DEN", ["DEN"], out=SM["DEN"][:], in_=SM["DEN"][:])
        OP("dve", "tensor_tensor", out=SM["T0"][:], in0=SM["LBR"][:], in1=LAM[:, 0, :], op=ALU.mult, r=["sLBR", "sLAM"], w=["sT0"])
        OP("dve", "tensor_tensor", out=SM["T1"][:], in0=SM["LBI"][:], in1=LAM[:, 1, :], op=ALU.mult, r=["sLBI", "sLAM"], w=["sT1"])
        TT("KR", "T0", "T1", ALU.add)
        TT("KR", "KR", "DEN", ALU.mult)
        OP("dve", "tensor_tensor", out=SM["T0"][:], in0=SM["LBI"][:], in1=LAM[:, 0, :], op=ALU.mult, r=["sLBI", "sLAM"], w=["sT0"])
        OP("dve", "tensor_tensor", out=SM["T1"][:], in0=SM["LBR"][:], in1=LAM[:, 1, :], op=ALU.mult, r=["sLBR", "sLAM"], w=["sT1"])
        TT("KI", "T0", "T1", ALU.subtract)
        TT("KI", "KI", "DEN", ALU.mult)

        nchunk = NT // TC
        for d in range(2):
            qs = slice(16 * d, 16 * d + 16)
            for i, nm in enumerate(("s5B1", "s5B2")):
                DMA("sp", BP[i][:], g.dram[nm + "_bf"][l, d].rearrange("g p s -> p g s"), r=["%s_bf%d" % (nm, l), "%s_bf%d_d0" % (nm, l)], w=["sBP%d" % i])
            for i, nm in enumerate(("s5C1", "s5C2")):
                DMA("sp", CP[i][:], g.dram[nm + "_bf"][l, d].rearrange("g p s -> p g s"), r=["%s_bf%d" % (nm, l), "%s_bf%d_d0" % (nm, l)], w=["sCP%d" % i])
            for which in range(2):
                i0 = 0 if d == 0 else TC - 1
                if which == 0:
                    OP("dve", "tensor_copy", out=Z[:, :, i0], in_=SM["KR"][:, qs], r=["sKR"], w=["sZ0"])
                    OP("dve", "tensor_copy", out=W[:, :, i0], in_=SM["KI"][:, qs], r=["sKI"], w=["sW0"])
                else:
                    OP("dve", "memset", ap=Z[:, :, i0:i0 + 1], constant=1.0, w=["sZ0"])
                    OP("dve", "memset", ap=W[:, :, i0:i0 + 1], constant=0.0, w=["sW0"])
                OP("dve", "tensor_copy", out=SM["EC"][:, 0:16], in_=SM["CS"][:, qs], r=["sCS"], w=["sEC"])
                if which == 0:
                    OP("dve", "tensor_scalar", out=SM["ES"][:, 0:16], in0=SM["SN"][:, qs], scalar1=-1.0, scalar2=None, op0=ALU.mult,
                       r=["sSN"], w=["sES"])
                else:
                    OP("dve", "tensor_copy", out=SM["ES"][:, 0:16], in_=SM["SN"][:, qs], r=["sSN"], w=["sES"])
                n = 1
                while n < TC:
                    if d == 0:
                        src, dst = slice(0, n), slice(n, 2 * n)
                    else:
                        src, dst = slice(TC - n, TC), slice(TC - 2 * n, TC - n)
                    ecb = SM["EC"][:, 0:16].unsqueeze(2).to_broadcast([128, 16, n])
                    esb = SM["ES"][:, 0:16].unsqueeze(2).to_broadcast([128, 16, n])
                    OP("dve", "tensor_tensor", out=ZA[:, :, :].rearrange("p a b -> p (a b)")[:, 0:16 * n].rearrange("p (g n) -> p g n", n=n),
                       in0=Z[:, :, src], in1=ecb, op=ALU.mult, r=["sZ0", "sEC"], w=["sZA0"])
                    OP("dve", "tensor_tensor", out=ZB[:, :, :].rearrange("p a b -> p (a b)")[:, 0:16 * n].rearrange("p (g n) -> p g n", n=n),
                       in0=W[:, :, src], in1=esb, op=ALU.mult, r=["sW0", "sES"], w=["sZB0"])
                    OP("dve", "tensor_tensor", out=Z[:, :, dst],
                       in0=ZA[:, :, :].rearrange("p a b -> p (a b)")[:, 0:16 * n].rearrange("p (g n) -> p g n", n=n),
                       in1=ZB[:, :, :].rearrange("p a b -> p (a b)")[:, 0:16 * n].rearrange("p (g n) -> p g n", n=n),
                       op=ALU.subtract, r=["sZA0", "sZB0"], w=["sZ0"])
                    OP("dve", "tensor_tensor", out=ZA[:, :, :].rearrange("p a b -> p (a b)")[:, 0:16 * n].rearrange("p (g n) -> p g n", n=n),
                       in0=W[:, :, src], in1=ecb, op=ALU.mult, r=["sW0", "sEC"], w=["sZA0"])
                    OP("dve", "tensor_tensor", out=ZB[:, :, :].rearrange("p a b -> p (a b)")[:, 0:16 * n].rearrange("p (g n) -> p g n", n=n),
                       in0=Z[:, :, src], in1=esb, op=ALU.mult, r=["sZ0", "sES"], w=["sZB0"])
                    OP("dve", "tensor_tensor", out=W[:, :, dst],
                       in0=ZA[:, :, :].rearrange("p a b -> p (a b)")[:, 0:16 * n].rearrange("p (g n) -> p g n", n=n),
                       in1=ZB[:, :, :].rearrange("p a b -> p (a b)")[:, 0:16 * n].rearrange("p (g n) -> p g n", n=n),
                       op=ALU.add, r=["sZA0", "sZB0"], w=["sW0"])
                    OP("dve", "tensor_tensor", out=SM["EC2"][:, 0:16], in0=SM["EC"][:, 0:16], in1=SM["EC"][:, 0:16], op=ALU.mult,
                       r=["sEC"], w=["sEC2"])
                    OP("dve", "tensor_tensor", out=SM["ES2"][:, 0:16], in0=SM["ES"][:, 0:16], in1=SM["ES"][:, 0:16], op=ALU.mult,
                       r=["sES"], w=["sES2"])
                    OP("dve", "tensor_tensor", out=SM["ES"][:, 0:16], in0=SM["ES"][:, 0:16], in1=SM["EC"][:, 0:16], op=ALU.mult,
                       r=["sES", "sEC"], w=["sES"])
                    OP("dve", "tensor_scalar", out=SM["ES"][:, 0:16], in0=SM["ES"][:, 0:16], scalar1=2.0, scalar2=None, op0=ALU.mult,
                       r=["sES"], w=["sES"])
                    OP("dve", "tensor_tensor", out=SM["EC"][:, 0:16], in0=SM["EC2"][:, 0:16], in1=SM["ES2"][:, 0:16], op=ALU.subtract,
                       r=["sEC2", "sES2"], w=["sEC"])
                    n *= 2
                if which == 0:
                    OP("dve", "tensor_copy", out=TAB[0][:], in_=Z[:], r=["sZ0"], w=["sTAB0"])
                    OP("dve", "tensor_scalar", out=TAB[1][:], in0=W[:], scalar1=SGN[:, 0:1], scalar2=None, op0=ALU.mult,
                       r=["sW0", "sSGN"], w=["sTAB1"])
                else:
                    OP("dve", "tensor_scalar", out=TAB[2][:], in0=Z[:], scalar1=SGN[:, 1:2], scalar2=None, op0=ALU.mult,
                       r=["sZ0", "sSGN"], w=["sTAB2"])
                    OP("dve", "tensor_scalar", out=TAB[3][:], in0=W[:], scalar1=-1.0, scalar2=None, op0=ALU.mult,
                       r=["sW0"], w=["sTAB3"])
                    OP("dve", "tensor_copy", out=SM["EC2"][:, 0:16], in_=SM["EC"][:, 0:16], r=["sEC"], w=["sEC2"])
                    OP("dve", "tensor_copy", out=SM["ES2"][:, 0:16], in_=SM["ES"][:, 0:16], r=["sES"], w=["sES2"])
            p.barrier()
            OP("dve", "memset", ap=INIT[:], constant=0.0, w=["sINIT"])
            order = list(range(nchunk)) if d == 0 else list(range(NCX // TC - 1, -1, -1)) + list(range(nchunk - 1, NCX // TC - 1, -1))
            allw = lambda pr: ["sW%d_%d" % (pr, gi) for gi in range(16)]
            def stage1(ci):
                m = order[ci]
                t0 = m * TC
                cp = ci % 2
                Zc = Zs[cp]
                for cc in range(2):
                    par = cc
                    b1, b2 = PS[0 + par], PS[2 + par]
                    k1, k2 = "ps%d" % (0 + par), "ps%d" % (2 + par)
                    za, zb = ZAs[par], ZBs[par]
                    zak, zbk = "sZA%d" % par, "sZB%d" % par
                    zk = "sZ%d_%d" % (cp, cc)
                    for gg in range(8):
                        gi = 8 * cc + gg
                        MM(b1[:, gg * TC:(gg + 1) * TC], BP[0][:, gi, :], UT[:, cc, t0:t0 + TC], True, True, ["sBP0", "sUT"], [k1])
                        MM(b2[:, gg * TC:(gg + 1) * TC], BP[1][:, gi, :], UT[:, cc, t0:t0 + TC], True, True, ["sBP1", "sUT"], [k2])
                    gsl = slice(8 * cc, 8 * cc + 8)
                    OP("dve", "tensor_tensor", out=za[:], in0=b1[:, :].rearrange("p (g n) -> p g n", n=TC), in1=TAB[0][:, gsl, :],
                       op=ALU.mult, r=[k1, "sTAB0"], w=[zak])
                    OP("dve", "tensor_tensor", out=zb[:], in0=b2[:, :].rearrange("p (g n) -> p g n", n=TC), in1=TAB[1][:, gsl, :],
                       op=ALU.mult, r=[k2, "sTAB1"], w=[zbk])
                    OP("pool", "tensor_tensor", out=Zc[:, gsl, :], in0=za[:], in1=zb[:], op=ALU.add, r=[zak, zbk], w=[zk])

            def stage2(ci):
                m = order[ci]
                t0 = m * TC
                cp = ci % 2
                Zc, Wc = Zs[cp], Ws[cp]
                for cc in range(2):
                    par = cc
                    by, ky = PS[4 + par], "ps%d" % (4 + par)
                    hc, hs = HCs[par], HSs[par]
                    hck, hsk = "sHC%d" % par, "sHS%d" % par
                    zk = "sZ%d_%d" % (cp, cc)
                    gsl = slice(8 * cc, 8 * cc + 8)
                    wks = ["sW%d_%d" % (cp, 8 * cc + gg) for gg in range(8)]
                    for gg in range(8):
                        gi = 8 * cc + gg
                        q = 16 * d + gi
                        rho_b = SM["RHO"][:, q:q + 1].to_broadcast([128, TC])
                        if d == 0:
                            zin, wout = Zc[:, gi, :], Wc[:, gi, :]
                        else:
                            zin, wout = Zc[:, gi, ::-1], Wc[:, gi, ::-1]
                        OP("dve", "tensor_tensor_scan", out=wout, data0=rho_b, data1=zin, initial=INIT[:, gi:gi + 1], op0=ALU.mult,
                           op1=ALU.add, r=[zk, "sRHO", "sINIT"], w=[wks[gg]])
                    OP("pool", "tensor_tensor", out=hc[:], in0=Wc[:, gsl, :], in1=TAB[2][:, gsl, :], op=ALU.mult, r=wks + ["sTAB2"],
                       w=[hck])
                    OP("pool", "tensor_tensor", out=hs[:], in0=Wc[:, gsl, :], in1=TAB[3][:, gsl, :], op=ALU.mult, r=wks + ["sTAB3"],
                       w=[hsk])
                    for gg in range(8):
                        gi = 8 * cc + gg
                        MM(by[:, 0:TC], CP[0][:, gi, :], hc[:, gg, :], gg == 0, False, ["sCP0", hck], [ky])
                        MM(by[:, 0:TC], CP[1][:, gi, :], hs[:, gg, :], False, gg == 7, ["sCP1", hsk], [ky])
                    if d == 0:
                        OP("act", "activation", out=YF[:, cc, t0:t0 + TC], in_=by[:, 0:TC], func=AF.Copy, r=[ky], w=["sYF"])
                    else:
                        tmy, tmk = TMPYs[par], "sTMPY%d" % par
                        OP("dve", "scalar_tensor_tensor", out=tmy[:], in0=UT[:, cc, t0:t0 + TC], scalar=DSK[:, l, cc:cc + 1],
                           in1=YF[:, cc, t0:t0 + TC], op0=ALU.mult, op1=ALU.add, r=["sUT", "sDSK", "sYF"], w=[tmk])
                        OP("dve", "tensor_tensor", out=YF[:, cc, t0:t0 + TC], in0=tmy[:], in1=by[:, 0:TC], op=ALU.add,
                           r=[tmk, ky], w=["sYF"])
                ecol = TC - 1 if d == 0 else 0
                OP("dve", "tensor_copy", out=ENDS[:], in_=Wc[:, :, ecol], r=allw(cp), w=["sENDS"])
                MM(PS[6][:, 0:16], JM[:], ENDS[:], True, True, ["sJM", "sENDS"], ["ps6"])
                OP("dve", "tensor_tensor", out=RT1[:], in0=ENDS[:], in1=SM["EC2"][:, 0:16], op=ALU.mult, r=["sENDS", "sEC2"], w=["sRT1"])
                OP("dve", "tensor_tensor", out=ENDS[:], in0=PS[6][:, 0:16], in1=SM["ES2"][:, 0:16], op=ALU.mult, r=["ps6", "sES2"],
                   w=["sENDS"])
                OP("dve", "tensor_tensor", out=INIT[:], in0=RT1[:], in1=ENDS[:], op=ALU.add, r=["sRT1", "sENDS"], w=["sINIT"])

            stage1(0)
            for ci in range(len(order)):
                if ci + 1 < len(order):
                    stage1(ci + 1)
                stage2(ci)
            p.barrier()
        for (a, b) in TB:
            w = b - a
            ZT = ZA[:, :, :].rearrange("p a b -> p (a b)")
            for kc in range(2):
                OP("act", "activation", out=HC[:, :, :].rearrange("p a b -> p (a b)")[:, :w] if kc == 0 else
                   HS[:, :, :].rearrange("p a b -> p (a b)")[:, :w], in_=YF[:, kc, a:b], func=AF.Gelu_apprx_tanh, r=["sYF"],
                   w=["sHC0" if kc == 0 else "sHS0"])
            zt = [HC[:, :, :].rearrange("p a b -> p (a b)"), HS[:, :, :].rearrange("p a b -> p (a b)")]
            for oc in range(2):
                for kc in range(2):
                    MM(PS[7][:, :w], WGL[:, kc, oc * 128:(oc + 1) * 128], zt[kc][:, :w], kc == 0, kc == 1,
                       ["sWGL", "sHC0", "sHS0"], ["ps7"])
                OP("act", "activation", out=ZT[:, :w], in_=PS[7][:, :w], func=AF.Sigmoid, r=["ps7"], w=["sZA0"])
                OP("dve", "tensor_tensor", out=YT[:, oc, a:b], in0=zt[oc][:, :w], in1=ZT[:, :w], op=ALU.mult,
                   r=["sHC0", "sHS0", "sZA0"], w=["YT"])


def gla_mixer(g, bi, l):
    nc, p, OP, MM, DMA = g.nc, g.p, g.OP, g.MM, g.DMA
    PS, HT, YT = g.PS, g.HT, g.YT
    wkey = g.wkeys("w_in_bf", l, 8)
    with contextlib.ExitStack() as sm:
        def sb(name, shape, dt):
            return sm.enter_context(nc.sbuf_tensor(UN() + name, list(shape), dt))
        QT = sb("gQT", [128, NT], BF16)
        KT = sb("gKT", [128, NT], BF16)
        SR = sb("gSR", [128, NT], BF16)
        KTK = sb("gKTK", [128, 18, 128], BF16)
        VTK = sb("gVTK", [128, 18, 128], BF16)
        OF = sb("gOF", [128, NT], F32)
        GA = [sb("gGA%d" % d, [32, NT], BF16) for d in range(2)]
        WG = sb("gWG", [32, 2, 256], BF16)
        TRI = sb("gTRI", [128, 2, 128], F32)
        BD = sb("gBD", [128, 128], BF16)
        GN = sb("gGN", [128, LD[0]], F32)
        ONE = sb("gONE", [128, 1], F32)
        EPS_ = sb("gEPS", [128, 1], F32)
        WB = [sb("gWB%d" % i, [128, 8, 128], BF16) for i in range(2)]
        WGB = sb("gWGB", [128, 8, 32], BF16)
        EX = sb("gEX", [128, 128], F32)
        NEG = EX
        E1s = [sb("gE1%d" % i, [128, 128], F32) for i in range(2)]
        OT = sb("gOT", [128, 128], F32)
        DECPs = [sb("gDECP%d" % i, [128, 1], F32) for i in range(2)]
        E2 = sb("gE2", [128, 128], F32)
        E2T = sb("gE2T", [128, 128], F32)
        QDs = [sb("gQD%d" % i, [128, 128], BF16) for i in range(2)]
        KD = sb("gKD", [128, 128], BF16)
        KDTs = [sb("gKDT%d" % i, [128, 128], BF16) for i in range(2)]
        AMs = [[sb("gAM%d_%d" % (i, hh), [128, 128], BF16) for hh in range(2)] for i in range(2)]
        S = sb("gS", [128, 64], F32)
        Ss = [S, sb("gS1", [128, 64], F32)]
        SB_ = sb("gSB", [128, 64], BF16)
        SQ = sb("gSQ", [128, 512], BF16)
        RS = sb("gRS", [128, 512], F32)
        OP("dve", "memset", ap=ONE[:], constant=1.0, w=["gONE"])
        OP("dve", "memset", ap=EPS_[:], constant=EPS, w=["gEPS"])
        DMA("sp", TRI[:], g.dram["tri"], w=["gTRI"])
        DMA("sp", BD[:], g.dram["bd64"], w=["gBD"])
        DMA("sp", GN[:], g.dram["gnT"], w=["gGN"])
        for d in range(2):
            DMA("pool", WG[0:16, d, :], g.dram["gla_w_gate2"][l, d], w=["gWG"])
            DMA("pool", WG[16:17, d, :], g.dram["gla_b_gate"][l, d:d + 1, :], w=["gWG"])
        wb_it = [0]

        def load_w(c0):
            i = wb_it[0] % 2
            wb_it[0] += 1
            DMA("sp", WB[i][:], g.w_in_bf[l, :, c0:c0 + 128].rearrange("(k p) c -> p k c", p=128), r=wkey, w=["gWB%d" % i])
            return WB[i], "gWB%d" % i

        DMA("sp", WGB[:], g.w_in_bf[l, :, C_GF:C_GF + 32].rearrange("(k p) c -> p k c", p=128), r=wkey, w=["gWGB"])
        for d in range(2):
            OP("pool", "memset", ap=GA[d][:], constant=1.0, w=["gGA%d" % d])
            for (a, b) in TB:
                w = b - a
                for k in range(8):
                    MM(PS[0][0:16, :w], WGB[:, k, 16 * d:16 * d + 16], HT[:, k, a:b], k == 0, k == 7, ["gWGB", "HT"], ["ps0"])
                OP("act", "activation", out=GA[d][0:16, a:b], in_=PS[0][0:16, :w], func=AF.Copy, r=["ps0"], w=["gGA%d" % d])
        for c in range(2):
            for (dst, dk, c0, fn, sc) in ((QT, "gQT", C_GQ + 128 * c, AF.Copy, 0.125), (KT, "gKT", C_GK + 128 * c, AF.Copy, 1.0),
                                          (SR, "gSR", C_GR + 128 * c, AF.Silu, 1.0)):
                w1, w1k = load_w(c0)
                for (a, b) in TB:
                    w = b - a
                    for k in range(8):
                        MM(PS[0][:, :w], w1[:, k, :], HT[:, k, a:b], k == 0, k == 7, [w1k, "HT"], ["ps0"])
                    OP("act", "activation", out=dst[:, a:b], in_=PS[0][:, :w], func=fn, scale=sc, r=["ps0"], w=[dk])
            for (dst, dk, c0) in ((KTK, "gKTK", C_GK + 128 * c), (VTK, "gVTK", C_GV + 128 * c)):
                wv, wvk = load_w(c0)
                for t4 in range(0, 18, 4):
                    nt = min(4, 18 - t4)
                    for ti in range(nt):
                        tt = t4 + ti
                        for k in range(8):
                            MM(PS[1][:, ti * 128:(ti + 1) * 128], HT[:, k, tt * 128:(tt + 1) * 128], wv[:, k, :], k == 0, k == 7,
                               [wvk, "HT"], ["ps1"])
                    OP("act", "activation", out=dst[:, t4:t4 + nt, :],
                       in_=PS[1][:, 0:nt * 128].rearrange("p (t c) -> p t c", c=128), func=AF.Copy, r=["ps1"], w=[dk])
            for d in range(2):
                OP("dve", "memset", ap=SB_[:], constant=0.0, w=["gSB"])
                tiles = list(range(18)) if d == 0 else [1, 0] + list(range(17, 1, -1))
                chunks = (0, 1) if d == 0 else (1, 0)
                def prep(i):
                    tt = tiles[i]
                    t0 = tt * 128
                    pr = i % 2
                    e1, qd, kdt = E1s[pr], QDs[pr], KDTs[pr]
                    e1k, qdk, kdtk = "gE1%d" % pr, "gQD%d" % pr, "gKDT%d" % pr
                    MM(PS[2][:, 0:128], GA[d][0:17, t0:t0 + 128], WG[0:17, d, 128 * c:128 * c + 128], True, True,
                       ["gGA%d" % d, "gWG"], ["ps2"])
                    OP("act", "activation", out=EX[:], in_=PS[2][:, 0:128], func=AF.Exp, scale=-1.0, r=["ps2"], w=["gEX"])
                    OP("act", "activation", out=NEG[:], in_=EX[:], func=AF.Ln, bias=ONE[:, 0:1], scale=1.0, r=["gEX", "gONE"],
                       w=["gEX"])
                    MM(PS[3][:, 0:128], NEG[:], TRI[:, d, :], True, True, ["gEX", "gTRI"], ["ps3"])
                    MM(PS[3][:, 128:256], TRI[:, d, :], NEG[:], True, True, ["gEX", "gTRI"], ["ps3"])
                    OP("act", "activation", out=e1[:], in_=PS[3][:, 0:128], func=AF.Exp, scale=-1.0 / 16, r=["ps3"], w=[e1k])
                    OP("act", "activation", out=E2[:], in_=PS[3][:, 0:128], func=AF.Exp, scale=1.0 / 16, r=["ps3"], w=["gE2"])
                    OP("act", "activation", out=E2T[:], in_=PS[3][:, 128:256], func=AF.Exp, scale=1.0 / 16, r=["ps3"], w=["gE2T"])
                    OP("dve", "tensor_tensor", out=qd[:], in0=QT[:, t0:t0 + 128], in1=e1[:], op=ALU.mult, r=["gQT", e1k], w=[qdk])
                    OP("pool", "tensor_tensor", out=KD[:], in0=KT[:, t0:t0 + 128], in1=E2[:], op=ALU.mult, r=["gKT", "gE2"], w=["gKD"])
                    OP("pool", "tensor_tensor", out=kdt[:], in0=KTK[:, tt, :], in1=E2T[:], op=ALU.mult, r=["gKTK", "gE2T"],
                       w=[kdtk])
                    for hh in range(2):
                        hb = 64 * hh
                        am, amk = AMs[pr][hh], "gAM%d_%d" % (pr, hh)
                        MM(PS[4 + hh][:, 0:128], KD[hb:hb + 64, :], qd[hb:hb + 64, :], True, True, ["gKD", qdk], ["ps%d" % (4 + hh)])
                        OP("dve", "tensor_tensor", out=am[:], in0=PS[4 + hh][:, 0:128], in1=TRI[:, d, :], op=ALU.mult,
                           r=["ps%d" % (4 + hh), "gTRI"], w=[amk])

                def recur(i):
                    tt = tiles[i]
                    t0 = tt * 128
                    pr = i % 2
                    e1, qd, kdt = E1s[pr], QDs[pr], KDTs[pr]
                    e1k, qdk, kdtk = "gE1%d" % pr, "gQD%d" % pr, "gKDT%d" % pr
                    for hh in range(2):
                        hb = 64 * hh
                        am, amk = AMs[pr][hh], "gAM%d_%d" % (pr, hh)
                        MM(PS[6][hb:hb + 64, 0:128], VTK[:, tt, hb:hb + 64], am[:], True, False, ["gVTK", amk], ["ps6"])
                    for ci_, ch in enumerate(chunks):
                        cs = 64 * ch
                        first = (i == 0 and ci_ == 0)
                        kb = 7 if ci_ == 0 else 2
                        for hh in range(2):
                            hb = 64 * hh
                            MM(PS[6][hb:hb + 64, cs:cs + 64], SB_[hb:hb + 64, :], qd[hb:hb + 64, cs:cs + 64], False, True,
                               ["gSB", qdk], ["ps6"])
                            MM(PS[7][hb:hb + 64, 64 * ci_:64 * ci_ + 64], kdt[cs:cs + 64, hb:hb + 64], VTK[cs:cs + 64, tt, hb:hb + 64],
                               True, True, [kdtk, "gVTK"], ["ps7"])
                        dcol = cs + 63 if d == 0 else cs
                        gn = 2 * i + ci_
                        Sc, Sp = Ss[gn % 2], Ss[(gn + 1) % 2]
                        sck, spk = "gS%d" % (gn % 2), "gS%d" % ((gn + 1) % 2)
                        dcp, dck = DECPs[gn % 2], "gDECP%d" % (gn % 2)
                        dpp, dpk = DECPs[(gn + 1) % 2], "gDECP%d" % ((gn + 1) % 2)
                        if first:
                            OP("dve", "tensor_copy", out=Sc[:], in_=PS[7][:, 64 * ci_:64 * ci_ + 64], r=["ps7"], w=[sck])
                        else:
                            OP("dve", "scalar_tensor_tensor", out=Sc[:], in0=Sp[:], scalar=dpp[:, 0:1], in1=PS[7][:, 64 * ci_:64 * ci_ + 64],
                               op0=ALU.mult, op1=ALU.add, r=[spk, dpk, "ps7"], w=[sck])
                        OP("act", "activation", out=SB_[:], in_=Sc[:], func=AF.Identity, scale=e1[:, dcol:dcol + 1], r=[sck, e1k],
                           w=["gSB"])
                        OP("pool", "tensor_copy", out=dcp[:, 0:1], in_=e1[:, dcol:dcol + 1], r=[e1k], w=[dck])
                    if d == 0:
                        OP("act", "activation", out=OF[:, t0:t0 + 128], in_=PS[6][:, 0:128], func=AF.Copy, r=["ps6"], w=["gOF"])
                    else:
                        OP("pool", "tensor_copy", out=OT[:], in_=OF[:, t0:t0 + 128], r=["gOF"], w=["gOT"])
                        OP("dve", "tensor_tensor", out=OF[:, t0:t0 + 128], in0=OT[:], in1=PS[6][:, 0:128], op=ALU.add,
                           r=["gOT", "ps6"], w=["gOF"])

                prep(0)
                for i in range(len(tiles)):
                    if i + 1 < len(tiles):
                        prep(i + 1)
                    recur(i)
            for (a, b) in TB:
                w = b - a
                OP("act", "activation", out=SQ[:, :w], in_=OF[:, a:b], func=AF.Square, r=["gOF"], w=["gSQ"])
                MM(PS[0][:, :w], BD[:], SQ[:, :w], True, True, ["gBD", "gSQ"], ["ps0"])
                OP("act", "activation", out=RS[:, :w], in_=PS[0][:, :w], func=AF.Sqrt, scale=1.0 / 64, bias=EPS_[:, 0:1],
                   r=["ps0", "gEPS"], w=["gRS"])
                OP("dve", "reciprocal", out=RS[:, :w], in_=RS[:, :w], r=["gRS"], w=["gRS"])
                OP("dve", "scalar_tensor_tensor", out=RS[:, :w], in0=OF[:, a:b], scalar=GN[:, l:l + 1], in1=RS[:, :w], op0=ALU.mult,
                   op1=ALU.mult, r=["gOF", "gGN", "gRS"], w=["gRS"])
                OP("pool", "tensor_tensor", out=YT[:, 4 + c, a:b], in0=RS[:, :w], in1=SR[:, a:b], op=ALU.mult, r=["gRS", "gSR"],
                   w=["YT"])


def host_prep(inputs):
    f = np.float32
    w_in = np.asarray(inputs["w_in"], f)
    sq = w_in[:, :, C_SQ:C_SQ + 256]
    sk = w_in[:, :, C_SK:C_SK + 128]
    dup = np.concatenate([np.arange(64), np.arange(64), 64 + np.arange(64), 64 + np.arange(64)])
    w_in_ext = np.concatenate([w_in, sq[:, :, _rope_perm(4)], sk[:, :, dup], sk[:, :, _rope_perm(2)][:, :, dup]], axis=2)
    assert w_in_ext.shape[2] == NEXT
    rc, rs = _rope_tables()
    kl = np.arange(128)[:, None]
    ql = np.arange(128)[None, :]
    maskAB = np.stack([(kl <= ql), (ql <= kl)], 1).astype(f)
    gv = np.stack([inputs["g_pre_mix"], inputs["g_post_mix"], inputs["g_pre_ffn"], inputs["g_post_ffn"]], 1)
    gvec = np.ascontiguousarray(np.asarray(gv, f).reshape(DEPTH, 4, 8, 128).transpose(3, 0, 1, 2))
    b_modT = np.ascontiguousarray(np.asarray(inputs["b_mod"], f).reshape(DEPTH, 48, 128).transpose(2, 0, 1))
    convT = np.ascontiguousarray(np.asarray(inputs["ffn_conv"], f).reshape(DEPTH, 3, 22, 128).transpose(3, 0, 1, 2))
    sink = np.asarray(inputs["swa_sink"], f)
    sinkT = np.zeros((128, DEPTH, 2), f)
    for c in range(2):
        sinkT[0:64, :, c] = sink[None, :, 2 * c]
        sinkT[64:128, :, c] = sink[None, :, 2 * c + 1]
    rpb = np.asarray(inputs["na_rpb"], f)
    kc = np.arange(64)[:, None]
    qc = np.arange(64)[None, :]
    dcx = np.clip(kc - qc, -15, 15) + 15
    na_exp = rpb[:, :, ::-1, :][:, :, :, dcx]
    na_exp = np.ascontiguousarray(na_exp.transpose(0, 1, 3, 2, 4)).reshape(DEPTH, 4, 64, 15 * 64)
    ws = np.clip(np.arange(64) - 8, 0, 48)
    mc = ((kc >= ws[None, :]) & (kc < ws[None, :] + 16)).astype(f)
    mcol = np.tile(np.tile(mc[:, None, :], (1, 15, 1)).reshape(64, 960), (2, 1))
    jj = np.arange(128)[:, None]
    ii = np.arange(128)[None, :]
    same = (jj // 64) == (ii // 64)
    tri = np.stack([(same & (jj <= ii)), (same & (jj >= ii))], 1).astype(f)
    bd64 = same.astype(f)
    gnT = np.ascontiguousarray(np.tile(np.asarray(inputs["gla_g_norm"], f).T, (2, 1)))
    L = DEPTH
    lre = np.asarray(inputs["s5_lam_re"], f).reshape(L, 32, 64)
    lim = np.asarray(inputs["s5_lam_im"], f).reshape(L, 32, 64)
    lam = np.stack([lre, lim], 1)
    lamT = np.ascontiguousarray(np.tile(lam.transpose(3, 0, 1, 2), (2, 1, 1, 1)))
    stepT = np.ascontiguousarray(np.broadcast_to(np.asarray(inputs["s5_log_step"], f).reshape(L, 32)[None], (128, L, 32)))
    dskT = np.ascontiguousarray(np.asarray(inputs["s5_d"], f).reshape(L, 2, 128).transpose(2, 0, 1))
    sgn = np.ones((128, 2), f)
    sgn[0:64, 0] = -1.0
    sgn[64:128, 1] = -1.0
    jmat = np.zeros((128, 128), f)
    for sp in range(64):
        jmat[sp + 64, sp] = -1.0
        jmat[sp, sp + 64] = 1.0
    bre = np.asarray(inputs["s5_b_re"], f)
    bim = np.asarray(inputs["s5_b_im"], f)
    cre = np.asarray(inputs["s5_c_re"], f)
    cim = np.asarray(inputs["s5_c_im"], f)
    B1 = np.zeros((L, 2, 16, 128, 128), f)
    B2 = np.zeros((L, 2, 16, 128, 128), f)
    C1 = np.zeros((L, 2, 16, 128, 128), f)
    C2 = np.zeros((L, 2, 16, 128, 128), f)
    for gi in range(16):
        r0 = 16 * (gi % 8)
        B1[:, :, gi, r0:r0 + 16, 0:64] = bre[:, :, gi].transpose(0, 1, 3, 2)
        B1[:, :, gi, r0:r0 + 16, 64:128] = bim[:, :, gi].transpose(0, 1, 3, 2)
        B2[:, :, gi, r0:r0 + 16, 0:64] = bim[:, :, gi].transpose(0, 1, 3, 2)
        B2[:, :, gi, r0:r0 + 16, 64:128] = bre[:, :, gi].transpose(0, 1, 3, 2)
        C1[:, :, gi, 0:64, r0:r0 + 16] = cre[:, :, gi].transpose(0, 1, 3, 2)
        C1[:, :, gi, 64:128, r0:r0 + 16] = cim[:, :, gi].transpose(0, 1, 3, 2)
        C2[:, :, gi, 0:64, r0:r0 + 16] = cim[:, :, gi].transpose(0, 1, 3, 2)
        C2[:, :, gi, 64:128, r0:r0 + 16] = cre[:, :, gi].transpose(0, 1, 3, 2)
    shared = {
        "lamT": lamT, "stepT": stepT, "dskT": dskT, "sgn": sgn, "jmat": jmat, "s5_w_glu": np.asarray(inputs["s5_w_glu"], f),
        "s5B1": B1, "s5B2": B2, "s5C1": C1, "s5C2": C2,
        "tri": tri, "bd64": bd64.astype(ml_dtypes.bfloat16), "gnT": gnT,
        "gla_w_gate2": np.asarray(inputs["gla_w_gate2"], f), "gla_b_gate": np.asarray(inputs["gla_b_gate"], f),
        "w_mod": np.asarray(inputs["w_mod"], f), "b_modT": b_modT, "gvec": gvec, "w_in_ext": np.ascontiguousarray(w_in_ext),
        "w_out": np.asarray(inputs["w_out"], f), "ffn_w_up": np.asarray(inputs["ffn_w_up"], f),
        "ffn_w_down": np.asarray(inputs["ffn_w_down"], f), "convT": convT,
        "ropeC": rc.astype(ml_dtypes.bfloat16), "ropeS": rs.astype(ml_dtypes.bfloat16),
        "maskAB": maskAB.astype(ml_dtypes.bfloat16), "identf": np.eye(128, dtype=f), "sinkT": sinkT,
        "na_exp": na_exp, "mcol": np.ascontiguousarray(mcol), "mc8": np.ascontiguousarray(8.0 * mcol),
        "mcn": np.ascontiguousarray((mcol - 1.0) * 240000.0),
        "idb": (np.arange(128)[:, None] % 64 == np.arange(64)[None, :]).astype(ml_dtypes.bfloat16),
    }
    x = np.asarray(inputs["x"], f)
    ctx = np.asarray(inputs["ctx"], f)
    c = np.asarray(inputs["c"], f)
    cc = np.asarray(inputs["c_ctx"], f)
    in_maps = []
    for core in range(8):
        b0 = 2 * core
        xcat = np.concatenate([ctx[b0:b0 + 2], x[b0:b0 + 2]], axis=1)
        cs = np.stack([c[b0], c[b0 + 1], cc], 0)
        cTm = np.ascontiguousarray(cs.reshape(3, 8, 128).transpose(2, 1, 0))
        m = dict(shared)
        m["xcat"] = np.ascontiguousarray(xcat)
        m["cT"] = cTm
        in_maps.append(m)
    return in_maps


L_FIRST = ("w_mod", "w_in_ext", "w_out", "ffn_w_up", "ffn_w_down", "na_exp", "s5B1", "s5B2", "s5C1", "s5C2", "s5_w_glu",
           "gla_w_gate2", "gla_b_gate")
L_SECOND = ("b_modT", "gvec", "convT", "sinkT", "lamT", "stepT", "dskT")

FUSED = True


def kernel(**inputs):
    in_maps = host_prep(inputs)
    if FUSED:
        nc = bass.Bass("TRN2", target_bir_lowering=False)
        build(nc)
        res = run_bass_kernel_spmd(nc, in_maps, core_ids=list(range(8)))
        outs = [r["out"] for r in res.results]
        return np.concatenate(outs, axis=0).astype(np.float32)
    cur = [m["xcat"] for m in in_maps]
    for l in range(DEPTH):
        base = {}
        m0 = in_maps[0]
        for k, v in m0.items():
            if k in ("xcat", "cT"):
                continue
            if k in L_FIRST:
                base[k] = np.ascontiguousarray(v[l:l + 1])
            elif k in L_SECOND:
                base[k] = np.ascontiguousarray(v[:, l:l + 1])
            elif k == "gnT":
                base[k] = np.ascontiguousarray(v[:, l:l + 1])
            else:
                base[k] = v
        for bi in range(2):
            maps = []
            for core in range(8):
                m = dict(base)
                m["xcat"] = np.ascontiguousarray(cur[core][[bi, 1 - bi]])
                cTm = in_maps[core]["cT"]
                m["cT"] = np.ascontiguousarray(cTm[:, :, [bi, 1 - bi, 2]])
                maps.append(m)
            nc = bass.Bass("TRN2", target_bir_lowering=False)
            build(nc, nlayers=1, nbatch=1, ldim=1, full_out=True)
            res = run_bass_kernel_spmd(nc, maps, core_ids=list(range(8)))
            for core in range(8):
                new = np.array(cur[core])
                new[bi] = res.results[core]["out"][0]
                cur[core] = new
    outs = [c[:, NCX:, :] for c in cur]
    return np.concatenate(outs, axis=0).astype(np.float32)
```

```python
import contextlib
import math
import numpy as np
import ml_dtypes
import concourse.bass as bass
import concourse.mybir as mybir
from concourse.bass_utils import run_bass_kernel_spmd

F32 = mybir.dt.float32
BF16 = mybir.dt.bfloat16
ALU = mybir.AluOpType
AF = mybir.ActivationFunctionType

ENG = ("pe", "act", "dve", "pool", "sp")
NDMA = 24


class P:
    def __init__(self, nc, same_eng_sync=True):
        self.nc = nc
        self.ops = {e: [] for e in ENG}
        self.cnt = {e: 0 for e in ENG}
        self.waited = {e: {} for e in ENG}
        self.last_w = {}
        self.readers = {}
        self.dma_nextq = {}
        self.dma_cnt = [0] * NDMA
        self.dma_last_tok = [None] * NDMA
        self.same = same_eng_sync
        self.out_toks = []
        self.bar = []

    def barrier(self):
        self.bar = [("e", e, self.cnt[e]) for e in ENG if self.cnt[e]] + \
                   [("d", k, self.dma_cnt[k]) for k in range(NDMA) if self.dma_cnt[k]]

    def _deps(self, eng, reads, writes):
        deps = list(self.bar)
        for k in reads:
            t = self.last_w.get(k)
            if t is not None:
                deps.append(t)
        for k in writes:
            t = self.last_w.get(k)
            if t is not None:
                deps.append(t)
            deps.extend(self.readers.get(k, ()))
        return deps

    def _waits(self, eng, deps):
        w = self.waited[eng]
        best = {}
        for t in deps:
            if t[0] == "e":
                _, e2, idx = t
                if e2 == eng and (not self.same or eng == "pe"):
                    continue
                key = e2
            else:
                key = ("d", t[1])
            if w.get(key, 0) >= t[2]:
                continue
            if key not in best or best[key][2] < t[2]:
                best[key] = t
        for key, t in best.items():
            w[key] = t[2]
        return list(best.values())

    def _record(self, tok, reads, writes):
        for k in reads:
            lst = self.readers.setdefault(k, [])
            lst.append(tok)
            if len(lst) > 64:
                best = {}
                for t in lst:
                    kk = t[:2]
                    if kk not in best or best[kk][2] < t[2]:
                        best[kk] = t
                self.readers[k] = list(best.values())
        for k in writes:
            self.last_w[k] = tok
            self.readers[k] = []

    def op(self, eng, fn, reads=(), writes=()):
        deps = self._deps(eng, reads, writes)
        waits = self._waits(eng, deps)
        self.cnt[eng] += 1
        tok = ("e", eng, self.cnt[eng])
        self.ops[eng].append((waits, fn, ("e", eng)))
        self._record(tok, reads, writes)
        return tok

    def dma(self, q, fn, reads=(), writes=(), is_out=False):
        lo, n = (0, 16) if q == "sp" else (16, NDMA - 16)
        cur = self.dma_nextq.get(q, 0)
        k = lo + cur
        self.dma_nextq[q] = (cur + 1) % n
        deps = self._deps(q, reads, writes)
        if self.dma_last_tok[k] is not None:
            deps.append(self.dma_last_tok[k])
        waits = self._waits(q, deps)
        self.dma_cnt[k] += 16
        tok = ("d", k, self.dma_cnt[k])
        self.dma_last_tok[k] = tok
        self.ops[q].append((waits, fn, ("d", k)))
        self._record(tok, reads, writes)
        if is_out:
            self.out_toks.append(tok)
        return tok

    def emit(self):
        nc = self.nc
        with contextlib.ExitStack() as es:
            esem = {e: es.enter_context(nc.semaphore("s_" + e)) for e in ENG}
            dsem = [es.enter_context(nc.semaphore("d%d" % i)) for i in range(NDMA)]
            fin = list(self.out_toks)
            for e in ENG:
                if self.cnt[e]:
                    fin.append(("e", e, self.cnt[e]))
            for k in range(NDMA):
                if self.dma_cnt[k]:
                    fin.append(("d", k, self.dma_cnt[k]))
            block = es.enter_context(nc.Block())

            def run(eng_name, eng):
                for waits, fn, kind in self.ops[eng_name]:
                    for t in waits:
                        if t[0] == "e":
                            eng.wait_ge(esem[t[1]], t[2])
                        else:
                            eng.wait_ge(dsem[t[1]], t[2])
                    ins = fn(eng)
                    if kind[0] == "e":
                        ins.then_inc(esem[kind[1]], 1)
                    else:
                        ins.then_inc(dsem[kind[1]], 16)

            @block.tensor
            def _(e):
                run("pe", e)

            @block.scalar
            def _(e):
                run("act", e)

            @block.vector
            def _(e):
                run("dve", e)

            @block.gpsimd
            def _(e):
                run("pool", e)

            @block.sync
            def _(e):
                run("sp", e)
                for t in fin:
                    if t[0] == "e":
                        if t[1] != "sp":
                            e.wait_ge(esem[t[1]], t[2])
                    else:
                        e.wait_ge(dsem[t[1]], t[2])


NT = 2304
NCX = 256
NLAT = 2048
D = 1024
DEPTH = 4
LD = [4]
DFF = 2816
EPS = 1e-6
TB = [(0, 256)] + [(256 + 512 * i, 256 + 512 * (i + 1)) for i in range(4)]
C_A, C_NAQ, C_NAK, C_NAV = 0, 256, 512, 768
C_GQ, C_GK, C_GV, C_GF, C_GB, C_GR = 1024, 1280, 1536, 1792, 1808, 1824
C_SQ, C_SK, C_SV = 2080, 2336, 2464
C_SQP, C_SKD, C_SKDP, NEXT = 2592, 2848, 3104, 3360


def _rope_perm(nh):
    idx = np.arange(nh * 64).reshape(nh, 4, 16)
    return idx[:, [1, 0, 3, 2], :].reshape(-1)


def _rope_tables():
    cos = np.ones((64, NT), np.float32)
    sin = np.zeros((64, NT), np.float32)
    t = np.arange(NLAT)
    pos = (t // 64, t % 64)
    inv = 10000.0 ** (-np.arange(0, 32, 2, dtype=np.float32) / 32)
    for half in range(2):
        ang = pos[half].astype(np.float32)[None, :] * inv[:, None]
        c, s = np.cos(ang), np.sin(ang)
        b = 32 * half
        cos[b:b + 16, NCX:] = c
        cos[b + 16:b + 32, NCX:] = c
        sin[b:b + 16, NCX:] = -s
        sin[b + 16:b + 32, NCX:] = s
    return np.concatenate([cos, cos], 0), np.concatenate([sin, sin], 0)


class Ctx:
    pass


_UN = [0]


def UN():
    _UN[0] += 1
    return "t%d_" % _UN[0]


def build(nc, dbg=None, nlayers=DEPTH, nbatch=2, mixers=("s5", "na", "gla", "swa"), ldim=DEPTH, full_out=False):
    LD[0] = ldim
    _UN[0] = 0
    p = P(nc)
    g = Ctx()
    g.p, g.nc = p, nc
    dram = {}

    def din(name, shape, dt=F32):
        dram[name] = nc.dram_tensor(name, list(shape), dt, kind="ExternalInput").ap()
        return dram[name]

    xcat = din("xcat", [2, NT, D])
    cT = din("cT", [128, 8, 3])
    w_mod = din("w_mod", [LD[0], D, 6 * D])
    b_modT = din("b_modT", [128, LD[0], 48])
    gvec = din("gvec", [128, LD[0], 4, 8])
    w_in = din("w_in_ext", [LD[0], D, NEXT])
    w_out = din("w_out", [LD[0], D, D])
    w_up = din("ffn_w_up", [LD[0], D, 2 * DFF])
    w_down = din("ffn_w_down", [LD[0], DFF, D])
    convT = din("convT", [128, LD[0], 3, 22])
    ropeC = din("ropeC", [128, NT], BF16)
    ropeS = din("ropeS", [128, NT], BF16)
    maskAB = din("maskAB", [128, 2, 128], BF16)
    identf = din("identf", [128, 128])
    sinkT = din("sinkT", [128, LD[0], 2])
    na_exp = din("na_exp", [LD[0], 4, 64, 15 * 64])
    mcol = din("mcol", [128, 15 * 64])
    din("mc8", [128, 15 * 64])
    din("mcn", [128, 15 * 64])
    din("idb", [128, 64], BF16)
    din("idb2", [128, 2, 128], BF16)
    din("negm", [128, 2, 2, 128], BF16)
    din("tri", [128, 2, 128])
    din("bd64", [128, 128], BF16)
    din("gnT", [128, LD[0]])
    din("gla_w_gate2", [LD[0], 2, 16, 256])
    din("gla_b_gate", [LD[0], 2, 256])
    din("lamT", [128, LD[0], 2, 32])
    din("stepT", [128, LD[0], 32])
    din("dskT", [128, LD[0], 2])
    din("sgn", [128, 2])
    din("jmat", [128, 128])
    din("s5_w_glu", [LD[0], 256, 256])
    for nm in ("s5B1", "s5B2", "s5C1", "s5C2"):
        din(nm, [LD[0], 2, 16, 128, 128])
        dram[nm + "_bf"] = nc.dram_tensor(nm + "_bf", [LD[0], 2, 16, 128, 128], BF16).ap()
    g.dram = dram
    out = nc.dram_tensor("out", [2, NT if full_out else NLAT, D], F32, kind="ExternalOutput").ap()
    dbg_aps = {}
    if dbg:
        for name, shape in dbg.items():
            dbg_aps[name] = nc.dram_tensor("dbg_" + name, list(shape), F32, kind="ExternalOutput").ap()

    w_in_bf = nc.dram_tensor("w_in_bf", [LD[0], D, NEXT], BF16).ap()
    w_out_bf = nc.dram_tensor("w_out_bf", [LD[0], D, D], BF16).ap()
    w_up_bf = nc.dram_tensor("w_up_bf", [LD[0], D, 2 * DFF], BF16).ap()
    w_down_bf = nc.dram_tensor("w_down_bf", [LD[0], DFF, D], BF16).ap()

    def wkeys(key, l, n):
        return ["%s%d_%d" % (key, l, i) for i in range(n)]
    g.wkeys = wkeys
    g._uid = [0]

    def OP(eng, method, r=(), w=(), **kw):
        return p.op(eng, lambda e, kw=kw, method=method: getattr(e, method)(**kw), reads=r, writes=w)

    def MM(out_, lhsT, rhs, start, stop, r, w):
        return p.op("pe", lambda e: e.matmul(out_, lhsT=lhsT, rhs=rhs, start=start, stop=stop), reads=r, writes=w)

    def DMA(q, out_, in_, r=(), w=(), is_out=False, **kw):
        return p.dma(q, lambda e, kw=kw: e.dma_start(out=out_, in_=in_, **kw), reads=r, writes=w, is_out=is_out)

    g.OP, g.MM, g.DMA = OP, MM, DMA

    for l in range(nlayers):
        for (src, dst, rows, key) in ((w_in, w_in_bf, D, "w_in_bf"), (w_out, w_out_bf, D, "w_out_bf"),
                                      (w_up, w_up_bf, D, "w_up_bf"), (w_down, w_down_bf, DFF, "w_down_bf")):
            for r0 in range(0, rows, 128):
                DMA("pool", dst[l, r0:r0 + 128, :], src[l, r0:r0 + 128, :], w=["%s%d_%d" % (key, l, r0 // 128)],
                    max_dma_last_dim=4096)

    for l in range(nlayers):
        for nm in ("s5B1", "s5B2", "s5C1", "s5C2"):
            for d in range(2):
                DMA("pool", dram[nm + "_bf"][l, d].rearrange("g p s -> (g p) s"), dram[nm][l, d].rearrange("g p s -> (g p) s"),
                    w=["%s_bf%d" % (nm, l)] if d == 1 else ["%s_bf%d_d0" % (nm, l)], max_dma_last_dim=4096)

    es = contextlib.ExitStack()

    def sb(name, shape, dt):
        return es.enter_context(nc.sbuf_tensor(UN() + name, list(shape), dt))

    MODT = sb("MODT", [128, LD[0], 48, 3], F32)
    DER = sb("DER", [128, LD[0], 3, 6, 8], F32)
    GV = sb("GV", [128, LD[0], 4, 8], F32)
    CONV = sb("CONV", [128, LD[0], 3, 22], F32)
    ONESB = sb("ONESB", [128, 128], BF16)
    IDF = sb("IDF", [128, 128], F32)
    MAB = sb("MAB", [128, 2, 128], BF16)
    ESINK = sb("ESINK", [128, LD[0], 2], F32)
    PS = [es.enter_context(nc.psum_tensor("ps%d" % i, [128, 512], F32)) for i in range(8)]
    g.PS = PS

    OP("dve", "memset", ap=ONESB[:], constant=1.0, w=["ONESB"])
    DMA("sp", IDF[:], identf, w=["IDF"])
    DMA("sp", MAB[:], maskAB, w=["MAB"])
    DMA("sp", GV[:], gvec, w=["GV"])
    DMA("sp", CONV[:], convT, w=["CONV"])
    DMA("sp", ESINK[:], sinkT, w=["ESINK"])
    OP("act", "activation", out=ESINK[:], in_=ESINK[:], func=AF.Exp, r=["ESINK"], w=["ESINK"])

    with contextlib.ExitStack() as s1:
        SCT = s1.enter_context(nc.sbuf_tensor(UN() + "SCT", [128, 8, 3], F32))
        BM = s1.enter_context(nc.sbuf_tensor(UN() + "BM", [128, LD[0], 48], F32))
        WM = [s1.enter_context(nc.sbuf_tensor(UN() + "WM%d" % i, [128, 8, 512], F32)) for i in range(2)]
        DMA("sp", SCT[:], cT, w=["SCT"])
        DMA("sp", BM[:], b_modT, w=["BM"])
        OP("act", "activation", out=SCT[:], in_=SCT[:], func=AF.Silu, r=["SCT"], w=["SCT"])
        it = 0
        for l in range(nlayers):
            for cg in range(12):
                wb = WM[it % 2]
                wk = "WM%d" % (it % 2)
                it += 1
                DMA("sp", wb[:], w_mod[l, :, cg * 512:(cg + 1) * 512].rearrange("(k p) c -> p k c", p=128), w=[wk])
                for fc in range(4):
                    f = cg * 4 + fc
                    for k in range(8):
                        MM(PS[0][:, f * 3:f * 3 + 3], wb[:, k, fc * 128:(fc + 1) * 128], SCT[:, k, :], k == 0, k == 7,
                           [wk, "SCT"], ["ps0"])
            for j in range(3):
                OP("dve", "tensor_tensor", out=MODT[:, l, :, j], in0=PS[0][:, 0:144].rearrange("p (f j) -> p f j", j=3)[:, :, j],
                   in1=BM[:, l, :], op=ALU.add, r=["ps0", "BM"], w=["MODT"])
            for j in range(3):
                OP("dve", "scalar_tensor_tensor", out=DER[:, l, j, 0, :], in0=MODT[:, l, 8:16, j], scalar=1.0, in1=GV[:, l, 0, :],
                   op0=ALU.add, op1=ALU.mult, r=["MODT", "GV"], w=["DER"])
                OP("dve", "tensor_copy", out=DER[:, l, j, 1, :], in_=MODT[:, l, 0:8, j], r=["MODT"], w=["DER"])
                OP("dve", "tensor_tensor", out=DER[:, l, j, 2, :], in0=MODT[:, l, 16:24, j], in1=GV[:, l, 1, :], op=ALU.mult,
                   r=["MODT", "GV"], w=["DER"])
                OP("dve", "scalar_tensor_tensor", out=DER[:, l, j, 3, :], in0=MODT[:, l, 32:40, j], scalar=1.0, in1=GV[:, l, 2, :],
                   op0=ALU.add, op1=ALU.mult, r=["MODT", "GV"], w=["DER"])
                OP("dve", "tensor_copy", out=DER[:, l, j, 4, :], in_=MODT[:, l, 24:32, j], r=["MODT"], w=["DER"])
                OP("dve", "tensor_tensor", out=DER[:, l, j, 5, :], in0=MODT[:, l, 40:48, j], in1=GV[:, l, 3, :], op=ALU.mult,
                   r=["MODT", "GV"], w=["DER"])
    p.barrier()

    X = sb("X", [128, 8, NT], F32)
    g.X, g.DER, g.ONESB, g.MAB, g.ESINK, g.CONV = X, DER, ONESB, MAB, ESINK, CONV
    g.w_in_bf, g.w_out_bf, g.w_up_bf, g.w_down_bf = w_in_bf, w_out_bf, w_up_bf, w_down_bf
    g.ropeC, g.ropeS, g.na_exp, g.mcol = ropeC, ropeS, na_exp, mcol
    g.dbg_aps = dbg_aps

    def dump(name, ap, keys):
        if name in dbg_aps:
            DMA("pool", dbg_aps[name], ap, r=keys, is_out=True, max_dma_last_dim=2048)
    g.dump = dump

    for bi in range(nbatch):
        with contextlib.ExitStack() as s2:
            XS = [s2.enter_context(nc.sbuf_tensor(UN() + "XS%d" % i, [128, D], F32)) for i in range(2)]
            for tt in range(18):
                xs, xk = XS[tt % 2], "XS%d" % (tt % 2)
                DMA("sp", xs[:], xcat[bi, tt * 128:(tt + 1) * 128, :], w=[xk])
                for hh in range(2):
                    bank = PS[hh]
                    for kk in range(4):
                        k = hh * 4 + kk
                        p.op("pe", lambda e, o=bank[:, kk * 128:(kk + 1) * 128], i=xs[:, k * 128:(k + 1) * 128]:
                             e.transpose(out=o, in_=i, identity=IDF[:]), reads=[xk, "IDF"], writes=["ps%d" % hh])
                    if hh == 0:
                        OP("act", "activation", out=X[:, 0:4, tt * 128:(tt + 1) * 128],
                           in_=bank[:, :].rearrange("p (k t) -> p k t", t=128), func=AF.Copy, r=["ps0"], w=["X"])
                    else:
                        OP("dve", "tensor_copy", out=X[:, 4:8, tt * 128:(tt + 1) * 128],
                           in_=bank[:, :].rearrange("p (k t) -> p k t", t=128), r=["ps1"], w=["X"])
        p.barrier()
        for l in range(nlayers):
            layer(g, bi, l, (l == DEPTH - 1) and not full_out, mixers)
        with contextlib.ExitStack() as s3:
            OS_ = [s3.enter_context(nc.sbuf_tensor(UN() + "OST%d" % i, [128, D], F32)) for i in range(2)]
            for tt in range(18 if full_out else 16):
                ot, ok = OS_[tt % 2], "OST%d" % (tt % 2)
                t0 = (0 if full_out else NCX) + tt * 128
                for hh in range(2):
                    bank = PS[hh]
                    for kk in range(4):
                        k = hh * 4 + kk
                        p.op("pe", lambda e, o=bank[:, kk * 128:(kk + 1) * 128], i=X[:, k, t0:t0 + 128]:
                             e.transpose(out=o, in_=i, identity=IDF[:]), reads=["X", "IDF"], writes=["ps%d" % hh])
                    if hh == 0:
                        OP("act", "activation", out=ot[:, 0:512], in_=bank[:, :], func=AF.Copy, r=["ps0"], w=[ok])
                    else:
                        OP("dve", "tensor_copy", out=ot[:, 512:1024], in_=bank[:, :], r=["ps1"], w=[ok])
                DMA("sp", out[bi, tt * 128:(tt + 1) * 128, :], ot[:], r=[ok], is_out=True)
        p.barrier()

    p.emit()
    es.close()
    return nc


def rms_stats(g, src_fn, nk, w, SQ, RS, psb, src_keys):
    OP, MM = g.OP, g.MM
    for k in range(nk):
        OP("act", "activation", out=SQ[:, k, :w], in_=src_fn(k), func=AF.Square, r=src_keys, w=["SQ"])
    for k in range(nk):
        MM(g.PS[psb][:, :w], g.ONESB[:], SQ[:, k, :w], k == 0, k == nk - 1, ["SQ", "ONESB"], ["ps%d" % psb])
    OP("act", "activation", out=RS[:, :w], in_=g.PS[psb][:, :w], func=AF.Sqrt, scale=1.0 / D, bias=g.EPSC[:, 0:1],
       r=["ps%d" % psb, "EPSC"], w=["RS"])
    OP("dve", "reciprocal", out=RS[:, :w], in_=RS[:, :w], r=["RS"], w=["RS"])


def layer(g, bi, l, last, mixers):
    nc, p, OP, MM, DMA = g.nc, g.p, g.OP, g.MM, g.DMA
    X, DER, PS = g.X, g.DER, g.PS
    with contextlib.ExitStack() as sl:
        def sb(name, shape, dt):
            return sl.enter_context(nc.sbuf_tensor(UN() + name, list(shape), dt))
        YT = sb("YT", [128, 8, NT], BF16)
        g.YT = YT
        EPSC = sb("EPSC", [128, 1], F32)
        g.EPSC = EPSC
        OP("dve", "memset", ap=EPSC[:], constant=EPS, w=["EPSC"])
        with contextlib.ExitStack() as sh:
            HT = sh.enter_context(nc.sbuf_tensor(UN() + "HT", [128, 8, NT], BF16))
            g.HT = HT
            with contextlib.ExitStack() as sa:
                SQ = sa.enter_context(nc.sbuf_tensor(UN() + "SQ", [128, 8, 512], BF16))
                RS = sa.enter_context(nc.sbuf_tensor(UN() + "RS", [128, 512], F32))
                TMP = [sa.enter_context(nc.sbuf_tensor(UN() + "TMPa%d" % i, [128, 512], F32)) for i in range(2)]
                for (a, b) in TB:
                    w = b - a
                    j = 2 if a < NCX else bi
                    rms_stats(g, lambda k: X[:, k, a:b], 8, w, SQ, RS, 0, ["X"])
                    for k in range(8):
                        tm, tk = TMP[k % 2], "TMPa%d" % (k % 2)
                        OP("dve", "tensor_tensor", out=tm[:, :w], in0=X[:, k, a:b], in1=RS[:, :w], op=ALU.mult,
                           r=["X", "RS"], w=[tk])
                        OP("act", "activation", out=HT[:, k, a:b], in_=tm[:, :w], func=AF.Identity,
                           scale=DER[:, l, j, 0, k:k + 1], bias=DER[:, l, j, 1, k:k + 1], r=[tk, "DER"], w=["HT"])
            p.barrier()
            g.dump("HT%d_%d" % (bi, l), HT[:], ["HT"])
            for nm, chs in (("s5", (0, 1)), ("na", (2, 3)), ("gla", (4, 5)), ("swa", (6, 7))):
                if nm not in mixers:
                    OP("pool", "memset", ap=YT[:, chs[0]:chs[1] + 1, :], constant=0.0, w=["YT"])
            if "s5" in mixers:
                s5_project(g, bi, l)
                p.barrier()
            if "swa" in mixers:
                attn_mixer(g, bi, l, "swa")
                p.barrier()
            if "na" in mixers:
                attn_mixer(g, bi, l, "na")
                p.barrier()
            if "gla" in mixers:
                gla_mixer(g, bi, l)
                p.barrier()
        p.barrier()
        if "s5" in mixers:
            s5_main(g, bi, l)
            p.barrier()
        g.dump("YT%d_%d" % (bi, l), YT[:], ["YT"])
        with contextlib.ExitStack() as sc:
            WO = sc.enter_context(nc.sbuf_tensor(UN() + "WO", [128, 8, D], BF16))
            OS_ = sc.enter_context(nc.sbuf_tensor(UN() + "OS", [128, 8, 512], F32))
            SQ = sc.enter_context(nc.sbuf_tensor(UN() + "SQ", [128, 8, 512], BF16))
            RS = sc.enter_context(nc.sbuf_tensor(UN() + "RS", [128, 512], F32))
            TMP = [sc.enter_context(nc.sbuf_tensor(UN() + "TMPc%d" % i, [128, 512], F32)) for i in range(2)]
            DMA("sp", WO[:], g.w_out_bf[l].rearrange("(k p) c -> p k c", p=128), r=g.wkeys("w_out_bf", l, 8), w=["WO"])
            for (a, b) in TB:
                if last and a < NCX:
                    continue
                w = b - a
                j = 2 if a < NCX else bi
                for dc in range(8):
                    bank = 1 + dc % 2
                    for k in range(8):
                        MM(PS[bank][:, :w], WO[:, k, dc * 128:(dc + 1) * 128], YT[:, k, a:b], k == 0, k == 7,
                           ["WO", "YT"], ["ps%d" % bank])
                    OP("dve", "tensor_copy", out=OS_[:, dc, :w], in_=PS[bank][:, :w], r=["ps%d" % bank], w=["OS%d" % dc])
                rms_stats(g, lambda k: OS_[:, k, :w], 8, w, SQ, RS, 0, ["OS%d" % k for k in range(8)])
                for k in range(8):
                    tm, tk = TMP[k % 2], "TMPc%d" % (k % 2)
                    OP("pool", "tensor_tensor", out=tm[:, :w], in0=OS_[:, k, :w], in1=RS[:, :w], op=ALU.mult,
                       r=["OS%d" % k, "RS"], w=[tk])
                    OP("dve", "scalar_tensor_tensor", out=X[:, k, a:b], in0=tm[:, :w], scalar=DER[:, l, j, 2, k:k + 1],
                       in1=X[:, k, a:b], op0=ALU.mult, op1=ALU.add, r=[tk, "DER", "X"], w=["X"])
    p.barrier()
    g.dump("X1_%d_%d" % (bi, l), X[:], ["X"])
    ffn(g, bi, l, last)
    p.barrier()
    g.dump("X2_%d_%d" % (bi, l), X[:], ["X"])


def ffn_blocks():
    blks = [(0, NCX, 0, NCX)]
    for i in range(5):
        oa = NCX + 410 * i
        ob = min(NCX + 410 * (i + 1), NT)
        blks.append((max(oa - 1, NCX), min(ob + 1, NT), oa, ob))
    return blks


def ffn(g, bi, l, last):
    nc, p, OP, MM, DMA = g.nc, g.p, g.OP, g.MM, g.DMA
    X, DER, PS = g.X, g.DER, g.PS
    with contextlib.ExitStack() as sf:
        def sb(name, shape, dt):
            return sf.enter_context(nc.sbuf_tensor(UN() + name, list(shape), dt))
        EPSC = sb("EPSC", [128, 1], F32)
        g.EPSC = EPSC
        OP("dve", "memset", ap=EPSC[:], constant=EPS, w=["EPSC"])
        HBs = [sb("HB%d" % i, [128, 8, 512], BF16) for i in range(2)]
        GB = sb("GB", [128, 22, 512], BF16)
        SQ = sb("SQ", [128, 8, 512], BF16)
        RS = sb("RS", [128, 512], F32)
        OS_ = sb("OS", [128, 8, 512], F32)
        TMP = [sb("TMPf%d" % i, [128, 512], F32) for i in range(2)]
        GS = [sb("GS%d" % i, [128, 514], F32) for i in range(2)]
        CV = [sb("CV%d" % i, [128, 512], F32) for i in range(2)]
        U1 = [sb("U1%d" % i, [128, 512], F32) for i in range(2)]
        WU = [sb("WU%d" % i, [128, 8, 256], BF16) for i in range(4)]
        WD = [sb("WD%d" % i, [128, 22, 128], BF16) for i in range(2)]
        wu_it = 0
        wd_it = 0
        blks = [bk for bk in ffn_blocks() if not (last and bk[0] < NCX)]

        def make_hb(bidx):
            (ca, cb, oa, ob) = blks[bidx]
            w = cb - ca
            j = 2 if ca < NCX else bi
            HB, hbk = HBs[bidx % 2], "HB%d" % (bidx % 2)
            rms_stats(g, lambda k: X[:, k, ca:cb], 8, w, SQ, RS, 0, ["X"])
            for k in range(8):
                tm, tk = TMP[k % 2], "TMPf%d" % (k % 2)
                OP("dve", "tensor_tensor", out=tm[:, :w], in0=X[:, k, ca:cb], in1=RS[:, :w], op=ALU.mult,
                   r=["X", "RS"], w=[tk])
                OP("act", "activation", out=HB[:, k, :w], in_=tm[:, :w], func=AF.Identity,
                   scale=DER[:, l, j, 3, k:k + 1], bias=DER[:, l, j, 4, k:k + 1], r=[tk, "DER"], w=[hbk])

        make_hb(0)
        for bidx, (ca, cb, oa, ob) in enumerate(blks):
            w = cb - ca
            wo = ob - oa
            off = oa - ca
            j = 2 if ca < NCX else bi
            HB, hbk = HBs[bidx % 2], "HB%d" % (bidx % 2)
            if bidx + 1 < len(blks):
                make_hb(bidx + 1)
            for jc in range(22):
                wu, wuk = WU[wu_it % 4], "WU%d" % (wu_it % 4)
                wu_it += 1
                DMA("sp", wu[:, :, 0:128], g.w_up_bf[l, :, jc * 128:(jc + 1) * 128].rearrange("(k p) c -> p k c", p=128),
                    r=g.wkeys("w_up_bf", l, 8), w=[wuk + "g"])
                DMA("sp", wu[:, :, 128:256],
                    g.w_up_bf[l, :, DFF + jc * 128:DFF + (jc + 1) * 128].rearrange("(k p) c -> p k c", p=128),
                    r=g.wkeys("w_up_bf", l, 8), w=[wuk + "v"])
                bg, bv = (1, 2)[jc % 2], (3, 4, 7, 5)[jc % 4]
                for k in range(8):
                    MM(PS[bg][:, :w], wu[:, k, 0:128], HB[:, k, :w], k == 0, k == 7, [wuk + "g", hbk], ["ps%d" % bg])
                for k in range(8):
                    MM(PS[bv][:, :w], wu[:, k, 128:256], HB[:, k, :w], k == 0, k == 7, [wuk + "v", hbk], ["ps%d" % bv])
                gs, gk = GS[jc % 2], "GS%d" % (jc % 2)
                cv, ck = CV[jc % 2], "CV%d" % (jc % 2)
                u1, uk = U1[jc % 2], "U1%d" % (jc % 2)
                OP("pool", "memset", ap=gs[:, 0:1], constant=0.0, w=[gk])
                OP("pool", "memset", ap=gs[:, w + 1:w + 2], constant=0.0, w=[gk])
                OP("act", "activation", out=gs[:, 1:w + 1], in_=PS[bg][:, :w], func=AF.Copy, r=["ps%d" % bg], w=[gk])
                s = 1 + off
                OP("pool", "tensor_scalar", out=cv[:, :wo], in0=gs[:, s - 1:s - 1 + wo], scalar1=g.CONV[:, l, 0, jc:jc + 1],
                   scalar2=0.0, op0=ALU.mult, op1=ALU.add, r=[gk, "CONV"], w=[ck])
                OP("dve", "scalar_tensor_tensor", out=cv[:, :wo], in0=gs[:, s:s + wo], scalar=g.CONV[:, l, 1, jc:jc + 1],
                   in1=cv[:, :wo], op0=ALU.mult, op1=ALU.add, r=[gk, "CONV", ck], w=[ck])
                OP("dve", "scalar_tensor_tensor", out=cv[:, :wo], in0=gs[:, s + 1:s + 1 + wo], scalar=g.CONV[:, l, 2, jc:jc + 1],
                   in1=cv[:, :wo], op0=ALU.mult, op1=ALU.add, r=[gk, "CONV", ck], w=[ck])
                OP("act", "activation", out=u1[:, :wo], in_=cv[:, :wo], func=AF.Gelu_apprx_tanh, r=[ck], w=[uk])
                OP("dve", "tensor_tensor", out=GB[:, jc, :wo], in0=PS[bv][:, off:off + wo], in1=u1[:, :wo], op=ALU.mult,
                   r=["ps%d" % bv, uk], w=["GB"])
            for dc in range(8):
                wd, wdk = WD[wd_it % 2], "WD%d" % (wd_it % 2)
                wd_it += 1
                DMA("sp", wd[:], g.w_down_bf[l, :, dc * 128:(dc + 1) * 128].rearrange("(k p) c -> p k c", p=128),
                    r=g.wkeys("w_down_bf", l, 22), w=[wdk])
                bank = (6, 0)[dc % 2]
                for jc in range(22):
                    MM(PS[bank][:, :wo], wd[:, jc, :], GB[:, jc, :wo], jc == 0, jc == 21, [wdk, "GB"], ["ps%d" % bank])
                OP("dve", "tensor_copy", out=OS_[:, dc, :wo], in_=PS[bank][:, :wo], r=["ps%d" % bank], w=["OS%d" % dc])
            rms_stats(g, lambda k: OS_[:, k, :wo], 8, wo, SQ, RS, 0, ["OS%d" % k for k in range(8)])
            for k in range(8):
                tm, tk = TMP[k % 2], "TMPf%d" % (k % 2)
                OP("pool", "tensor_tensor", out=tm[:, :wo], in0=OS_[:, k, :wo], in1=RS[:, :wo], op=ALU.mult,
                   r=["OS%d" % k, "RS"], w=[tk])
                OP("dve", "scalar_tensor_tensor", out=X[:, k, oa:ob], in0=tm[:, :wo], scalar=DER[:, l, j, 5, k:k + 1],
                   in1=X[:, k, oa:ob], op0=ALU.mult, op1=ALU.add, r=[tk, "DER", "X"], w=["X"])


def na_rows(kr):
    rs = [r for r in range(32) if min(max(r - 4, 0), 24) <= kr <= min(max(r - 4, 0), 24) + 7]
    assert rs == list(range(rs[0], rs[-1] + 1))
    return rs[0], rs[-1] + 1


def attn_mixer(g, bi, l, kind):
    nc, p, OP, MM, DMA = g.nc, g.p, g.OP, g.MM, g.DMA
    PS, HT, YT = g.PS, g.HT, g.YT
    wkey = g.wkeys("w_in_bf", l, 8)
    swa = kind == "swa"
    with contextlib.ExitStack() as sm:
        def sb(name, shape, dt):
            return sm.enter_context(nc.sbuf_tensor(UN() + name, list(shape), dt))
        QT = sb("QT", [128, NT], BF16)
        KT = sb("KT", [128, NT], BF16)
        VT = sb("VT", [128, 18, 128], BF16)
        WB = [sb("WB%d" % i, [128, 8, 128], BF16) for i in range(3)]
        T1 = sb("T1", [128, 512], F32)
        T2 = sb("T2", [128, 512], F32)
        PT = [sb("PT%d" % i, [128, 512], BF16) for i in range(2)]
        REC = sb("REC", [128, 512], F32)
        if swa:
            RC = sb("RC", [128, NT], BF16)
            RSN = sb("RSN", [128, NT], BF16)
            DMA("sp", RC[:], g.ropeC, w=["RC"])
            DMA("sp", RSN[:], g.ropeS, w=["RSN"])
            IDB2 = sb("IDB2", [128, 2, 128], BF16)
            NEGM = sb("NEGM", [128, 2, 2, 128], BF16)
            DMA("sp", IDB2[:], g.dram["idb2"], w=["IDB2"])
            DMA("sp", NEGM[:], g.dram["negm"], w=["NEGM"])
        else:
            UT = sb("UT", [128, 2, 960], BF16)
            UF = sb("UF", [128, 960], F32)
            MC8 = sb("MC8", [128, 960], F32)
            MCN = sb("MCN", [128, 960], F32)
            IDB = sb("IDB", [128, 64], BF16)
            DMA("sp", MC8[:], g.dram["mc8"], w=["MC8"])
            DMA("sp", MCN[:], g.dram["mcn"], w=["MCN"])
            DMA("sp", IDB[:], g.dram["idb"], w=["IDB"])
        wb_it = [0]

        def load_w(c0):
            i = wb_it[0] % 3
            wb_it[0] += 1
            DMA("sp", WB[i][:], g.w_in_bf[l, :, c0:c0 + 128].rearrange("(k p) c -> p k c", p=128), r=wkey, w=["WB%d" % i])
            return WB[i], "WB%d" % i

        for c in range(2):
            if swa:
                cq, cqp, ck, ckp, cv_ = C_SQ + 128 * c, C_SQP + 128 * c, C_SKD + 128 * c, C_SKDP + 128 * c, C_SV
            else:
                cq, ck, cv_ = C_NAQ + 128 * c, C_NAK + 128 * c, C_NAV + 128 * c
            for (dst, dk, c1, c2) in ((QT, "QT", cq, cqp if swa else None), (KT, "KT", ck, ckp if swa else None)):
                w1, w1k = load_w(c1)
                if swa:
                    w2, w2k = load_w(c2)
                for (a, b) in TB:
                    w = b - a
                    for k in range(8):
                        MM(PS[0][:, :w], w1[:, k, :], HT[:, k, a:b], k == 0, k == 7, [w1k, "HT"], ["ps0"])
                    if swa:
                        for k in range(8):
                            MM(PS[1][:, :w], w2[:, k, :], HT[:, k, a:b], k == 0, k == 7, [w2k, "HT"], ["ps1"])
                        OP("dve", "tensor_tensor", out=T1[:, :w], in0=PS[0][:, :w], in1=RC[:, a:b], op=ALU.mult,
                           r=["ps0", "RC"], w=["T1"])
                        OP("dve", "tensor_tensor", out=T2[:, :w], in0=PS[1][:, :w], in1=RSN[:, a:b], op=ALU.mult,
                           r=["ps1", "RSN"], w=["T2"])
                        OP("pool", "tensor_tensor", out=dst[:, a:b], in0=T1[:, :w], in1=T2[:, :w], op=ALU.add,
                           r=["T1", "T2"], w=[dk])
                    else:
                        OP("act", "activation", out=dst[:, a:b], in_=PS[0][:, :w], func=AF.Copy, r=["ps0"], w=[dk])
            if (not swa) or c == 0:
                wv, wvk = load_w(cv_)
                for t4 in range(0, 18, 4):
                    nt = min(4, 18 - t4)
                    for ti in range(nt):
                        tt = t4 + ti
                        for k in range(8):
                            MM(PS[2][:, ti * 128:(ti + 1) * 128], HT[:, k, tt * 128:(tt + 1) * 128], wv[:, k, :], k == 0, k == 7,
                               [wvk, "HT"], ["ps2"])
                    OP("act", "activation", out=VT[:, t4:t4 + nt, :],
                       in_=PS[2][:, 0:nt * 128].rearrange("p (t c) -> p t c", c=128), func=AF.Copy, r=["ps2"], w=["VT"])
            if not swa:
                for hh in range(2):
                    for half in range(2):
                        DMA("sp", UF[half * 64:(half + 1) * 64, :], g.na_exp[l, 2 * c + hh], w=["UF"])
                    OP("dve", "tensor_tensor", out=UF[:], in0=UF[:], in1=MC8[:], op=ALU.mult, r=["UF", "MC8"], w=["UF"])
                    OP("dve", "tensor_tensor", out=UT[:, hh, :], in0=UF[:], in1=MCN[:], op=ALU.add, r=["UF", "MCN"], w=["UT"])
            for (qa, qb) in TB:
                qw = qb - qa
                for hh in range(2):
                    h = 2 * c + hh
                    hb = 64 * hh
                    items = []
                    for kc in range(2):
                        items.append((kc * 128, 128, 0, kc, qa, qb, None))
                    if qa >= NCX:
                        if swa:
                            for kb in range(16):
                                ka = NCX + 128 * kb
                                a_ = max(qa, ka - 128)
                                b_ = min(qb, ka + 256)
                                if a_ < b_:
                                    items.append((ka, 128, 0, 2 + kb, a_, b_, ("swa", ka)))
                        else:
                            for kr in range(32):
                                r0, r1 = na_rows(kr)
                                a_ = max(qa, NCX + 64 * r0)
                                b_ = min(qb, NCX + 64 * r1)
                                if a_ < b_:
                                    items.append((NCX + 64 * kr, 64, 64 * (kr % 2), 2 + kr // 2, a_, b_, ("na", kr)))
                    vc0 = 64 * (h // 2) if swa else 64 * hh
                    def s_mm(ii):
                        (ka, nk, pb, vt, a_, b_, post) = items[ii]
                        n = b_ - a_
                        sbank = 3 + ii % 2
                        nab = post is not None and post[0] == "na"
                        subs = []
                        if post is not None and post[0] == "swa":
                            kst = post[1]
                            if a_ < kst:
                                subs.append((0, 0))
                            if b_ > kst + 128:
                                subs.append((kst + 128 - a_, 1))
                        MM(PS[sbank][pb:pb + nk, :n], KT[hb:hb + 64, ka:ka + nk], QT[hb:hb + 64, a_:b_], True, not (nab or subs),
                           ["KT", "QT"], ["ps%d" % sbank])
                        for si, (o_, mk) in enumerate(subs):
                            for hf in range(2):
                                MM(PS[sbank][:, o_:o_ + 128], IDB2[hb:hb + 64, hf, :], NEGM[hb:hb + 64, mk, hf, :], False,
                                   si == len(subs) - 1 and hf == 1, ["IDB2", "NEGM"], ["ps%d" % sbank])
                        if nab:
                            kr = post[1]
                            i0 = (a_ - NCX) // 64 - kr + 7
                            MM(PS[sbank][pb:pb + nk, :n], IDB[hb:hb + 64, :], UT[hb:hb + 64, hh, i0 * 64:i0 * 64 + n], False, True,
                               ["IDB", "UT"], ["ps%d" % sbank])

                    for ii, (ka, nk, pb, vt, a_, b_, post) in enumerate(items):
                        n = b_ - a_
                        sbank = 3 + ii % 2
                        pt, ptk = PT[ii % 2], "PT%d" % (ii % 2)
                        s_mm(ii)
                        OP("act", "activation", out=pt[pb:pb + nk, :n], in_=PS[sbank][pb:pb + nk, :n], func=AF.Exp, scale=0.125,
                           r=["ps%d" % sbank], w=[ptk])
                        MM(PS[5][hb:hb + 64, a_ - qa:b_ - qa], VT[pb:pb + nk, vt, vc0:vc0 + 64], pt[pb:pb + nk, :n], ii == 0,
                           ii == len(items) - 1, ["VT", ptk], ["ps5"])
                        MM(PS[6][hb:hb + 64, a_ - qa:b_ - qa], g.ONESB[pb:pb + nk, 0:64], pt[pb:pb + nk, :n], ii == 0,
                           ii == len(items) - 1, ["ONESB", ptk], ["ps6"])
                if swa:
                    OP("dve", "tensor_scalar", out=REC[:, :qw], in0=PS[6][:, :qw], scalar1=g.ESINK[:, l, c:c + 1], scalar2=None,
                       op0=ALU.add, r=["ps6", "ESINK"], w=["REC"])
                    OP("dve", "reciprocal", out=REC[:, :qw], in_=REC[:, :qw], r=["REC"], w=["REC"])
                else:
                    OP("dve", "reciprocal", out=REC[:, :qw], in_=PS[6][:, :qw], r=["ps6"], w=["REC"])
                yc = (6 if swa else 2) + c
                OP("dve", "tensor_tensor", out=YT[:, yc, qa:qb], in0=PS[5][:, :qw], in1=REC[:, :qw], op=ALU.mult,
                   r=["ps5", "REC"], w=["YT"])


TC = 64


def s5_project(g, bi, l):
    nc, p, OP, MM, DMA = g.nc, g.p, g.OP, g.MM, g.DMA
    PS, HT, YT = g.PS, g.HT, g.YT
    wkey = g.wkeys("w_in_bf", l, 8)
    with contextlib.ExitStack() as sm:
        WA = [sm.enter_context(nc.sbuf_tensor(UN() + "sWA%d" % i, [128, 8, 128], BF16)) for i in range(2)]
        for cc in range(2):
            wa, wak = WA[cc], "sWA%d" % cc
            DMA("sp", wa[:], g.w_in_bf[l, :, C_A + 128 * cc:C_A + 128 * cc + 128].rearrange("(k p) c -> p k c", p=128), r=wkey,
                w=[wak])
            for (a, b) in TB:
                w = b - a
                for k in range(8):
                    MM(PS[7][:, :w], wa[:, k, :], HT[:, k, a:b], k == 0, k == 7, [wak, "HT"], ["ps7"])
                OP("act", "activation", out=YT[:, cc, a:b], in_=PS[7][:, :w], func=AF.Copy, r=["ps7"], w=["sUT"])


def s5_main(g, bi, l):
    nc, p, OP, MM, DMA = g.nc, g.p, g.OP, g.MM, g.DMA
    PS, YT = g.PS, g.YT
    wkey = g.wkeys("w_in_bf", l, 8)
    PI = math.pi
    with contextlib.ExitStack() as sm:
        def sb(name, shape, dt):
            return sm.enter_context(nc.sbuf_tensor(UN() + name, list(shape), dt))
        UT = YT[:, 0:2, :]
        YF = sb("sYF", [128, 2, NT], BF16)
        BP = [sb("sBP%d" % i, [128, 16, 128], BF16) for i in range(2)]
        CP = [sb("sCP%d" % i, [128, 16, 128], BF16) for i in range(2)]
        TAB = [sb("sTAB%d" % i, [128, 16, TC], BF16) for i in range(4)]
        Zs = [sb("sZ%d" % i, [128, 16, TC], F32) for i in range(2)]
        Ws = [sb("sW%d" % i, [128, 16, TC], F32) for i in range(2)]
        ZAs = [sb("sZA%d" % i, [128, 8, TC], F32) for i in range(2)]
        ZBs = [sb("sZB%d" % i, [128, 8, TC], F32) for i in range(2)]
        HCs = [sb("sHC%d" % i, [128, 8, TC], BF16) for i in range(2)]
        HSs = [sb("sHS%d" % i, [128, 8, TC], BF16) for i in range(2)]
        Z, W, ZA, ZB, HC, HS = Zs[0], Ws[0], ZAs[0], ZBs[0], HCs[0], HSs[0]
        WGL = sb("sWGL", [128, 2, 256], BF16)
        LAM = sb("sLAM", [128, 2, 32], F32)
        STP = sb("sSTP", [128, 32], F32)
        DSK = sb("sDSK", [128, LD[0], 2], F32)
        SGN = sb("sSGN", [128, 2], F32)
        JM = sb("sJM", [128, 128], F32)
        HPI = sb("sHPI", [128, 1], F32)
        sm_names = ["RHO", "TH", "M", "SH", "CH", "SN", "CS", "LBR", "LBI", "DEN", "KR", "KI", "T0", "T1", "EC", "ES", "EC2", "ES2"]
        SM = {n: sb("s" + n, [128, 32], F32) for n in sm_names}
        INIT = sb("sINIT", [128, 16], F32)
        ENDS = sb("sENDS", [128, 16], F32)
        RT1 = sb("sRT1", [128, 16], F32)
        TMPYs = [sb("sTMPY%d" % i, [128, TC], F32) for i in range(2)]
        DMA("sp", LAM[:], g.dram["lamT"][:, l], w=["sLAM"])
        DMA("sp", STP[:], g.dram["stepT"][:, l], w=["sSTP"])
        DMA("sp", DSK[:], g.dram["dskT"], w=["sDSK"])
        DMA("sp", SGN[:], g.dram["sgn"], w=["sSGN"])
        DMA("sp", JM[:], g.dram["jmat"], w=["sJM"])
        DMA("pool", WGL[:], g.dram["s5_w_glu"][l].rearrange("(k p) c -> p k c", p=128), w=["sWGL"])
        OP("dve", "memset", ap=HPI[:], constant=PI / 2, w=["sHPI"])

        def V(eng, method, outn, r, **kw):
            OP(eng, method, r=["s" + x for x in r], w=["s" + outn], **kw)

        def TT(outn, an, bn, op):
            V("dve", "tensor_tensor", outn, [an, bn], out=SM[outn][:], in0=SM[an][:], in1=SM[bn][:], op=op)

        OP("act", "activation", out=STP[:], in_=STP[:], func=AF.Exp, r=["sSTP"], w=["sSTP"])
        OP("dve", "tensor_tensor", out=SM["T0"][:], in0=LAM[:, 0, :], in1=STP[:], op=ALU.mult, r=["sLAM", "sSTP"], w=["sT0"])
        OP("act", "activation", out=SM["RHO"][:], in_=SM["T0"][:], func=AF.Exp, r=["sT0"], w=["sRHO"])
        OP("dve", "tensor_tensor", out=SM["TH"][:], in0=LAM[:, 1, :], in1=STP[:], op=ALU.mult, r=["sLAM", "sSTP"], w=["sTH"])
        for _ in range(5):
            V("dve", "tensor_scalar", "M", ["TH"], out=SM["M"][:], in0=SM["TH"][:], scalar1=PI, scalar2=-2 * PI, op0=ALU.is_gt,
              op1=ALU.mult)
            TT("TH", "TH", "M", ALU.add)
        for _ in range(2):
            V("dve", "tensor_scalar", "M", ["TH"], out=SM["M"][:], in0=SM["TH"][:], scalar1=-PI, scalar2=2 * PI, op0=ALU.is_lt,
              op1=ALU.mult)
            TT("TH", "TH", "M", ALU.add)
        OP("act", "activation", out=SM["SH"][:], in_=SM["TH"][:], func=AF.Sin, scale=0.5, r=["sTH"], w=["sSH"])
        OP("act", "activation", out=SM["CH"][:], in_=SM["TH"][:], func=AF.Sin, scale=0.5, bias=HPI[:, 0:1], r=["sTH", "sHPI"],
           w=["sCH"])
        TT("SN", "SH", "CH", ALU.mult)
        V("dve", "tensor_scalar", "SN", ["SN"], out=SM["SN"][:], in0=SM["SN"][:], scalar1=2.0, scalar2=None, op0=ALU.mult)
        TT("T0", "CH", "CH", ALU.mult)
        TT("T1", "SH", "SH", ALU.mult)
        TT("CS", "T0", "T1", ALU.subtract)
        TT("LBR", "RHO", "CS", ALU.mult)
        TT("LBI", "RHO", "SN", ALU.mult)
        V("dve", "tensor_scalar", "LBR", ["LBR"], out=SM["LBR"][:], in0=SM["LBR"][:], scalar1=-1.0, scalar2=None, op0=ALU.add)
        OP("dve", "tensor_tensor", out=SM["T0"][:], in0=LAM[:, 0, :], in1=LAM[:, 0, :], op=ALU.mult, r=["sLAM"], w=["sT0"])
        OP("dve", "tensor_tensor", out=SM["T1"][:], in0=LAM[:, 1, :], in1=LAM[:, 1, :], op=ALU.mult, r=["sLAM"], w=["sT1"])
        TT("DEN", "T0", "T1", ALU.add)
        V("dve", "reciprocal", "DEN", ["DEN"], out=SM["DEN"][:], in_=SM["DEN"][:])
        OP("dve", "tensor_tensor", out=SM["T0"][:], in0=SM["LBR"][:], in1=LAM[:, 0, :], op=ALU.mult, r=["sLBR", "sLAM"], w=["sT0"])
        OP("dve", "tensor_tensor", out=SM["T1"][:], in0=SM["LBI"][:], in1=LAM[:, 1, :], op=ALU.mult, r=["sLBI", "sLAM"], w=["sT1"])
        TT("KR", "T0", "T1", ALU.add)
        TT("KR", "KR", "DEN", ALU.mult)
        OP("dve", "tensor_tensor", out=SM["T0"][:], in0=SM["LBI"][:], in1=LAM[:, 0, :], op=ALU.mult, r=["sLBI", "sLAM"], w=["sT0"])
        OP("dve", "tensor_tensor", out=SM["T1"][:], in0=SM["LBR"][:], in1=LAM[:, 1, :], op=ALU.mult, r=["sLBR", "sLAM"], w=["sT1"])
        TT("KI", "T0", "T1", ALU.subtract)
        TT("KI", "KI", "DEN", ALU.mult)

        nchunk = NT // TC
        for d in range(2):
            qs = slice(16 * d, 16 * d + 16)
            for i, nm in enumerate(("s5B1", "s5B2")):
                DMA("sp", BP[i][:], g.dram[nm + "_bf"][l, d].rearrange("g p s -> p g s"), r=["%s_bf%d" % (nm, l), "%s_bf%d_d0" % (nm, l)], w=["sBP%d" % i])
            for i, nm in enumerate(("s5C1", "s5C2")):
                DMA("sp", CP[i][:], g.dram[nm + "_bf"][l, d].rearrange("g p s -> p g s"), r=["%s_bf%d" % (nm, l), "%s_bf%d_d0" % (nm, l)], w=["sCP%d" % i])
            for which in range(2):
                i0 = 0 if d == 0 else TC - 1
                if which == 0:
                    OP("dve", "tensor_copy", out=Z[:, :, i0], in_=SM["KR"][:, qs], r=["sKR"], w=["sZ0"])
                    OP("dve", "tensor_copy", out=W[:, :, i0], in_=SM["KI"][:, qs], r=["sKI"], w=["sW0"])
                else:
                    OP("dve", "memset", ap=Z[:, :, i0:i0 + 1], constant=1.0, w=["sZ0"])
                    OP("dve", "memset", ap=W[:, :, i0:i0 + 1], constant=0.0, w=["sW0"])
                OP("dve", "tensor_copy", out=SM["EC"][:, 0:16], in_=SM["CS"][:, qs], r=["sCS"], w=["sEC"])
                if which == 0:
                    OP("dve", "tensor_scalar", out=SM["ES"][:, 0:16], in0=SM["SN"][:, qs], scalar1=-1.0, scalar2=None, op0=ALU.mult,
                       r=["sSN"], w=["sES"])
                else:
                    OP("dve", "tensor_copy", out=SM["ES"][:, 0:16], in_=SM["SN"][:, qs], r=["sSN"], w=["sES"])
                n = 1
                while n < TC:
                    if d == 0:
                        src, dst = slice(0, n), slice(n, 2 * n)
                    else:
                        src, dst = slice(TC - n, TC), slice(TC - 2 * n, TC - n)
                    ecb = SM["EC"][:, 0:16].unsqueeze(2).to_broadcast([128, 16, n])
                    esb = SM["ES"][:, 0:16].unsqueeze(2).to_broadcast([128, 16, n])
                    OP("dve", "tensor_tensor", out=ZA[:, :, :].rearrange("p a b -> p (a b)")[:, 0:16 * n].rearrange("p (g n) -> p g n", n=n),
                       in0=Z[:, :, src], in1=ecb, op=ALU.mult, r=["sZ0", "sEC"], w=["sZA0"])
                    OP("dve", "tensor_tensor", out=ZB[:, :, :].rearrange("p a b -> p (a b)")[:, 0:16 * n].rearrange("p (g n) -> p g n", n=n),
                       in0=W[:, :, src], in1=esb, op=ALU.mult, r=["sW0", "sES"], w=["sZB0"])
                    OP("dve", "tensor_tensor", out=Z[:, :, dst],
                       in0=ZA[:, :, :].rearrange("p a b -> p (a b)")[:, 0:16 * n].rearrange("p (g n) -> p g n", n=n),
                       in1=ZB[:, :, :].rearrange("p a b -> p (a b)")[:, 0:16 * n].rearrange("p (g n) -> p g n", n=n),
                       op=ALU.subtract, r=["sZA0", "sZB0"], w=["sZ0"])
                    OP("dve", "tensor_tensor", out=ZA[:, :, :].rearrange("p a b -> p (a b)")[:, 0:16 * n].rearrange("p (g n) -> p g n", n=n),
                       in0=W[:, :, src], in1=ecb, op=ALU.mult, r=["sW0", "sEC"], w=["sZA0"])
                    OP("dve", "tensor_tensor", out=ZB[:, :, :].rearrange("p a b -> p (a b)")[:, 0:16 * n].rearrange("p (g n) -> p g n", n=n),
                       in0=Z[:, :, src], in1=esb, op=ALU.mult, r=["sZ0", "sES"], w=["sZB0"])
                    OP("dve", "tensor_tensor", out=W[:, :, dst],
                       in0=ZA[:, :, :].rearrange("p a b -> p (a b)")[:, 0:16 * n].rearrange("p (g n) -> p g n", n=n),
                       in1=ZB[:, :, :].rearrange("p a b -> p (a b)")[:, 0:16 * n].rearrange("p (g n) -> p g n", n=n),
                       op=ALU.add, r=["sZA0", "sZB0"], w=["sW0"])
                    OP("dve", "tensor_tensor", out=SM["EC2"][:, 0:16], in0=SM["EC"][:, 0:16], in1=SM["EC"][:, 0:16], op=ALU.mult,
                       r=["sEC"], w=["sEC2"])
                    OP("dve", "tensor_tensor", out=SM["ES2"][:, 0:16], in0=SM["ES"][:, 0:16], in1=SM["ES"][:, 0:16], op=ALU.mult,
                       r=["sES"], w=["sES2"])
                    OP("dve", "tensor_tensor", out=SM["ES"][:, 0:16], in0=SM["ES"][:, 0:16], in1=SM["EC"][:, 0:16], op=ALU.mult,
                       r=["sES", "sEC"], w=["sES"])
                    OP("dve", "tensor_scalar", out=SM["ES"][:, 0:16], in0=SM["ES"][:, 0:16], scalar1=2.0, scalar2=None, op0=ALU.mult,
                       r=["sES"], w=["sES"])
                    OP("dve", "tensor_tensor", out=SM["EC"][:, 0:16], in0=SM["EC2"][:, 0:16], in1=SM["ES2"][:, 0:16], op=ALU.subtract,
                       r=["sEC2", "sES2"], w=["sEC"])
                    n *= 2
                if which == 0:
                    OP("dve", "tensor_copy", out=TAB[0][:], in_=Z[:], r=["sZ0"], w=["sTAB0"])
                    OP("dve", "tensor_scalar", out=TAB[1][:], in0=W[:], scalar1=SGN[:, 0:1], scalar2=None, op0=ALU.mult,
                       r=["sW0", "sSGN"], w=["sTAB1"])
                else:
                    OP("dve", "tensor_scalar", out=TAB[2][:], in0=Z[:], scalar1=SGN[:, 1:2], scalar2=None, op0=ALU.mult,
                       r=["sZ0", "sSGN"], w=["sTAB2"])
                    OP("dve", "tensor_scalar", out=TAB[3][:], in0=W[:], scalar1=-1.0, scalar2=None, op0=ALU.mult,
                       r=["sW0"], w=["sTAB3"])
                    OP("dve", "tensor_copy", out=SM["EC2"][:, 0:16], in_=SM["EC"][:, 0:16], r=["sEC"], w=["sEC2"])
                    OP("dve", "tensor_copy", out=SM["ES2"][:, 0:16], in_=SM["ES"][:, 0:16], r=["sES"], w=["sES2"])
            p.barrier()
            OP("dve", "memset", ap=INIT[:], constant=0.0, w=["sINIT"])
            order = list(range(nchunk)) if d == 0 else list(range(NCX // TC - 1, -1, -1)) + list(range(nchunk - 1, NCX // TC - 1, -1))
            allw = lambda pr: ["sW%d_%d" % (pr, gi) for gi in range(16)]
            def stage1(ci):
                m = order[ci]
                t0 = m * TC
                cp = ci % 2
                Zc = Zs[cp]
                for cc in range(2):
                    par = cc
                    b1, b2 = PS[0 + par], PS[2 + par]
                    k1, k2 = "ps%d" % (0 + par), "ps%d" % (2 + par)
                    za, zb = ZAs[par], ZBs[par]
                    zak, zbk = "sZA%d" % par, "sZB%d" % par
                    zk = "sZ%d_%d" % (cp, cc)
                    for gg in range(8):
                        gi = 8 * cc + gg
                        MM(b1[:, gg * TC:(gg + 1) * TC], BP[0][:, gi, :], UT[:, cc, t0:t0 + TC], True, True, ["sBP0", "sUT"], [k1])
                        MM(b2[:, gg * TC:(gg + 1) * TC], BP[1][:, gi, :], UT[:, cc, t0:t0 + TC], True, True, ["sBP1", "sUT"], [k2])
                    gsl = slice(8 * cc, 8 * cc + 8)
                    OP("dve", "tensor_tensor", out=za[:], in0=b1[:, :].rearrange("p (g n) -> p g n", n=TC), in1=TAB[0][:, gsl, :],
                       op=ALU.mult, r=[k1, "sTAB0"], w=[zak])
                    OP("dve", "tensor_tensor", out=zb[:], in0=b2[:, :].rearrange("p (g n) -> p g n", n=TC), in1=TAB[1][:, gsl, :],
                       op=ALU.mult, r=[k2, "sTAB1"], w=[zbk])
                    OP("pool", "tensor_tensor", out=Zc[:, gsl, :], in0=za[:], in1=zb[:], op=ALU.add, r=[zak, zbk], w=[zk])

            def stage2(ci):
                m = order[ci]
                t0 = m * TC
                cp = ci % 2
                Zc, Wc = Zs[cp], Ws[cp]
                for cc in range(2):
                    par = cc
                    by, ky = PS[4 + par], "ps%d" % (4 + par)
                    hc, hs = HCs[par], HSs[par]
                    hck, hsk = "sHC%d" % par, "sHS%d" % par
                    zk = "sZ%d_%d" % (cp, cc)
                    gsl = slice(8 * cc, 8 * cc + 8)
                    wks = ["sW%d_%d" % (cp, 8 * cc + gg) for gg in range(8)]
                    for gg in range(8):
                        gi = 8 * cc + gg
                        q = 16 * d + gi
                        rho_b = SM["RHO"][:, q:q + 1].to_broadcast([128, TC])
                        if d == 0:
                            zin, wout = Zc[:, gi, :], Wc[:, gi, :]
                        else:
                            zin, wout = Zc[:, gi, ::-1], Wc[:, gi, ::-1]
                        OP("dve", "tensor_tensor_scan", out=wout, data0=rho_b, data1=zin, initial=INIT[:, gi:gi + 1], op0=ALU.mult,
                           op1=ALU.add, r=[zk, "sRHO", "sINIT"], w=[wks[gg]])
                    OP("pool", "tensor_tensor", out=hc[:], in0=Wc[:, gsl, :], in1=TAB[2][:, gsl, :], op=ALU.mult, r=wks + ["sTAB2"],
                       w=[hck])
                    OP("pool", "tensor_tensor", out=hs[:], in0=Wc[:, gsl, :], in1=TAB[3][:, gsl, :], op=ALU.mult, r=wks + ["sTAB3"],
                       w=[hsk])
                    for gg in range(8):
                        gi = 8 * cc + gg
                        MM(by[:, 0:TC], CP[0][:, gi, :], hc[:, gg, :], gg == 0, False, ["sCP0", hck], [ky])
                        MM(by[:, 0:TC], CP[1][:, gi, :], hs[:, gg, :], False, gg == 7, ["sCP1", hsk], [ky])
                    if d == 0:
                        OP("act", "activation", out=YF[:, cc, t0:t0 + TC], in_=by[:, 0:TC], func=AF.Copy, r=[ky], w=["sYF"])
                    else:
                        tmy, tmk = TMPYs[par], "sTMPY%d" % par
                        OP("dve", "scalar_tensor_tensor", out=tmy[:], in0=UT[:, cc, t0:t0 + TC], scalar=DSK[:, l, cc:cc + 1],
                           in1=YF[:, cc, t0:t0 + TC], op0=ALU.mult, op1=ALU.add, r=["sUT", "sDSK", "sYF"], w=[tmk])
                        OP("dve", "tensor_tensor", out=YF[:, cc, t0:t0 + TC], in0=tmy[:], in1=by[:, 0:TC], op=ALU.add,
                           r=[tmk, ky], w=["sYF"])
                ecol = TC - 1 if d == 0 else 0
                OP("dve", "tensor_copy", out=ENDS[:], in_=Wc[:, :, ecol], r=allw(cp), w=["sENDS"])
                MM(PS[6][:, 0:16], JM[:], ENDS[:], True, True, ["sJM", "sENDS"], ["ps6"])
                OP("dve", "tensor_tensor", out=RT1[:], in0=ENDS[:], in1=SM["EC2"][:, 0:16], op=ALU.mult, r=["sENDS", "sEC2"], w=["sRT1"])
                OP("dve", "tensor_tensor", out=ENDS[:], in0=PS[6][:, 0:16], in1=SM["ES2"][:, 0:16], op=ALU.mult, r=["ps6", "sES2"],
                   w=["sENDS"])
                OP("dve", "tensor_tensor", out=INIT[:], in0=RT1[:], in1=ENDS[:], op=ALU.add, r=["sRT1", "sENDS"], w=["sINIT"])

            stage1(0)
            for ci in range(len(order)):
                if ci + 1 < len(order):
                    stage1(ci + 1)
                stage2(ci)
            p.barrier()
        for (a, b) in TB:
            w = b - a
            ZT = ZA[:, :, :].rearrange("p a b -> p (a b)")
            for kc in range(2):
                OP("act", "activation", out=HC[:, :, :].rearrange("p a b -> p (a b)")[:, :w] if kc == 0 else
                   HS[:, :, :].rearrange("p a b -> p (a b)")[:, :w], in_=YF[:, kc, a:b], func=AF.Gelu_apprx_tanh, r=["sYF"],
                   w=["sHC0" if kc == 0 else "sHS0"])
            zt = [HC[:, :, :].rearrange("p a b -> p (a b)"), HS[:, :, :].rearrange("p a b -> p (a b)")]
            for oc in range(2):
                for kc in range(2):
                    MM(PS[7][:, :w], WGL[:, kc, oc * 128:(oc + 1) * 128], zt[kc][:, :w], kc == 0, kc == 1,
                       ["sWGL", "sHC0", "sHS0"], ["ps7"])
                OP("act", "activation", out=ZT[:, :w], in_=PS[7][:, :w], func=AF.Sigmoid, r=["ps7"], w=["sZA0"])
                OP("dve", "tensor_tensor", out=YT[:, oc, a:b], in0=zt[oc][:, :w], in1=ZT[:, :w], op=ALU.mult,
                   r=["sHC0", "sHS0", "sZA0"], w=["YT"])


def gla_mixer(g, bi, l):
    nc, p, OP, MM, DMA = g.nc, g.p, g.OP, g.MM, g.DMA
    PS, HT, YT = g.PS, g.HT, g.YT
    wkey = g.wkeys("w_in_bf", l, 8)
    with contextlib.ExitStack() as sm:
        def sb(name, shape, dt):
            return sm.enter_context(nc.sbuf_tensor(UN() + name, list(shape), dt))
        QT = sb("gQT", [128, NT], BF16)
        KT = sb("gKT", [128, NT], BF16)
        SR = sb("gSR", [128, NT], BF16)
        KTK = sb("gKTK", [128, 18, 128], BF16)
        VTK = sb("gVTK", [128, 18, 128], BF16)
        OF = sb("gOF", [128, NT], F32)
        GA = [sb("gGA%d" % d, [32, NT], BF16) for d in range(2)]
        WG = sb("gWG", [32, 2, 256], BF16)
        TRI = sb("gTRI", [128, 2, 128], F32)
        BD = sb("gBD", [128, 128], BF16)
        GN = sb("gGN", [128, LD[0]], F32)
        ONE = sb("gONE", [128, 1], F32)
        EPS_ = sb("gEPS", [128, 1], F32)
        WB = [sb("gWB%d" % i, [128, 8, 128], BF16) for i in range(2)]
        WGB = sb("gWGB", [128, 8, 32], BF16)
        EX = sb("gEX", [128, 128], F32)
        NEG = EX
        E1s = [sb("gE1%d" % i, [128, 128], F32) for i in range(2)]
        OT = sb("gOT", [128, 128], F32)
        DECPs = [sb("gDECP%d" % i, [128, 1], F32) for i in range(2)]
        E2 = sb("gE2", [128, 128], F32)
        E2T = sb("gE2T", [128, 128], F32)
        QDs = [sb("gQD%d" % i, [128, 128], BF16) for i in range(2)]
        KD = sb("gKD", [128, 128], BF16)
        KDTs = [sb("gKDT%d" % i, [128, 128], BF16) for i in range(2)]
        AMs = [[sb("gAM%d_%d" % (i, hh), [128, 128], BF16) for hh in range(2)] for i in range(2)]
        S = sb("gS", [128, 64], F32)
        Ss = [S, sb("gS1", [128, 64], F32)]
        SB_ = sb("gSB", [128, 64], BF16)
        SQ = sb("gSQ", [128, 512], BF16)
        RS = sb("gRS", [128, 512], F32)
        OP("dve", "memset", ap=ONE[:], constant=1.0, w=["gONE"])
        OP("dve", "memset", ap=EPS_[:], constant=EPS, w=["gEPS"])
        DMA("sp", TRI[:], g.dram["tri"], w=["gTRI"])
        DMA("sp", BD[:], g.dram["bd64"], w=["gBD"])
        DMA("sp", GN[:], g.dram["gnT"], w=["gGN"])
        for d in range(2):
            DMA("pool", WG[0:16, d, :], g.dram["gla_w_gate2"][l, d], w=["gWG"])
            DMA("pool", WG[16:17, d, :], g.dram["gla_b_gate"][l, d:d + 1, :], w=["gWG"])
        wb_it = [0]

        def load_w(c0):
            i = wb_it[0] % 2
            wb_it[0] += 1
            DMA("sp", WB[i][:], g.w_in_bf[l, :, c0:c0 + 128].rearrange("(k p) c -> p k c", p=128), r=wkey, w=["gWB%d" % i])
            return WB[i], "gWB%d" % i

        DMA("sp", WGB[:], g.w_in_bf[l, :, C_GF:C_GF + 32].rearrange("(k p) c -> p k c", p=128), r=wkey, w=["gWGB"])
        for d in range(2):
            OP("pool", "memset", ap=GA[d][:], constant=1.0, w=["gGA%d" % d])
            for (a, b) in TB:
                w = b - a
                for k in range(8):
                    MM(PS[0][0:16, :w], WGB[:, k, 16 * d:16 * d + 16], HT[:, k, a:b], k == 0, k == 7, ["gWGB", "HT"], ["ps0"])
                OP("act", "activation", out=GA[d][0:16, a:b], in_=PS[0][0:16, :w], func=AF.Copy, r=["ps0"], w=["gGA%d" % d])
        for c in range(2):
            for (dst, dk, c0, fn, sc) in ((QT, "gQT", C_GQ + 128 * c, AF.Copy, 0.125), (KT, "gKT", C_GK + 128 * c, AF.Copy, 1.0),
                                          (SR, "gSR", C_GR + 128 * c, AF.Silu, 1.0)):
                w1, w1k = load_w(c0)
                for (a, b) in TB:
                    w = b - a
                    for k in range(8):
                        MM(PS[0][:, :w], w1[:, k, :], HT[:, k, a:b], k == 0, k == 7, [w1k, "HT"], ["ps0"])
                    OP("act", "activation", out=dst[:, a:b], in_=PS[0][:, :w], func=fn, scale=sc, r=["ps0"], w=[dk])
            for (dst, dk, c0) in ((KTK, "gKTK", C_GK + 128 * c), (VTK, "gVTK", C_GV + 128 * c)):
                wv, wvk = load_w(c0)
                for t4 in range(0, 18, 4):
                    nt = min(4, 18 - t4)
                    for ti in range(nt):
                        tt = t4 + ti
                        for k in range(8):
                            MM(PS[1][:, ti * 128:(ti + 1) * 128], HT[:, k, tt * 128:(tt + 1) * 128], wv[:, k, :], k == 0, k == 7,
                               [wvk, "HT"], ["ps1"])
                    OP("act", "activation", out=dst[:, t4:t4 + nt, :],
                       in_=PS[1][:, 0:nt * 128].rearrange("p (t c) -> p t c", c=128), func=AF.Copy, r=["ps1"], w=[dk])
            for d in range(2):
                OP("dve", "memset", ap=SB_[:], constant=0.0, w=["gSB"])
                tiles = list(range(18)) if d == 0 else [1, 0] + list(range(17, 1, -1))
                chunks = (0, 1) if d == 0 else (1, 0)
                def prep(i):
                    tt = tiles[i]
                    t0 = tt * 128
                    pr = i % 2
                    e1, qd, kdt = E1s[pr], QDs[pr], KDTs[pr]
                    e1k, qdk, kdtk = "gE1%d" % pr, "gQD%d" % pr, "gKDT%d" % pr
                    MM(PS[2][:, 0:128], GA[d][0:17, t0:t0 + 128], WG[0:17, d, 128 * c:128 * c + 128], True, True,
                       ["gGA%d" % d, "gWG"], ["ps2"])
                    OP("act", "activation", out=EX[:], in_=PS[2][:, 0:128], func=AF.Exp, scale=-1.0, r=["ps2"], w=["gEX"])
                    OP("act", "activation", out=NEG[:], in_=EX[:], func=AF.Ln, bias=ONE[:, 0:1], scale=1.0, r=["gEX", "gONE"],
                       w=["gEX"])
                    MM(PS[3][:, 0:128], NEG[:], TRI[:, d, :], True, True, ["gEX", "gTRI"], ["ps3"])
                    MM(PS[3][:, 128:256], TRI[:, d, :], NEG[:], True, True, ["gEX", "gTRI"], ["ps3"])
                    OP("act", "activation", out=e1[:], in_=PS[3][:, 0:128], func=AF.Exp, scale=-1.0 / 16, r=["ps3"], w=[e1k])
                    OP("act", "activation", out=E2[:], in_=PS[3][:, 0:128], func=AF.Exp, scale=1.0 / 16, r=["ps3"], w=["gE2"])
                    OP("act", "activation", out=E2T[:], in_=PS[3][:, 128:256], func=AF.Exp, scale=1.0 / 16, r=["ps3"], w=["gE2T"])
                    OP("dve", "tensor_tensor", out=qd[:], in0=QT[:, t0:t0 + 128], in1=e1[:], op=ALU.mult, r=["gQT", e1k], w=[qdk])
                    OP("pool", "tensor_tensor", out=KD[:], in0=KT[:, t0:t0 + 128], in1=E2[:], op=ALU.mult, r=["gKT", "gE2"], w=["gKD"])
                    OP("pool", "tensor_tensor", out=kdt[:], in0=KTK[:, tt, :], in1=E2T[:], op=ALU.mult, r=["gKTK", "gE2T"],
                       w=[kdtk])
                    for hh in range(2):
                        hb = 64 * hh
                        am, amk = AMs[pr][hh], "gAM%d_%d" % (pr, hh)
                        MM(PS[4 + hh][:, 0:128], KD[hb:hb + 64, :], qd[hb:hb + 64, :], True, True, ["gKD", qdk], ["ps%d" % (4 + hh)])
                        OP("dve", "tensor_tensor", out=am[:], in0=PS[4 + hh][:, 0:128], in1=TRI[:, d, :], op=ALU.mult,
                           r=["ps%d" % (4 + hh), "gTRI"], w=[amk])

                def recur(i):
                    tt = tiles[i]
                    t0 = tt * 128
                    pr = i % 2
                    e1, qd, kdt = E1s[pr], QDs[pr], KDTs[pr]
                    e1k, qdk, kdtk = "gE1%d" % pr, "gQD%d" % pr, "gKDT%d" % pr
                    for hh in range(2):
                        hb = 64 * hh
                        am, amk = AMs[pr][hh], "gAM%d_%d" % (pr, hh)
                        MM(PS[6][hb:hb + 64, 0:128], VTK[:, tt, hb:hb + 64], am[:], True, False, ["gVTK", amk], ["ps6"])
                    for ci_, ch in enumerate(chunks):
                        cs = 64 * ch
                        first = (i == 0 and ci_ == 0)
                        kb = 7 if ci_ == 0 else 2
                        for hh in range(2):
                            hb = 64 * hh
                            MM(PS[6][hb:hb + 64, cs:cs + 64], SB_[hb:hb + 64, :], qd[hb:hb + 64, cs:cs + 64], False, True,
                               ["gSB", qdk], ["ps6"])
                            MM(PS[7][hb:hb + 64, 64 * ci_:64 * ci_ + 64], kdt[cs:cs + 64, hb:hb + 64], VTK[cs:cs + 64, tt, hb:hb + 64],
                               True, True, [kdtk, "gVTK"], ["ps7"])
                        dcol = cs + 63 if d == 0 else cs
                        gn = 2 * i + ci_
                        Sc, Sp = Ss[gn % 2], Ss[(gn + 1) % 2]
                        sck, spk = "gS%d" % (gn % 2), "gS%d" % ((gn + 1) % 2)
                        dcp, dck = DECPs[gn % 2], "gDECP%d" % (gn % 2)
                        dpp, dpk = DECPs[(gn + 1) % 2], "gDECP%d" % ((gn + 1) % 2)
                        if first:
                            OP("dve", "tensor_copy", out=Sc[:], in_=PS[7][:, 64 * ci_:64 * ci_ + 64], r=["ps7"], w=[sck])
                        else:
                            OP("dve", "scalar_tensor_tensor", out=Sc[:], in0=Sp[:], scalar=dpp[:, 0:1], in1=PS[7][:, 64 * ci_:64 * ci_ + 64],
                               op0=ALU.mult, op1=ALU.add, r=[spk, dpk, "ps7"], w=[sck])
                        OP("act", "activation", out=SB_[:], in_=Sc[:], func=AF.Identity, scale=e1[:, dcol:dcol + 1], r=[sck, e1k],
                           w=["gSB"])
                        OP("pool", "tensor_copy", out=dcp[:, 0:1], in_=e1[:, dcol:dcol + 1], r=[e1k], w=[dck])
                    if d == 0:
                        OP("act", "activation", out=OF[:, t0:t0 + 128], in_=PS[6][:, 0:128], func=AF.Copy, r=["ps6"], w=["gOF"])
                    else:
                        OP("pool", "tensor_copy", out=OT[:], in_=OF[:, t0:t0 + 128], r=["gOF"], w=["gOT"])
                        OP("dve", "tensor_tensor", out=OF[:, t0:t0 + 128], in0=OT[:], in1=PS[6][:, 0:128], op=ALU.add,
                           r=["gOT", "ps6"], w=["gOF"])

                prep(0)
                for i in range(len(tiles)):
                    if i + 1 < len(tiles):
                        prep(i + 1)
                    recur(i)
            for (a, b) in TB:
                w = b - a
                OP("act", "activation", out=SQ[:, :w], in_=OF[:, a:b], func=AF.Square, r=["gOF"], w=["gSQ"])
                MM(PS[0][:, :w], BD[:], SQ[:, :w], True, True, ["gBD", "gSQ"], ["ps0"])
                OP("act", "activation", out=RS[:, :w], in_=PS[0][:, :w], func=AF.Sqrt, scale=1.0 / 64, bias=EPS_[:, 0:1],
                   r=["ps0", "gEPS"], w=["gRS"])
                OP("dve", "reciprocal", out=RS[:, :w], in_=RS[:, :w], r=["gRS"], w=["gRS"])
                OP("dve", "scalar_tensor_tensor", out=RS[:, :w], in0=OF[:, a:b], scalar=GN[:, l:l + 1], in1=RS[:, :w], op0=ALU.mult,
                   op1=ALU.mult, r=["gOF", "gGN", "gRS"], w=["gRS"])
                OP("pool", "tensor_tensor", out=YT[:, 4 + c, a:b], in0=RS[:, :w], in1=SR[:, a:b], op=ALU.mult, r=["gRS", "gSR"],
                   w=["YT"])


def host_prep(inputs):
    f = np.float32
    w_in = np.asarray(inputs["w_in"], f)
    sq = w_in[:, :, C_SQ:C_SQ + 256]
    sk = w_in[:, :, C_SK:C_SK + 128]
    dup = np.concatenate([np.arange(64), np.arange(64), 64 + np.arange(64), 64 + np.arange(64)])
    w_in_ext = np.concatenate([w_in, sq[:, :, _rope_perm(4)], sk[:, :, dup], sk[:, :, _rope_perm(2)][:, :, dup]], axis=2)
    assert w_in_ext.shape[2] == NEXT
    rc, rs = _rope_tables()
    kl = np.arange(128)[:, None]
    ql = np.arange(128)[None, :]
    maskAB = np.stack([(kl <= ql), (ql <= kl)], 1).astype(f)
    gv = np.stack([inputs["g_pre_mix"], inputs["g_post_mix"], inputs["g_pre_ffn"], inputs["g_post_ffn"]], 1)
    gvec = np.ascontiguousarray(np.asarray(gv, f).reshape(DEPTH, 4, 8, 128).transpose(3, 0, 1, 2))
    b_modT = np.ascontiguousarray(np.asarray(inputs["b_mod"], f).reshape(DEPTH, 48, 128).transpose(2, 0, 1))
    convT = np.ascontiguousarray(np.asarray(inputs["ffn_conv"], f).reshape(DEPTH, 3, 22, 128).transpose(3, 0, 1, 2))
    sink = np.asarray(inputs["swa_sink"], f)
    sinkT = np.zeros((128, DEPTH, 2), f)
    for c in range(2):
        sinkT[0:64, :, c] = sink[None, :, 2 * c]
        sinkT[64:128, :, c] = sink[None, :, 2 * c + 1]
    rpb = np.asarray(inputs["na_rpb"], f)
    kc = np.arange(64)[:, None]
    qc = np.arange(64)[None, :]
    dcx = np.clip(kc - qc, -15, 15) + 15
    na_exp = rpb[:, :, ::-1, :][:, :, :, dcx]
    na_exp = np.ascontiguousarray(na_exp.transpose(0, 1, 3, 2, 4)).reshape(DEPTH, 4, 64, 15 * 64)
    ws = np.clip(np.arange(64) - 8, 0, 48)
    mc = ((kc >= ws[None, :]) & (kc < ws[None, :] + 16)).astype(f)
    mcol = np.tile(np.tile(mc[:, None, :], (1, 15, 1)).reshape(64, 960), (2, 1))
    jj = np.arange(128)[:, None]
    ii = np.arange(128)[None, :]
    same = (jj // 64) == (ii // 64)
    tri = np.stack([(same & (jj <= ii)), (same & (jj >= ii))], 1).astype(f)
    bd64 = same.astype(f)
    gnT = np.ascontiguousarray(np.tile(np.asarray(inputs["gla_g_norm"], f).T, (2, 1)))
    L = DEPTH
    lre = np.asarray(inputs["s5_lam_re"], f).reshape(L, 32, 64)
    lim = np.asarray(inputs["s5_lam_im"], f).reshape(L, 32, 64)
    lam = np.stack([lre, lim], 1)
    lamT = np.ascontiguousarray(np.tile(lam.transpose(3, 0, 1, 2), (2, 1, 1, 1)))
    stepT = np.ascontiguousarray(np.broadcast_to(np.asarray(inputs["s5_log_step"], f).reshape(L, 32)[None], (128, L, 32)))
    dskT = np.ascontiguousarray(np.asarray(inputs["s5_d"], f).reshape(L, 2, 128).transpose(2, 0, 1))
    sgn = np.ones((128, 2), f)
    sgn[0:64, 0] = -1.0
    sgn[64:128, 1] = -1.0
    jmat = np.zeros((128, 128), f)
    for sp in range(64):
        jmat[sp + 64, sp] = -1.0
        jmat[sp, sp + 64] = 1.0
    bre = np.asarray(inputs["s5_b_re"], f)
    bim = np.asarray(inputs["s5_b_im"], f)
    cre = np.asarray(inputs["s5_c_re"], f)
    cim = np.asarray(inputs["s5_c_im"], f)
    B1 = np.zeros((L, 2, 16, 128, 128), f)
    B2 = np.zeros((L, 2, 16, 128, 128), f)
    C1 = np.zeros((L, 2, 16, 128, 128), f)
    C2 = np.zeros((L, 2, 16, 128, 128), f)
    for gi in range(16):
        r0 = 16 * (gi % 8)
        B1[:, :, gi, r0:r0 + 16, 0:64] = bre[:, :, gi].transpose(0, 1, 3, 2)
        B1[:, :, gi, r0:r0 + 16, 64:128] = bim[:, :, gi].transpose(0, 1, 3, 2)
        B2[:, :, gi, r0:r0 + 16, 0:64] = bim[:, :, gi].transpose(0, 1, 3, 2)
        B2[:, :, gi, r0:r0 + 16, 64:128] = bre[:, :, gi].transpose(0, 1, 3, 2)
        C1[:, :, gi, 0:64, r0:r0 + 16] = cre[:, :, gi].transpose(0, 1, 3, 2)
        C1[:, :, gi, 64:128, r0:r0 + 16] = cim[:, :, gi].transpose(0, 1, 3, 2)
        C2[:, :, gi, 0:64, r0:r0 + 16] = cim[:, :, gi].transpose(0, 1, 3, 2)
        C2[:, :, gi, 64:128, r0:r0 + 16] = cre[:, :, gi].transpose(0, 1, 3, 2)
    pp = np.arange(128)
    idb2 = np.zeros((128, 2, 128), f)
    for hf in range(2):
        idb2[pp, hf, 64 * hf + pp % 64] = 1.0
    biasm = np.stack([np.where(kl <= ql, 0.0, -240000.0), np.where(ql <= kl, 0.0, -240000.0)], 0).astype(f)
    negm = np.zeros((128, 2, 2, 128), f)
    for hf in range(2):
        negm[:, :, hf, :] = biasm[:, 64 * hf + pp % 64, :].transpose(1, 0, 2)
    shared = {
        "lamT": lamT, "stepT": stepT, "dskT": dskT, "sgn": sgn, "jmat": jmat, "s5_w_glu": np.asarray(inputs["s5_w_glu"], f),
        "s5B1": B1, "s5B2": B2, "s5C1": C1, "s5C2": C2,
        "tri": tri, "bd64": bd64.astype(ml_dtypes.bfloat16), "gnT": gnT,
        "gla_w_gate2": np.asarray(inputs["gla_w_gate2"], f), "gla_b_gate": np.asarray(inputs["gla_b_gate"], f),
        "w_mod": np.asarray(inputs["w_mod"], f), "b_modT": b_modT, "gvec": gvec, "w_in_ext": np.ascontiguousarray(w_in_ext),
        "w_out": np.asarray(inputs["w_out"], f), "ffn_w_up": np.asarray(inputs["ffn_w_up"], f),
        "ffn_w_down": np.asarray(inputs["ffn_w_down"], f), "convT": convT,
        "ropeC": rc.astype(ml_dtypes.bfloat16), "ropeS": rs.astype(ml_dtypes.bfloat16),
        "maskAB": maskAB.astype(ml_dtypes.bfloat16), "identf": np.eye(128, dtype=f), "sinkT": sinkT,
        "na_exp": na_exp, "mcol": np.ascontiguousarray(mcol), "mc8": np.ascontiguousarray(8.0 * mcol),
        "mcn": np.ascontiguousarray((mcol - 1.0) * 240000.0),
        "idb": (np.arange(128)[:, None] % 64 == np.arange(64)[None, :]).astype(ml_dtypes.bfloat16),
        "idb2": idb2.astype(ml_dtypes.bfloat16), "negm": negm.astype(ml_dtypes.bfloat16),
    }
    x = np.asarray(inputs["x"], f)
    ctx = np.asarray(inputs["ctx"], f)
    c = np.asarray(inputs["c"], f)
    cc = np.asarray(inputs["c_ctx"], f)
    in_maps = []
    for core in range(8):
        b0 = 2 * core
        xcat = np.concatenate([ctx[b0:b0 + 2], x[b0:b0 + 2]], axis=1)
        cs = np.stack([c[b0], c[b0 + 1], cc], 0)
        cTm = np.ascontiguousarray(cs.reshape(3, 8, 128).transpose(2, 1, 0))
        m = dict(shared)
        m["xcat"] = np.ascontiguousarray(xcat)
        m["cT"] = cTm
        in_maps.append(m)
    return in_maps


L_FIRST = ("w_mod", "w_in_ext", "w_out", "ffn_w_up", "ffn_w_down", "na_exp", "s5B1", "s5B2", "s5C1", "s5C2", "s5_w_glu",
           "gla_w_gate2", "gla_b_gate")
L_SECOND = ("b_modT", "gvec", "convT", "sinkT", "lamT", "stepT", "dskT")

FUSED = True


def kernel(**inputs):
    in_maps = host_prep(inputs)
    if FUSED:
        nc = bass.Bass("TRN2", target_bir_lowering=False)
        build(nc)
        res = run_bass_kernel_spmd(nc, in_maps, core_ids=list(range(8)))
        outs = [r["out"] for r in res.results]
        return np.concatenate(outs, axis=0).astype(np.float32)
    cur = [m["xcat"] for m in in_maps]
    for l in range(DEPTH):
        base = {}
        m0 = in_maps[0]
        for k, v in m0.items():
            if k in ("xcat", "cT"):
                continue
            if k in L_FIRST:
                base[k] = np.ascontiguousarray(v[l:l + 1])
            elif k in L_SECOND:
                base[k] = np.ascontiguousarray(v[:, l:l + 1])
            elif k == "gnT":
                base[k] = np.ascontiguousarray(v[:, l:l + 1])
            else:
                base[k] = v
        for bi in range(2):
            maps = []
            for core in range(8):
                m = dict(base)
                m["xcat"] = np.ascontiguousarray(cur[core][[bi, 1 - bi]])
                cTm = in_maps[core]["cT"]
                m["cT"] = np.ascontiguousarray(cTm[:, :, [bi, 1 - bi, 2]])
                maps.append(m)
            nc = bass.Bass("TRN2", target_bir_lowering=False)
            build(nc, nlayers=1, nbatch=1, ldim=1, full_out=True)
            res = run_bass_kernel_spmd(nc, maps, core_ids=list(range(8)))
            for core in range(8):
                new = np.array(cur[core])
                new[bi] = res.results[core]["out"][0]
                cur[core] = new
    outs = [c[:, NCX:, :] for c in cur]
    return np.concatenate(outs, axis=0).astype(np.float32)
```

```python
import contextlib
import math
import numpy as np
import ml_dtypes
import concourse.bass as bass
import concourse.mybir as mybir
from concourse.bass_utils import run_bass_kernel_spmd

F32 = mybir.dt.float32
BF16 = mybir.dt.bfloat16
ALU = mybir.AluOpType
AF = mybir.ActivationFunctionType

ENG = ("pe", "act", "dve", "pool", "sp")
NDMA = 24


class P:
    def __init__(self, nc, same_eng_sync=True):
        self.nc = nc
        self.ops = {e: [] for e in ENG}
        self.cnt = {e: 0 for e in ENG}
        self.waited = {e: {} for e in ENG}
        self.last_w = {}
        self.readers = {}
        self.dma_nextq = {}
        self.dma_cnt = [0] * NDMA
        self.dma_last_tok = [None] * NDMA
        self.same = same_eng_sync
        self.out_toks = []
        self.bar = []

    def barrier(self):
        self.bar = [("e", e, self.cnt[e]) for e in ENG if self.cnt[e]] + \
                   [("d", k, self.dma_cnt[k]) for k in range(NDMA) if self.dma_cnt[k]]

    def _deps(self, eng, reads, writes):
        deps = list(self.bar)
        for k in reads:
            t = self.last_w.get(k)
            if t is not None:
                deps.append(t)
        for k in writes:
            t = self.last_w.get(k)
            if t is not None:
                deps.append(t)
            deps.extend(self.readers.get(k, ()))
        return deps

    def _waits(self, eng, deps):
        w = self.waited[eng]
        best = {}
        for t in deps:
            if t[0] == "e":
                _, e2, idx = t
                if e2 == eng and (not self.same or eng == "pe"):
                    continue
                key = e2
            else:
                key = ("d", t[1])
            if w.get(key, 0) >= t[2]:
                continue
            if key not in best or best[key][2] < t[2]:
                best[key] = t
        for key, t in best.items():
            w[key] = t[2]
        return list(best.values())

    def _record(self, tok, reads, writes):
        for k in reads:
            lst = self.readers.setdefault(k, [])
            lst.append(tok)
            if len(lst) > 64:
                best = {}
                for t in lst:
                    kk = t[:2]
                    if kk not in best or best[kk][2] < t[2]:
                        best[kk] = t
                self.readers[k] = list(best.values())
        for k in writes:
            self.last_w[k] = tok
            self.readers[k] = []

    def op(self, eng, fn, reads=(), writes=()):
        deps = self._deps(eng, reads, writes)
        waits = self._waits(eng, deps)
        self.cnt[eng] += 1
        tok = ("e", eng, self.cnt[eng])
        self.ops[eng].append((waits, fn, ("e", eng)))
        self._record(tok, reads, writes)
        return tok

    def dma(self, q, fn, reads=(), writes=(), is_out=False):
        lo, n = (0, 16) if q == "sp" else (16, NDMA - 16)
        cur = self.dma_nextq.get(q, 0)
        k = lo + cur
        self.dma_nextq[q] = (cur + 1) % n
        deps = self._deps(q, reads, writes)
        if self.dma_last_tok[k] is not None:
            deps.append(self.dma_last_tok[k])
        waits = self._waits(q, deps)
        self.dma_cnt[k] += 16
        tok = ("d", k, self.dma_cnt[k])
        self.dma_last_tok[k] = tok
        self.ops[q].append((waits, fn, ("d", k)))
        self._record(tok, reads, writes)
        if is_out:
            self.out_toks.append(tok)
        return tok

    def emit(self):
        nc = self.nc
        with contextlib.ExitStack() as es:
            esem = {e: es.enter_context(nc.semaphore("s_" + e)) for e in ENG}
            dsem = [es.enter_context(nc.semaphore("d%d" % i)) for i in range(NDMA)]
            fin = list(self.out_toks)
            for e in ENG:
                if self.cnt[e]:
                    fin.append(("e", e, self.cnt[e]))
            for k in range(NDMA):
                if self.dma_cnt[k]:
                    fin.append(("d", k, self.dma_cnt[k]))
            block = es.enter_context(nc.Block())

            def run(eng_name, eng):
                for waits, fn, kind in self.ops[eng_name]:
                    for t in waits:
                        if t[0] == "e":
                            eng.wait_ge(esem[t[1]], t[2])
                        else:
                            eng.wait_ge(dsem[t[1]], t[2])
                    ins = fn(eng)
                    if kind[0] == "e":
                        ins.then_inc(esem[kind[1]], 1)
                    else:
                        ins.then_inc(dsem[kind[1]], 16)

            @block.tensor
            def _(e):
                run("pe", e)

            @block.scalar
            def _(e):
                run("act", e)

            @block.vector
            def _(e):
                run("dve", e)

            @block.gpsimd
            def _(e):
                run("pool", e)

            @block.sync
            def _(e):
                run("sp", e)
                for t in fin:
                    if t[0] == "e":
                        if t[1] != "sp":
                            e.wait_ge(esem[t[1]], t[2])
                    else:
                        e.wait_ge(dsem[t[1]], t[2])


NT = 2304
NCX = 256
NLAT = 2048
D = 1024
DEPTH = 4
LD = [4]
DFF = 2816
EPS = 1e-6
TB = [(0, 256)] + [(256 + 512 * i, 256 + 512 * (i + 1)) for i in range(4)]
C_A, C_NAQ, C_NAK, C_NAV = 0, 256, 512, 768
C_GQ, C_GK, C_GV, C_GF, C_GB, C_GR = 1024, 1280, 1536, 1792, 1808, 1824
C_SQ, C_SK, C_SV = 2080, 2336, 2464
C_SQP, C_SKD, C_SKDP, NEXT = 2592, 2848, 3104, 3360


def _rope_perm(nh):
    idx = np.arange(nh * 64).reshape(nh, 4, 16)
    return idx[:, [1, 0, 3, 2], :].reshape(-1)


def _rope_tables():
    cos = np.ones((64, NT), np.float32)
    sin = np.zeros((64, NT), np.float32)
    t = np.arange(NLAT)
    pos = (t // 64, t % 64)
    inv = 10000.0 ** (-np.arange(0, 32, 2, dtype=np.float32) / 32)
    for half in range(2):
        ang = pos[half].astype(np.float32)[None, :] * inv[:, None]
        c, s = np.cos(ang), np.sin(ang)
        b = 32 * half
        cos[b:b + 16, NCX:] = c
        cos[b + 16:b + 32, NCX:] = c
        sin[b:b + 16, NCX:] = -s
        sin[b + 16:b + 32, NCX:] = s
    return np.concatenate([cos, cos], 0), np.concatenate([sin, sin], 0)


class Ctx:
    pass


_UN = [0]


def UN():
    _UN[0] += 1
    return "t%d_" % _UN[0]


def build(nc, dbg=None, nlayers=DEPTH, nbatch=2, mixers=("s5", "na", "gla", "swa"), ldim=DEPTH, full_out=False):
    LD[0] = ldim
    _UN[0] = 0
    p = P(nc)
    g = Ctx()
    g.p, g.nc = p, nc
    dram = {}

    def din(name, shape, dt=F32):
        dram[name] = nc.dram_tensor(name, list(shape), dt, kind="ExternalInput").ap()
        return dram[name]

    xcat = din("xcat", [2, NT, D])
    cT = din("cT", [128, 8, 3])
    w_mod = din("w_mod", [LD[0], D, 6 * D])
    b_modT = din("b_modT", [128, LD[0], 48])
    gvec = din("gvec", [128, LD[0], 4, 8])
    w_in = din("w_in_ext", [LD[0], D, NEXT])
    w_out = din("w_out", [LD[0], D, D])
    w_up = din("ffn_w_up", [LD[0], D, 2 * DFF])
    w_down = din("ffn_w_down", [LD[0], DFF, D])
    convT = din("convT", [128, LD[0], 3, 22])
    ropeC = din("ropeC", [128, NT], BF16)
    ropeS = din("ropeS", [128, NT], BF16)
    maskAB = din("maskAB", [128, 2, 128], BF16)
    identf = din("identf", [128, 128])
    sinkT = din("sinkT", [128, LD[0], 2])
    na_exp = din("na_exp", [LD[0], 4, 64, 15 * 64])
    mcol = din("mcol", [128, 15 * 64])
    din("mc8", [128, 15 * 64])
    din("mcn", [128, 15 * 64])
    din("idb", [128, 64], BF16)
    din("idb2", [128, 2, 128], BF16)
    din("negm", [128, 2, 2, 128], BF16)
    din("tri", [128, 2, 128])
    din("bd64", [128, 128], BF16)
    din("gnT", [128, LD[0]])
    din("gla_w_gate2", [LD[0], 2, 16, 256])
    din("gla_b_gate", [LD[0], 2, 256])
    din("lamT", [128, LD[0], 2, 32])
    din("stepT", [128, LD[0], 32])
    din("dskT", [128, LD[0], 2])
    din("sgn", [128, 2])
    din("jmat", [128, 128])
    din("s5_w_glu", [LD[0], 256, 256])
    for nm in ("s5B1", "s5B2", "s5C1", "s5C2"):
        din(nm, [LD[0], 2, 16, 128, 128])
        dram[nm + "_bf"] = nc.dram_tensor(nm + "_bf", [LD[0], 2, 16, 128, 128], BF16).ap()
    g.dram = dram
    out = nc.dram_tensor("out", [2, NT if full_out else NLAT, D], F32, kind="ExternalOutput").ap()
    dbg_aps = {}
    if dbg:
        for name, shape in dbg.items():
            dbg_aps[name] = nc.dram_tensor("dbg_" + name, list(shape), F32, kind="ExternalOutput").ap()

    w_in_bf = nc.dram_tensor("w_in_bf", [LD[0], D, NEXT], BF16).ap()
    w_out_bf = nc.dram_tensor("w_out_bf", [LD[0], D, D], BF16).ap()
    w_up_bf = nc.dram_tensor("w_up_bf", [LD[0], D, 2 * DFF], BF16).ap()
    w_down_bf = nc.dram_tensor("w_down_bf", [LD[0], DFF, D], BF16).ap()

    def wkeys(key, l, n):
        return ["%s%d_%d" % (key, l, i) for i in range(n)]
    g.wkeys = wkeys
    g._uid = [0]

    def OP(eng, method, r=(), w=(), **kw):
        return p.op(eng, lambda e, kw=kw, method=method: getattr(e, method)(**kw), reads=r, writes=w)

    def MM(out_, lhsT, rhs, start, stop, r, w):
        return p.op("pe", lambda e: e.matmul(out_, lhsT=lhsT, rhs=rhs, start=start, stop=stop), reads=r, writes=w)

    def DMA(q, out_, in_, r=(), w=(), is_out=False, **kw):
        return p.dma(q, lambda e, kw=kw: e.dma_start(out=out_, in_=in_, **kw), reads=r, writes=w, is_out=is_out)

    g.OP, g.MM, g.DMA = OP, MM, DMA

    for l in range(nlayers):
        for (src, dst, rows, key) in ((w_in, w_in_bf, D, "w_in_bf"), (w_out, w_out_bf, D, "w_out_bf"),
                                      (w_up, w_up_bf, D, "w_up_bf"), (w_down, w_down_bf, DFF, "w_down_bf")):
            for r0 in range(0, rows, 128):
                DMA("pool", dst[l, r0:r0 + 128, :], src[l, r0:r0 + 128, :], w=["%s%d_%d" % (key, l, r0 // 128)],
                    max_dma_last_dim=4096)

    for l in range(nlayers):
        for nm in ("s5B1", "s5B2", "s5C1", "s5C2"):
            for d in range(2):
                DMA("pool", dram[nm + "_bf"][l, d].rearrange("g p s -> (g p) s"), dram[nm][l, d].rearrange("g p s -> (g p) s"),
                    w=["%s_bf%d" % (nm, l)] if d == 1 else ["%s_bf%d_d0" % (nm, l)], max_dma_last_dim=4096)

    es = contextlib.ExitStack()

    def sb(name, shape, dt):
        return es.enter_context(nc.sbuf_tensor(UN() + name, list(shape), dt))

    MODT = sb("MODT", [128, LD[0], 48, 3], F32)
    DER = sb("DER", [128, LD[0], 3, 6, 8], F32)
    GV = sb("GV", [128, LD[0], 4, 8], F32)
    CONV = sb("CONV", [128, LD[0], 3, 22], F32)
    ONESB = sb("ONESB", [128, 128], BF16)
    IDF = sb("IDF", [128, 128], F32)
    MAB = sb("MAB", [128, 2, 128], BF16)
    ESINK = sb("ESINK", [128, LD[0], 2], F32)
    PS = [es.enter_context(nc.psum_tensor("ps%d" % i, [128, 512], F32)) for i in range(8)]
    g.PS = PS

    OP("dve", "memset", ap=ONESB[:], constant=1.0, w=["ONESB"])
    DMA("sp", IDF[:], identf, w=["IDF"])
    DMA("sp", MAB[:], maskAB, w=["MAB"])
    DMA("sp", GV[:], gvec, w=["GV"])
    DMA("sp", CONV[:], convT, w=["CONV"])
    DMA("sp", ESINK[:], sinkT, w=["ESINK"])
    OP("act", "activation", out=ESINK[:], in_=ESINK[:], func=AF.Exp, r=["ESINK"], w=["ESINK"])

    with contextlib.ExitStack() as s1:
        SCT = s1.enter_context(nc.sbuf_tensor(UN() + "SCT", [128, 8, 3], F32))
        BM = s1.enter_context(nc.sbuf_tensor(UN() + "BM", [128, LD[0], 48], F32))
        WM = [s1.enter_context(nc.sbuf_tensor(UN() + "WM%d" % i, [128, 8, 512], F32)) for i in range(2)]
        DMA("sp", SCT[:], cT, w=["SCT"])
        DMA("sp", BM[:], b_modT, w=["BM"])
        OP("act", "activation", out=SCT[:], in_=SCT[:], func=AF.Silu, r=["SCT"], w=["SCT"])
        it = 0
        for l in range(nlayers):
            for cg in range(12):
                wb = WM[it % 2]
                wk = "WM%d" % (it % 2)
                it += 1
                DMA("sp", wb[:], w_mod[l, :, cg * 512:(cg + 1) * 512].rearrange("(k p) c -> p k c", p=128), w=[wk])
                for fc in range(4):
                    f = cg * 4 + fc
                    for k in range(8):
                        MM(PS[0][:, f * 3:f * 3 + 3], wb[:, k, fc * 128:(fc + 1) * 128], SCT[:, k, :], k == 0, k == 7,
                           [wk, "SCT"], ["ps0"])
            for j in range(3):
                OP("dve", "tensor_tensor", out=MODT[:, l, :, j], in0=PS[0][:, 0:144].rearrange("p (f j) -> p f j", j=3)[:, :, j],
                   in1=BM[:, l, :], op=ALU.add, r=["ps0", "BM"], w=["MODT"])
            for j in range(3):
                OP("dve", "scalar_tensor_tensor", out=DER[:, l, j, 0, :], in0=MODT[:, l, 8:16, j], scalar=1.0, in1=GV[:, l, 0, :],
                   op0=ALU.add, op1=ALU.mult, r=["MODT", "GV"], w=["DER"])
                OP("dve", "tensor_copy", out=DER[:, l, j, 1, :], in_=MODT[:, l, 0:8, j], r=["MODT"], w=["DER"])
                OP("dve", "tensor_tensor", out=DER[:, l, j, 2, :], in0=MODT[:, l, 16:24, j], in1=GV[:, l, 1, :], op=ALU.mult,
                   r=["MODT", "GV"], w=["DER"])
                OP("dve", "scalar_tensor_tensor", out=DER[:, l, j, 3, :], in0=MODT[:, l, 32:40, j], scalar=1.0, in1=GV[:, l, 2, :],
                   op0=ALU.add, op1=ALU.mult, r=["MODT", "GV"], w=["DER"])
                OP("dve", "tensor_copy", out=DER[:, l, j, 4, :], in_=MODT[:, l, 24:32, j], r=["MODT"], w=["DER"])
                OP("dve", "tensor_tensor", out=DER[:, l, j, 5, :], in0=MODT[:, l, 40:48, j], in1=GV[:, l, 3, :], op=ALU.mult,
                   r=["MODT", "GV"], w=["DER"])
    p.barrier()

    X = sb("X", [128, 8, NT], F32)
    g.X, g.DER, g.ONESB, g.MAB, g.ESINK, g.CONV = X, DER, ONESB, MAB, ESINK, CONV
    g.w_in_bf, g.w_out_bf, g.w_up_bf, g.w_down_bf = w_in_bf, w_out_bf, w_up_bf, w_down_bf
    g.ropeC, g.ropeS, g.na_exp, g.mcol = ropeC, ropeS, na_exp, mcol
    g.dbg_aps = dbg_aps

    def dump(name, ap, keys):
        if name in dbg_aps:
            DMA("pool", dbg_aps[name], ap, r=keys, is_out=True, max_dma_last_dim=2048)
    g.dump = dump

    for bi in range(nbatch):
        with contextlib.ExitStack() as s2:
            XS = [s2.enter_context(nc.sbuf_tensor(UN() + "XS%d" % i, [128, D], F32)) for i in range(2)]
            for tt in range(18):
                xs, xk = XS[tt % 2], "XS%d" % (tt % 2)
                DMA("sp", xs[:], xcat[bi, tt * 128:(tt + 1) * 128, :], w=[xk])
                for hh in range(2):
                    bank = PS[hh]
                    for kk in range(4):
                        k = hh * 4 + kk
                        p.op("pe", lambda e, o=bank[:, kk * 128:(kk + 1) * 128], i=xs[:, k * 128:(k + 1) * 128]:
                             e.transpose(out=o, in_=i, identity=IDF[:]), reads=[xk, "IDF"], writes=["ps%d" % hh])
                    if hh == 0:
                        OP("act", "activation", out=X[:, 0:4, tt * 128:(tt + 1) * 128],
                           in_=bank[:, :].rearrange("p (k t) -> p k t", t=128), func=AF.Copy, r=["ps0"], w=["X"])
                    else:
                        OP("dve", "tensor_copy", out=X[:, 4:8, tt * 128:(tt + 1) * 128],
                           in_=bank[:, :].rearrange("p (k t) -> p k t", t=128), r=["ps1"], w=["X"])
        p.barrier()
        for l in range(nlayers):
            layer(g, bi, l, (l == DEPTH - 1) and not full_out, mixers)
        with contextlib.ExitStack() as s3:
            OS_ = [s3.enter_context(nc.sbuf_tensor(UN() + "OST%d" % i, [128, D], F32)) for i in range(2)]
            for tt in range(18 if full_out else 16):
                ot, ok = OS_[tt % 2], "OST%d" % (tt % 2)
                t0 = (0 if full_out else NCX) + tt * 128
                for hh in range(2):
                    bank = PS[hh]
                    for kk in range(4):
                        k = hh * 4 + kk
                        p.op("pe", lambda e, o=bank[:, kk * 128:(kk + 1) * 128], i=X[:, k, t0:t0 + 128]:
                             e.transpose(out=o, in_=i, identity=IDF[:]), reads=["X", "IDF"], writes=["ps%d" % hh])
                    if hh == 0:
                        OP("act", "activation", out=ot[:, 0:512], in_=bank[:, :], func=AF.Copy, r=["ps0"], w=[ok])
                    else:
                        OP("dve", "tensor_copy", out=ot[:, 512:1024], in_=bank[:, :], r=["ps1"], w=[ok])
                DMA("sp", out[bi, tt * 128:(tt + 1) * 128, :], ot[:], r=[ok], is_out=True)
        p.barrier()

    p.emit()
    es.close()
    return nc


def rms_stats(g, src_fn, nk, w, SQ, RS, psb, src_keys):
    OP, MM = g.OP, g.MM
    for k in range(nk):
        OP("act", "activation", out=SQ[:, k, :w], in_=src_fn(k), func=AF.Square, r=src_keys, w=["SQ"])
    for k in range(nk):
        MM(g.PS[psb][:, :w], g.ONESB[:], SQ[:, k, :w], k == 0, k == nk - 1, ["SQ", "ONESB"], ["ps%d" % psb])
    OP("act", "activation", out=RS[:, :w], in_=g.PS[psb][:, :w], func=AF.Sqrt, scale=1.0 / D, bias=g.EPSC[:, 0:1],
       r=["ps%d" % psb, "EPSC"], w=["RS"])
    OP("dve", "reciprocal", out=RS[:, :w], in_=RS[:, :w], r=["RS"], w=["RS"])


def layer(g, bi, l, last, mixers):
    nc, p, OP, MM, DMA = g.nc, g.p, g.OP, g.MM, g.DMA
    X, DER, PS = g.X, g.DER, g.PS
    with contextlib.ExitStack() as sl:
        def sb(name, shape, dt):
            return sl.enter_context(nc.sbuf_tensor(UN() + name, list(shape), dt))
        YT = sb("YT", [128, 8, NT], BF16)
        g.YT = YT
        EPSC = sb("EPSC", [128, 1], F32)
        g.EPSC = EPSC
        OP("dve", "memset", ap=EPSC[:], constant=EPS, w=["EPSC"])
        with contextlib.ExitStack() as sh:
            HT = sh.enter_context(nc.sbuf_tensor(UN() + "HT", [128, 8, NT], BF16))
            g.HT = HT
            with contextlib.ExitStack() as sa:
                SQ = sa.enter_context(nc.sbuf_tensor(UN() + "SQ", [128, 8, 512], BF16))
                RS = sa.enter_context(nc.sbuf_tensor(UN() + "RS", [128, 512], F32))
                TMP = [sa.enter_context(nc.sbuf_tensor(UN() + "TMPa%d" % i, [128, 512], F32)) for i in range(2)]
                for (a, b) in TB:
                    w = b - a
                    j = 2 if a < NCX else bi
                    rms_stats(g, lambda k: X[:, k, a:b], 8, w, SQ, RS, 0, ["X"])
                    for k in range(8):
                        tm, tk = TMP[k % 2], "TMPa%d" % (k % 2)
                        OP("dve", "tensor_tensor", out=tm[:, :w], in0=X[:, k, a:b], in1=RS[:, :w], op=ALU.mult,
                           r=["X", "RS"], w=[tk])
                        OP("act", "activation", out=HT[:, k, a:b], in_=tm[:, :w], func=AF.Identity,
                           scale=DER[:, l, j, 0, k:k + 1], bias=DER[:, l, j, 1, k:k + 1], r=[tk, "DER"], w=["HT"])
            p.barrier()
            g.dump("HT%d_%d" % (bi, l), HT[:], ["HT"])
            for nm, chs in (("s5", (0, 1)), ("na", (2, 3)), ("gla", (4, 5)), ("swa", (6, 7))):
                if nm not in mixers:
                    OP("pool", "memset", ap=YT[:, chs[0]:chs[1] + 1, :], constant=0.0, w=["YT"])
            if "s5" in mixers:
                s5_project(g, bi, l)
                p.barrier()
            if "swa" in mixers:
                attn_mixer(g, bi, l, "swa")
                p.barrier()
            if "na" in mixers:
                attn_mixer(g, bi, l, "na")
                p.barrier()
            if "gla" in mixers:
                gla_mixer(g, bi, l)
                p.barrier()
        p.barrier()
        if "s5" in mixers:
            s5_main(g, bi, l)
            p.barrier()
        g.dump("YT%d_%d" % (bi, l), YT[:], ["YT"])
        with contextlib.ExitStack() as sc:
            WO = sc.enter_context(nc.sbuf_tensor(UN() + "WO", [128, 8, D], BF16))
            OS_ = sc.enter_context(nc.sbuf_tensor(UN() + "OS", [128, 8, 512], F32))
            SQ = sc.enter_context(nc.sbuf_tensor(UN() + "SQ", [128, 8, 512], BF16))
            RS = sc.enter_context(nc.sbuf_tensor(UN() + "RS", [128, 512], F32))
            TMP = [sc.enter_context(nc.sbuf_tensor(UN() + "TMPc%d" % i, [128, 512], F32)) for i in range(2)]
            DMA("sp", WO[:], g.w_out_bf[l].rearrange("(k p) c -> p k c", p=128), r=g.wkeys("w_out_bf", l, 8), w=["WO"])
            for (a, b) in TB:
                if last and a < NCX:
                    continue
                w = b - a
                j = 2 if a < NCX else bi
                for dc in range(8):
                    bank = 1 + dc % 2
                    for k in range(8):
                        MM(PS[bank][:, :w], WO[:, k, dc * 128:(dc + 1) * 128], YT[:, k, a:b], k == 0, k == 7,
                           ["WO", "YT"], ["ps%d" % bank])
                    OP("dve", "tensor_copy", out=OS_[:, dc, :w], in_=PS[bank][:, :w], r=["ps%d" % bank], w=["OS%d" % dc])
                rms_stats(g, lambda k: OS_[:, k, :w], 8, w, SQ, RS, 0, ["OS%d" % k for k in range(8)])
                for k in range(8):
                    tm, tk = TMP[k % 2], "TMPc%d" % (k % 2)
                    OP("pool", "tensor_tensor", out=tm[:, :w], in0=OS_[:, k, :w], in1=RS[:, :w], op=ALU.mult,
                       r=["OS%d" % k, "RS"], w=[tk])
                    OP("dve", "scalar_tensor_tensor", out=X[:, k, a:b], in0=tm[:, :w], scalar=DER[:, l, j, 2, k:k + 1],
                       in1=X[:, k, a:b], op0=ALU.mult, op1=ALU.add, r=[tk, "DER", "X"], w=["X"])
    p.barrier()
    g.dump("X1_%d_%d" % (bi, l), X[:], ["X"])
    ffn(g, bi, l, last)
    p.barrier()
    g.dump("X2_%d_%d" % (bi, l), X[:], ["X"])


def ffn_blocks():
    blks = [(0, NCX, 0, NCX)]
    for i in range(5):
        oa = NCX + 410 * i
        ob = min(NCX + 410 * (i + 1), NT)
        blks.append((max(oa - 1, NCX), min(ob + 1, NT), oa, ob))
    return blks


def ffn(g, bi, l, last):
    nc, p, OP, MM, DMA = g.nc, g.p, g.OP, g.MM, g.DMA
    X, DER, PS = g.X, g.DER, g.PS
    with contextlib.ExitStack() as sf:
        def sb(name, shape, dt):
            return sf.enter_context(nc.sbuf_tensor(UN() + name, list(shape), dt))
        EPSC = sb("EPSC", [128, 1], F32)
        g.EPSC = EPSC
        OP("dve", "memset", ap=EPSC[:], constant=EPS, w=["EPSC"])
        HBs = [sb("HB%d" % i, [128, 8, 512], BF16) for i in range(2)]
        GB = sb("GB", [128, 22, 512], BF16)
        SQ = sb("SQ", [128, 8, 512], BF16)
        RS = sb("RS", [128, 512], F32)
        OS_ = sb("OS", [128, 8, 512], F32)
        TMP = [sb("TMPf%d" % i, [128, 512], F32) for i in range(2)]
        GS = [sb("GS%d" % i, [128, 514], F32) for i in range(2)]
        CV = [sb("CV%d" % i, [128, 512], F32) for i in range(2)]
        U1 = [sb("U1%d" % i, [128, 512], F32) for i in range(2)]
        WU = [sb("WU%d" % i, [128, 8, 256], BF16) for i in range(4)]
        WD = [sb("WD%d" % i, [128, 22, 128], BF16) for i in range(2)]
        wu_it = 0
        wd_it = 0
        blks = [bk for bk in ffn_blocks() if not (last and bk[0] < NCX)]

        def make_hb(bidx):
            (ca, cb, oa, ob) = blks[bidx]
            w = cb - ca
            j = 2 if ca < NCX else bi
            HB, hbk = HBs[bidx % 2], "HB%d" % (bidx % 2)
            rms_stats(g, lambda k: X[:, k, ca:cb], 8, w, SQ, RS, 0, ["X"])
            for k in range(8):
                tm, tk = TMP[k % 2], "TMPf%d" % (k % 2)
                OP("dve", "tensor_tensor", out=tm[:, :w], in0=X[:, k, ca:cb], in1=RS[:, :w], op=ALU.mult,
                   r=["X", "RS"], w=[tk])
                OP("act", "activation", out=HB[:, k, :w], in_=tm[:, :w], func=AF.Identity,
                   scale=DER[:, l, j, 3, k:k + 1], bias=DER[:, l, j, 4, k:k + 1], r=[tk, "DER"], w=[hbk])

        make_hb(0)
        for bidx, (ca, cb, oa, ob) in enumerate(blks):
            w = cb - ca
            wo = ob - oa
            off = oa - ca
            j = 2 if ca < NCX else bi
            HB, hbk = HBs[bidx % 2], "HB%d" % (bidx % 2)
            if bidx + 1 < len(blks):
                make_hb(bidx + 1)
            for jc in range(22):
                wu, wuk = WU[wu_it % 4], "WU%d" % (wu_it % 4)
                wu_it += 1
                DMA("sp", wu[:, :, 0:128], g.w_up_bf[l, :, jc * 128:(jc + 1) * 128].rearrange("(k p) c -> p k c", p=128),
                    r=g.wkeys("w_up_bf", l, 8), w=[wuk + "g"])
                DMA("sp", wu[:, :, 128:256],
                    g.w_up_bf[l, :, DFF + jc * 128:DFF + (jc + 1) * 128].rearrange("(k p) c -> p k c", p=128),
                    r=g.wkeys("w_up_bf", l, 8), w=[wuk + "v"])
                bg, bv = (1, 2)[jc % 2], (3, 4, 7, 5)[jc % 4]
                for k in range(8):
                    MM(PS[bg][:, :w], wu[:, k, 0:128], HB[:, k, :w], k == 0, k == 7, [wuk + "g", hbk], ["ps%d" % bg])
                for k in range(8):
                    MM(PS[bv][:, :w], wu[:, k, 128:256], HB[:, k, :w], k == 0, k == 7, [wuk + "v", hbk], ["ps%d" % bv])
                gs, gk = GS[jc % 2], "GS%d" % (jc % 2)
                cv, ck = CV[jc % 2], "CV%d" % (jc % 2)
                u1, uk = U1[jc % 2], "U1%d" % (jc % 2)
                OP("pool", "memset", ap=gs[:, 0:1], constant=0.0, w=[gk])
                OP("pool", "memset", ap=gs[:, w + 1:w + 2], constant=0.0, w=[gk])
                OP("act", "activation", out=gs[:, 1:w + 1], in_=PS[bg][:, :w], func=AF.Copy, r=["ps%d" % bg], w=[gk])
                s = 1 + off
                OP("pool", "tensor_scalar", out=cv[:, :wo], in0=gs[:, s - 1:s - 1 + wo], scalar1=g.CONV[:, l, 0, jc:jc + 1],
                   scalar2=0.0, op0=ALU.mult, op1=ALU.add, r=[gk, "CONV"], w=[ck])
                OP("dve", "scalar_tensor_tensor", out=cv[:, :wo], in0=gs[:, s:s + wo], scalar=g.CONV[:, l, 1, jc:jc + 1],
                   in1=cv[:, :wo], op0=ALU.mult, op1=ALU.add, r=[gk, "CONV", ck], w=[ck])
                OP("dve", "scalar_tensor_tensor", out=cv[:, :wo], in0=gs[:, s + 1:s + 1 + wo], scalar=g.CONV[:, l, 2, jc:jc + 1],
                   in1=cv[:, :wo], op0=ALU.mult, op1=ALU.add, r=[gk, "CONV", ck], w=[ck])
                OP("act", "activation", out=u1[:, :wo], in_=cv[:, :wo], func=AF.Gelu_apprx_tanh, r=[ck], w=[uk])
                OP("dve", "tensor_tensor", out=GB[:, jc, :wo], in0=PS[bv][:, off:off + wo], in1=u1[:, :wo], op=ALU.mult,
                   r=["ps%d" % bv, uk], w=["GB"])
            for dc in range(8):
                wd, wdk = WD[wd_it % 2], "WD%d" % (wd_it % 2)
                wd_it += 1
                DMA("sp", wd[:], g.w_down_bf[l, :, dc * 128:(dc + 1) * 128].rearrange("(k p) c -> p k c", p=128),
                    r=g.wkeys("w_down_bf", l, 22), w=[wdk])
                bank = (6, 0)[dc % 2]
                for jc in range(22):
                    MM(PS[bank][:, :wo], wd[:, jc, :], GB[:, jc, :wo], jc == 0, jc == 21, [wdk, "GB"], ["ps%d" % bank])
                OP("dve", "tensor_copy", out=OS_[:, dc, :wo], in_=PS[bank][:, :wo], r=["ps%d" % bank], w=["OS%d" % dc])
            rms_stats(g, lambda k: OS_[:, k, :wo], 8, wo, SQ, RS, 0, ["OS%d" % k for k in range(8)])
            for k in range(8):
                tm, tk = TMP[k % 2], "TMPf%d" % (k % 2)
                OP("pool", "tensor_tensor", out=tm[:, :wo], in0=OS_[:, k, :wo], in1=RS[:, :wo], op=ALU.mult,
                   r=["OS%d" % k, "RS"], w=[tk])
                OP("dve", "scalar_tensor_tensor", out=X[:, k, oa:ob], in0=tm[:, :wo], scalar=DER[:, l, j, 5, k:k + 1],
                   in1=X[:, k, oa:ob], op0=ALU.mult, op1=ALU.add, r=[tk, "DER", "X"], w=["X"])


def na_rows(kr):
    rs = [r for r in range(32) if min(max(r - 4, 0), 24) <= kr <= min(max(r - 4, 0), 24) + 7]
    assert rs == list(range(rs[0], rs[-1] + 1))
    return rs[0], rs[-1] + 1


def attn_mixer(g, bi, l, kind):
    nc, p, OP, MM, DMA = g.nc, g.p, g.OP, g.MM, g.DMA
    PS, HT, YT = g.PS, g.HT, g.YT
    wkey = g.wkeys("w_in_bf", l, 8)
    swa = kind == "swa"
    with contextlib.ExitStack() as sm:
        def sb(name, shape, dt):
            return sm.enter_context(nc.sbuf_tensor(UN() + name, list(shape), dt))
        QT = sb("QT", [128, NT], BF16)
        KT = sb("KT", [128, NT], BF16)
        VT = sb("VT", [128, 18, 128], BF16)
        WB = [sb("WB%d" % i, [128, 8, 128], BF16) for i in range(3)]
        T1 = sb("T1", [128, 512], F32)
        T2 = sb("T2", [128, 512], F32)
        PT = [sb("PT%d" % i, [128, 512], BF16) for i in range(2)]
        REC = sb("REC", [128, 512], F32)
        if swa:
            RC = sb("RC", [128, NT], BF16)
            RSN = sb("RSN", [128, NT], BF16)
            DMA("sp", RC[:], g.ropeC, w=["RC"])
            DMA("sp", RSN[:], g.ropeS, w=["RSN"])
            IDB2 = sb("IDB2", [128, 2, 128], BF16)
            NEGM = sb("NEGM", [128, 2, 2, 128], BF16)
            DMA("sp", IDB2[:], g.dram["idb2"], w=["IDB2"])
            DMA("sp", NEGM[:], g.dram["negm"], w=["NEGM"])
        else:
            UT = sb("UT", [128, 2, 960], BF16)
            UF = sb("UF", [128, 960], F32)
            MC8 = sb("MC8", [128, 960], F32)
            MCN = sb("MCN", [128, 960], F32)
            IDB = sb("IDB", [128, 64], BF16)
            DMA("sp", MC8[:], g.dram["mc8"], w=["MC8"])
            DMA("sp", MCN[:], g.dram["mcn"], w=["MCN"])
            DMA("sp", IDB[:], g.dram["idb"], w=["IDB"])
        wb_it = [0]

        def load_w(c0):
            i = wb_it[0] % 3
            wb_it[0] += 1
            DMA("sp", WB[i][:], g.w_in_bf[l, :, c0:c0 + 128].rearrange("(k p) c -> p k c", p=128), r=wkey, w=["WB%d" % i])
            return WB[i], "WB%d" % i

        for c in range(2):
            if swa:
                cq, cqp, ck, ckp, cv_ = C_SQ + 128 * c, C_SQP + 128 * c, C_SKD + 128 * c, C_SKDP + 128 * c, C_SV
            else:
                cq, ck, cv_ = C_NAQ + 128 * c, C_NAK + 128 * c, C_NAV + 128 * c
            for (dst, dk, c1, c2) in ((QT, "QT", cq, cqp if swa else None), (KT, "KT", ck, ckp if swa else None)):
                w1, w1k = load_w(c1)
                if swa:
                    w2, w2k = load_w(c2)
                for (a, b) in TB:
                    w = b - a
                    for k in range(8):
                        MM(PS[0][:, :w], w1[:, k, :], HT[:, k, a:b], k == 0, k == 7, [w1k, "HT"], ["ps0"])
                    if swa:
                        for k in range(8):
                            MM(PS[1][:, :w], w2[:, k, :], HT[:, k, a:b], k == 0, k == 7, [w2k, "HT"], ["ps1"])
                        OP("dve", "tensor_tensor", out=T1[:, :w], in0=PS[0][:, :w], in1=RC[:, a:b], op=ALU.mult,
                           r=["ps0", "RC"], w=["T1"])
                        OP("dve", "tensor_tensor", out=T2[:, :w], in0=PS[1][:, :w], in1=RSN[:, a:b], op=ALU.mult,
                           r=["ps1", "RSN"], w=["T2"])
                        OP("pool", "tensor_tensor", out=dst[:, a:b], in0=T1[:, :w], in1=T2[:, :w], op=ALU.add,
                           r=["T1", "T2"], w=[dk])
                    else:
                        OP("act", "activation", out=dst[:, a:b], in_=PS[0][:, :w], func=AF.Copy, r=["ps0"], w=[dk])
            if (not swa) or c == 0:
                wv, wvk = load_w(cv_)
                for t4 in range(0, 18, 4):
                    nt = min(4, 18 - t4)
                    for ti in range(nt):
                        tt = t4 + ti
                        for k in range(8):
                            MM(PS[2][:, ti * 128:(ti + 1) * 128], HT[:, k, tt * 128:(tt + 1) * 128], wv[:, k, :], k == 0, k == 7,
                               [wvk, "HT"], ["ps2"])
                    OP("act", "activation", out=VT[:, t4:t4 + nt, :],
                       in_=PS[2][:, 0:nt * 128].rearrange("p (t c) -> p t c", c=128), func=AF.Copy, r=["ps2"], w=["VT"])
            if not swa:
                for hh in range(2):
                    for half in range(2):
                        DMA("sp", UF[half * 64:(half + 1) * 64, :], g.na_exp[l, 2 * c + hh], w=["UF"])
                    OP("dve", "tensor_tensor", out=UF[:], in0=UF[:], in1=MC8[:], op=ALU.mult, r=["UF", "MC8"], w=["UF"])
                    OP("dve", "tensor_tensor", out=UT[:, hh, :], in0=UF[:], in1=MCN[:], op=ALU.add, r=["UF", "MCN"], w=["UT"])
            for (qa, qb) in TB:
                qw = qb - qa
                for hh in range(2):
                    h = 2 * c + hh
                    hb = 64 * hh
                    items = []
                    for kc in range(2):
                        items.append((kc * 128, 128, 0, kc, qa, qb, None))
                    if qa >= NCX:
                        if swa:
                            for kb in range(16):
                                ka = NCX + 128 * kb
                                a_ = max(qa, ka - 128)
                                b_ = min(qb, ka + 256)
                                if a_ < b_:
                                    items.append((ka, 128, 0, 2 + kb, a_, b_, ("swa", ka)))
                        else:
                            for kr in range(32):
                                r0, r1 = na_rows(kr)
                                a_ = max(qa, NCX + 64 * r0)
                                b_ = min(qb, NCX + 64 * r1)
                                if a_ < b_:
                                    items.append((NCX + 64 * kr, 64, 64 * (kr % 2), 2 + kr // 2, a_, b_, ("na", kr)))
                    vc0 = 64 * (h // 2) if swa else 64 * hh
                    def s_mm(ii):
                        (ka, nk, pb, vt, a_, b_, post) = items[ii]
                        n = b_ - a_
                        sbank = 3 + ii % 2
                        nab = post is not None and post[0] == "na"
                        subs = []
                        if post is not None and post[0] == "swa":
                            kst = post[1]
                            if a_ < kst:
                                subs.append((0, 0))
                            if b_ > kst + 128:
                                subs.append((kst + 128 - a_, 1))
                        MM(PS[sbank][pb:pb + nk, :n], KT[hb:hb + 64, ka:ka + nk], QT[hb:hb + 64, a_:b_], True, not (nab or subs),
                           ["KT", "QT"], ["ps%d" % sbank])
                        for si, (o_, mk) in enumerate(subs):
                            for hf in range(2):
                                MM(PS[sbank][:, o_:o_ + 128], IDB2[hb:hb + 64, hf, :], NEGM[hb:hb + 64, mk, hf, :], False,
                                   si == len(subs) - 1 and hf == 1, ["IDB2", "NEGM"], ["ps%d" % sbank])
                        if nab:
                            kr = post[1]
                            i0 = (a_ - NCX) // 64 - kr + 7
                            MM(PS[sbank][pb:pb + nk, :n], IDB[hb:hb + 64, :], UT[hb:hb + 64, hh, i0 * 64:i0 * 64 + n], False, True,
                               ["IDB", "UT"], ["ps%d" % sbank])

                    for ii, (ka, nk, pb, vt, a_, b_, post) in enumerate(items):
                        n = b_ - a_
                        sbank = 3 + ii % 2
                        pt, ptk = PT[ii % 2], "PT%d" % (ii % 2)
                        s_mm(ii)
                        OP("act", "activation", out=pt[pb:pb + nk, :n], in_=PS[sbank][pb:pb + nk, :n], func=AF.Exp, scale=0.125,
                           r=["ps%d" % sbank], w=[ptk])
                        MM(PS[5][hb:hb + 64, a_ - qa:b_ - qa], VT[pb:pb + nk, vt, vc0:vc0 + 64], pt[pb:pb + nk, :n], ii == 0,
                           ii == len(items) - 1, ["VT", ptk], ["ps5"])
                        MM(PS[6][hb:hb + 64, a_ - qa:b_ - qa], g.ONESB[pb:pb + nk, 0:64], pt[pb:pb + nk, :n], ii == 0,
                           ii == len(items) - 1, ["ONESB", ptk], ["ps6"])
                if swa:
                    OP("dve", "tensor_scalar", out=REC[:, :qw], in0=PS[6][:, :qw], scalar1=g.ESINK[:, l, c:c + 1], scalar2=None,
                       op0=ALU.add, r=["ps6", "ESINK"], w=["REC"])
                    OP("dve", "reciprocal", out=REC[:, :qw], in_=REC[:, :qw], r=["REC"], w=["REC"])
                else:
                    OP("dve", "reciprocal", out=REC[:, :qw], in_=PS[6][:, :qw], r=["ps6"], w=["REC"])
                yc = (6 if swa else 2) + c
                OP("dve", "tensor_tensor", out=YT[:, yc, qa:qb], in0=PS[5][:, :qw], in1=REC[:, :qw], op=ALU.mult,
                   r=["ps5", "REC"], w=["YT"])


TC = 64


def s5_project(g, bi, l):
    nc, p, OP, MM, DMA = g.nc, g.p, g.OP, g.MM, g.DMA
    PS, HT, YT = g.PS, g.HT, g.YT
    wkey = g.wkeys("w_in_bf", l, 8)
    with contextlib.ExitStack() as sm:
        WA = [sm.enter_context(nc.sbuf_tensor(UN() + "sWA%d" % i, [128, 8, 128], BF16)) for i in range(2)]
        for cc in range(2):
            wa, wak = WA[cc], "sWA%d" % cc
            DMA("sp", wa[:], g.w_in_bf[l, :, C_A + 128 * cc:C_A + 128 * cc + 128].rearrange("(k p) c -> p k c", p=128), r=wkey,
                w=[wak])
            for (a, b) in TB:
                w = b - a
                for k in range(8):
                    MM(PS[7][:, :w], wa[:, k, :], HT[:, k, a:b], k == 0, k == 7, [wak, "HT"], ["ps7"])
                OP("act", "activation", out=YT[:, cc, a:b], in_=PS[7][:, :w], func=AF.Copy, r=["ps7"], w=["sUT"])


def s5_main(g, bi, l):
    nc, p, OP, MM, DMA = g.nc, g.p, g.OP, g.MM, g.DMA
    PS, YT = g.PS, g.YT
    wkey = g.wkeys("w_in_bf", l, 8)
    PI = math.pi
    with contextlib.ExitStack() as sm:
        def sb(name, shape, dt):
            return sm.enter_context(nc.sbuf_tensor(UN() + name, list(shape), dt))
        UT = YT[:, 0:2, :]
        YF = sb("sYF", [128, 2, NT], BF16)
        BP = [sb("sBP%d" % i, [128, 16, 128], BF16) for i in range(2)]
        CP = [sb("sCP%d" % i, [128, 16, 128], BF16) for i in range(2)]
        TAB = [sb("sTAB%d" % i, [128, 16, TC], BF16) for i in range(4)]
        Zs = [sb("sZ%d" % i, [128, 16, TC], F32) for i in range(2)]
        Ws = [sb("sW%d" % i, [128, 16, TC], F32) for i in range(2)]
        ZAs = [sb("sZA%d" % i, [128, 8, TC], F32) for i in range(2)]
        ZBs = [sb("sZB%d" % i, [128, 8, TC], F32) for i in range(2)]
        HCs = [sb("sHC%d" % i, [128, 8, TC], BF16) for i in range(2)]
        HSs = [sb("sHS%d" % i, [128, 8, TC], BF16) for i in range(2)]
        Z, W, ZA, ZB, HC, HS = Zs[0], Ws[0], ZAs[0], ZBs[0], HCs[0], HSs[0]
        WGL = sb("sWGL", [128, 2, 256], BF16)
        LAM = sb("sLAM", [128, 2, 32], F32)
        STP = sb("sSTP", [128, 32], F32)
        DSK = sb("sDSK", [128, LD[0], 2], F32)
        SGN = sb("sSGN", [128, 2], F32)
        JM = sb("sJM", [128, 128], F32)
        HPI = sb("sHPI", [128, 1], F32)
        sm_names = ["RHO", "TH", "M", "SH", "CH", "SN", "CS", "LBR", "LBI", "DEN", "KR", "KI", "T0", "T1", "EC", "ES", "EC2", "ES2"]
        SM = {n: sb("s" + n, [128, 32], F32) for n in sm_names}
        INIT = sb("sINIT", [128, 16], F32)
        ENDS = sb("sENDS", [128, 16], F32)
        RT1 = sb("sRT1", [128, 16], F32)
        TMPYs = [sb("sTMPY%d" % i, [128, TC], F32) for i in range(2)]
        DMA("sp", LAM[:], g.dram["lamT"][:, l], w=["sLAM"])
        DMA("sp", STP[:], g.dram["stepT"][:, l], w=["sSTP"])
        DMA("sp", DSK[:], g.dram["dskT"], w=["sDSK"])
        DMA("sp", SGN[:], g.dram["sgn"], w=["sSGN"])
        DMA("sp", JM[:], g.dram["jmat"], w=["sJM"])
        DMA("pool", WGL[:], g.dram["s5_w_glu"][l].rearrange("(k p) c -> p k c", p=128), w=["sWGL"])
        OP("dve", "memset", ap=HPI[:], constant=PI / 2, w=["sHPI"])

        def V(eng, method, outn, r, **kw):
            OP(eng, method, r=["s" + x for x in r], w=["s" + outn], **kw)

        def TT(outn, an, bn, op):
            V("dve", "tensor_tensor", outn, [an, bn], out=SM[outn][:], in0=SM[an][:], in1=SM[bn][:], op=op)

        OP("act", "activation", out=STP[:], in_=STP[:], func=AF.Exp, r=["sSTP"], w=["sSTP"])
        OP("dve", "tensor_tensor", out=SM["T0"][:], in0=LAM[:, 0, :], in1=STP[:], op=ALU.mult, r=["sLAM", "sSTP"], w=["sT0"])
        OP("act", "activation", out=SM["RHO"][:], in_=SM["T0"][:], func=AF.Exp, r=["sT0"], w=["sRHO"])
        OP("dve", "tensor_tensor", out=SM["TH"][:], in0=LAM[:, 1, :], in1=STP[:], op=ALU.mult, r=["sLAM", "sSTP"], w=["sTH"])
        for _ in range(5):
            V("dve", "tensor_scalar", "M", ["TH"], out=SM["M"][:], in0=SM["TH"][:], scalar1=PI, scalar2=-2 * PI, op0=ALU.is_gt,
              op1=ALU.mult)
            TT("TH", "TH", "M", ALU.add)
        for _ in range(2):
            V("dve", "tensor_scalar", "M", ["TH"], out=SM["M"][:], in0=SM["TH"][:], scalar1=-PI, scalar2=2 * PI, op0=ALU.is_lt,
              op1=ALU.mult)
            TT("TH", "TH", "M", ALU.add)
        OP("act", "activation", out=SM["SH"][:], in_=SM["TH"][:], func=AF.Sin, scale=0.5, r=["sTH"], w=["sSH"])
        OP("act", "activation", out=SM["CH"][:], in_=SM["TH"][:], func=AF.Sin, scale=0.5, bias=HPI[:, 0:1], r=["sTH", "sHPI"],
           w=["sCH"])
        TT("SN", "SH", "CH", ALU.mult)
        V("dve", "tensor_scalar", "SN", ["SN"], out=SM["SN"][:], in0=SM["SN"][:], scalar1=2.0, scalar2=None, op0=ALU.mult)
        TT("T0", "CH", "CH", ALU.mult)
        TT("T1", "SH", "SH", ALU.mult)
        TT("CS", "T0", "T1", ALU.subtract)
        TT("LBR", "RHO", "CS", ALU.mult)
        TT("LBI", "RHO", "SN", ALU.mult)
        V("dve", "tensor_scalar", "LBR", ["LBR"], out=SM["LBR"][:], in0=SM["LBR"][:], scalar1=-1.0, scalar2=None, op0=ALU.add)
        OP("dve", "tensor_tensor", out=SM["T0"][:], in0=LAM[:, 0, :], in1=LAM[:, 0, :], op=ALU.mult, r=["sLAM"], w=["sT0"])
        OP("dve", "tensor_tensor", out=SM["T1"][:], in0=LAM[:, 1, :], in1=LAM[:, 1, :], op=ALU.mult, r=["sLAM"], w=["sT1"])
        TT("DEN", "T0", "T1", ALU.add)
        V("dve", "reciprocal", "DEN", ["DEN"], out=SM["DEN"][:], in_=SM["DEN"][:])
        OP("dve", "tensor_tensor", out=SM["T0"][:], in0=SM["LBR"][:], in1=LAM[:, 0, :], op=ALU.mult, r=["sLBR", "sLAM"], w=["sT0"])
        OP("dve", "tensor_tensor", out=SM["T1"][:], in0=SM["LBI"][:], in1=LAM[:, 1, :], op=ALU.mult, r=["sLBI", "sLAM"], w=["sT1"])
        TT("KR", "T0", "T1", ALU.add)
        TT("KR", "KR", "DEN", ALU.mult)
        OP("dve", "tensor_tensor", out=SM["T0"][:], in0=SM["LBI"][:], in1=LAM[:, 0, :], op=ALU.mult, r=["sLBI", "sLAM"], w=["sT0"])
        OP("dve", "tensor_tensor", out=SM["T1"][:], in0=SM["LBR"][:], in1=LAM[:, 1, :], op=ALU.mult, r=["sLBR", "sLAM"], w=["sT1"])
        TT("KI", "T0", "T1", ALU.subtract)
        TT("KI", "KI", "DEN", ALU.mult)

        nchunk = NT // TC
        for d in range(2):
            qs = slice(16 * d, 16 * d + 16)
            for i, nm in enumerate(("s5B1", "s5B2")):
                DMA("sp", BP[i][:], g.dram[nm + "_bf"][l, d].rearrange("g p s -> p g s"), r=["%s_bf%d" % (nm, l), "%s_bf%d_d0" % (nm, l)], w=["sBP%d" % i])
            for i, nm in enumerate(("s5C1", "s5C2")):
                DMA("sp", CP[i][:], g.dram[nm + "_bf"][l, d].rearrange("g p s -> p g s"), r=["%s_bf%d" % (nm, l), "%s_bf%d_d0" % (nm, l)], w=["sCP%d" % i])
            for which in range(2):
                i0 = 0 if d == 0 else TC - 1
                if which == 0:
                    OP("dve", "tensor_copy", out=Z[:, :, i0], in_=SM["KR"][:, qs], r=["sKR"], w=["sZ0"])
                    OP("dve", "tensor_copy", out=W[:, :, i0], in_=SM["KI"][:, qs], r=["sKI"], w=["sW0"])
                else:
                    OP("dve", "memset", ap=Z[:, :, i0:i0 + 1], constant=1.0, w=["sZ0"])
                    OP("dve", "memset", ap=W[:, :, i0:i0 + 1], constant=0.0, w=["sW0"])
                OP("dve", "tensor_copy", out=SM["EC"][:, 0:16], in_=SM["CS"][:, qs], r=["sCS"], w=["sEC"])
                if which == 0:
                    OP("dve", "tensor_scalar", out=SM["ES"][:, 0:16], in0=SM["SN"][:, qs], scalar1=-1.0, scalar2=None, op0=ALU.mult,
                       r=["sSN"], w=["sES"])
                else:
                    OP("dve", "tensor_copy", out=SM["ES"][:, 0:16], in_=SM["SN"][:, qs], r=["sSN"], w=["sES"])
                n = 1
                while n < TC:
                    if d == 0:
                        src, dst = slice(0, n), slice(n, 2 * n)
                    else:
                        src, dst = slice(TC - n, TC), slice(TC - 2 * n, TC - n)
                    ecb = SM["EC"][:, 0:16].unsqueeze(2).to_broadcast([128, 16, n])
                    esb = SM["ES"][:, 0:16].unsqueeze(2).to_broadcast([128, 16, n])
                    OP("dve", "tensor_tensor", out=ZA[:, :, :].rearrange("p a b -> p (a b)")[:, 0:16 * n].rearrange("p (g n) -> p g n", n=n),
                       in0=Z[:, :, src], in1=ecb, op=ALU.mult, r=["sZ0", "sEC"], w=["sZA0"])
                    OP("dve", "tensor_tensor", out=ZB[:, :, :].rearrange("p a b -> p (a b)")[:, 0:16 * n].rearrange("p (g n) -> p g n", n=n),
                       in0=W[:, :, src], in1=esb, op=ALU.mult, r=["sW0", "sES"], w=["sZB0"])
                    OP("dve", "tensor_tensor", out=Z[:, :, dst],
                       in0=ZA[:, :, :].rearrange("p a b -> p (a b)")[:, 0:16 * n].rearrange("p (g n) -> p g n", n=n),
                       in1=ZB[:, :, :].rearrange("p a b -> p (a b)")[:, 0:16 * n].rearrange("p (g n) -> p g n", n=n),
                       op=ALU.subtract, r=["sZA0", "sZB0"], w=["sZ0"])
                    OP("dve", "tensor_tensor", out=ZA[:, :, :].rearrange("p a b -> p (a b)")[:, 0:16 * n].rearrange("p (g n) -> p g n", n=n),
                       in0=W[:, :, src], in1=ecb, op=ALU.mult, r=["sW0", "sEC"], w=["sZA0"])
                    OP("dve", "tensor_tensor", out=ZB[:, :, :].rearrange("p a b -> p (a b)")[:, 0:16 * n].rearrange("p (g n) -> p g n", n=n),
                       in0=Z[:, :, src], in1=esb, op=ALU.mult, r=["sZ0", "sES"], w=["sZB0"])
                    OP("dve", "tensor_tensor", out=W[:, :, dst],
                       in0=ZA[:, :, :].rearrange("p a b -> p (a b)")[:, 0:16 * n].rearrange("p (g n) -> p g n", n=n),
                       in1=ZB[:, :, :].rearrange("p a b -> p (a b)")[:, 0:16 * n].rearrange("p (g n) -> p g n", n=n),
                       op=ALU.add, r=["sZA0", "sZB0"], w=["sW0"])
                    OP("dve", "tensor_tensor", out=SM["EC2"][:, 0:16], in0=SM["EC"][:, 0:16], in1=SM["EC"][:, 0:16], op=ALU.mult,
                       r=["sEC"], w=["sEC2"])
                    OP("dve", "tensor_tensor", out=SM["ES2"][:, 0:16], in0=SM["ES"][:, 0:16], in1=SM["ES"][:, 0:16], op=ALU.mult,
                       r=["sES"], w=["sES2"])
                    OP("dve", "tensor_tensor", out=SM["ES"][:, 0:16], in0=SM["ES"][:, 0:16], in1=SM["EC"][:, 0:16], op=ALU.mult,
                       r=["sES", "sEC"], w=["sES"])
                    OP("dve", "tensor_scalar", out=SM["ES"][:, 0:16], in0=SM["ES"][:, 0:16], scalar1=2.0, scalar2=None, op0=ALU.mult,
                       r=["sES"], w=["sES"])
                    OP("dve", "tensor_tensor", out=SM["EC"][:, 0:16], in0=SM["EC2"][:, 0:16], in1=SM["ES2"][:, 0:16], op=ALU.subtract,
                       r=["sEC2", "sES2"], w=["sEC"])
                    n *= 2
                if which == 0:
                    OP("dve", "tensor_copy", out=TAB[0][:], in_=Z[:], r=["sZ0"], w=["sTAB0"])
                    OP("dve", "tensor_scalar", out=TAB[1][:], in0=W[:], scalar1=SGN[:, 0:1], scalar2=None, op0=ALU.mult,
                       r=["sW0", "sSGN"], w=["sTAB1"])
                else:
                    OP("dve", "tensor_scalar", out=TAB[2][:], in0=Z[:], scalar1=SGN[:, 1:2], scalar2=None, op0=ALU.mult,
                       r=["sZ0", "sSGN"], w=["sTAB2"])
                    OP("dve", "tensor_scalar", out=TAB[3][:], in0=W[:], scalar1=-1.0, scalar2=None, op0=ALU.mult,
                       r=["sW0"], w=["sTAB3"])
                    OP("dve", "tensor_copy", out=SM["EC2"][:, 0:16], in_=SM["EC"][:, 0:16], r=["sEC"], w=["sEC2"])
                    OP("dve", "tensor_copy", out=SM["ES2"][:, 0:16], in_=SM["ES"][:, 0:16], r=["sES"], w=["sES2"])
            p.barrier()
            OP("dve", "memset", ap=INIT[:], constant=0.0, w=["sINIT"])
            order = list(range(nchunk)) if d == 0 else list(range(NCX // TC - 1, -1, -1)) + list(range(nchunk - 1, NCX // TC - 1, -1))
            allw = lambda pr: ["sW%d_%d" % (pr, gi) for gi in range(16)]
            def stage1(ci):
                m = order[ci]
                t0 = m * TC
                cp = ci % 2
                Zc = Zs[cp]
                for cc in range(2):
                    par = cc
                    b1, b2 = PS[0 + par], PS[2 + par]
                    k1, k2 = "ps%d" % (0 + par), "ps%d" % (2 + par)
                    za, zb = ZAs[par], ZBs[par]
                    zak, zbk = "sZA%d" % par, "sZB%d" % par
                    zk = "sZ%d_%d" % (cp, cc)
                    for gg in range(8):
                        gi = 8 * cc + gg
                        MM(b1[:, gg * TC:(gg + 1) * TC], BP[0][:, gi, :], UT[:, cc, t0:t0 + TC], True, True, ["sBP0", "sUT"], [k1])
                        MM(b2[:, gg * TC:(gg + 1) * TC], BP[1][:, gi, :], UT[:, cc, t0:t0 + TC], True, True, ["sBP1", "sUT"], [k2])
                    gsl = slice(8 * cc, 8 * cc + 8)
                    OP("dve", "tensor_tensor", out=za[:], in0=b1[:, :].rearrange("p (g n) -> p g n", n=TC), in1=TAB[0][:, gsl, :],
                       op=ALU.mult, r=[k1, "sTAB0"], w=[zak])
                    OP("dve", "tensor_tensor", out=zb[:], in0=b2[:, :].rearrange("p (g n) -> p g n", n=TC), in1=TAB[1][:, gsl, :],
                       op=ALU.mult, r=[k2, "sTAB1"], w=[zbk])
                    OP("pool", "tensor_tensor", out=Zc[:, gsl, :], in0=za[:], in1=zb[:], op=ALU.add, r=[zak, zbk], w=[zk])

            def stage2(ci):
                m = order[ci]
                t0 = m * TC
                cp = ci % 2
                Zc, Wc = Zs[cp], Ws[cp]
                for cc in range(2):
                    par = cc
                    by, ky = PS[4 + par], "ps%d" % (4 + par)
                    hc, hs = HCs[par], HSs[par]
                    hck, hsk = "sHC%d" % par, "sHS%d" % par
                    zk = "sZ%d_%d" % (cp, cc)
                    gsl = slice(8 * cc, 8 * cc + 8)
                    wks = ["sW%d_%d" % (cp, 8 * cc + gg) for gg in range(8)]
                    for gg in range(8):
                        gi = 8 * cc + gg
                        q = 16 * d + gi
                        rho_b = SM["RHO"][:, q:q + 1].to_broadcast([128, TC])
                        if d == 0:
                            zin, wout = Zc[:, gi, :], Wc[:, gi, :]
                        else:
                            zin, wout = Zc[:, gi, ::-1], Wc[:, gi, ::-1]
                        OP("dve", "tensor_tensor_scan", out=wout, data0=rho_b, data1=zin, initial=INIT[:, gi:gi + 1], op0=ALU.mult,
                           op1=ALU.add, r=[zk, "sRHO", "sINIT"], w=[wks[gg]])
                    OP("pool", "tensor_tensor", out=hc[:], in0=Wc[:, gsl, :], in1=TAB[2][:, gsl, :], op=ALU.mult, r=wks + ["sTAB2"],
                       w=[hck])
                    OP("pool", "tensor_tensor", out=hs[:], in0=Wc[:, gsl, :], in1=TAB[3][:, gsl, :], op=ALU.mult, r=wks + ["sTAB3"],
                       w=[hsk])
                    for gg in range(8):
                        gi = 8 * cc + gg
                        MM(by[:, 0:TC], CP[0][:, gi, :], hc[:, gg, :], gg == 0, False, ["sCP0", hck], [ky])
                        MM(by[:, 0:TC], CP[1][:, gi, :], hs[:, gg, :], False, gg == 7, ["sCP1", hsk], [ky])
                    if d == 0:
                        OP("act", "activation", out=YF[:, cc, t0:t0 + TC], in_=by[:, 0:TC], func=AF.Copy, r=[ky], w=["sYF"])
                    else:
                        tmy, tmk = TMPYs[par], "sTMPY%d" % par
                        OP("dve", "scalar_tensor_tensor", out=tmy[:], in0=UT[:, cc, t0:t0 + TC], scalar=DSK[:, l, cc:cc + 1],
                           in1=YF[:, cc, t0:t0 + TC], op0=ALU.mult, op1=ALU.add, r=["sUT", "sDSK", "sYF"], w=[tmk])
                        OP("dve", "tensor_tensor", out=YF[:, cc, t0:t0 + TC], in0=tmy[:], in1=by[:, 0:TC], op=ALU.add,
                           r=[tmk, ky], w=["sYF"])
                ecol = TC - 1 if d == 0 else 0
                OP("dve", "tensor_copy", out=ENDS[:], in_=Wc[:, :, ecol], r=allw(cp), w=["sENDS"])
                MM(PS[6][:, 0:16], JM[:], ENDS[:], True, True, ["sJM", "sENDS"], ["ps6"])
                OP("dve", "tensor_tensor", out=RT1[:], in0=ENDS[:], in1=SM["EC2"][:, 0:16], op=ALU.mult, r=["sENDS", "sEC2"], w=["sRT1"])
                OP("dve", "tensor_tensor", out=ENDS[:], in0=PS[6][:, 0:16], in1=SM["ES2"][:, 0:16], op=ALU.mult, r=["ps6", "sES2"],
                   w=["sENDS"])
                OP("dve", "tensor_tensor", out=INIT[:], in0=RT1[:], in1=ENDS[:], op=ALU.add, r=["sRT1", "sENDS"], w=["sINIT"])

            stage1(0)
            for ci in range(len(order)):
                if ci + 1 < len(order):
                    stage1(ci + 1)
                stage2(ci)
            p.barrier()
        for (a, b) in TB:
            w = b - a
            ZT = ZA[:, :, :].rearrange("p a b -> p (a b)")
            for kc in range(2):
                OP("act", "activation", out=HC[:, :, :].rearrange("p a b -> p (a b)")[:, :w] if kc == 0 else
                   HS[:, :, :].rearrange("p a b -> p (a b)")[:, :w], in_=YF[:, kc, a:b], func=AF.Gelu_apprx_tanh, r=["sYF"],
                   w=["sHC0" if kc == 0 else "sHS0"])
            zt = [HC[:, :, :].rearrange("p a b -> p (a b)"), HS[:, :, :].rearrange("p a b -> p (a b)")]
            for oc in range(2):
                for kc in range(2):
                    MM(PS[7][:, :w], WGL[:, kc, oc * 128:(oc + 1) * 128], zt[kc][:, :w], kc == 0, kc == 1,
                       ["sWGL", "sHC0", "sHS0"], ["ps7"])
                OP("act", "activation", out=ZT[:, :w], in_=PS[7][:, :w], func=AF.Sigmoid, r=["ps7"], w=["sZA0"])
                OP("dve", "tensor_tensor", out=YT[:, oc, a:b], in0=zt[oc][:, :w], in1=ZT[:, :w], op=ALU.mult,
                   r=["sHC0", "sHS0", "sZA0"], w=["YT"])


def gla_mixer(g, bi, l):
    nc, p, OP, MM, DMA = g.nc, g.p, g.OP, g.MM, g.DMA
    PS, HT, YT = g.PS, g.HT, g.YT
    wkey = g.wkeys("w_in_bf", l, 8)
    with contextlib.ExitStack() as sm:
        def sb(name, shape, dt):
            return sm.enter_context(nc.sbuf_tensor(UN() + name, list(shape), dt))
        QT = sb("gQT", [128, NT], BF16)
        KT = sb("gKT", [128, NT], BF16)
        SR = sb("gSR", [128, NT], BF16)
        KTK = sb("gKTK", [128, 18, 128], BF16)
        VTK = sb("gVTK", [128, 18, 128], BF16)
        OF = sb("gOF", [128, NT], F32)
        GA = [sb("gGA%d" % d, [32, NT], BF16) for d in range(2)]
        WG = sb("gWG", [32, 2, 256], BF16)
        TRI = sb("gTRI", [128, 2, 128], F32)
        BD = sb("gBD", [128, 128], BF16)
        GN = sb("gGN", [128, LD[0]], F32)
        ONE = sb("gONE", [128, 1], F32)
        EPS_ = sb("gEPS", [128, 1], F32)
        WB = [sb("gWB%d" % i, [128, 8, 128], BF16) for i in range(2)]
        WGB = sb("gWGB", [128, 8, 32], BF16)
        EX = sb("gEX", [128, 128], F32)
        NEG = EX
        E1s = [sb("gE1%d" % i, [128, 128], F32) for i in range(2)]
        OT = sb("gOT", [128, 128], F32)
        DECPs = [sb("gDECP%d" % i, [128, 1], F32) for i in range(2)]
        E2 = sb("gE2", [128, 128], F32)
        E2T = sb("gE2T", [128, 128], F32)
        QDs = [sb("gQD%d" % i, [128, 128], BF16) for i in range(2)]
        KD = sb("gKD", [128, 128], BF16)
        KDTs = [sb("gKDT%d" % i, [128, 128], BF16) for i in range(2)]
        AMs = [[sb("gAM%d_%d" % (i, hh), [128, 128], BF16) for hh in range(2)] for i in range(2)]
        S = sb("gS", [128, 64], F32)
        Ss = [S, sb("gS1", [128, 64], F32)]
        SB_ = sb("gSB", [128, 64], BF16)
        SQ = sb("gSQ", [128, 512], BF16)
        RS = sb("gRS", [128, 512], F32)
        OP("dve", "memset", ap=ONE[:], constant=1.0, w=["gONE"])
        OP("dve", "memset", ap=EPS_[:], constant=EPS, w=["gEPS"])
        DMA("sp", TRI[:], g.dram["tri"], w=["gTRI"])
        DMA("sp", BD[:], g.dram["bd64"], w=["gBD"])
        DMA("sp", GN[:], g.dram["gnT"], w=["gGN"])
        for d in range(2):
            DMA("pool", WG[0:16, d, :], g.dram["gla_w_gate2"][l, d], w=["gWG"])
            DMA("pool", WG[16:17, d, :], g.dram["gla_b_gate"][l, d:d + 1, :], w=["gWG"])
        wb_it = [0]

        def load_w(c0):
            i = wb_it[0] % 2
            wb_it[0] += 1
            DMA("sp", WB[i][:], g.w_in_bf[l, :, c0:c0 + 128].rearrange("(k p) c -> p k c", p=128), r=wkey, w=["gWB%d" % i])
            return WB[i], "gWB%d" % i

        DMA("sp", WGB[:], g.w_in_bf[l, :, C_GF:C_GF + 32].rearrange("(k p) c -> p k c", p=128), r=wkey, w=["gWGB"])
        for d in range(2):
            OP("pool", "memset", ap=GA[d][:], constant=1.0, w=["gGA%d" % d])
            for (a, b) in TB:
                w = b - a
                for k in range(8):
                    MM(PS[0][0:16, :w], WGB[:, k, 16 * d:16 * d + 16], HT[:, k, a:b], k == 0, k == 7, ["gWGB", "HT"], ["ps0"])
                OP("act", "activation", out=GA[d][0:16, a:b], in_=PS[0][0:16, :w], func=AF.Copy, r=["ps0"], w=["gGA%d" % d])
        for c in range(2):
            for (dst, dk, c0, fn, sc) in ((QT, "gQT", C_GQ + 128 * c, AF.Copy, 0.125), (KT, "gKT", C_GK + 128 * c, AF.Copy, 1.0),
                                          (SR, "gSR", C_GR + 128 * c, AF.Silu, 1.0)):
                w1, w1k = load_w(c0)
                for (a, b) in TB:
                    w = b - a
                    for k in range(8):
                        MM(PS[0][:, :w], w1[:, k, :], HT[:, k, a:b], k == 0, k == 7, [w1k, "HT"], ["ps0"])
                    OP("act", "activation", out=dst[:, a:b], in_=PS[0][:, :w], func=fn, scale=sc, r=["ps0"], w=[dk])
            for (dst, dk, c0) in ((KTK, "gKTK", C_GK + 128 * c), (VTK, "gVTK", C_GV + 128 * c)):
                wv, wvk = load_w(c0)
                for t4 in range(0, 18, 4):
                    nt = min(4, 18 - t4)
                    for ti in range(nt):
                        tt = t4 + ti
                        for k in range(8):
                            MM(PS[1][:, ti * 128:(ti + 1) * 128], HT[:, k, tt * 128:(tt + 1) * 128], wv[:, k, :], k == 0, k == 7,
                               [wvk, "HT"], ["ps1"])
                    OP("act", "activation", out=dst[:, t4:t4 + nt, :],
                       in_=PS[1][:, 0:nt * 128].rearrange("p (t c) -> p t c", c=128), func=AF.Copy, r=["ps1"], w=[dk])
            for d in range(2):
                OP("dve", "memset", ap=SB_[:], constant=0.0, w=["gSB"])
                tiles = list(range(18)) if d == 0 else [1, 0] + list(range(17, 1, -1))
                chunks = (0, 1) if d == 0 else (1, 0)
                def prep(i):
                    tt = tiles[i]
                    t0 = tt * 128
                    pr = i % 2
                    e1, qd, kdt = E1s[pr], QDs[pr], KDTs[pr]
                    e1k, qdk, kdtk = "gE1%d" % pr, "gQD%d" % pr, "gKDT%d" % pr
                    MM(PS[2][:, 0:128], GA[d][0:17, t0:t0 + 128], WG[0:17, d, 128 * c:128 * c + 128], True, True,
                       ["gGA%d" % d, "gWG"], ["ps2"])
                    OP("act", "activation", out=EX[:], in_=PS[2][:, 0:128], func=AF.Exp, scale=-1.0, r=["ps2"], w=["gEX"])
                    OP("act", "activation", out=NEG[:], in_=EX[:], func=AF.Ln, bias=ONE[:, 0:1], scale=1.0, r=["gEX", "gONE"],
                       w=["gEX"])
                    MM(PS[3][:, 0:128], NEG[:], TRI[:, d, :], True, True, ["gEX", "gTRI"], ["ps3"])
                    MM(PS[3][:, 128:256], TRI[:, d, :], NEG[:], True, True, ["gEX", "gTRI"], ["ps3"])
                    OP("act", "activation", out=e1[:], in_=PS[3][:, 0:128], func=AF.Exp, scale=-1.0 / 16, r=["ps3"], w=[e1k])
                    OP("act", "activation", out=E2[:], in_=PS[3][:, 0:128], func=AF.Exp, scale=1.0 / 16, r=["ps3"], w=["gE2"])
                    OP("act", "activation", out=E2T[:], in_=PS[3][:, 128:256], func=AF.Exp, scale=1.0 / 16, r=["ps3"], w=["gE2T"])
                    OP("dve", "tensor_tensor", out=qd[:], in0=QT[:, t0:t0 + 128], in1=e1[:], op=ALU.mult, r=["gQT", e1k], w=[qdk])
                    OP("pool", "tensor_tensor", out=KD[:], in0=KT[:, t0:t0 + 128], in1=E2[:], op=ALU.mult, r=["gKT", "gE2"], w=["gKD"])
                    OP("pool", "tensor_tensor", out=kdt[:], in0=KTK[:, tt, :], in1=E2T[:], op=ALU.mult, r=["gKTK", "gE2T"],
                       w=[kdtk])
                    for hh in range(2):
                        hb = 64 * hh
                        am, amk = AMs[pr][hh], "gAM%d_%d" % (pr, hh)
                        MM(PS[4 + hh][:, 0:128], KD[hb:hb + 64, :], qd[hb:hb + 64, :], True, True, ["gKD", qdk], ["ps%d" % (4 + hh)])
                        OP("dve", "tensor_tensor", out=am[:], in0=PS[4 + hh][:, 0:128], in1=TRI[:, d, :], op=ALU.mult,
                           r=["ps%d" % (4 + hh), "gTRI"], w=[amk])

                def recur(i):
                    tt = tiles[i]
                    t0 = tt * 128
                    pr = i % 2
                    e1, qd, kdt = E1s[pr], QDs[pr], KDTs[pr]
                    e1k, qdk, kdtk = "gE1%d" % pr, "gQD%d" % pr, "gKDT%d" % pr
                    for hh in range(2):
                        hb = 64 * hh
                        am, amk = AMs[pr][hh], "gAM%d_%d" % (pr, hh)
                        MM(PS[6][hb:hb + 64, 0:128], VTK[:, tt, hb:hb + 64], am[:], True, False, ["gVTK", amk], ["ps6"])
                    for ci_, ch in enumerate(chunks):
                        cs = 64 * ch
                        first = (i == 0 and ci_ == 0)
                        kb = 7 if ci_ == 0 else 2
                        for hh in range(2):
                            hb = 64 * hh
                            MM(PS[6][hb:hb + 64, cs:cs + 64], SB_[hb:hb + 64, :], qd[hb:hb + 64, cs:cs + 64], False, ci_ == 1,
                               ["gSB", qdk], ["ps6"])
                            MM(PS[7][hb:hb + 64, 64 * ci_:64 * ci_ + 64], kdt[cs:cs + 64, hb:hb + 64], VTK[cs:cs + 64, tt, hb:hb + 64],
                               True, True, [kdtk, "gVTK"], ["ps7"])
                        dcol = cs + 63 if d == 0 else cs
                        gn = 2 * i + ci_
                        Sc, Sp = Ss[gn % 2], Ss[(gn + 1) % 2]
                        sck, spk = "gS%d" % (gn % 2), "gS%d" % ((gn + 1) % 2)
                        dcp, dck = DECPs[gn % 2], "gDECP%d" % (gn % 2)
                        dpp, dpk = DECPs[(gn + 1) % 2], "gDECP%d" % ((gn + 1) % 2)
                        if first:
                            OP("dve", "tensor_copy", out=Sc[:], in_=PS[7][:, 64 * ci_:64 * ci_ + 64], r=["ps7"], w=[sck])
                        else:
                            OP("dve", "scalar_tensor_tensor", out=Sc[:], in0=Sp[:], scalar=dpp[:, 0:1], in1=PS[7][:, 64 * ci_:64 * ci_ + 64],
                               op0=ALU.mult, op1=ALU.add, r=[spk, dpk, "ps7"], w=[sck])
                        OP("act", "activation", out=SB_[:], in_=Sc[:], func=AF.Identity, scale=e1[:, dcol:dcol + 1], r=[sck, e1k],
                           w=["gSB"])
                        OP("pool", "tensor_copy", out=dcp[:, 0:1], in_=e1[:, dcol:dcol + 1], r=[e1k], w=[dck])
                    if d == 0:
                        OP("act", "activation", out=OF[:, t0:t0 + 128], in_=PS[6][:, 0:128], func=AF.Copy, r=["ps6"], w=["gOF"])
                    else:
                        OP("pool", "tensor_copy", out=OT[:], in_=OF[:, t0:t0 + 128], r=["gOF"], w=["gOT"])
                        OP("dve", "tensor_tensor", out=OF[:, t0:t0 + 128], in0=OT[:], in1=PS[6][:, 0:128], op=ALU.add,
                           r=["gOT", "ps6"], w=["gOF"])

                prep(0)
                for i in range(len(tiles)):
                    if i + 1 < len(tiles):
                        prep(i + 1)
                    recur(i)
            for (a, b) in TB:
                w = b - a
                OP("act", "activation", out=SQ[:, :w], in_=OF[:, a:b], func=AF.Square, r=["gOF"], w=["gSQ"])
                MM(PS[0][:, :w], BD[:], SQ[:, :w], True, True, ["gBD", "gSQ"], ["ps0"])
                OP("act", "activation", out=RS[:, :w], in_=PS[0][:, :w], func=AF.Sqrt, scale=1.0 / 64, bias=EPS_[:, 0:1],
                   r=["ps0", "gEPS"], w=["gRS"])
                OP("dve", "reciprocal", out=RS[:, :w], in_=RS[:, :w], r=["gRS"], w=["gRS"])
                OP("dve", "scalar_tensor_tensor", out=RS[:, :w], in0=OF[:, a:b], scalar=GN[:, l:l + 1], in1=RS[:, :w], op0=ALU.mult,
                   op1=ALU.mult, r=["gOF", "gGN", "gRS"], w=["gRS"])
                OP("pool", "tensor_tensor", out=YT[:, 4 + c, a:b], in0=RS[:, :w], in1=SR[:, a:b], op=ALU.mult, r=["gRS", "gSR"],
                   w=["YT"])


def host_prep(inputs):
    f = np.float32
    w_in = np.asarray(inputs["w_in"], f)
    sq = w_in[:, :, C_SQ:C_SQ + 256]
    sk = w_in[:, :, C_SK:C_SK + 128]
    dup = np.concatenate([np.arange(64), np.arange(64), 64 + np.arange(64), 64 + np.arange(64)])
    w_in_ext = np.concatenate([w_in, sq[:, :, _rope_perm(4)], sk[:, :, dup], sk[:, :, _rope_perm(2)][:, :, dup]], axis=2)
    assert w_in_ext.shape[2] == NEXT
    rc, rs = _rope_tables()
    kl = np.arange(128)[:, None]
    ql = np.arange(128)[None, :]
    maskAB = np.stack([(kl <= ql), (ql <= kl)], 1).astype(f)
    gv = np.stack([inputs["g_pre_mix"], inputs["g_post_mix"], inputs["g_pre_ffn"], inputs["g_post_ffn"]], 1)
    gvec = np.ascontiguousarray(np.asarray(gv, f).reshape(DEPTH, 4, 8, 128).transpose(3, 0, 1, 2))
    b_modT = np.ascontiguousarray(np.asarray(inputs["b_mod"], f).reshape(DEPTH, 48, 128).transpose(2, 0, 1))
    convT = np.ascontiguousarray(np.asarray(inputs["ffn_conv"], f).reshape(DEPTH, 3, 22, 128).transpose(3, 0, 1, 2))
    sink = np.asarray(inputs["swa_sink"], f)
    sinkT = np.zeros((128, DEPTH, 2), f)
    for c in range(2):
        sinkT[0:64, :, c] = sink[None, :, 2 * c]
        sinkT[64:128, :, c] = sink[None, :, 2 * c + 1]
    rpb = np.asarray(inputs["na_rpb"], f)
    kc = np.arange(64)[:, None]
    qc = np.arange(64)[None, :]
    dcx = np.clip(kc - qc, -15, 15) + 15
    na_exp = rpb[:, :, ::-1, :][:, :, :, dcx]
    na_exp = np.ascontiguousarray(na_exp.transpose(0, 1, 3, 2, 4)).reshape(DEPTH, 4, 64, 15 * 64)
    ws = np.clip(np.arange(64) - 8, 0, 48)
    mc = ((kc >= ws[None, :]) & (kc < ws[None, :] + 16)).astype(f)
    mcol = np.tile(np.tile(mc[:, None, :], (1, 15, 1)).reshape(64, 960), (2, 1))
    jj = np.arange(128)[:, None]
    ii = np.arange(128)[None, :]
    same = (jj // 64) == (ii // 64)
    tri = np.stack([(same & (jj <= ii)), (same & (jj >= ii))], 1).astype(f)
    bd64 = same.astype(f)
    gnT = np.ascontiguousarray(np.tile(np.asarray(inputs["gla_g_norm"], f).T, (2, 1)))
    L = DEPTH
    lre = np.asarray(inputs["s5_lam_re"], f).reshape(L, 32, 64)
    lim = np.asarray(inputs["s5_lam_im"], f).reshape(L, 32, 64)
    lam = np.stack([lre, lim], 1)
    lamT = np.ascontiguousarray(np.tile(lam.transpose(3, 0, 1, 2), (2, 1, 1, 1)))
    stepT = np.ascontiguousarray(np.broadcast_to(np.asarray(inputs["s5_log_step"], f).reshape(L, 32)[None], (128, L, 32)))
    dskT = np.ascontiguousarray(np.asarray(inputs["s5_d"], f).reshape(L, 2, 128).transpose(2, 0, 1))
    sgn = np.ones((128, 2), f)
    sgn[0:64, 0] = -1.0
    sgn[64:128, 1] = -1.0
    jmat = np.zeros((128, 128), f)
    for sp in range(64):
        jmat[sp + 64, sp] = -1.0
        jmat[sp, sp + 64] = 1.0
    bre = np.asarray(inputs["s5_b_re"], f)
    bim = np.asarray(inputs["s5_b_im"], f)
    cre = np.asarray(inputs["s5_c_re"], f)
    cim = np.asarray(inputs["s5_c_im"], f)
    B1 = np.zeros((L, 2, 16, 128, 128), f)
    B2 = np.zeros((L, 2, 16, 128, 128), f)
    C1 = np.zeros((L, 2, 16, 128, 128), f)
    C2 = np.zeros((L, 2, 16, 128, 128), f)
    for gi in range(16):
        r0 = 16 * (gi % 8)
        B1[:, :, gi, r0:r0 + 16, 0:64] = bre[:, :, gi].transpose(0, 1, 3, 2)
        B1[:, :, gi, r0:r0 + 16, 64:128] = bim[:, :, gi].transpose(0, 1, 3, 2)
        B2[:, :, gi, r0:r0 + 16, 0:64] = bim[:, :, gi].transpose(0, 1, 3, 2)
        B2[:, :, gi, r0:r0 + 16, 64:128] = bre[:, :, gi].transpose(0, 1, 3, 2)
        C1[:, :, gi, 0:64, r0:r0 + 16] = cre[:, :, gi].transpose(0, 1, 3, 2)
        C1[:, :, gi, 64:128, r0:r0 + 16] = cim[:, :, gi].transpose(0, 1, 3, 2)
        C2[:, :, gi, 0:64, r0:r0 + 16] = cim[:, :, gi].transpose(0, 1, 3, 2)
        C2[:, :, gi, 64:128, r0:r0 + 16] = cre[:, :, gi].transpose(0, 1, 3, 2)
    pp = np.arange(128)
    idb2 = np.zeros((128, 2, 128), f)
    for hf in range(2):
        idb2[pp, hf, 64 * hf + pp % 64] = 1.0
    biasm = np.stack([np.where(kl <= ql, 0.0, -240000.0), np.where(ql <= kl, 0.0, -240000.0)], 0).astype(f)
    negm = np.zeros((128, 2, 2, 128), f)
    for hf in range(2):
        negm[:, :, hf, :] = biasm[:, 64 * hf + pp % 64, :].transpose(1, 0, 2)
    shared = {
        "lamT": lamT, "stepT": stepT, "dskT": dskT, "sgn": sgn, "jmat": jmat, "s5_w_glu": np.asarray(inputs["s5_w_glu"], f),
        "s5B1": B1, "s5B2": B2, "s5C1": C1, "s5C2": C2,
        "tri": tri, "bd64": bd64.astype(ml_dtypes.bfloat16), "gnT": gnT,
        "gla_w_gate2": np.asarray(inputs["gla_w_gate2"], f), "gla_b_gate": np.asarray(inputs["gla_b_gate"], f),
        "w_mod": np.asarray(inputs["w_mod"], f), "b_modT": b_modT, "gvec": gvec, "w_in_ext": np.ascontiguousarray(w_in_ext),
        "w_out": np.asarray(inputs["w_out"], f), "ffn_w_up": np.asarray(inputs["ffn_w_up"], f),
        "ffn_w_down": np.asarray(inputs["ffn_w_down"], f), "convT": convT,
        "ropeC": rc.astype(ml_dtypes.bfloat16), "ropeS": rs.astype(ml_dtypes.bfloat16),
        "maskAB": maskAB.astype(ml_dtypes.bfloat16), "identf": np.eye(128, dtype=f), "sinkT": sinkT,
        "na_exp": na_exp, "mcol": np.ascontiguousarray(mcol), "mc8": np.ascontiguousarray(8.0 * mcol),
        "mcn": np.ascontiguousarray((mcol - 1.0) * 240000.0),
        "idb": (np.arange(128)[:, None] % 64 == np.arange(64)[None, :]).astype(ml_dtypes.bfloat16),
        "idb2": idb2.astype(ml_dtypes.bfloat16), "negm": negm.astype(ml_dtypes.bfloat16),
    }
    x = np.asarray(inputs["x"], f)
    ctx = np.asarray(inputs["ctx"], f)
    c = np.asarray(inputs["c"], f)
    cc = np.asarray(inputs["c_ctx"], f)
    in_maps = []
    for core in range(8):
        b0 = 2 * core
        xcat = np.concatenate([ctx[b0:b0 + 2], x[b0:b0 + 2]], axis=1)
        cs = np.stack([c[b0], c[b0 + 1], cc], 0)
        cTm = np.ascontiguousarray(cs.reshape(3, 8, 128).transpose(2, 1, 0))
        m = dict(shared)
        m["xcat"] = np.ascontiguousarray(xcat)
        m["cT"] = cTm
        in_maps.append(m)
    return in_maps


L_FIRST = ("w_mod", "w_in_ext", "w_out", "ffn_w_up", "ffn_w_down", "na_exp", "s5B1", "s5B2", "s5C1", "s5C2", "s5_w_glu",
           "gla_w_gate2", "gla_b_gate")
L_SECOND = ("b_modT", "gvec", "convT", "sinkT", "lamT", "stepT", "dskT")

FUSED = True


def kernel(**inputs):
    in_maps = host_prep(inputs)
    if FUSED:
        nc = bass.Bass("TRN2", target_bir_lowering=False)
        build(nc)
        res = run_bass_kernel_spmd(nc, in_maps, core_ids=list(range(8)))
        outs = [r["out"] for r in res.results]
        return np.concatenate(outs, axis=0).astype(np.float32)
    cur = [m["xcat"] for m in in_maps]
    for l in range(DEPTH):
        base = {}
        m0 = in_maps[0]
        for k, v in m0.items():
            if k in ("xcat", "cT"):
                continue
            if k in L_FIRST:
                base[k] = np.ascontiguousarray(v[l:l + 1])
            elif k in L_SECOND:
                base[k] = np.ascontiguousarray(v[:, l:l + 1])
            elif k == "gnT":
                base[k] = np.ascontiguousarray(v[:, l:l + 1])
            else:
                base[k] = v
        for bi in range(2):
            maps = []
            for core in range(8):
                m = dict(base)
                m["xcat"] = np.ascontiguousarray(cur[core][[bi, 1 - bi]])
                cTm = in_maps[core]["cT"]
                m["cT"] = np.ascontiguousarray(cTm[:, :, [bi, 1 - bi, 2]])
                maps.append(m)
            nc = bass.Bass("TRN2", target_bir_lowering=False)
            build(nc, nlayers=1, nbatch=1, ldim=1, full_out=True)
            res = run_bass_kernel_spmd(nc, maps, core_ids=list(range(8)))
            for core in range(8):
                new = np.array(cur[core])
                new[bi] = res.results[core]["out"][0]
                cur[core] = new
    outs = [c[:, NCX:, :] for c in cur]
    return np.concatenate(outs, axis=0).astype(np.float32)
```
